# Optimizing a Trainium2 kernel written in Bass

```python
import math
import jax, jax.numpy as jnp
from jax import lax
import numpy as np

D_MODEL = 1024
BATCH = 8
SEQ = 2048
DEPTH = 4
DEC_BATCH = 128
DEC_SEQ = 1
PAST_LEN = 16384
PAGE_SIZE = 128

N_MIXERS = 3
N_POOL_LAYERS = (DEPTH + 2) // 3
N_GDN_LAYERS = (DEPTH + 1) // 3
N_RET_LAYERS = DEPTH // 3

POOL_WINDOWS = (2, 4, 8, 16)
N_POOL_GROUPS = len(POOL_WINDOWS)
POOL_GROUP = D_MODEL // N_POOL_GROUPS
POOL_BUF = max(POOL_WINDOWS) - 1

GDN_HEADS = 8
GDN_DK = D_MODEL // GDN_HEADS
GDN_DV = D_MODEL // GDN_HEADS
GDN_CONV = 4
GDN_CONV_CH = GDN_HEADS * (2 * GDN_DK + GDN_DV)
GDN_IN = GDN_CONV_CH + GDN_HEADS * GDN_DV + 2 * GDN_HEADS

RET_HEADS = 8
RET_DK = D_MODEL // RET_HEADS
RET_DV = 2 * RET_DK
RET_IN = 2 * RET_HEADS * RET_DK + 2 * RET_HEADS * RET_DV
ROPE_BASE = 10000.0

D_FF = -(-8 * D_MODEL // (3 * 256)) * 256
CHUNK = 64
DN_ALPHA = (2 * DEPTH) ** 0.25
DN_BETA = (8 * DEPTH) ** -0.25
LN_EPS = 1e-5
RMS_EPS = 1e-6

kernel_name = 'hybrid_pool_gdn_retention_decode_step'

F32 = jnp.float32


def chunk_len(T):
    return CHUNK if T % CHUNK == 0 else T


def layer_norm(x, g, b):
    x32 = x.astype(F32)
    mu = jnp.mean(x32, -1, keepdims=True)
    var = jnp.mean(jnp.square(x32 - mu), -1, keepdims=True)
    return ((x32 - mu) * lax.rsqrt(var + LN_EPS) * g.astype(F32) + b.astype(F32)).astype(x.dtype)


def swiglu(x, w13, w2):
    a, b = jnp.split(x @ w13, 2, axis=-1)
    return (jax.nn.silu(a) * b) @ w2


def l2norm(x):
    return x * lax.rsqrt(jnp.sum(x * x, -1, keepdims=True) + RMS_EPS)


def rope(x, pos):
    half = x.shape[-1] // 2
    freqs = ROPE_BASE ** (-jnp.arange(half, dtype=F32) / half)
    ang = pos[:, None] * freqs[None, :]
    cos = jnp.cos(ang)[None, :, None, :]
    sin = jnp.sin(ang)[None, :, None, :]
    x1, x2 = x[..., :half], x[..., half:]
    return jnp.concatenate([x1 * cos - x2 * sin, x1 * sin + x2 * cos], axis=-1)


def pool_mixer(x, buf, n_valid, w, scale):
    T = x.shape[1]
    P = POOL_BUF
    ext = jnp.concatenate([buf.astype(x.dtype), x], axis=1)
    c = jnp.pad(jnp.cumsum(ext.astype(F32), axis=1), ((0, 0), (1, 0), (0, 0)))
    t = jnp.arange(T)
    means = []
    for gi, win in enumerate(POOL_WINDOWS):
        lo, hi = gi * POOL_GROUP, (gi + 1) * POOL_GROUP
        s = c[:, P + 1:P + 1 + T, lo:hi] - c[:, P + 1 - win:P + 1 - win + T, lo:hi]
        cnt = jnp.minimum(t + 1 + n_valid, win).astype(F32)
        means.append(s / cnt[None, :, None])
    pooled = jnp.concatenate(means, axis=-1) - x.astype(F32)
    B = x.shape[0]
    pg = pooled.reshape(B, T, N_POOL_GROUPS, POOL_GROUP)
    out = jnp.einsum('btgc,gcd->btgd', pg, w.astype(F32)).reshape(B, T, D_MODEL)
    return (out * scale.astype(F32)).astype(x.dtype), ext[:, -P:]


def causal_conv(u, buf, w):
    T = u.shape[1]
    ext = jnp.concatenate([buf.astype(u.dtype), u], axis=1)
    y = ext[:, 0:T] * w[0]
    for j in range(1, GDN_CONV):
        y = y + ext[:, j:j + T] * w[j]
    return jax.nn.silu(y), ext[:, -(GDN_CONV - 1):]


def gated_delta_rule(q, k, v, g, beta, S0):
    B, T, H, dk = q.shape
    dv = v.shape[-1]
    C = chunk_len(T)
    N = T // C

    def blocks(a):
        a = jnp.moveaxis(a, 2, 1)
        return a.reshape((B, H, N, C) + a.shape[3:])

    q, k, v, g, beta = blocks(q), blocks(k), blocks(v), blocks(g), blocks(beta)
    gc = jnp.cumsum(g, axis=-1)
    lower = jnp.tril(jnp.ones((C, C), bool))
    strict = jnp.tril(jnp.ones((C, C), bool), -1)
    diff = gc[..., :, None] - gc[..., None, :]
    decay = jnp.where(lower, jnp.exp(jnp.where(lower, diff, 0.0)), 0.0)
    kb = k * beta[..., None]
    Lm = jnp.where(strict, jnp.einsum('bhnid,bhnjd->bhnij', kb, k) * decay, 0.0)
    eye = jnp.eye(C, dtype=F32)
    Tinv = lax.linalg.triangular_solve(eye + Lm, jnp.broadcast_to(eye, Lm.shape),
                                       left_side=True, lower=True, unit_diagonal=True)
    u = jnp.einsum('bhnij,bhnje->bhnie', Tinv, v * beta[..., None])
    wk = jnp.einsum('bhnij,bhnjd->bhnid', Tinv, kb * jnp.exp(gc)[..., None])
    attn = jnp.einsum('bhnid,bhnjd->bhnij', q, k) * decay
    qg = q * jnp.exp(gc)[..., None]
    kg = k * jnp.exp(gc[..., -1:] - gc)[..., None]
    glast = jnp.exp(gc[..., -1])
    xs = tuple(jnp.moveaxis(a, 2, 0) for a in (u, wk, attn, qg, kg, glast))

    def step(S, xn):
        u_n, w_n, a_n, qg_n, kg_n, gl_n = xn
        v_new = u_n - jnp.einsum('bhcd,bhde->bhce', w_n, S)
        o_n = jnp.einsum('bhcd,bhde->bhce', qg_n, S) + jnp.einsum('bhij,bhje->bhie', a_n, v_new)
        S = S * gl_n[..., None, None] + jnp.einsum('bhcd,bhce->bhde', kg_n, v_new)
        return S, o_n

    S, o = lax.scan(step, S0, xs)
    o = jnp.moveaxis(o, 0, 2).reshape(B, H, T, dv)
    return jnp.moveaxis(o, 1, 2), S


def gdn_mixer(x, conv_buf, S0, w_in, conv_w, a_log, dt_bias, norm_g, w_out):
    B, T, _ = x.shape
    H, dk, dv = GDN_HEADS, GDN_DK, GDN_DV
    proj = x @ w_in
    qkv, new_buf = causal_conv(proj[..., :GDN_CONV_CH], conv_buf, conv_w)
    o1 = GDN_CONV_CH + H * dv
    z = proj[..., GDN_CONV_CH:o1].astype(F32)
    b_logit = proj[..., o1:o1 + H].astype(F32)
    a_in = proj[..., o1 + H:].astype(F32)
    qkv = qkv.astype(F32)
    q = l2norm(qkv[..., :H * dk].reshape(B, T, H, dk)) * dk ** -0.5
    k = l2norm(qkv[..., H * dk:2 * H * dk].reshape(B, T, H, dk))
    v = qkv[..., 2 * H * dk:].reshape(B, T, H, dv)
    beta = jax.nn.sigmoid(b_logit)
    g = -jnp.exp(a_log.astype(F32)) * jax.nn.softplus(a_in + dt_bias.astype(F32))
    o, S = gated_delta_rule(q, k, v, g, beta, S0.astype(F32))
    o = o * lax.rsqrt(jnp.mean(o * o, -1, keepdims=True) + RMS_EPS) * norm_g.astype(F32)
    o = (o.reshape(B, T, H * dv) * jax.nn.silu(z)).astype(x.dtype)
    return o @ w_out, new_buf, S.astype(S0.dtype)


def retention(q, k, v, S0):
    B, T, H, dk = q.shape
    dv = v.shape[-1]
    C = chunk_len(T)
    N = T // C
    lg = jnp.log(1.0 - 2.0 ** (-5.0 - jnp.arange(H, dtype=F32)))
    idx = jnp.arange(C, dtype=F32)
    diff = idx[:, None] - idx[None, :]
    decay = jnp.where(diff >= 0, jnp.exp(jnp.maximum(diff, 0.0)[None] * lg[:, None, None]), 0.0)
    xi = jnp.exp((idx + 1.0)[None, :] * lg[:, None])
    zeta = jnp.exp((C - 1.0 - idx)[None, :] * lg[:, None])
    gC = jnp.exp(C * lg)

    def blocks(a):
        a = jnp.moveaxis(a, 2, 1)
        return a.reshape((B, H, N, C) + a.shape[3:])

    q, k, v = blocks(q), blocks(k), blocks(v)
    attn = jnp.einsum('bhnid,bhnjd->bhnij', q, k) * decay[:, None]
    inner = jnp.einsum('bhnij,bhnje->bhnie', attn, v)
    qx = q * xi[:, None, :, None]
    kz = k * zeta[:, None, :, None]
    xs = tuple(jnp.moveaxis(a, 2, 0) for a in (inner, qx, kz, v))

    def step(S, xn):
        in_n, qx_n, kz_n, v_n = xn
        o_n = in_n + jnp.einsum('bhcd,bhde->bhce', qx_n, S)
        S = gC[:, None, None] * S + jnp.einsum('bhcd,bhce->bhde', kz_n, v_n)
        return S, o_n

    S, o = lax.scan(step, S0, xs)
    o = jnp.moveaxis(o, 0, 2).reshape(B, H, T, dv)
    return jnp.moveaxis(o, 1, 2), S


def retention_mixer(x, S0, pos0, w_in, w_out):
    B, T, _ = x.shape
    H, dk, dv = RET_HEADS, RET_DK, RET_DV
    proj = (x @ w_in).astype(F32)
    q = proj[..., :H * dk].reshape(B, T, H, dk)
    k = proj[..., H * dk:2 * H * dk].reshape(B, T, H, dk)
    v = proj[..., 2 * H * dk:2 * H * dk + H * dv].reshape(B, T, H, dv)
    gate = proj[..., 2 * H * dk + H * dv:]
    pos = pos0 + jnp.arange(T, dtype=F32)
    q = rope(q, pos)
    k = rope(k, pos) * dk ** -0.5
    o, S = retention(q, k, v, S0.astype(F32))
    mu = jnp.mean(o, -1, keepdims=True)
    var = jnp.mean(jnp.square(o - mu), -1, keepdims=True)
    o = (o - mu) * lax.rsqrt(var + LN_EPS)
    o = (jax.nn.silu(gate) * o.reshape(B, T, H * dv)).astype(x.dtype)
    return o @ w_out, S.astype(S0.dtype)


def trunk(x, pool_buf, pool_valid, conv_buf, gdn_S, ret_S, pos0,
          pool_w, pool_scale, gdn_w_in, gdn_conv_w, gdn_a_log, gdn_dt_bias, gdn_norm_g, gdn_w_out,
          ret_w_in, ret_w_out, ffn_w13, ffn_w2, ln_g, ln_b):
    new_pool, new_conv, new_gdn, new_ret = [], [], [], []
    for i in range(DEPTH):
        kind, j = i % N_MIXERS, i // N_MIXERS
        if kind == 0:
            mix, nb = pool_mixer(x, pool_buf[j], pool_valid, pool_w[j], pool_scale[j])
            new_pool.append(nb)
        elif kind == 1:
            mix, nc, ns = gdn_mixer(x, conv_buf[j], gdn_S[j], gdn_w_in[j], gdn_conv_w[j], gdn_a_log[j],
                                    gdn_dt_bias[j], gdn_norm_g[j], gdn_w_out[j])
            new_conv.append(nc)
            new_gdn.append(ns)
        else:
            mix, ns = retention_mixer(x, ret_S[j], pos0, ret_w_in[j], ret_w_out[j])
            new_ret.append(ns)
        x = layer_norm(DN_ALPHA * x + mix, ln_g[i, 0], ln_b[i, 0])
        x = layer_norm(DN_ALPHA * x + swiglu(x, ffn_w13[i], ffn_w2[i]), ln_g[i, 1], ln_b[i, 1])
    return x, jnp.stack(new_pool), jnp.stack(new_conv), jnp.stack(new_gdn), jnp.stack(new_ret)


def setup_inputs(seed: int = 0) -> dict:
    key = jax.random.key(seed)
    ks = jax.random.split(key, 20)

    def nrm(k, shape, s):
        return jax.random.normal(k, shape, F32) * s

    x_prompt = nrm(ks[0], (BATCH, SEQ, D_MODEL), 1.0)
    x_sample = nrm(ks[1], (DEC_BATCH, DEC_SEQ, D_MODEL), 1.0)
    state_pool = nrm(ks[2], (N_POOL_LAYERS, DEC_BATCH, POOL_BUF, D_MODEL), 1.0)
    state_gdn_conv = nrm(ks[3], (N_GDN_LAYERS, DEC_BATCH, GDN_CONV - 1, GDN_CONV_CH), 1.0)
    state_gdn = nrm(ks[4], (N_GDN_LAYERS, DEC_BATCH, GDN_HEADS, GDN_DK, GDN_DV), GDN_DK ** -0.5)
    state_ret = nrm(ks[5], (N_RET_LAYERS, DEC_BATCH, RET_HEADS, RET_DK, RET_DV), 0.1)
    pool_w = nrm(ks[6], (N_POOL_LAYERS, N_POOL_GROUPS, POOL_GROUP, POOL_GROUP), POOL_GROUP ** -0.5 * DN_BETA)
    pool_scale = 1.0 + nrm(ks[7], (N_POOL_LAYERS, D_MODEL), 0.05)
    gdn_w_in = nrm(ks[8], (N_GDN_LAYERS, D_MODEL, GDN_IN), D_MODEL ** -0.5)
    gdn_conv_w = nrm(ks[9], (N_GDN_LAYERS, GDN_CONV, GDN_CONV_CH), GDN_CONV ** -0.5)
    gdn_a_log = jnp.log(jax.random.uniform(ks[10], (N_GDN_LAYERS, GDN_HEADS), F32, 1.0, 16.0))
    dt = jnp.exp(jax.random.uniform(ks[11], (N_GDN_LAYERS, GDN_HEADS), F32, math.log(1e-3), math.log(1e-1)))
    gdn_dt_bias = dt + jnp.log(-jnp.expm1(-dt))
    gdn_norm_g = 1.0 + nrm(ks[12], (N_GDN_LAYERS, GDN_DV), 0.05)
    gdn_w_out = nrm(ks[13], (N_GDN_LAYERS, GDN_HEADS * GDN_DV, D_MODEL), (GDN_HEADS * GDN_DV) ** -0.5 * DN_BETA)
    ret_w_in = nrm(ks[14], (N_RET_LAYERS, D_MODEL, RET_IN), D_MODEL ** -0.5)
    ret_w_out = nrm(ks[15], (N_RET_LAYERS, RET_HEADS * RET_DV, D_MODEL), (RET_HEADS * RET_DV) ** -0.5 * DN_BETA)
    ffn_w13 = nrm(ks[16], (DEPTH, D_MODEL, 2 * D_FF), D_MODEL ** -0.5)
    ffn_w2 = nrm(ks[17], (DEPTH, D_FF, D_MODEL), D_FF ** -0.5 * DN_BETA)
    ln_g = 1.0 + nrm(ks[18], (DEPTH, 2, D_MODEL), 0.05)
    ln_b = nrm(ks[19], (DEPTH, 2, D_MODEL), 0.02)
    return {'x_prompt': x_prompt, 'x_sample': x_sample,
            'state_pool': state_pool, 'state_gdn_conv': state_gdn_conv,
            'state_gdn': state_gdn, 'state_ret': state_ret,
            'pool_w': pool_w, 'pool_scale': pool_scale,
            'gdn_w_in': gdn_w_in, 'gdn_conv_w': gdn_conv_w, 'gdn_a_log': gdn_a_log,
            'gdn_dt_bias': gdn_dt_bias, 'gdn_norm_g': gdn_norm_g, 'gdn_w_out': gdn_w_out,
            'ret_w_in': ret_w_in, 'ret_w_out': ret_w_out,
            'ffn_w13': ffn_w13, 'ffn_w2': ffn_w2, 'ln_g': ln_g, 'ln_b': ln_b}


def reference(x_prompt, x_sample, state_pool, state_gdn_conv, state_gdn, state_ret,
              pool_w, pool_scale, gdn_w_in, gdn_conv_w, gdn_a_log, gdn_dt_bias, gdn_norm_g, gdn_w_out,
              ret_w_in, ret_w_out, ffn_w13, ffn_w2, ln_g, ln_b):
    pool0 = jnp.zeros((N_POOL_LAYERS, BATCH, POOL_BUF, D_MODEL), x_prompt.dtype)
    conv0 = jnp.zeros((N_GDN_LAYERS, BATCH, GDN_CONV - 1, GDN_CONV_CH), x_prompt.dtype)
    gdn0 = jnp.zeros((N_GDN_LAYERS, BATCH, GDN_HEADS, GDN_DK, GDN_DV), state_gdn.dtype)
    ret0 = jnp.zeros((N_RET_LAYERS, BATCH, RET_HEADS, RET_DK, RET_DV), state_ret.dtype)
    y_prompt, pool_p, conv_p, gdn_p, ret_p = trunk(
        x_prompt, pool0, 0, conv0, gdn0, ret0, 0,
        pool_w, pool_scale, gdn_w_in, gdn_conv_w, gdn_a_log, gdn_dt_bias, gdn_norm_g, gdn_w_out,
        ret_w_in, ret_w_out, ffn_w13, ffn_w2, ln_g, ln_b)
    y_sample, pool_s, conv_s, gdn_s, ret_s = trunk(
        x_sample, state_pool, min(PAST_LEN, POOL_BUF), state_gdn_conv, state_gdn, state_ret, PAST_LEN,
        pool_w, pool_scale, gdn_w_in, gdn_conv_w, gdn_a_log, gdn_dt_bias, gdn_norm_g, gdn_w_out,
        ret_w_in, ret_w_out, ffn_w13, ffn_w2, ln_g, ln_b)
    return (y_prompt, y_sample, pool_p, pool_s, conv_p, conv_s, gdn_p, gdn_s, ret_p, ret_s)
```

```python
import math
from contextlib import ExitStack
import numpy as np
import concourse.bass as bass
import concourse.mybir as mybir
from concourse.bass_utils import run_bass_kernel_spmd

F32 = mybir.dt.float32
BF16 = mybir.dt.bfloat16
ALU = mybir.AluOpType
AF = mybir.ActivationFunctionType
AX = mybir.AxisListType

D = 1024
DFF = 2816
NCORES = 8
PAST_LEN = 16384
ALPHA = 8.0 ** 0.25
LN_EPS = 1e-5
RMS_EPS = 1e-6
COMPUTE = ("pe", "act", "dve", "pool")


class Res:
    __slots__ = ("last_w", "readers")

    def __init__(self, inherit=None):
        self.last_w = None
        self.readers = dict(inherit) if inherit else {}


class Op:
    __slots__ = ("eng", "fn", "deps", "is_dma", "dkey", "dcount", "need_inc", "cnt", "idx")

    def __init__(self, eng, fn, is_dma=False, dkey=None):
        self.eng = eng
        self.fn = fn
        self.deps = []
        self.is_dma = is_dma
        self.dkey = dkey
        self.dcount = 0
        self.need_inc = False
        self.cnt = 0
        self.idx = 0


class Prog:
    def __init__(self, nc):
        self.nc = nc
        self.ops = []
        self.dma_counts = {}
        self.open_batch = {}

    def _add(self, op, reads, writes):
        op.idx = len(self.ops)
        deps = {}
        for r in reads:
            w = r.last_w
            if w is not None:
                deps[id(w)] = (w, True)
        for r in writes:
            w = r.last_w
            if w is not None and id(w) not in deps:
                deps[id(w)] = (w, False)
            for rd in r.readers.values():
                if id(rd) not in deps:
                    deps[id(rd)] = (rd, False)
        for (d, raw) in deps.values():
            if d is op:
                continue
            if (not d.is_dma) and (not op.is_dma) and d.eng == op.eng and d.eng == "pe":
                continue
            if d.is_dma and op.is_dma and d.dkey == op.dkey and d in self.open_batch.get(op.dkey, ()):
                continue
            op.deps.append(d)
            d.need_inc = True
        for r in reads:
            key = (op.eng, op.dkey) if op.is_dma else op.eng
            r.readers[key] = op
        for r in writes:
            r.last_w = op
            r.readers = {}
        self.ops.append(op)
        return op

    def op(self, eng, fn, reads=(), writes=()):
        return self._add(Op(eng, fn), reads, writes)

    def dma(self, eng, key, out, in_, reads=(), writes=()):
        def fn(e, out=out, in_=in_):
            return e.dma_start(out=out, in_=in_)
        o = Op(eng, fn, is_dma=True, dkey=key)
        self.dma_counts[key] = self.dma_counts.get(key, 0) + 16
        o.dcount = self.dma_counts[key]
        o.need_inc = True
        self.open_batch.setdefault(key, []).append(o)
        return self._add(o, reads, writes)

    def commit(self, key):
        for o in self.open_batch.get(key, []):
            o.dcount = self.dma_counts[key]
        self.open_batch[key] = []

    def emit(self, st):
        nc = self.nc
        sems = {e: st.enter_context(nc.semaphore("s_" + e)) for e in COMPUTE}
        dsems = {k: st.enter_context(nc.semaphore("d_%s" % (k,))) for k in self.dma_counts}
        cnts = {e: 0 for e in COMPUTE}
        for o in self.ops:
            if not o.is_dma and o.need_inc:
                cnts[o.eng] += 1
                o.cnt = cnts[o.eng]
        final_waits = {("d", k): (dsems[k], v) for k, v in self.dma_counts.items()}
        streams = {}
        for o in self.ops:
            streams.setdefault(o.eng, []).append(o)
        block = st.enter_context(nc.Block())

        def run(engname, e):
            wm = {}
            for o in streams.get(engname, []):
                need = {}
                for d in o.deps:
                    if d.is_dma:
                        k = ("d", d.dkey)
                        s, v = dsems[d.dkey], d.dcount
                    else:
                        k = ("c", d.eng)
                        s, v = sems[d.eng], d.cnt
                    if v > wm.get(k, 0) and v > need.get(k, (None, 0))[1]:
                        need[k] = (s, v)
                for k, (s, v) in need.items():
                    e.wait_ge(s, v)
                    wm[k] = v
                ins = o.fn(e)
                if o.is_dma:
                    ins.then_inc(dsems[o.dkey], 16)
                elif o.need_inc:
                    ins.then_inc(sems[o.eng], 1)
            if engname == "sp":
                for k, (s, v) in final_waits.items():
                    if v > wm.get(k, 0):
                        e.wait_ge(s, v)

        @block.sync
        def _(e):
            run("sp", e)

        @block.tensor
        def _(e):
            run("pe", e)

        @block.scalar
        def _(e):
            run("act", e)

        @block.vector
        def _(e):
            run("dve", e)

        @block.gpsimd
        def _(e):
            run("pool", e)


class Buf:
    def __init__(self, ap, lo, hi, inherit):
        self.ap = ap
        self.lo = lo
        self.hi = hi
        self.inherit = inherit
        self.ch = {}

    def r(self, key=None):
        c = self.ch.get(key)
        if c is None:
            c = Res(self.inherit)
            self.ch[key] = c
        return c

    def f32(self, *shape):
        return _shape(self.ap, shape)

    def bf(self, *shape):
        return _shape(self.ap.bitcast(BF16), shape)


class Sub:
    def __init__(self, parent, lo, hi):
        self.parent = parent
        self.ap = parent.ap[:, lo:hi]

    def r(self, key=None):
        return self.parent.r(key)

    def f32(self, *shape):
        return _shape(self.ap, shape)

    def bf(self, *shape):
        return _shape(self.ap.bitcast(BF16), shape)


def _shape(ap, shape):
    if len(shape) <= 1:
        return ap[:, 0:shape[0]] if shape else ap
    n = 1
    for s in shape:
        n *= s
    ap = ap[:, 0:n]
    if len(shape) == 2:
        return ap.rearrange("p (a b) -> p a b", a=shape[0], b=shape[1])
    if len(shape) == 3:
        return ap.rearrange("p (a b c) -> p a b c", a=shape[0], b=shape[1], c=shape[2])
    raise ValueError(shape)


class Arena:
    def __init__(self, ap, words):
        self.ap = ap
        self.words = words
        self.top = 0
        self.dead = []

    def alloc(self, words):
        words = (words + 7) // 8 * 8
        lo, hi = self.top, self.top + words
        assert hi <= self.words, ("arena overflow", hi, self.words)
        self.top = hi
        self.hw = max(getattr(self, "hw", 0), hi)
        inh = {}
        keep = []
        for b in self.dead:
            if b.hi <= lo or b.lo >= hi:
                keep.append(b)
                continue
            for c in b.ch.values():
                cand = list(c.readers.items())
                if c.last_w is not None:
                    w = c.last_w
                    cand.append(((w.eng, w.dkey) if w.is_dma else w.eng, w))
                for k, o in cand:
                    if k not in inh or inh[k].idx < o.idx:
                        inh[k] = o
            if not (b.lo >= lo and b.hi <= hi):
                keep.append(b)
        self.dead = keep
        return Buf(self.ap[:, lo:hi], lo, hi, inh)

    def f32(self, n):
        return self.alloc(n)

    def bf(self, n):
        return self.alloc((n + 1) // 2)

    def mark(self):
        return (self.top, [])

    def release(self, mark, bufs):
        self.top = mark[0]
        self.dead.extend(bufs)


def _consts(T, NS):
    c = {}
    c["ident"] = np.eye(128, dtype=np.float32)
    ii = np.arange(128)
    rc = np.zeros((128, 4, 16), np.float32)
    for g, win in enumerate((2, 4, 8, 16)):
        for t in range(16):
            rc[:, g, t] = 1.0 / min(t + 1, win)
    c["rc"] = rc
    c["maskS"] = np.where(ii[None, :] < ii[:, None], 0.0, 30000.0).astype(np.float32)
    c["negmaskT"] = np.where(ii[None, :] >= ii[:, None], 0.0, -30000.0).astype(np.float32)
    c["U"] = (ii[:, None] <= ii[None, :]).astype(np.float32)
    bm = lambda sz: ((ii[:, None] // sz) == (ii[None, :] // sz)).astype(np.float32)
    msk = np.zeros((128, 5, 128), np.float32)
    msk[:, 0, :] = bm(8)
    for li_, sz in enumerate((8, 16, 32, 64)):
        msk[:, 1 + li_, :] = bm(2 * sz) - bm(sz)
    c["bmask"] = msk
    H = 8
    lg = np.log(1.0 - 2.0 ** (-5.0 - np.arange(H, dtype=np.float64)))
    sc = 128.0 ** -0.5
    diff = (ii[None, :] - ii[:, None]).astype(np.float64)
    decT = np.where(diff[None] >= 0, np.exp(np.maximum(diff, 0.0)[None] * lg[:, None, None]), 0.0) * sc
    c["decT"] = np.ascontiguousarray(decT.transpose(1, 0, 2)).astype(np.float32)
    xi = np.exp((ii + 1.0)[None, :] * lg[:, None])
    c["xi"] = np.ascontiguousarray(np.broadcast_to(xi[None], (128, H, 128))).astype(np.float32)
    zeta = np.exp((127.0 - ii)[None, :] * lg[:, None]) * sc
    c["zeta"] = np.ascontiguousarray(zeta.T).astype(np.float32)
    c["gC"] = [float(np.exp(128.0 * lg[h])) for h in range(H)]
    c["gam"] = [float(np.exp(lg[h])) for h in range(H)]
    gb = np.zeros((128, 2, H), np.float32)
    gb[:, 0, :] = np.exp(lg)[None, :]
    gb[:, 1, :] = (np.exp(lg) * 0 + sc)[None, :]
    c["gamb"] = gb
    half = 64
    freqs = (np.float32(10000.0) ** (-np.arange(half, dtype=np.float32) / np.float32(half))).astype(np.float32)
    pos = np.concatenate([np.arange(T, dtype=np.float32), np.full((NS,), float(PAST_LEN), np.float32)])
    ang = (pos[None, :] * freqs[:, None]).astype(np.float32)
    cs, sn = np.cos(ang).astype(np.float32), np.sin(ang).astype(np.float32)
    c["cos2"] = np.concatenate([cs, cs], 0)
    c["sin2"] = np.concatenate([-sn, sn], 0)
    angs = (np.float32(PAST_LEN) * freqs).astype(np.float32)
    cst = np.zeros((128, 2, half), np.float32)
    cst[:, 0, :] = np.cos(angs)[None]
    cst[:, 1, :] = np.sin(angs)[None]
    c["ropes"] = cst
    dl = np.zeros((128, 16, 16), np.float32)
    for s in range(16):
        dl[:, s, s] = 1.0
    c["delta"] = dl
    dcol = np.zeros((128, 16), np.float32)
    for s in range(16):
        dcol[s, s] = 1.0
    c["dcol"] = dcol
    return c


CONST_SHAPES = {"ident": [128, 128], "rc": [128, 4, 16], "maskS": [128, 128], "negmaskT": [128, 128],
                "U": [128, 128], "bmask": [128, 5, 128], "decT": [128, 8, 128], "xi": [128, 8, 128], "zeta": [128, 8],
                "gamb": [128, 2, 8], "ropes": [128, 2, 64], "delta": [128, 16, 16], "dcol": [128, 16]}


AWO = None
DBG = set()


def build(T, NS, nlayers=4):
    assert T % 128 == 0 and NS == 16
    nc = bass.Bass("TRN2", target_bir_lowering=False)
    NB = T // 128
    NT = T + NS
    TTS = min(512, T)
    tiles = [(c0, TTS) for c0 in range(0, T, TTS)] + [(T, NS)]
    NTI = len(tiles)
    cst = _consts(T, NS)

    def din(name, shape):
        return nc.dram_tensor(name, list(shape), F32, kind="ExternalInput").ap()

    def dout(name, shape):
        return nc.dram_tensor(name, list(shape), F32, kind="ExternalOutput").ap()

    xp = din("xp", [T, D])
    xs = din("xs", [NS, D])
    spool = din("spool", [2, NS, 15, D])
    sconv = din("sconv", [NS, 3, 3072])
    sgdn = din("sgdn", [NS, 8, 128, 128])
    sret = din("sret", [NS, 8, 128, 256])
    pool_w = din("pool_w", [2, 4, 256, 256])
    pool_scale = din("pool_scale", [2, D])
    gdn_w_in = din("gdn_w_in", [D, 4112])
    gdn_conv_w = din("gdn_conv_w", [4, 3072])
    gdn_a_log = din("gdn_a_log", [1, 8])
    gdn_dt_bias = din("gdn_dt_bias", [1, 8])
    gdn_norm_g = din("gdn_norm_g", [1, 128])
    gdn_w_out = din("gdn_w_out", [D, D])
    ret_w_in = din("ret_w_in", [D, 6144])
    ret_w_out = din("ret_w_out", [2048, D])
    ffn_w13 = din("ffn_w13", [4, D, 2 * DFF])
    ffn_w2 = din("ffn_w2", [4, DFF, D])
    ln_g = din("ln_g", [8, D])
    ln_b = din("ln_b", [8, D])
    cd = {k: din("c_" + k, CONST_SHAPES[k]) for k in CONST_SHAPES}
    cd["cos2"] = din("c_cos2", [128, NT])
    cd["sin2"] = din("c_sin2", [128, NT])

    yp = dout("yp", [T, D])
    ys = dout("ys", [NS, D])
    o_pool_p = dout("pool_p", [2, 15, D])
    o_pool_s = dout("pool_s", [2, NS, 15, D])
    o_conv_p = dout("conv_p", [3, 3072])
    o_conv_s = dout("conv_s", [NS, 3, 3072])
    o_gdn_p = dout("gdn_p", [8, 128, 128])
    o_gdn_s = dout("gdn_s", [NS, 8, 128, 128])
    o_ret_p = dout("ret_p", [8, 128, 256])
    o_ret_s = dout("ret_s", [NS, 8, 128, 256])

    P = Prog(nc)
    st = ExitStack()
    with st:
        def sb(name, shape, dt):
            return st.enter_context(nc.sbuf_tensor(name, shape, dt))

        xf = sb("xf", [128, 8, NT], F32)
        xb = sb("xb", [128, 8, NT], BF16)
        ident = sb("ident", [128, 128], F32)
        identb = sb("identb", [128, 128], BF16)
        onesb = sb("onesb", [128, 128], BF16)
        onesn = sb("onesn", [128, 128], BF16)
        onesf = sb("onesf", [128, 128], F32)
        lnp = sb("lnp", [128, 128], F32)
        prm = sb("prm", [128, 128], F32)
        rcs = sb("rcs", [128, 4, 16], F32)
        AW = (nc.sbuf_bytes_remaining - 4096) // 4 // 8 * 8 if AWO is None else AWO
        arena_t = sb("arena", [128, AW], F32)
        A = Arena(arena_t[:], AW)
        banks = [st.enter_context(nc.psum_tensor("bank%d" % i, [128, 512], F32)) for i in range(8)]
        rbank = [Res() for _ in range(8)]
        r_xf = [[Res() for _ in range(NTI)] for _ in range(8)]
        r_xb = [[Res() for _ in range(NTI)] for _ in range(8)]
        r_c = Res()

        def rx(rs, ks=range(8), ts=range(NTI)):
            return [rs[k][t] for k in ks for t in ts]

        def tiles_of(c0, n):
            return [ti for ti, (a, m) in enumerate(tiles) if a < c0 + n and c0 < a + m]

        P.dma("sp", "c0", ident[:], cd["ident"], writes=[r_c])
        P.dma("sp", "c0", rcs[:], cd["rc"], writes=[r_c])
        P.commit("c0")
        P.op("dve", lambda e: e.tensor_copy(identb[:], ident[:]), reads=[r_c], writes=[r_c])
        P.op("dve", lambda e: e.memset(onesb[:], 1.0), writes=[r_c])
        P.op("dve", lambda e: e.memset(onesn[:], 1.0 / D), writes=[r_c])
        P.op("dve", lambda e: e.memset(onesf[:], 1.0), writes=[r_c])
        wring = [A.bf(2 * 8 * 128) for _ in range(4)]
        wr_i = [0]

        def wslot():
            b = wring[wr_i[0] % len(wring)]
            k = "w%d" % (wr_i[0] % len(wring))
            wr_i[0] += 1
            return b, k

        m0 = A.mark()
        if "noparams" in DBG:
            raise_ = None
        stg = A.f32(128)
        stg2 = A.f32(128)
        if "noparams" not in DBG:
          P.dma("sp", "c1", stg.f32(128)[0:64, :], ln_g.rearrange("l (k p) -> (l k) p", p=128), writes=[stg.r()])
          P.dma("sp", "c1", stg.f32(128)[64:128, :], ln_b.rearrange("l (k p) -> (l k) p", p=128), writes=[stg.r()])
          P.op("dve", lambda e: e.memset(stg2.f32(128), 0.0), writes=[stg2.r()])
          P.dma("sp", "c2", stg2.f32(128)[0:16, :], pool_scale.rearrange("j (k p) -> (j k) p", p=128), reads=[], writes=[stg2.r()])
          P.dma("sp", "c2", stg2.f32(128)[16:112, :], gdn_conv_w.rearrange("j (c p) -> (j c) p", p=128), writes=[stg2.r()])
          P.dma("sp", "c2", stg2.f32(128)[112:113, :], gdn_norm_g, writes=[stg2.r()])
          P.op("pe", lambda e: e.transpose(banks[0][:, 0:128], stg.f32(128), ident[:]), reads=[stg.r(), r_c], writes=[rbank[0]])
          P.op("pe", lambda e: e.transpose(banks[0][:, 128:256], stg2.f32(128), ident[:]), reads=[stg2.r(), r_c], writes=[rbank[0]])
          P.op("dve", lambda e: e.tensor_copy(lnp[:], banks[0][:, 0:128]), reads=[rbank[0]], writes=[r_c])
          P.op("dve", lambda e: e.tensor_copy(prm[:], banks[0][:, 128:256]), reads=[rbank[0]], writes=[r_c])
        A.release(m0, [stg, stg2])

        epsb = {}
        for ev in (LN_EPS, RMS_EPS, RMS_EPS * 128.0, 1.0):
            t = sb("eps%d" % len(epsb), [128, 1], F32)
            P.op("dve", lambda e, t=t, ev=ev: e.memset(t[:], ev), writes=[r_c])
            epsb[ev] = t

        def eps_tile(v):
            return epsb[v][:]

        def rsqrt_eps(out, in_, eps, res, scale=1.0):
            eb = epsb[eps]
            np_ = in_.shape[0]
            P.op("act", lambda e: e.activation(out, in_, AF.Ln, bias=eb[0:np_, :], scale=scale), reads=[res, r_c], writes=[res])
            P.op("act", lambda e: e.activation(out, out, AF.Exp, scale=-0.5), reads=[res], writes=[res])

        def load_x():
            m = A.mark()
            sg = [A.f32(D), A.f32(D)]
            blocks = [(xp[b * 128:(b + 1) * 128, :], 128, b * 128) for b in range(NB)] + [(xs, NS, T)]
            if "nosamp" in DBG:
                blocks = blocks[:-1]
            for bi, (src, n, c0) in enumerate(blocks):
                s = sg[bi % 2]
                P.dma("sp", "xl%d" % (bi % 2), s.f32(D)[0:n, :], src, writes=[s.r()])
                tis = tiles_of(c0, n)
                for half in range(2):
                    bk = (bi * 2 + half) % 2
                    for kk in range(4):
                        k = half * 4 + kk
                        P.op("pe", lambda e, s=s, k=k, kk=kk, n=n, bk=bk: e.transpose(
                            banks[bk][:, kk * 128:kk * 128 + n], s.f32(D)[0:n, k * 128:(k + 1) * 128], ident[0:n, 0:n]),
                            reads=[s.r(), r_c], writes=[rbank[bk]])
                    src_ps = banks[bk][:].rearrange("p (a b) -> p a b", a=4, b=128)[:, :, 0:n]
                    ks = range(half * 4, half * 4 + 4)
                    P.op("dve", lambda e, src_ps=src_ps, half=half, c0=c0, n=n: e.tensor_copy(
                        xf[:, half * 4:half * 4 + 4, c0:c0 + n], src_ps),
                        reads=[rbank[bk]], writes=rx(r_xf, ks, tis))
                    if "noact" in DBG:
                        continue
                    P.op("act", lambda e, src_ps=src_ps, half=half, c0=c0, n=n: e.activation(
                        xb[:, half * 4:half * 4 + 4, c0:c0 + n], xf[:, half * 4:half * 4 + 4, c0:c0 + n], AF.Identity),
                        reads=rx(r_xf, ks, tis), writes=rx(r_xb, ks, tis))
            A.release(m, sg)

        def layer_norm(li):
            m = A.mark()
            bufs = []
            sets = []
            for _ in range(2):
                sets.append((A.bf(8 * TTS), A.bf(8 * TTS), A.f32(TTS), A.f32(TTS), A.f32(TTS)))
                bufs += list(sets[-1])
            for ti, (c0, n) in enumerate(tiles):
                rb, sq, mean, rstd, m2 = sets[ti % 2]
                xv = xf[:, :, c0:c0 + n]
                P.op("act", lambda e, rb=rb, xv=xv, n=n: e.activation(rb.bf(8, n), xv, AF.Copy),
                     reads=rx(r_xf, ts=[ti]), writes=[rb.r()])
                P.op("act", lambda e, sq=sq, xv=xv, n=n: e.activation(sq.bf(8, n), xv, AF.Square),
                     reads=rx(r_xf, ts=[ti]), writes=[sq.r()])
                for k in range(8):
                    P.op("pe", lambda e, rb=rb, k=k, n=n: e.matmul(banks[6][:, 0:n], onesn[:], rb.bf(8, n)[:, k, :],
                                                                   start=(k == 0), stop=(k == 7)),
                         reads=[rb.r(), r_c], writes=[rbank[6]])
                for k in range(8):
                    P.op("pe", lambda e, sq=sq, k=k, n=n: e.matmul(banks[7][:, 0:n], onesn[:], sq.bf(8, n)[:, k, :],
                                                                   start=(k == 0), stop=(k == 7)),
                         reads=[sq.r(), r_c], writes=[rbank[7]])
                P.op("act", lambda e, mean=mean, n=n: e.activation(mean.f32(n), banks[6][:, 0:n], AF.Copy),
                     reads=[rbank[6]], writes=[mean.r()])
                P.op("dve", lambda e, mean=mean, m2=m2, n=n: e.tensor_tensor(m2.f32(n), mean.f32(n), mean.f32(n), ALU.mult),
                     reads=[mean.r()], writes=[m2.r()])
                P.op("dve", lambda e, m2=m2, rstd=rstd, n=n: e.tensor_tensor(rstd.f32(n), banks[7][:, 0:n], m2.f32(n), ALU.subtract),
                     reads=[m2.r(), rbank[7]], writes=[rstd.r()])
                rsqrt_eps(rstd.f32(n), rstd.f32(n), LN_EPS, rstd.r())
                P.op("dve", lambda e, xv=xv, mean=mean, n=n: e.tensor_tensor(
                    xv, xv, mean.f32(n).unsqueeze(1).to_broadcast([128, 8, n]), ALU.subtract),
                    reads=rx(r_xf, ts=[ti]) + [mean.r()], writes=rx(r_xf, ts=[ti]))
                P.op("dve", lambda e, xv=xv, rstd=rstd, n=n: e.tensor_tensor(
                    xv, xv, rstd.f32(n).unsqueeze(1).to_broadcast([128, 8, n]), ALU.mult),
                    reads=rx(r_xf, ts=[ti]) + [rstd.r()], writes=rx(r_xf, ts=[ti]))
                for k in range(8):
                    P.op("act", lambda e, k=k, c0=c0, n=n: e.activation(
                        xf[:, k, c0:c0 + n], xf[:, k, c0:c0 + n], AF.Identity,
                        bias=lnp[:, 64 + li * 8 + k:64 + li * 8 + k + 1], scale=lnp[:, li * 8 + k:li * 8 + k + 1]),
                        reads=[r_xf[k][ti], r_c], writes=[r_xf[k][ti]])
                P.op("dve", lambda e, xv=xv, c0=c0, n=n: e.tensor_copy(xb[:, :, c0:c0 + n], xv),
                     reads=rx(r_xf, ts=[ti]), writes=rx(r_xb, ts=[ti]))
            A.release(m, bufs)

        def ffn(layer):
            m = A.mark()
            h = A.bf(11 * NT)
            w2 = A.bf(11 * D)
            sa = [A.f32(TTS), A.f32(TTS)]
            hv = h.bf(11, NT)
            w2v = w2.bf(11, D)
            cnt = 0
            for pas in range(2):
                for il in range(11):
                    i = pas * 11 + il
                    wb, wk = wslot()
                    wv = wb.bf(2, 8, 128)
                    P.dma("pool", wk, wv[:, 0], ffn_w13[layer, :, i * 128:(i + 1) * 128].rearrange("(k p) n -> p k n", p=128),
                          writes=[wb.r()])
                    P.dma("pool", wk, wv[:, 1], ffn_w13[layer, :, DFF + i * 128:DFF + (i + 1) * 128].rearrange("(k p) n -> p k n", p=128),
                          writes=[wb.r()])
                    P.commit(wk)
                    if il == 0:
                        for f in range(11):
                            P.dma("pool", "fw2", w2v[:, f, :], ffn_w2[layer, (pas * 11 + f) * 128:(pas * 11 + f + 1) * 128, :],
                                  writes=[w2.r(f)])
                        P.commit("fw2")
                    for ti, (c0, n) in enumerate(tiles):
                        ba, bb = banks[cnt % 2], banks[2 + cnt % 2]
                        ra, rbb = rbank[cnt % 2], rbank[2 + cnt % 2]
                        s = sa[cnt % 2]
                        cnt += 1
                        for k in range(8):
                            P.op("pe", lambda e, ba=ba, wv=wv, k=k, c0=c0, n=n: e.matmul(
                                ba[:, 0:n], wv[:, 0, k, :], xb[:, k, c0:c0 + n], start=(k == 0), stop=(k == 7)),
                                reads=[wb.r(), r_xb[k][ti]], writes=[ra])
                        for k in range(8):
                            P.op("pe", lambda e, bb=bb, wv=wv, k=k, c0=c0, n=n: e.matmul(
                                bb[:, 0:n], wv[:, 1, k, :], xb[:, k, c0:c0 + n], start=(k == 0), stop=(k == 7)),
                                reads=[wb.r(), r_xb[k][ti]], writes=[rbb])
                        P.op("act", lambda e, s=s, ba=ba, n=n: e.activation(s.f32(n), ba[:, 0:n], AF.Silu),
                             reads=[ra], writes=[s.r()])
                        P.op("dve", lambda e, s=s, bb=bb, il=il, c0=c0, n=n: e.tensor_tensor(
                            hv[:, il, c0:c0 + n], s.f32(n), bb[:, 0:n], ALU.mult),
                            reads=[s.r(), rbb], writes=[h.r((il, ti))])
                for dc in range(8):
                    for ti, (c0, n) in enumerate(tiles):
                        by, ry = banks[4 + cnt % 2], rbank[4 + cnt % 2]
                        cnt += 1
                        for f in range(11):
                            P.op("pe", lambda e, by=by, f=f, dc=dc, c0=c0, n=n: e.matmul(
                                by[:, 0:n], w2v[:, f, dc * 128:(dc + 1) * 128], hv[:, f, c0:c0 + n],
                                start=(f == 0), stop=(f == 10)),
                                reads=[w2.r(f), h.r((f, ti))], writes=[ry])
                        xv = xf[:, dc, c0:c0 + n]
                        if pas == 0:
                            P.op("dve", lambda e, xv=xv, by=by, n=n: e.scalar_tensor_tensor(
                                xv, xv, ALPHA, by[:, 0:n], ALU.mult, ALU.add),
                                reads=[ry, r_xf[dc][ti]], writes=[r_xf[dc][ti]])
                        else:
                            P.op("dve", lambda e, xv=xv, by=by, n=n: e.tensor_tensor(xv, xv, by[:, 0:n], ALU.add),
                                 reads=[ry, r_xf[dc][ti]], writes=[r_xf[dc][ti]])
            A.release(m, [h, w2] + sa)

        def store_y():
            m = A.mark()
            sg = [A.f32(D), A.f32(D)]
            blocks = [(yp[b * 128:(b + 1) * 128, :], 128, b * 128) for b in range(NB)] + [(ys, NS, T)]
            for bi, (dst, n, c0) in enumerate(blocks):
                s = sg[bi % 2]
                tis = tiles_of(c0, n)
                for half in range(2):
                    bk = (bi * 2 + half) % 2
                    for kk in range(4):
                        k = half * 4 + kk
                        P.op("pe", lambda e, k=k, kk=kk, n=n, bk=bk, c0=c0: e.transpose(
                            banks[bk][0:n, kk * 128:(kk + 1) * 128], xf[:, k, c0:c0 + n], ident[:]),
                            reads=rx(r_xf, [k], tis) + [r_c], writes=[rbank[bk]])
                    if half == 0:
                        P.op("dve", lambda e, s=s, n=n, bk=bk: e.tensor_copy(s.f32(D)[0:n, 0:512], banks[bk][0:n, :]),
                             reads=[rbank[bk]], writes=[s.r()])
                    else:
                        P.op("act", lambda e, s=s, n=n, bk=bk: e.activation(s.f32(D)[0:n, 512:1024], banks[bk][0:n, :], AF.Copy),
                             reads=[rbank[bk]], writes=[s.r()])
                P.dma("sp", "ys%d" % (bi % 2), dst, s.f32(D)[0:n, :], reads=[s.r()])
            A.release(m, sg)

        def pool_mixer(j):
            m = A.mark()
            bufs = []
            if j == 0:
                P.dma("sp", "po", o_pool_p[0], xp[T - 15:T, :])
                P.dma("sp", "po", o_pool_s[0, :, 14, :], xs)
            else:
                so = A.f32(D)
                bufs.append(so)
                nn = 15 + NS
                for half in range(2):
                    for kk in range(4):
                        k = half * 4 + kk
                        P.op("pe", lambda e, k=k, kk=kk, half=half: e.transpose(
                            banks[half][0:nn, kk * 128:(kk + 1) * 128], xf[:, k, T - 15:T + NS], ident[:]),
                            reads=rx(r_xf, [k], tiles_of(T - 15, nn)) + [r_c], writes=[rbank[half]])
                    P.op("dve", lambda e, half=half: e.tensor_copy(so.f32(D)[0:nn, half * 512:(half + 1) * 512], banks[half][0:nn, :]),
                         reads=[rbank[half]], writes=[so.r()])
                P.dma("sp", "po", o_pool_p[1], so.f32(D)[0:15, :], reads=[so.r()])
                P.dma("sp", "po", o_pool_s[1, :, 14, :], so.f32(D)[15:15 + NS, :], reads=[so.r()])
            P.dma("sp", "po", o_pool_s[j, :, 0:14, :], spool[j, :, 1:15, :])
            P.commit("po")
            hist = A.f32(8 * NS * 15)
            bufs.append(hist)
            hv = hist.f32(8, NS * 15)
            hs = [A.f32(D), A.f32(D)]
            bufs += hs
            rows = NS * 15 // 2
            src = spool[j].rearrange("s r d -> (s r) d")
            for b2 in range(2):
                s = hs[b2]
                P.dma("sp", "ph%d" % b2, s.f32(D)[0:rows, :], src[b2 * rows:(b2 + 1) * rows, :], writes=[s.r()])
                for half in range(2):
                    bk = half
                    for kk in range(4):
                        k = half * 4 + kk
                        P.op("pe", lambda e, s=s, k=k, kk=kk, bk=bk: e.transpose(
                            banks[bk][:, kk * 128:kk * 128 + rows], s.f32(D)[0:rows, k * 128:(k + 1) * 128], ident[0:rows, 0:rows]),
                            reads=[s.r(), r_c], writes=[rbank[bk]])
                    P.op("dve", lambda e, half=half, bk=bk, b2=b2: e.tensor_copy(
                        hv[:, half * 4:half * 4 + 4, b2 * rows:(b2 + 1) * rows],
                        banks[bk][:].rearrange("p (a b) -> p a b", a=4, b=128)[:, :, 0:rows]),
                        reads=[rbank[bk]], writes=[hist.r()])
            pooled = A.bf(8 * NT)
            bufs.append(pooled)
            pv = pooled.bf(8, NT)
            E = [A.f32(16 + T), A.f32(16 + T)]
            bufs += E
            ssum = A.f32(NS)
            bufs.append(ssum)
            for eb in E:
                P.op("pool", lambda e, eb=eb: e.memset(eb.f32(16 + T)[:, 0:16], 0.0), writes=[eb.r()])
            for k in range(8):
                g = k // 2
                win = 2 << g
                e0, e1 = E[0], E[1]
                tp = list(range(NTI - 1))
                P.op("pool", lambda e, e0=e0, k=k: e.tensor_copy(e0.f32(16 + T)[:, 16:16 + T], xf[:, k, 0:T]),
                     reads=rx(r_xf, [k], tp), writes=[e0.r()])
                cur, nxt = e0, e1
                sh = 1
                while sh < win:
                    P.op("dve", lambda e, cur=cur, nxt=nxt, sh=sh: e.tensor_tensor(
                        nxt.f32(16 + T)[:, 16:16 + T], cur.f32(16 + T)[:, 16:16 + T], cur.f32(16 + T)[:, 16 - sh:16 + T - sh], ALU.add),
                        reads=[cur.r()], writes=[nxt.r()])
                    cur, nxt = nxt, cur
                    sh *= 2
                P.op("dve", lambda e, cur=cur, k=k, win=win: e.scalar_tensor_tensor(
                    pv[:, k, 0:T], cur.f32(16 + T)[:, 16:16 + T], 1.0 / win, xf[:, k, 0:T], ALU.mult, ALU.subtract),
                    reads=[cur.r()] + rx(r_xf, [k], tp), writes=[pooled.r(k)])
                P.op("dve", lambda e, cur=cur, nxt=nxt, g=g: e.tensor_tensor(
                    nxt.f32(16 + T)[:, 16:31], cur.f32(16 + T)[:, 16:31], rcs[:, g, 0:15], ALU.mult),
                    reads=[cur.r(), r_c], writes=[nxt.r()])
                P.op("dve", lambda e, nxt=nxt, k=k: e.tensor_tensor(
                    pv[:, k, 0:15], nxt.f32(16 + T)[:, 16:31], xf[:, k, 0:15], ALU.subtract),
                    reads=[nxt.r()] + rx(r_xf, [k], [0]), writes=[pooled.r(k)])
                hk = hv[:, k, :].rearrange("p (s r) -> p s r", s=NS, r=15)
                P.op("dve", lambda e, hk=hk, win=win: e.tensor_reduce(
                    ssum.f32(NS), hk[:, :, 16 - win:15], AX.X, ALU.add),
                    reads=[hist.r()], writes=[ssum.r()])
                P.op("dve", lambda e, k=k: e.tensor_tensor(ssum.f32(NS), ssum.f32(NS), xf[:, k, T:NT], ALU.add),
                     reads=[ssum.r(), r_xf[k][NTI - 1]], writes=[ssum.r()])
                P.op("dve", lambda e, k=k, win=win: e.scalar_tensor_tensor(
                    pv[:, k, T:NT], ssum.f32(NS), 1.0 / win, xf[:, k, T:NT], ALU.mult, ALU.subtract),
                    reads=[ssum.r(), r_xf[k][NTI - 1]], writes=[pooled.r(k)])
            pw = A.bf(4 * 2 * 256)
            bufs.append(pw)
            pwv = pw.bf(4, 2, 256)
            P.dma("pool", "pw", pwv, pool_w[j].rearrange("g (cc p) d -> p g cc d", p=128), writes=[pw.r()])
            tm = [A.f32(TTS), A.f32(TTS)]
            bufs += tm
            cnt = 0
            for g in range(4):
                for dc in range(2):
                    k = 2 * g + dc
                    for ti, (c0, n) in enumerate(tiles):
                        bk = cnt % 2
                        t_ = tm[cnt % 2]
                        cnt += 1
                        for cc in range(2):
                            P.op("pe", lambda e, g=g, dc=dc, cc=cc, c0=c0, n=n, bk=bk: e.matmul(
                                banks[bk][:, 0:n], pwv[:, g, cc, dc * 128:(dc + 1) * 128], pv[:, 2 * g + cc, c0:c0 + n],
                                start=(cc == 0), stop=(cc == 1)),
                                reads=[pw.r(), pooled.r(2 * g + cc)], writes=[rbank[bk]])
                        P.op("act", lambda e, t_=t_, bk=bk, n=n, k=k: e.activation(
                            t_.f32(n), banks[bk][:, 0:n], AF.Copy, scale=prm[:, j * 8 + k:j * 8 + k + 1]),
                            reads=[rbank[bk], r_c], writes=[t_.r()])
                        xv = xf[:, k, c0:c0 + n]
                        P.op("dve", lambda e, xv=xv, t_=t_, n=n: e.scalar_tensor_tensor(
                            xv, xv, ALPHA, t_.f32(n), ALU.mult, ALU.add),
                            reads=[t_.r(), r_xf[k][ti]], writes=[r_xf[k][ti]])
            A.release(m, bufs)

        def MM(out, lhsT, rhs, r, w, start=True, stop=True):
            P.op("pe", lambda e: e.matmul(out, lhsT, rhs, start=start, stop=stop), reads=r, writes=w)

        def TR(out, in_, idn, r, w):
            P.op("pe", lambda e: e.transpose(out, in_, idn), reads=r, writes=w)

        def TT(eng, out, in0, in1, op, r, w):
            P.op(eng, lambda e: e.tensor_tensor(out, in0, in1, op), reads=r, writes=w)

        def STT(eng, out, in0, scalar, in1, op0, op1, r, w):
            P.op(eng, lambda e: e.scalar_tensor_tensor(out, in0, scalar, in1, op0, op1), reads=r, writes=w)

        def TS(eng, out, in0, s1, s2, op0, op1, r, w):
            if op1 is None:
                P.op(eng, lambda e: e.tensor_scalar(out, in0, s1, None, op0), reads=r, writes=w)
            else:
                P.op(eng, lambda e: e.tensor_scalar(out, in0, s1, s2, op0, op1), reads=r, writes=w)

        def CP(eng, out, in_, r, w):
            if eng == "act":
                P.op("act", lambda e: e.activation(out, in_, AF.Identity), reads=r, writes=w)
            else:
                P.op(eng, lambda e: e.tensor_copy(out, in_), reads=r, writes=w)

        def ACT(out, in_, func, r, w, scale=1.0, bias=None, accum=None):
            def fn(e):
                kw = {}
                if bias is not None:
                    kw["bias"] = bias
                if accum is not None:
                    kw["accum_out"] = accum
                return e.activation(out, in_, func, scale=scale, **kw)
            P.op("act", fn, reads=r, writes=w)

        def RED(eng, out, in_, r, w, op=ALU.add):
            P.op(eng, lambda e: e.tensor_reduce(out, in_, AX.X, op), reads=r, writes=w)

        def MS(eng, out, val, w):
            P.op(eng, lambda e: e.memset(out, val), writes=w)

        def bfps(bank, c0, n):
            return banks[bank][:, c0:c0 + n // 2].bitcast(BF16)

        ptiles = tiles[:-1]
        PT_ = list(range(NTI - 1))

        def out_proj_samples(w_dram, nk, ogsT, ogs_res):
            for dc in range(8):
                for kc in range(nk):
                    wb_, wk_ = wslot()
                    wv_ = wb_.bf(2, 8, 128)
                    if kc % 16 == 0:
                        pass
                    P.dma("pool", wk_, wv_[:, 0, 0, :], w_dram[kc * 128:(kc + 1) * 128, dc * 128:(dc + 1) * 128], writes=[wb_.r()])
                    P.commit(wk_)
                    MM(banks[4][:, 0:NS], wv_[:, 0, 0, :], ogsT[:, kc, :], [wb_.r(), ogs_res], [rbank[4]], start=(kc == 0), stop=(kc == nk - 1))
                xv = xf[:, dc, T:NT]
                STT("dve", xv, xv, ALPHA, banks[4][:, 0:NS], ALU.mult, ALU.add, [r_xf[dc][NTI - 1]], [r_xf[dc][NTI - 1], rbank[4]])

        def gdn_mixer():
            G = 4 if NB >= 4 else 2
            m = A.mark()
            bufs = []

            def al(n, bf=False):
                b = A.bf(n) if bf else A.f32(n)
                bufs.append(b)
                return b
            maskS = al(128); negT = al(128); Um = al(128); gn = al(128); gnb = al(128)
            bmask = al(5 * 128)
            bmv = bmask.f32(5, 128)
            alog = al(8); dtb = al(8); nea = al(8)
            P.dma("sp", "gc", maskS.f32(128), cd["maskS"], writes=[maskS.r()])
            P.dma("sp", "gc", negT.f32(128), cd["negmaskT"], writes=[negT.r()])
            P.dma("sp", "gc", Um.f32(128), cd["U"], writes=[Um.r()])
            P.dma("sp", "gc", bmask.f32(5, 128), cd["bmask"], writes=[bmask.r()])
            P.dma("sp", "gc", gn.f32(128), gdn_norm_g.partition_broadcast(128)[:, 0, :], writes=[gn.r()])
            P.dma("sp", "gc", alog.f32(8), gdn_a_log.partition_broadcast(128)[:, 0, :], writes=[alog.r()])
            P.dma("sp", "gc", dtb.f32(8), gdn_dt_bias.partition_broadcast(128)[:, 0, :], writes=[dtb.r()])
            P.commit("gc")
            TS("dve", gnb.f32(128), gn.f32(128), math.sqrt(128.0), None, ALU.mult, None, [gn.r()], [gnb.r()])
            ACT(nea.f32(8), alog.f32(8), AF.Exp, [alog.r()], [nea.r()])
            TS("dve", nea.f32(8), nea.f32(8), -1.0, None, ALU.mult, None, [nea.r()], [nea.r()])
            NG = NB + 1
            wg = al(8 * 16, bf=True)
            wgv = wg.bf(8, 16)
            P.dma("pool", "gw", wgv, gdn_w_in[:, 4096:4112].rearrange("(k p) n -> p k n", p=128), writes=[wg.r()])
            GA = al(NG * 16)
            GAv = GA.f32(NG, 16)
            MS("dve", GA.f32(NG * 16), 0.0, [GA.r()])
            for b in range(NB):
                for k in range(8):
                    MM(banks[0][:, b * 16:(b + 1) * 16], xb[:, k, b * 128:(b + 1) * 128], wgv[:, k, :],
                       [wg.r()] + rx(r_xb, [k], tiles_of(b * 128, 128)), [rbank[0]], start=(k == 0), stop=(k == 7))
            for k in range(8):
                MM(banks[0][0:NS, NB * 16:NG * 16], xb[:, k, T:NT], wgv[:, k, :],
                   [wg.r(), r_xb[k][NTI - 1]], [rbank[0]], start=(k == 0), stop=(k == 7))
            CP("dve", GA.f32(NB * 16), banks[0][:, 0:NB * 16], [], [GA.r(), rbank[0]])
            CP("dve", GAv[0:NS, NB, :], banks[0][0:NS, NB * 16:NG * 16], [], [GA.r(), rbank[0]])
            beta = al(NG * 8); nbeta = al(NG * 8); gg = al(NG * 8); gcol = al(NG * 8); eg = al(NG * 8)
            beg = al(NG * 8); egl = al(NG * 8); gl = al(NG * 8); t1 = al(NG * 8); t2 = al(NG * 8)
            v8 = lambda b_: b_.f32(NG, 8)
            ACT(v8(beta), GAv[:, :, 0:8], AF.Sigmoid, [GA.r()], [beta.r()])
            TS("dve", v8(nbeta), v8(beta), -1.0, None, ALU.mult, None, [beta.r()], [nbeta.r()])
            TT("dve", v8(t1), GAv[:, :, 8:16], dtb.f32(8).unsqueeze(1).to_broadcast([128, NG, 8]), ALU.add, [GA.r(), dtb.r()], [t1.r()])
            TS("dve", v8(t2), v8(t1), -1.0, None, ALU.mult, None, [t1.r()], [t2.r()])
            TT("dve", v8(t2), v8(t2), v8(t1), ALU.max, [t1.r(), t2.r()], [t2.r()])
            ACT(v8(t2), v8(t2), AF.Exp, [t2.r()], [t2.r()], scale=-1.0)
            ACT(v8(t2), v8(t2), AF.Ln, [t2.r(), r_c], [t2.r()], bias=eps_tile(1.0))
            TS("dve", v8(t1), v8(t1), 0.0, None, ALU.max, None, [t1.r()], [t1.r()])
            TT("dve", v8(t1), v8(t1), v8(t2), ALU.add, [t1.r(), t2.r()], [t1.r()])
            TT("dve", v8(gg), v8(t1), nea.f32(8).unsqueeze(1).to_broadcast([128, NG, 8]), ALU.mult, [t1.r(), nea.r()], [gg.r()])
            MM(banks[0][:, 0:NB * 8], Um.f32(128), gg.f32(NB * 8), [Um.r(), gg.r()], [rbank[0]])
            MM(banks[0][:, 256:256 + NB * 8], onesf[:], gg.f32(NB * 8), [r_c, gg.r()], [rbank[0]])
            CP("dve", gcol.f32(NB * 8), banks[0][:, 0:NB * 8], [], [gcol.r(), rbank[0]])
            CP("dve", gcol.f32(NG, 8)[:, NB, :], gg.f32(NG, 8)[:, NB, :], [gg.r()], [gcol.r()])
            ACT(eg.f32(NG * 8), gcol.f32(NG * 8), AF.Exp, [gcol.r()], [eg.r()])
            TT("dve", beg.f32(NG * 8), beta.f32(NG * 8), eg.f32(NG * 8), ALU.mult, [beta.r(), eg.r()], [beg.r()])
            ACT(gl.f32(NB * 8), banks[0][:, 256:256 + NB * 8], AF.Exp, [], [gl.r(), rbank[0]])
            TT("dve", egl.f32(NB * 8), banks[0][:, 256:256 + NB * 8], gcol.f32(NB * 8), ALU.subtract, [gcol.r()], [egl.r(), rbank[0]])
            ACT(egl.f32(NB * 8), egl.f32(NB * 8), AF.Exp, [egl.r()], [egl.r()])
            projsT = al(4 * 8 * NS)
            pjT = projsT.f32(4, 8, NS)
            lastU = al(24 * 3)
            luv = lastU.f32(24, 3)
            base_bufs = list(bufs)
            mH = A.mark()
            del bufs[:]
            Ub = al(T + 8)
            accq = al(T); vf = al(T)
            knf = vf
            qnb = al(T, bf=True); knb = al(T, bf=True)
            szf = accq
            sqb_ap = Ub.ap[:, 8:8 + T // 2].bitcast(BF16)
            ogb_ap = Ub.ap[:, 8 + T // 2:8 + T].bitcast(BF16)
            rsq = [al(TTS)]
            Sf = al(128); Sb = al(128, bf=True)
            wo = al(1024, bf=True)
            Ubv = Ub.ap[:, 5:8 + T]
            MS("pool", Ubv[:, 0:3], 0.0, [Ub.r()])

            class Ch:
                pass
            chains = []
            for ci in range(G):
                c = Ch()
                c.Ug = al(128); c.e1 = al(128); c.e2 = al(128); c.egrow = al(128)
                c.attnT = al(128, bf=True); c.qgT = al(128, bf=True)
                c.Xf = al(128); c.XTf = al(128); c.X8 = al(128); c.Z8 = al(128)
                c.Y1 = c.Ug; c.Z1 = c.e1; c.Y2 = c.e2; c.E0 = al(128); c.E1 = c.egrow
                c.Xo = al(4 * 128, bf=True); c.Zo = al(128, bf=True)
                c.Xb = al(128, bf=True); c.XTb = al(128, bf=True)
                c.Db = [al(128, bf=True), al(128, bf=True)]; c.Eb = [al(128, bf=True), al(128, bf=True)]
                c.M1 = al(128, bf=True); c.M1p = al(128, bf=True)
                c.PTb = c.Eb[0]
                c.vb = Sub(c.Xo, 0, 64); c.kbg = Sub(c.Xo, 64, 128); c.kg = Sub(c.Xo, 128, 192)
                c.u = c.Xf; c.wkT = Sub(c.Xo, 192, 256); c.vnew = Sub(c.Zo, 0, 64); c.on = c.XTf; c.ssq = al(8)
                c.osb = c.X8
                c.bA = ci
                c.bN = ci
                chains.append(c)
            B = lambda b_: b_.bf(128)
            F = lambda b_: b_.f32(128)

            for h in range(8):
                slots = []
                for typ in range(4):
                    if typ % 2 == 0:
                        wb_, wk_ = wslot()
                        wv_ = wb_.bf(2, 8, 128)
                    col = typ * 1024 + h * 128
                    P.dma("pool", wk_, wv_[:, typ % 2], gdn_w_in[:, col:col + 128].rearrange("(k p) n -> p k n", p=128), writes=[wb_.r()])
                    slots.append((wb_, wv_[:, typ % 2]))
                    if typ % 2 == 1:
                        P.commit(wk_)
                if True:
                    P.dma("pool", "gwo", wo.bf(1024), gdn_w_out[h * 128:(h + 1) * 128, :], writes=[wo.r()])
                cntb = [0]

                def project(typ, sink):
                    wb_, wv_ = slots[typ]
                    for ti, (c0, n) in enumerate(ptiles):
                        bk = 4 + cntb[0] % 2
                        cntb[0] += 1
                        for k in range(8):
                            MM(banks[bk][:, 0:n], wv_[:, k, :], xb[:, k, c0:c0 + n], [wb_.r(), r_xb[k][ti]], [rbank[bk]], start=(k == 0), stop=(k == 7))
                        sink(bk, c0, n)
                    bk = 4 + cntb[0] % 2
                    cntb[0] += 1
                    for k in range(8):
                        MM(banks[bk][:, 0:NS], wv_[:, k, :], xb[:, k, T:NT], [wb_.r(), r_xb[k][NTI - 1]], [rbank[bk]], start=(k == 0), stop=(k == 7))
                    CP("dve", pjT[:, typ, h, :], banks[bk][:, 0:NS], [], [projsT.r(), rbank[bk]])

                for typ in range(3):
                    ch = typ * 8 + h
                    project(typ, lambda bk, c0, n: CP("act", Ubv[:, 3 + c0:3 + c0 + n], banks[bk][:, 0:n], [], [Ub.r(), rbank[bk]]))
                    CP("act", luv[:, ch, :], Ubv[:, T:T + 3], [Ub.r()], [lastU.r()])
                    acc = [accq, knf, vf][typ]
                    ce = "dve"
                    cw = lambda j_: prm[:, 16 + j_ * 24 + ch:16 + j_ * 24 + ch + 1]
                    TS(ce, acc.f32(T), Ubv[:, 3:3 + T], cw(3), None, ALU.mult, None, [Ub.r(), r_c], [acc.r()])
                    for j_ in range(3):
                        STT(ce, acc.f32(T), Ubv[:, j_:j_ + T], cw(j_), acc.f32(T), ALU.mult, ALU.add, [Ub.r(), r_c, acc.r()], [acc.r()])
                    ACT(acc.f32(T), acc.f32(T), AF.Silu, [acc.r()], [acc.r()])
                    if typ < 2:
                        ACT(sqb_ap, acc.f32(T), AF.Square, [acc.r()], [Ub.r()])
                        for ti, (c0, n) in enumerate(ptiles):
                            bk = 4 + cntb[0] % 2
                            cntb[0] += 1
                            rs_ = rsq[0]
                            MM(banks[bk][:, 0:n], onesb[:], sqb_ap[:, c0:c0 + n], [Ub.r(), r_c], [rbank[bk]])
                            CP("dve", rs_.f32(n), banks[bk][:, 0:n], [], [rs_.r(), rbank[bk]])
                            rsqrt_eps(rs_.f32(n), rs_.f32(n), RMS_EPS, rs_.r())
                            if typ == 0:
                                STT("dve", qnb.bf(T)[:, c0:c0 + n], acc.f32(T)[:, c0:c0 + n], 128.0 ** -0.5, rs_.f32(n), ALU.mult, ALU.mult,
                                    [acc.r(), rs_.r()], [qnb.r()])
                            else:
                                TT("dve", acc.f32(T)[:, c0:c0 + n], acc.f32(T)[:, c0:c0 + n], rs_.f32(n), ALU.mult, [acc.r(), rs_.r()], [acc.r()])
                        if typ == 1:
                            CP("pool", knb.bf(T), knf.f32(T), [knf.r()], [knb.r()])
                project(3, lambda bk, c0, n: ACT(szf.f32(T)[:, c0:c0 + n], banks[bk][:, 0:n], AF.Silu, [], [szf.r(), rbank[bk]]))
                MS("dve", F(Sf), 0.0, [Sf.r()])
                MS("dve", B(Sb), 0.0, [Sb.r()])

                def st_a(c, b):
                    bs = slice(b * 128, (b + 1) * 128)
                    ACT(F(c.Ug), F(Um), AF.Identity, [Um.r(), gg.r()], [c.Ug.r()], scale=gg.f32(NG, 8)[:, b, h:h + 1])
                    yield
                    bk = banks[c.bA]
                    MM(bk[:, 0:128], knb.bf(T)[:, bs], knb.bf(T)[:, bs], [knb.r()], [rbank[c.bA]])
                    MM(bk[:, 128:256], knb.bf(T)[:, bs], qnb.bf(T)[:, bs], [knb.r(), qnb.r()], [rbank[c.bA]])
                    MM(bk[:, 256:384], onesf[:], F(c.Ug), [r_c, c.Ug.r()], [rbank[c.bA]])
                    yield
                    gc_ = gcol.f32(NG, 8)[:, b, h:h + 1]
                    STT("dve", F(c.e1), bk[:, 256:384], gc_, F(maskS), ALU.subtract, ALU.max, [gcol.r(), maskS.r()], [c.e1.r(), rbank[c.bA]])
                    STT("dve", F(c.e2), bk[:, 256:384], gc_, F(negT), ALU.subtract, ALU.min, [gcol.r(), negT.r()], [c.e2.r(), rbank[c.bA]])
                    ACT(F(c.egrow), bk[:, 256:384], AF.Exp, [], [c.egrow.r(), rbank[c.bA]])
                    yield
                    ACT(F(c.e1), F(c.e1), AF.Exp, [c.e1.r()], [c.e1.r()], scale=-1.0)
                    ACT(F(c.e2), F(c.e2), AF.Exp, [c.e2.r()], [c.e2.r()])
                    yield
                    STT("dve", F(c.Xf), bk[:, 0:128], nbeta.f32(NG, 8)[:, b, h:h + 1], F(c.e1), ALU.mult, ALU.mult, [nbeta.r(), c.e1.r()], [c.Xf.r(), rbank[c.bA]])
                    TT("dve", B(c.attnT), bk[:, 128:256], F(c.e2), ALU.mult, [c.e2.r()], [c.attnT.r(), rbank[c.bA]])
                    TT("dve", B(c.qgT), qnb.bf(T)[:, bs], F(c.egrow), ALU.mult, [qnb.r(), c.egrow.r()], [c.qgT.r()])

                def st_b(c, b):
                    bk = banks[c.bN]
                    TR(bk[:, 0:128], F(c.Xf), ident[:], [c.Xf.r(), r_c], [rbank[c.bN]])
                    yield
                    CP("act", F(c.XTf), bk[:, 0:128], [], [c.XTf.r(), rbank[c.bN]])
                    CP("act", B(c.XTb), bk[:, 0:128], [], [c.XTb.r(), rbank[c.bN]])
                    CP("act", B(c.Xb), F(c.Xf), [c.Xf.r()], [c.Xb.r()])
                    yield
                    TT("dve", F(c.X8), F(c.Xf), bmv[:, 0, :], ALU.mult, [c.Xf.r(), bmask.r()], [c.X8.r()])
                    TT("dve", F(c.Z8), F(c.XTf), bmv[:, 0, :], ALU.mult, [c.XTf.r(), bmask.r()], [c.Z8.r()])
                    yield
                    TT("dve", F(c.E0), F(c.Z8), ident[:], ALU.add, [c.Z8.r(), r_c], [c.E0.r()])

                def st_base1(c, b):
                    bk = banks[c.bN]
                    MM(bk[:, 0:128], F(c.Z8), F(c.X8), [c.Z8.r(), c.X8.r()], [rbank[c.bN]])
                    MM(bk[:, 128:256], F(c.X8), F(c.Z8), [c.Z8.r(), c.X8.r()], [rbank[c.bN]])
                    yield
                    CP("act", F(c.Y1), bk[:, 0:128], [], [c.Y1.r(), rbank[c.bN]])
                    CP("act", F(c.Z1), bk[:, 128:256], [], [c.Z1.r(), rbank[c.bN]])
                    yield
                    MM(bk[:, 256:384], F(c.Y1), F(c.E0), [c.Y1.r(), c.E0.r()], [rbank[c.bN]])
                    yield
                    TT("dve", F(c.E1), F(c.E0), bk[:, 256:384], ALU.add, [c.E0.r()], [c.E1.r(), rbank[c.bN]])

                def st_base2(c, b):
                    bk = banks[c.bN]
                    MM(bk[:, 0:128], F(c.Z1), F(c.Y1), [c.Z1.r(), c.Y1.r()], [rbank[c.bN]])
                    yield
                    CP("act", F(c.Y2), bk[:, 0:128], [], [c.Y2.r(), rbank[c.bN]])
                    yield
                    MM(bk[:, 128:256], F(c.Y2), F(c.E1), [c.Y2.r(), c.E1.r()], [rbank[c.bN]])
                    yield
                    TT("dve", F(c.E0), F(c.E1), bk[:, 128:256], ALU.add, [c.E1.r()], [c.E0.r(), rbank[c.bN]])
                    yield
                    TR(bk[:, 256:384], F(c.E0), ident[:], [c.E0.r(), r_c], [rbank[c.bN]])
                    yield
                    CP("act", B(c.Db[0]), bk[:, 256:384], [], [c.Db[0].r(), rbank[c.bN]])
                    CP("act", B(c.Eb[0]), F(c.E0), [c.E0.r()], [c.Eb[0].r()])

                def st_merge(l):
                    def f(c, b):
                        bk = banks[c.bN]
                        Dp, Ep = c.Db[l % 2], c.Eb[l % 2]
                        Dn, En = c.Db[(l + 1) % 2], c.Eb[(l + 1) % 2]
                        mk = bmv[:, 1 + l, :]
                        MM(bk[:, 0:128], B(c.XTb), B(Dp), [c.XTb.r(), Dp.r()], [rbank[c.bN]])
                        MM(bk[:, 128:256], B(c.Xb), B(Ep), [c.Xb.r(), Ep.r()], [rbank[c.bN]])
                        yield
                        TT("dve", B(c.M1), bk[:, 0:128], mk, ALU.mult, [bmask.r()], [c.M1.r(), rbank[c.bN]])
                        TT("dve", B(c.M1p), bk[:, 128:256], mk, ALU.mult, [bmask.r()], [c.M1p.r(), rbank[c.bN]])
                        yield
                        MM(bk[:, 256:384], identb[:], B(Dp), [r_c, Dp.r()], [rbank[c.bN]], start=True, stop=False)
                        MM(bk[:, 256:384], B(Ep), B(c.M1), [Ep.r(), c.M1.r()], [rbank[c.bN]], start=False, stop=True)
                        MM(bk[:, 384:512], identb[:], B(Ep), [r_c, Ep.r()], [rbank[c.bN]], start=True, stop=False)
                        MM(bk[:, 384:512], B(Dp), B(c.M1p), [Dp.r(), c.M1p.r()], [rbank[c.bN]], start=False, stop=True)
                        yield
                        CP("act", B(Dn), bk[:, 256:384], [], [Dn.r(), rbank[c.bN]])
                        CP("act", B(En), bk[:, 384:512], [], [En.r(), rbank[c.bN]])
                    return f

                def st_c(c, b):
                    bs = slice(b * 128, (b + 1) * 128)
                    bk = banks[c.bA]
                    TR(bfps(c.bA, 0, 128), knb.bf(T)[:, bs], identb[:], [knb.r(), r_c], [rbank[c.bA]])
                    TR(bk[:, 128:256], vf.f32(T)[:, bs], ident[:], [vf.r(), r_c], [rbank[c.bA]])
                    yield
                    ACT(B(c.vb), bk[:, 128:256], AF.Identity, [beta.r()], [c.vb.r(), rbank[c.bA]], scale=beta.f32(NG, 8)[:, b, h:h + 1])
                    ACT(B(c.kbg), bfps(c.bA, 0, 128), AF.Identity, [beg.r()], [c.kbg.r(), rbank[c.bA]], scale=beg.f32(NG, 8)[:, b, h:h + 1])
                    ACT(B(c.kg), bfps(c.bA, 0, 128), AF.Identity, [egl.r()], [c.kg.r(), rbank[c.bA]], scale=egl.f32(NG, 8)[:, b, h:h + 1])
                    yield
                    MM(bk[:, 256:384], B(c.PTb), B(c.vb), [c.PTb.r(), c.vb.r()], [rbank[c.bA]])
                    MM(bk[:, 384:512], B(c.kbg), B(c.PTb), [c.PTb.r(), c.kbg.r()], [rbank[c.bA]])
                    yield
                    CP("act", F(c.u), bk[:, 256:384], [], [c.u.r(), rbank[c.bA]])
                    CP("dve", B(c.wkT), bk[:, 384:512], [], [c.wkT.r(), rbank[c.bA]])

                def recur(c, b):
                    b6, b7 = banks[6], banks[7]
                    MM(b6[:, 0:128], B(c.wkT), B(Sb), [c.wkT.r(), Sb.r()], [rbank[6]])
                    TT("dve", B(c.vnew), F(c.u), b6[:, 0:128], ALU.subtract, [c.u.r()], [c.vnew.r(), rbank[6]])
                    MM(b7[:, 0:128], B(c.qgT), B(Sb), [c.qgT.r(), Sb.r()], [rbank[7]], start=True, stop=False)
                    MM(b7[:, 0:128], B(c.attnT), B(c.vnew), [c.attnT.r(), c.vnew.r()], [rbank[7]], start=False, stop=True)
                    MM(b6[:, 128:256], B(c.kg), B(c.vnew), [c.kg.r(), c.vnew.r()], [rbank[6]])
                    STT("dve", F(Sf), F(Sf), gl.f32(NB, 8)[:, b, h:h + 1], b6[:, 128:256], ALU.mult, ALU.add, [gl.r()], [Sf.r(), rbank[6]])
                    CP("act", B(Sb), F(Sf), [Sf.r()], [Sb.r()])
                    CP("act", F(c.osb), b7[:, 0:128], [], [c.osb.r(), rbank[7]])

                def recur_post(c, b):
                    bs = slice(b * 128, (b + 1) * 128)
                    ACT(F(c.on), F(c.osb), AF.Square, [c.osb.r()], [c.on.r(), c.ssq.r()], accum=c.ssq.f32(1))
                    rsqrt_eps(c.ssq.f32(1), c.ssq.f32(1), RMS_EPS * 128.0, c.ssq.r())
                    STT("dve", F(c.on), F(c.osb), c.ssq.f32(1), F(gnb), ALU.mult, ALU.mult, [c.osb.r(), c.ssq.r(), gnb.r()], [c.on.r()])
                    TR(banks[c.bN][:, 384:512], F(c.on), ident[:], [c.on.r(), r_c], [rbank[c.bN]])
                    TT("dve", ogb_ap[:, bs], banks[c.bN][:, 384:512], szf.f32(T)[:, bs], ALU.mult, [szf.r()], [Ub.r(), rbank[c.bN]])

                stages = [st_a, st_b, st_base1, st_base2] + [st_merge(l) for l in range(4)] + [st_c]
                pend_ = []
                for g0 in range(0, NB if "gdn_noblk" not in DBG else 0, G):
                    grp = [(chains[i], g0 + i) for i in range(min(G, NB - g0))]
                    for stg_ in stages:
                        gens_ = [stg_(c, b) for c, b in grp]
                        while gens_:
                            nx_ = []
                            for gn_ in gens_:
                                try:
                                    next(gn_)
                                    nx_.append(gn_)
                                except StopIteration:
                                    pass
                            gens_ = nx_
                    for c, b in grp:
                        recur(c, b)
                        if pend_:
                            recur_post(*pend_.pop())
                        pend_.append((c, b))
                    if pend_:
                        recur_post(*pend_.pop())
                if pend_:
                    recur_post(*pend_.pop())
                for dc in range(8):
                    for ti, (c0, n) in enumerate(ptiles):
                        bk = 4 + cntb[0] % 2
                        cntb[0] += 1
                        MM(banks[bk][:, 0:n], wo.bf(1024)[:, dc * 128:(dc + 1) * 128], ogb_ap[:, c0:c0 + n], [wo.r(), Ub.r()], [rbank[bk]])
                        xv = xf[:, dc, c0:c0 + n]
                        if h == 0:
                            STT("dve", xv, xv, ALPHA, banks[bk][:, 0:n], ALU.mult, ALU.add, [r_xf[dc][ti]], [r_xf[dc][ti], rbank[bk]])
                        else:
                            TT("dve", xv, xv, banks[bk][:, 0:n], ALU.add, [r_xf[dc][ti]], [r_xf[dc][ti], rbank[bk]])
                P.dma("sp", "gsp", o_gdn_p[h], F(Sf), reads=[Sf.r()])
            A.release(mH, list(bufs))
            del bufs[:]
            cst_ = A.f32(3072)
            for q4 in range(6):
                for i4 in range(4):
                    ch = q4 * 4 + i4
                    TR(banks[4][0:3, i4 * 128:(i4 + 1) * 128], luv[:, ch, :], ident[:], [lastU.r(), r_c], [rbank[4]])
                CP("dve", cst_.f32(3072)[0:3, q4 * 512:(q4 + 1) * 512], banks[4][0:3, :], [], [cst_.r(), rbank[4]])
            P.dma("sp", "gcp", o_conv_p, cst_.f32(3072)[0:3, :], reads=[cst_.r()])
            A.release(mH, [cst_])
            mS = A.mark()
            if "gdn_nosamp" in DBG:
                A.release(m, base_bufs)
                return
            sb_ = []

            def als(n, bf=False):
                b = A.bf(n) if bf else A.f32(n)
                sb_.append(b)
                return b
            projs = als(4096)
            for typ in range(4):
                for h2 in range(2):
                    bk = (typ * 2 + h2) % 2
                    for i4 in range(4):
                        hh = h2 * 4 + i4
                        TR(banks[bk][0:NS, i4 * 128:(i4 + 1) * 128], pjT[:, typ, hh, :], ident[:], [projsT.r(), r_c], [rbank[bk]])
                    CP("dve", projs.f32(4096)[0:NS, typ * 1024 + h2 * 512:typ * 1024 + (h2 + 1) * 512], banks[bk][0:NS, :], [], [projs.r(), rbank[bk]])
            P.dma("sp", "gcv", o_conv_s[:, 0:2, :], sconv[:, 1:3, :])
            P.dma("sp", "gcv", o_conv_s[:, 2, :], projs.f32(4096)[0:NS, 0:3072], reads=[projs.r()])
            P.commit("gcv")
            qkv = als(3072)
            zs = als(1024)
            mC = A.mark()
            PW = 256
            ext = A.f32(4 * PW); cwb = A.f32(4 * PW); prod = A.f32(4 * PW)
            for pc in range(3072 // PW):
                cs_ = slice(pc * PW, (pc + 1) * PW)
                P.dma("sp", "gsl", ext.f32(4, PW)[0:NS, 0:3, :], sconv[:, :, cs_], writes=[ext.r()])
                P.dma("sp", "gsl", cwb.f32(4, PW)[0:NS], gdn_conv_w[:, cs_].partition_broadcast(NS), writes=[cwb.r()])
                P.commit("gsl")
                CP("dve", ext.f32(4, PW)[0:NS, 3, :], projs.f32(4096)[0:NS, cs_], [projs.r()], [ext.r()])
                TT("dve", prod.f32(4, PW)[0:NS], ext.f32(4, PW)[0:NS], cwb.f32(4, PW)[0:NS], ALU.mult, [ext.r(), cwb.r()], [prod.r()])
                RED("dve", qkv.f32(3072)[0:NS, cs_], prod.f32(4, PW)[0:NS].rearrange("p j c -> p c j"), [prod.r()], [qkv.r()])
            A.release(mC, [ext, cwb, prod])
            ACT(qkv.f32(3072)[0:NS], qkv.f32(3072)[0:NS], AF.Silu, [qkv.r()], [qkv.r()])
            ACT(zs.f32(1024)[0:NS], projs.f32(4096)[0:NS, 3072:4096], AF.Silu, [projs.r()], [zs.r()])
            q3 = qkv.f32(3, 8, 128)[0:NS, 0]
            k3 = qkv.f32(3, 8, 128)[0:NS, 1]
            v3 = qkv.f32(3, 8, 128)[0:NS, 2]
            tmp = als(1024); ss = als(16)
            t3 = tmp.f32(8, 128)[0:NS]
            for (x3, scl) in ((q3, 128.0 ** -0.5), (k3, 1.0)):
                TT("dve", t3, x3, x3, ALU.mult, [qkv.r()], [tmp.r()])
                RED("dve", ss.f32(8)[0:NS], t3, [tmp.r()], [ss.r()])
                rsqrt_eps(ss.f32(8)[0:NS], ss.f32(8)[0:NS], RMS_EPS, ss.r())
                STT("dve", x3, x3, scl, ss.f32(8)[0:NS].unsqueeze(2).to_broadcast([NS, 8, 128]), ALU.mult, ALU.mult, [ss.r()], [qkv.r()])
            qk = als(8)
            TT("dve", t3, q3, k3, ALU.mult, [qkv.r()], [tmp.r()])
            RED("dve", qk.f32(8)[0:NS], t3, [tmp.r()], [qk.r()])
            kTs = als(8 * NS); qTs = als(8 * NS)
            for (x3, dst, bk) in ((k3, kTs, 4), (q3, qTs, 5)):
                for hh in range(8):
                    TR(banks[bk][:, hh * NS:(hh + 1) * NS], x3[:, hh, :], ident[0:NS, 0:NS], [qkv.r(), r_c], [rbank[bk]])
                CP("dve", dst.f32(8 * NS), banks[bk][:, 0:8 * NS], [], [dst.r(), rbank[bk]])
            dlt = als(NS * NS)
            P.dma("sp", "gsm", dlt.f32(NS, NS), cd["delta"], writes=[dlt.r()])
            dcol = als(NS)
            P.dma("sp", "gsm", dcol.f32(NS), cd["dcol"], writes=[dcol.r()])
            P.commit("gsm")
            KmL = [als(NS * NS), als(NS * NS)]; QmL = [als(NS * NS), als(NS * NS)]
            Sp = [als(128) for _ in range(4)]
            ci_ = 0
            for hh in range(8):
                Km = KmL[hh % 2]; Qm = QmL[hh % 2]
                for (src, dst) in ((kTs, Km), (qTs, Qm)):
                    TT("dve", dst.f32(NS, NS), src.f32(8, NS)[:, hh, :].unsqueeze(1).to_broadcast([128, NS, NS]),
                       dlt.f32(NS, NS), ALU.mult, [src.r(), dlt.r()], [dst.r()])
                for s in range(NS):
                    sp_ = Sp[ci_ % 4]
                    P.dma("sp", "gsq%d" % (ci_ % 4), sp_.f32(128), sgdn[s, hh], writes=[sp_.r()])
                    ci_ += 1
                    MM(banks[hh // 4][0:NS, (hh % 4) * 128:(hh % 4 + 1) * 128], Km.f32(NS, NS)[:, s, :], sp_.f32(128),
                       [Km.r(), sp_.r()], [rbank[hh // 4]], start=(s == 0), stop=(s == NS - 1))
                    MM(banks[2 + hh // 4][0:NS, (hh % 4) * 128:(hh % 4 + 1) * 128], Qm.f32(NS, NS)[:, s, :], sp_.f32(128),
                       [Qm.r(), sp_.r()], [rbank[2 + hh // 4]], start=(s == 0), stop=(s == NS - 1))
            KS = als(1024); QS = als(1024)
            for half in range(2):
                CP("dve", KS.f32(1024)[0:NS, half * 512:(half + 1) * 512], banks[half][0:NS, :], [], [KS.r(), rbank[half]])
                CP("dve", QS.f32(1024)[0:NS, half * 512:(half + 1) * 512], banks[2 + half][0:NS, :], [], [QS.r(), rbank[2 + half]])
            bc8 = lambda b_, col: b_.f32(NG, 8)[0:NS, col, :].unsqueeze(2).to_broadcast([NS, 8, 128])
            KS3 = KS.f32(8, 128)[0:NS]; QS3 = QS.f32(8, 128)[0:NS]
            vn = als(1024)
            vn3 = vn.f32(8, 128)[0:NS]
            TT("dve", KS3, KS3, bc8(eg, NB), ALU.mult, [eg.r()], [KS.r()])
            TT("dve", vn3, v3, KS3, ALU.subtract, [qkv.r(), KS.r()], [vn.r()])
            TT("dve", vn3, vn3, bc8(beta, NB), ALU.mult, [beta.r()], [vn.r()])
            TT("dve", QS3, QS3, bc8(eg, NB), ALU.mult, [eg.r()], [QS.r()])
            TT("dve", t3, vn3, qk.f32(8)[0:NS].unsqueeze(2).to_broadcast([NS, 8, 128]), ALU.mult, [vn.r(), qk.r()], [tmp.r()])
            TT("dve", QS3, QS3, t3, ALU.add, [tmp.r()], [QS.r()])
            TT("dve", t3, QS3, QS3, ALU.mult, [QS.r()], [tmp.r()])
            RED("dve", ss.f32(8)[0:NS], t3, [tmp.r()], [ss.r()])
            rsqrt_eps(ss.f32(8)[0:NS], ss.f32(8)[0:NS], RMS_EPS, ss.r(), scale=1.0 / 128.0)
            TT("dve", QS3, QS3, ss.f32(8)[0:NS].unsqueeze(2).to_broadcast([NS, 8, 128]), ALU.mult, [ss.r()], [QS.r()])
            TT("dve", QS3, QS3, gn.f32(128)[0:NS].unsqueeze(1).to_broadcast([NS, 8, 128]), ALU.mult, [gn.r()], [QS.r()])
            TT("dve", QS3, QS3, zs.f32(8, 128)[0:NS], ALU.mult, [zs.r()], [QS.r()])
            ogsT = als(8 * NS, bf=True)
            for hh in range(8):
                TR(banks[4][:, hh * NS:(hh + 1) * NS], QS3[:, hh, :], ident[0:NS, 0:NS], [QS.r(), r_c], [rbank[4]])
            CP("dve", ogsT.bf(8 * NS), banks[4][:, 0:8 * NS], [], [ogsT.r(), rbank[4]])
            out_proj_samples(gdn_w_out, 8, ogsT.bf(8, NS), ogsT.r())
            Rm = als(NS * 8)
            TT("dve", Rm.f32(NS, 8)[0:NS], dcol.f32(NS)[0:NS].unsqueeze(2).to_broadcast([NS, NS, 8]),
               eg.f32(NG, 8)[0:NS, NB, :].unsqueeze(1).to_broadcast([NS, NS, 8]), ALU.mult, [dcol.r(), eg.r()], [Rm.r()])
            EGb = als(NS * 8)
            MM(banks[4][:, 0:NS * 8], onesf[0:NS, :], Rm.f32(NS * 8)[0:NS], [Rm.r(), r_c], [rbank[4]])
            CP("dve", EGb.f32(NS * 8), banks[4][:, 0:NS * 8], [], [EGb.r(), rbank[4]])
            Vm = [als(1024), als(1024)]
            Sn = [als(128) for _ in range(4)]
            ci_ = 0
            for s in range(NS):
                vm_ = Vm[s % 2]
                TS("dve", vm_.f32(1024)[0:NS], vn.f32(1024)[0:NS], dcol.f32(NS)[0:NS, s:s + 1], None, ALU.mult, None, [vn.r(), dcol.r()], [vm_.r()])
                for hh in range(8):
                    sp_ = Sp[ci_ % 4]; sn_ = Sn[ci_ % 4]
                    bk = 4 + ci_ % 4
                    P.dma("sp", "gsq%d" % (ci_ % 4), sp_.f32(128), sgdn[s, hh], writes=[sp_.r()])
                    MM(banks[bk][:, 0:128], k3[:, hh, :], vm_.f32(8, 128)[0:NS, hh, :], [qkv.r(), vm_.r()], [rbank[bk]])
                    STT("dve", sn_.f32(128), sp_.f32(128), EGb.f32(NS, 8)[:, s, hh:hh + 1], banks[bk][:, 0:128], ALU.mult, ALU.add,
                        [sp_.r(), EGb.r()], [sn_.r(), rbank[bk]])
                    P.dma("sp", "gst%d" % (ci_ % 4), o_gdn_s[s, hh], sn_.f32(128), reads=[sn_.r()])
                    ci_ += 1
            A.release(mS, sb_)
            A.release(m, base_bufs)

        def ret_mixer():
            G = 2
            m = A.mark()
            bufs = []

            def al(n, bf=False):
                b = A.bf(n) if bf else A.f32(n)
                bufs.append(b)
                return b
            cos2 = al(NT); sin2 = al(NT); decT = al(128); xi = al(128); zeta = al(8); dcol = al(NS); dlt = al(NS * NS)
            P.dma("sp", "rc", cos2.f32(NT), cd["cos2"], writes=[cos2.r()])
            P.dma("sp", "rc", sin2.f32(NT), cd["sin2"], writes=[sin2.r()])
            P.dma("sp", "rc", zeta.f32(8), cd["zeta"], writes=[zeta.r()])
            P.dma("sp", "rc", dcol.f32(NS), cd["dcol"], writes=[dcol.r()])
            P.dma("sp", "rc", dlt.f32(NS, NS), cd["delta"], writes=[dlt.r()])
            P.commit("rc")
            SC = 128.0 ** -0.5
            qrb = al(NT, bf=True); krb = al(NT, bf=True); krf = al(NT); qsf = al(NS)
            t1 = [al(TTS), al(TTS)]; t2 = [al(TTS), al(TTS)]
            vtok = al(NB * 256, bf=True)
            vtv = vtok.bf(NB, 256)
            ogT = al(2 * NT, bf=True)
            ogv = ogT.bf(2, NT)
            Sf = al(256); Sb = al(256, bf=True)
            v_s = al(256); sg_s = al(256); kts = al(128); Qm = al(NS * NS); qk = al(8); tmp_s = al(256); o_s = al(256)
            Sp = [al(256) for _ in range(2)]
            Vm = [al(256), al(256)]; Sn = [al(256), al(256)]
            stat = [al(8) for _ in range(3)]

            class Ch:
                pass
            chains = []
            for ci in range(G):
                c = Ch()
                c.attnT = al(128, bf=True); c.qxT = al(128, bf=True); c.kz = al(128, bf=True)
                c.sg = al(256); c.on = al(256); c.osb = al(256)
                c.bA = ci % 2
                chains.append(c)
            B = lambda b_: b_.bf(128)

            def gnorm_gate(o_ap, np_, on_ap, on_res, sg_ap, sg_res, o_reads, o_writes):
                sm, sq_, mm = stat[0], stat[1], stat[2]
                ACT(on_ap, o_ap, AF.Identity, o_reads, [on_res, sm.r()] + o_writes, accum=sm.f32(1)[0:np_])
                ACT(on_ap, o_ap, AF.Square, o_reads, [on_res, sq_.r()] + o_writes, accum=sq_.f32(1)[0:np_])
                TS("dve", sm.f32(1)[0:np_], sm.f32(1)[0:np_], 1.0 / 256.0, None, ALU.mult, None, [sm.r()], [sm.r()])
                TT("dve", mm.f32(1)[0:np_], sm.f32(1)[0:np_], sm.f32(1)[0:np_], ALU.mult, [sm.r()], [mm.r()])
                STT("dve", sq_.f32(1)[0:np_], sq_.f32(1)[0:np_], 1.0 / 256.0, mm.f32(1)[0:np_], ALU.mult, ALU.subtract, [sq_.r(), mm.r()], [sq_.r()])
                rsqrt_eps(sq_.f32(1)[0:np_], sq_.f32(1)[0:np_], LN_EPS, sq_.r())
                TS("dve", on_ap, o_ap, sm.f32(1)[0:np_], sq_.f32(1)[0:np_], ALU.subtract, ALU.mult, o_reads + [sm.r(), sq_.r()], [on_res] + o_writes)
                TT("pool", on_ap, on_ap, sg_ap, ALU.mult, [on_res, sg_res], [on_res])

            for h in range(8):
                slots = []
                for xi_, base in enumerate((0, 1024)):
                    wb_, wk_ = wslot()
                    wv_ = wb_.bf(2, 8, 128)
                    c0_ = base + h * 128
                    r3 = lambda a, b_: ret_w_in[:, a:b_].rearrange("(k p) n -> p k n", p=128)
                    P.dma("pool", wk_, wv_[:, 0], r3(c0_, c0_ + 128), writes=[wb_.r()])
                    P.dma("pool", wk_, wv_[:, 1, :, 0:64], r3(c0_ + 64, c0_ + 128), writes=[wb_.r()])
                    P.dma("pool", wk_, wv_[:, 1, :, 64:128], r3(c0_, c0_ + 64), writes=[wb_.r()])
                    P.commit(wk_)
                    slots.append((wb_, wv_))
                P.dma("sp", "rdx", decT.f32(128), cd["decT"][:, h, :], writes=[decT.r()])
                P.dma("sp", "rdx", xi.f32(128), cd["xi"][:, h, :], writes=[xi.r()])
                P.commit("rdx")
                wv_t, wkv_ = wslot()
                P.dma("pool", wkv_, wv_t.bf(8, 256), ret_w_in[:, 2048 + h * 256:2048 + (h + 1) * 256].rearrange("(k p) n -> p k n", p=128), writes=[wv_t.r()])
                P.commit(wkv_)
                wg_t, wkg_ = wslot()
                P.dma("pool", wkg_, wg_t.bf(8, 256), ret_w_in[:, 4096 + h * 256:4096 + (h + 1) * 256].rearrange("(k p) n -> p k n", p=128), writes=[wg_t.r()])
                P.commit(wkg_)
                cn = 0
                for xi_ in range(2):
                    wb_, wv_ = slots[xi_]
                    for ti, (c0, n) in enumerate(tiles):
                        a1, a2 = t1[cn % 2], t2[cn % 2]
                        ba_, bb_ = 4 + 2 * (cn % 2), 5 + 2 * (cn % 2)
                        cn += 1
                        for k in range(8):
                            MM(banks[ba_][:, 0:n], wv_[:, 0, k, :], xb[:, k, c0:c0 + n], [wb_.r(), r_xb[k][ti]], [rbank[ba_]], start=(k == 0), stop=(k == 7))
                        for k in range(8):
                            MM(banks[bb_][:, 0:n], wv_[:, 1, k, :], xb[:, k, c0:c0 + n], [wb_.r(), r_xb[k][ti]], [rbank[bb_]], start=(k == 0), stop=(k == 7))
                        TT("dve", a1.f32(n), banks[ba_][:, 0:n], cos2.f32(NT)[:, c0:c0 + n], ALU.mult, [cos2.r()], [a1.r(), rbank[ba_]])
                        TT("dve", a2.f32(n), banks[bb_][:, 0:n], sin2.f32(NT)[:, c0:c0 + n], ALU.mult, [sin2.r()], [a2.r(), rbank[bb_]])
                        if xi_ == 0:
                            TT("pool", qrb.bf(NT)[:, c0:c0 + n], a1.f32(n), a2.f32(n), ALU.add, [a1.r(), a2.r()], [qrb.r()])
                            if ti == NTI - 1:
                                TT("pool", qsf.f32(NS), a1.f32(n), a2.f32(n), ALU.add, [a1.r(), a2.r()], [qsf.r()])
                        else:
                            TT("pool", krf.f32(NT)[:, c0:c0 + n], a1.f32(n), a2.f32(n), ALU.add, [a1.r(), a2.r()], [krf.r()])
                CP("pool", krb.bf(NT), krf.f32(NT), [krf.r()], [krb.r()])
                for b in range(NB):
                    bs = slice(b * 128, (b + 1) * 128)
                    bv_ = 6 + b % 2
                    for k in range(8):
                        MM(banks[bv_][:, 0:256], xb[:, k, bs], wv_t.bf(8, 256)[:, k, :], [wv_t.r()] + rx(r_xb, [k], tiles_of(b * 128, 128)), [rbank[bv_]],
                           start=(k == 0), stop=(k == 7))
                    CP("act", vtv[:, b, :], banks[bv_][:, 0:256], [], [vtok.r(), rbank[bv_]])
                for k in range(8):
                    MM(banks[6][0:NS, 256:512], xb[:, k, T:NT], wv_t.bf(8, 256)[:, k, :], [wv_t.r(), r_xb[k][NTI - 1]], [rbank[6]], start=(k == 0), stop=(k == 7))
                CP("act", v_s.f32(256)[0:NS], banks[6][0:NS, 256:512], [], [v_s.r(), rbank[6]])
                MS("dve", Sf.f32(256), 0.0, [Sf.r()])
                MS("dve", Sb.bf(256), 0.0, [Sb.r()])

                def st_a(c, b):
                    bs = slice(b * 128, (b + 1) * 128)
                    bk = banks[c.bA]
                    MM(bk[:, 0:128], krb.bf(NT)[:, bs], qrb.bf(NT)[:, bs], [krb.r(), qrb.r()], [rbank[c.bA]])
                    TR(bk[:, 128:256], krf.f32(NT)[:, bs], ident[:], [krf.r(), r_c], [rbank[c.bA]])
                    for k in range(8):
                        MM(bk[:, 256:512], xb[:, k, bs], wg_t.bf(8, 256)[:, k, :], [wg_t.r()] + rx(r_xb, [k], tiles_of(b * 128, 128)), [rbank[c.bA]],
                           start=(k == 0), stop=(k == 7))
                    yield
                    TT("dve", B(c.attnT), bk[:, 0:128], decT.f32(128), ALU.mult, [decT.r()], [c.attnT.r(), rbank[c.bA]])
                    ACT(B(c.kz), bk[:, 128:256], AF.Identity, [zeta.r()], [c.kz.r(), rbank[c.bA]], scale=zeta.f32(8)[:, h:h + 1])
                    ACT(c.sg.f32(256), bk[:, 256:512], AF.Silu, [], [c.sg.r(), rbank[c.bA]])
                    TT("pool", B(c.qxT), qrb.bf(NT)[:, bs], xi.f32(128), ALU.mult, [qrb.r(), xi.r()], [c.qxT.r()])

                def recur(c, b):
                    MM(banks[2][:, 0:256], B(c.qxT), Sb.bf(256), [c.qxT.r(), Sb.r()], [rbank[2]], start=True, stop=False)
                    MM(banks[2][:, 0:256], B(c.attnT), vtv[:, b, :], [c.attnT.r(), vtok.r()], [rbank[2]], start=False, stop=True)
                    MM(banks[3][:, 0:256], B(c.kz), vtv[:, b, :], [c.kz.r(), vtok.r()], [rbank[3]])
                    STT("dve", Sf.f32(256), Sf.f32(256), cst["gC"][h], banks[3][:, 0:256], ALU.mult, ALU.add, [], [Sf.r(), rbank[3]])
                    CP("act", Sb.bf(256), Sf.f32(256), [Sf.r()], [Sb.r()])
                    CP("act", c.osb.f32(256), banks[2][:, 0:256], [], [c.osb.r(), rbank[2]])

                def recur_post(c, b):
                    bs = slice(b * 128, (b + 1) * 128)
                    gnorm_gate(c.osb.f32(256), 128, c.on.f32(256), c.on.r(), c.sg.f32(256), c.sg.r(), [c.osb.r()], [])
                    for cc in range(2):
                        TR(banks[7][:, 256 + cc * 128:256 + (cc + 1) * 128], c.on.f32(256)[:, cc * 128:(cc + 1) * 128], ident[:], [c.on.r(), r_c], [rbank[7]])
                    CP("act", ogv[:, :, bs], banks[7][:, 256:512].rearrange("p (c t) -> p c t", c=2, t=128), [], [ogT.r(), rbank[7]])

                pend_ = []
                for g0 in range(0, NB, G):
                    grp = [(chains[i], g0 + i) for i in range(min(G, NB - g0))]
                    gens_ = [st_a(c, b) for c, b in grp]
                    while gens_:
                        nx_ = []
                        for gn_ in gens_:
                            try:
                                next(gn_)
                                nx_.append(gn_)
                            except StopIteration:
                                pass
                        gens_ = nx_
                    for c, b in grp:
                        recur(c, b)
                        if pend_:
                            recur_post(*pend_.pop())
                        pend_.append((c, b))
                    if pend_:
                        recur_post(*pend_.pop())
                if pend_:
                    recur_post(*pend_.pop())
                P.dma("sp", "rsp", o_ret_p[h], Sf.f32(256), reads=[Sf.r()])
                for k in range(8):
                    MM(banks[6][0:NS, 0:256], xb[:, k, T:NT], wg_t.bf(8, 256)[:, k, :], [wg_t.r(), r_xb[k][NTI - 1]], [rbank[6]], start=(k == 0), stop=(k == 7))
                ACT(sg_s.f32(256)[0:NS], banks[6][0:NS, 0:256], AF.Silu, [], [sg_s.r(), rbank[6]])
                ks_ = krf.f32(NT)[:, T:NT]
                TR(banks[7][0:NS, 0:128], ks_, ident[:], [krf.r(), r_c], [rbank[7]])
                CP("dve", kts.f32(128)[0:NS], banks[7][0:NS, 0:128], [], [kts.r(), rbank[7]])
                TT("dve", tmp_s.f32(NS), qsf.f32(NS), ks_, ALU.mult, [qsf.r(), krf.r()], [tmp_s.r()])
                MM(banks[7][0:NS, 128:129], tmp_s.f32(NS), onesf[:, 0:1], [tmp_s.r(), r_c], [rbank[7]])
                TS("dve", qk.f32(1)[0:NS], banks[7][0:NS, 128:129], SC, None, ALU.mult, None, [], [qk.r(), rbank[7]])
                TT("dve", Qm.f32(NS, NS), qsf.f32(NS).unsqueeze(1).to_broadcast([128, NS, NS]), dlt.f32(NS, NS), ALU.mult, [qsf.r(), dlt.r()], [Qm.r()])
                for s in range(NS):
                    sp_ = Sp[s % 2]
                    P.dma("sp", "rsq%d" % (s % 2), sp_.f32(256), sret[s, h], writes=[sp_.r()])
                    MM(banks[6][0:NS, 256:512], Qm.f32(NS, NS)[:, s, :], sp_.f32(256), [Qm.r(), sp_.r()], [rbank[6]], start=(s == 0), stop=(s == NS - 1))
                TS("dve", tmp_s.f32(256)[0:NS], v_s.f32(256)[0:NS], qk.f32(1)[0:NS], None, ALU.mult, None, [v_s.r(), qk.r()], [tmp_s.r()])
                STT("dve", o_s.f32(256)[0:NS], banks[6][0:NS, 256:512], cst["gam"][h], tmp_s.f32(256)[0:NS], ALU.mult, ALU.add, [tmp_s.r()], [o_s.r(), rbank[6]])
                gnorm_gate(o_s.f32(256)[0:NS], NS, tmp_s.f32(256)[0:NS], tmp_s.r(), sg_s.f32(256)[0:NS], sg_s.r(), [o_s.r()], [])
                for cc in range(2):
                    TR(banks[7][:, 256 + cc * NS:256 + (cc + 1) * NS], tmp_s.f32(256)[0:NS, cc * 128:(cc + 1) * 128], ident[0:NS, 0:NS], [tmp_s.r(), r_c], [rbank[7]])
                CP("dve", ogv[:, :, T:NT], banks[7][:, 256:256 + 2 * NS].rearrange("p (c t) -> p c t", c=2, t=NS), [], [ogT.r(), rbank[7]])
                for s in range(NS):
                    sp_ = Sp[s % 2]; vm_ = Vm[s % 2]; sn_ = Sn[s % 2]
                    bk = 4 + s % 2
                    P.dma("sp", "rsq%d" % (s % 2), sp_.f32(256), sret[s, h], writes=[sp_.r()])
                    TS("dve", vm_.f32(256)[0:NS], v_s.f32(256)[0:NS], dcol.f32(NS)[0:NS, s:s + 1], SC, ALU.mult, ALU.mult, [v_s.r(), dcol.r()], [vm_.r()])
                    MM(banks[bk][:, 0:256], kts.f32(128)[0:NS], vm_.f32(256)[0:NS], [kts.r(), vm_.r()], [rbank[bk]])
                    STT("dve", sn_.f32(256), sp_.f32(256), cst["gam"][h], banks[bk][:, 0:256], ALU.mult, ALU.add, [sp_.r()], [sn_.r(), rbank[bk]])
                    P.dma("sp", "rst%d" % (s % 2), o_ret_s[s, h], sn_.f32(256), reads=[sn_.r()])
                wo_t, wko_ = wslot()
                P.dma("pool", wko_, wo_t.bf(2, 1024), ret_w_out[h * 256:(h + 1) * 256, :].rearrange("(c p) n -> p c n", p=128), writes=[wo_t.r()])
                P.commit(wko_)
                for dc in range(8):
                    for ti, (c0, n) in enumerate(tiles):
                        bk = 4 + (dc * NTI + ti) % 2
                        for cc in range(2):
                            MM(banks[bk][:, 0:n], wo_t.bf(2, 1024)[:, cc, dc * 128:(dc + 1) * 128], ogv[:, cc, c0:c0 + n], [wo_t.r(), ogT.r()], [rbank[bk]],
                               start=(cc == 0), stop=(cc == 1))
                        xv = xf[:, dc, c0:c0 + n]
                        if h == 0:
                            STT("dve", xv, xv, ALPHA, banks[bk][:, 0:n], ALU.mult, ALU.add, [r_xf[dc][ti]], [r_xf[dc][ti], rbank[bk]])
                        else:
                            TT("dve", xv, xv, banks[bk][:, 0:n], ALU.add, [r_xf[dc][ti]], [r_xf[dc][ti], rbank[bk]])
            A.release(m, bufs)

        if "noload" not in DBG:
            load_x()
        for li in range(nlayers):
            kind = li % 3
            if kind == 0:
                pool_mixer(li // 3)
            elif kind == 1:
                gdn_mixer()
            else:
                ret_mixer()
            layer_norm(2 * li)
            ffn(li)
            layer_norm(2 * li + 1)
        if "nostore" not in DBG:
            store_y()
        if "arena" in DBG:
            print("arena words", AW, "high water", A.hw)
        P.emit(st)
    return nc, cst


_CACHE = {}


def _in_maps(inputs, T, NS, cst):
    f = lambda a: np.ascontiguousarray(np.asarray(a, dtype=np.float32))
    shared = {
        "pool_w": f(inputs["pool_w"]), "pool_scale": f(inputs["pool_scale"]),
        "gdn_w_in": f(inputs["gdn_w_in"][0]), "gdn_conv_w": f(inputs["gdn_conv_w"][0]),
        "gdn_a_log": f(inputs["gdn_a_log"]), "gdn_dt_bias": f(inputs["gdn_dt_bias"]),
        "gdn_norm_g": f(inputs["gdn_norm_g"]), "gdn_w_out": f(inputs["gdn_w_out"][0]),
        "ret_w_in": f(inputs["ret_w_in"][0]), "ret_w_out": f(inputs["ret_w_out"][0]),
        "ffn_w13": f(inputs["ffn_w13"]), "ffn_w2": f(inputs["ffn_w2"]),
        "ln_g": f(inputs["ln_g"]).reshape(8, D), "ln_b": f(inputs["ln_b"]).reshape(8, D),
    }
    for k in list(CONST_SHAPES) + ["cos2", "sin2"]:
        shared["c_" + k] = f(cst[k])
    maps = []
    for c in range(NCORES):
        sl = slice(c * NS, (c + 1) * NS)
        mp = dict(shared)
        mp["xp"] = f(inputs["x_prompt"][c])
        mp["xs"] = f(inputs["x_sample"][sl, 0])
        mp["spool"] = f(inputs["state_pool"][:, sl])
        mp["sconv"] = f(inputs["state_gdn_conv"][0, sl])
        mp["sgdn"] = f(inputs["state_gdn"][0, sl])
        mp["sret"] = f(inputs["state_ret"][0, sl])
        maps.append(mp)
    return maps


def kernel(**inputs):
    T, NS = 2048, 16
    if "nc" not in _CACHE:
        _CACHE["nc"] = build(T, NS)
    nc, cst = _CACHE["nc"]
    maps = _in_maps(inputs, T, NS, cst)
    res = run_bass_kernel_spmd(nc, maps, core_ids=list(range(NCORES))).results
    cat = lambda k, ax: np.concatenate([np.asarray(r[k], dtype=np.float32)[None] if ax is None else np.asarray(r[k], dtype=np.float32)
                                        for r in res], axis=0 if ax is None else ax)
    y_prompt = cat("yp", None)
    y_sample = cat("ys", 0)[:, None, :]
    pool_p = np.stack([np.asarray(r["pool_p"], np.float32) for r in res], axis=1)
    pool_s = cat("pool_s", 1)
    conv_p = cat("conv_p", None)[None]
    conv_s = cat("conv_s", 0)[None]
    gdn_p = cat("gdn_p", None)[None]
    gdn_s = cat("gdn_s", 0)[None]
    ret_p = cat("ret_p", None)[None]
    ret_s = cat("ret_s", 0)[None]
    return (y_prompt, y_sample, pool_p, pool_s, conv_p, conv_s, gdn_p, gdn_s, ret_p, ret_s)
```

```python
import math
from contextlib import ExitStack
import numpy as np
import concourse.bass as bass
import concourse.mybir as mybir
from concourse.bass_utils import run_bass_kernel_spmd

F32 = mybir.dt.float32
BF16 = mybir.dt.bfloat16
ALU = mybir.AluOpType
AF = mybir.ActivationFunctionType
AX = mybir.AxisListType

D = 1024
DFF = 2816
NCORES = 8
PAST_LEN = 16384
ALPHA = 8.0 ** 0.25
LN_EPS = 1e-5
RMS_EPS = 1e-6
COMPUTE = ("pe", "act", "dve", "pool")


class Res:
    __slots__ = ("last_w", "readers")

    def __init__(self, inherit=None):
        self.last_w = None
        self.readers = dict(inherit) if inherit else {}


class Op:
    __slots__ = ("eng", "fn", "deps", "is_dma", "dkey", "dcount", "need_inc", "cnt", "idx")

    def __init__(self, eng, fn, is_dma=False, dkey=None):
        self.eng = eng
        self.fn = fn
        self.deps = []
        self.is_dma = is_dma
        self.dkey = dkey
        self.dcount = 0
        self.need_inc = False
        self.cnt = 0
        self.idx = 0


class Prog:
    def __init__(self, nc):
        self.nc = nc
        self.ops = []
        self.dma_counts = {}
        self.open_batch = {}

    def _add(self, op, reads, writes):
        op.idx = len(self.ops)
        deps = {}
        for r in reads:
            w = r.last_w
            if w is not None:
                deps[id(w)] = (w, True)
        for r in writes:
            w = r.last_w
            if w is not None and id(w) not in deps:
                deps[id(w)] = (w, False)
            for rd in r.readers.values():
                if id(rd) not in deps:
                    deps[id(rd)] = (rd, False)
        for (d, raw) in deps.values():
            if d is op:
                continue
            if (not d.is_dma) and (not op.is_dma) and d.eng == op.eng and d.eng == "pe":
                continue
            if d.is_dma and op.is_dma and d.dkey == op.dkey and d in self.open_batch.get(op.dkey, ()):
                continue
            op.deps.append(d)
            d.need_inc = True
        for r in reads:
            key = (op.eng, op.dkey) if op.is_dma else op.eng
            r.readers[key] = op
        for r in writes:
            r.last_w = op
            r.readers = {}
        self.ops.append(op)
        return op

    def op(self, eng, fn, reads=(), writes=()):
        return self._add(Op(eng, fn), reads, writes)

    def dma(self, eng, key, out, in_, reads=(), writes=()):
        def fn(e, out=out, in_=in_):
            return e.dma_start(out=out, in_=in_)
        o = Op(eng, fn, is_dma=True, dkey=key)
        self.dma_counts[key] = self.dma_counts.get(key, 0) + 16
        o.dcount = self.dma_counts[key]
        o.need_inc = True
        self.open_batch.setdefault(key, []).append(o)
        return self._add(o, reads, writes)

    def commit(self, key):
        for o in self.open_batch.get(key, []):
            o.dcount = self.dma_counts[key]
        self.open_batch[key] = []

    def emit(self, st):
        nc = self.nc
        sems = {e: st.enter_context(nc.semaphore("s_" + e)) for e in COMPUTE}
        dsems = {k: st.enter_context(nc.semaphore("d_%s" % (k,))) for k in self.dma_counts}
        cnts = {e: 0 for e in COMPUTE}
        for o in self.ops:
            if not o.is_dma and o.need_inc:
                cnts[o.eng] += 1
                o.cnt = cnts[o.eng]
        final_waits = {("d", k): (dsems[k], v) for k, v in self.dma_counts.items()}
        streams = {}
        for o in self.ops:
            streams.setdefault(o.eng, []).append(o)
        block = st.enter_context(nc.Block())

        def run(engname, e):
            wm = {}
            for o in streams.get(engname, []):
                need = {}
                for d in o.deps:
                    if d.is_dma:
                        k = ("d", d.dkey)
                        s, v = dsems[d.dkey], d.dcount
                    else:
                        k = ("c", d.eng)
                        s, v = sems[d.eng], d.cnt
                    if v > wm.get(k, 0) and v > need.get(k, (None, 0))[1]:
                        need[k] = (s, v)
                for k, (s, v) in need.items():
                    e.wait_ge(s, v)
                    wm[k] = v
                ins = o.fn(e)
                if o.is_dma:
                    ins.then_inc(dsems[o.dkey], 16)
                elif o.need_inc:
                    ins.then_inc(sems[o.eng], 1)
            if engname == "sp":
                for k, (s, v) in final_waits.items():
                    if v > wm.get(k, 0):
                        e.wait_ge(s, v)

        @block.sync
        def _(e):
            run("sp", e)

        @block.tensor
        def _(e):
            run("pe", e)

        @block.scalar
        def _(e):
            run("act", e)

        @block.vector
        def _(e):
            run("dve", e)

        @block.gpsimd
        def _(e):
            run("pool", e)


class Buf:
    def __init__(self, ap, lo, hi, inherit):
        self.ap = ap
        self.lo = lo
        self.hi = hi
        self.inherit = inherit
        self.ch = {}

    def r(self, key=None):
        c = self.ch.get(key)
        if c is None:
            c = Res(self.inherit)
            self.ch[key] = c
        return c

    def f32(self, *shape):
        return _shape(self.ap, shape)

    def bf(self, *shape):
        return _shape(self.ap.bitcast(BF16), shape)


class Sub:
    def __init__(self, parent, lo, hi):
        self.parent = parent
        self.ap = parent.ap[:, lo:hi]

    def r(self, key=None):
        return self.parent.r(key)

    def f32(self, *shape):
        return _shape(self.ap, shape)

    def bf(self, *shape):
        return _shape(self.ap.bitcast(BF16), shape)


def _shape(ap, shape):
    if len(shape) <= 1:
        return ap[:, 0:shape[0]] if shape else ap
    n = 1
    for s in shape:
        n *= s
    ap = ap[:, 0:n]
    if len(shape) == 2:
        return ap.rearrange("p (a b) -> p a b", a=shape[0], b=shape[1])
    if len(shape) == 3:
        return ap.rearrange("p (a b c) -> p a b c", a=shape[0], b=shape[1], c=shape[2])
    raise ValueError(shape)


class Arena:
    def __init__(self, ap, words):
        self.ap = ap
        self.words = words
        self.top = 0
        self.dead = []

    def alloc(self, words):
        words = (words + 7) // 8 * 8
        lo, hi = self.top, self.top + words
        assert hi <= self.words, ("arena overflow", hi, self.words)
        self.top = hi
        self.hw = max(getattr(self, "hw", 0), hi)
        inh = {}
        keep = []
        for b in self.dead:
            if b.hi <= lo or b.lo >= hi:
                keep.append(b)
                continue
            for c in b.ch.values():
                cand = list(c.readers.items())
                if c.last_w is not None:
                    w = c.last_w
                    cand.append(((w.eng, w.dkey) if w.is_dma else w.eng, w))
                for k, o in cand:
                    if k not in inh or inh[k].idx < o.idx:
                        inh[k] = o
            if not (b.lo >= lo and b.hi <= hi):
                keep.append(b)
        self.dead = keep
        return Buf(self.ap[:, lo:hi], lo, hi, inh)

    def f32(self, n):
        return self.alloc(n)

    def bf(self, n):
        return self.alloc((n + 1) // 2)

    def mark(self):
        return (self.top, [])

    def release(self, mark, bufs):
        self.top = mark[0]
        self.dead.extend(bufs)


def _consts(T, NS):
    c = {}
    c["ident"] = np.eye(128, dtype=np.float32)
    ii = np.arange(128)
    rc = np.zeros((128, 4, 16), np.float32)
    for g, win in enumerate((2, 4, 8, 16)):
        for t in range(16):
            rc[:, g, t] = 1.0 / min(t + 1, win)
    c["rc"] = rc
    c["maskS"] = np.where(ii[None, :] < ii[:, None], 0.0, 30000.0).astype(np.float32)
    c["negmaskT"] = np.where(ii[None, :] >= ii[:, None], 0.0, -30000.0).astype(np.float32)
    c["U"] = (ii[:, None] <= ii[None, :]).astype(np.float32)
    bm = lambda sz: ((ii[:, None] // sz) == (ii[None, :] // sz)).astype(np.float32)
    msk = np.zeros((128, 5, 128), np.float32)
    msk[:, 0, :] = bm(8)
    for li_, sz in enumerate((8, 16, 32, 64)):
        msk[:, 1 + li_, :] = bm(2 * sz) - bm(sz)
    c["bmask"] = msk
    H = 8
    lg = np.log(1.0 - 2.0 ** (-5.0 - np.arange(H, dtype=np.float64)))
    sc = 128.0 ** -0.5
    diff = (ii[None, :] - ii[:, None]).astype(np.float64)
    decT = np.where(diff[None] >= 0, np.exp(np.maximum(diff, 0.0)[None] * lg[:, None, None]), 0.0) * sc
    c["decT"] = np.ascontiguousarray(decT.transpose(1, 0, 2)).astype(np.float32)
    xi = np.exp((ii + 1.0)[None, :] * lg[:, None])
    c["xi"] = np.ascontiguousarray(np.broadcast_to(xi[None], (128, H, 128))).astype(np.float32)
    zeta = np.exp((127.0 - ii)[None, :] * lg[:, None]) * sc
    c["zeta"] = np.ascontiguousarray(zeta.T).astype(np.float32)
    c["gC"] = [float(np.exp(128.0 * lg[h])) for h in range(H)]
    c["gam"] = [float(np.exp(lg[h])) for h in range(H)]
    gb = np.zeros((128, 2, H), np.float32)
    gb[:, 0, :] = np.exp(lg)[None, :]
    gb[:, 1, :] = (np.exp(lg) * 0 + sc)[None, :]
    c["gamb"] = gb
    half = 64
    freqs = (np.float32(10000.0) ** (-np.arange(half, dtype=np.float32) / np.float32(half))).astype(np.float32)
    pos = np.concatenate([np.arange(T, dtype=np.float32), np.full((NS,), float(PAST_LEN), np.float32)])
    ang = (pos[None, :] * freqs[:, None]).astype(np.float32)
    cs, sn = np.cos(ang).astype(np.float32), np.sin(ang).astype(np.float32)
    c["cos2"] = np.concatenate([cs, cs], 0)
    c["sin2"] = np.concatenate([-sn, sn], 0)
    angs = (np.float32(PAST_LEN) * freqs).astype(np.float32)
    cst = np.zeros((128, 2, half), np.float32)
    cst[:, 0, :] = np.cos(angs)[None]
    cst[:, 1, :] = np.sin(angs)[None]
    c["ropes"] = cst
    dl = np.zeros((128, 16, 16), np.float32)
    for s in range(16):
        dl[:, s, s] = 1.0
    c["delta"] = dl
    dcol = np.zeros((128, 16), np.float32)
    for s in range(16):
        dcol[s, s] = 1.0
    c["dcol"] = dcol
    return c


CONST_SHAPES = {"ident": [128, 128], "rc": [128, 4, 16], "maskS": [128, 128], "negmaskT": [128, 128],
                "U": [128, 128], "bmask": [128, 5, 128], "decT": [128, 8, 128], "xi": [128, 8, 128], "zeta": [128, 8],
                "gamb": [128, 2, 8], "ropes": [128, 2, 64], "delta": [128, 16, 16], "dcol": [128, 16]}


AWO = None
DBG = set()


def build(T, NS, nlayers=4):
    assert T % 128 == 0 and NS == 16
    nc = bass.Bass("TRN2", target_bir_lowering=False)
    NB = T // 128
    NT = T + NS
    TTS = min(512, T)
    tiles = [(c0, TTS) for c0 in range(0, T, TTS)] + [(T, NS)]
    NTI = len(tiles)
    cst = _consts(T, NS)

    def din(name, shape):
        return nc.dram_tensor(name, list(shape), F32, kind="ExternalInput").ap()

    def dout(name, shape):
        return nc.dram_tensor(name, list(shape), F32, kind="ExternalOutput").ap()

    xp = din("xp", [T, D])
    xs = din("xs", [NS, D])
    spool = din("spool", [2, NS, 15, D])
    sconv = din("sconv", [NS, 3, 3072])
    sgdn = din("sgdn", [NS, 8, 128, 128])
    sret = din("sret", [NS, 8, 128, 256])
    pool_w = din("pool_w", [2, 4, 256, 256])
    pool_scale = din("pool_scale", [2, D])
    gdn_w_in = din("gdn_w_in", [D, 4112])
    gdn_conv_w = din("gdn_conv_w", [4, 3072])
    gdn_a_log = din("gdn_a_log", [1, 8])
    gdn_dt_bias = din("gdn_dt_bias", [1, 8])
    gdn_norm_g = din("gdn_norm_g", [1, 128])
    gdn_w_out = din("gdn_w_out", [D, D])
    ret_w_in = din("ret_w_in", [D, 6144])
    ret_w_out = din("ret_w_out", [2048, D])
    ffn_w13 = din("ffn_w13", [4, D, 2 * DFF])
    ffn_w2 = din("ffn_w2", [4, DFF, D])
    ln_g = din("ln_g", [8, D])
    ln_b = din("ln_b", [8, D])
    cd = {k: din("c_" + k, CONST_SHAPES[k]) for k in CONST_SHAPES}
    cd["cos2"] = din("c_cos2", [128, NT])
    cd["sin2"] = din("c_sin2", [128, NT])

    yp = dout("yp", [T, D])
    ys = dout("ys", [NS, D])
    o_pool_p = dout("pool_p", [2, 15, D])
    o_pool_s = dout("pool_s", [2, NS, 15, D])
    o_conv_p = dout("conv_p", [3, 3072])
    o_conv_s = dout("conv_s", [NS, 3, 3072])
    o_gdn_p = dout("gdn_p", [8, 128, 128])
    o_gdn_s = dout("gdn_s", [NS, 8, 128, 128])
    o_ret_p = dout("ret_p", [8, 128, 256])
    o_ret_s = dout("ret_s", [NS, 8, 128, 256])

    P = Prog(nc)
    st = ExitStack()
    with st:
        def sb(name, shape, dt):
            return st.enter_context(nc.sbuf_tensor(name, shape, dt))

        xf = sb("xf", [128, 8, NT], F32)
        xb = sb("xb", [128, 8, NT], BF16)
        ident = sb("ident", [128, 128], F32)
        identb = sb("identb", [128, 128], BF16)
        onesb = sb("onesb", [128, 128], BF16)
        onesn = sb("onesn", [128, 128], BF16)
        onesf = sb("onesf", [128, 128], F32)
        lnp = sb("lnp", [128, 128], F32)
        prm = sb("prm", [128, 128], F32)
        rcs = sb("rcs", [128, 4, 16], F32)
        AW = (nc.sbuf_bytes_remaining - 4096) // 4 // 8 * 8 if AWO is None else AWO
        arena_t = sb("arena", [128, AW], F32)
        A = Arena(arena_t[:], AW)
        banks = [st.enter_context(nc.psum_tensor("bank%d" % i, [128, 512], F32)) for i in range(8)]
        rbank = [Res() for _ in range(8)]
        r_xf = [[Res() for _ in range(NTI)] for _ in range(8)]
        r_xb = [[Res() for _ in range(NTI)] for _ in range(8)]
        r_c = Res()

        def rx(rs, ks=range(8), ts=range(NTI)):
            return [rs[k][t] for k in ks for t in ts]

        def tiles_of(c0, n):
            return [ti for ti, (a, m) in enumerate(tiles) if a < c0 + n and c0 < a + m]

        P.dma("sp", "c0", ident[:], cd["ident"], writes=[r_c])
        P.dma("sp", "c0", rcs[:], cd["rc"], writes=[r_c])
        P.commit("c0")
        P.op("dve", lambda e: e.tensor_copy(identb[:], ident[:]), reads=[r_c], writes=[r_c])
        P.op("dve", lambda e: e.memset(onesb[:], 1.0), writes=[r_c])
        P.op("dve", lambda e: e.memset(onesn[:], 1.0 / D), writes=[r_c])
        P.op("dve", lambda e: e.memset(onesf[:], 1.0), writes=[r_c])
        wring = [A.bf(2 * 8 * 128) for _ in range(4)]
        wr_i = [0]

        def wslot():
            b = wring[wr_i[0] % len(wring)]
            k = "w%d" % (wr_i[0] % len(wring))
            wr_i[0] += 1
            return b, k

        m0 = A.mark()
        if "noparams" in DBG:
            raise_ = None
        stg = A.f32(128)
        stg2 = A.f32(128)
        if "noparams" not in DBG:
          P.dma("sp", "c1", stg.f32(128)[0:64, :], ln_g.rearrange("l (k p) -> (l k) p", p=128), writes=[stg.r()])
          P.dma("sp", "c1", stg.f32(128)[64:128, :], ln_b.rearrange("l (k p) -> (l k) p", p=128), writes=[stg.r()])
          P.op("dve", lambda e: e.memset(stg2.f32(128), 0.0), writes=[stg2.r()])
          P.dma("sp", "c2", stg2.f32(128)[0:16, :], pool_scale.rearrange("j (k p) -> (j k) p", p=128), reads=[], writes=[stg2.r()])
          P.dma("sp", "c2", stg2.f32(128)[16:112, :], gdn_conv_w.rearrange("j (c p) -> (j c) p", p=128), writes=[stg2.r()])
          P.dma("sp", "c2", stg2.f32(128)[112:113, :], gdn_norm_g, writes=[stg2.r()])
          P.op("pe", lambda e: e.transpose(banks[0][:, 0:128], stg.f32(128), ident[:]), reads=[stg.r(), r_c], writes=[rbank[0]])
          P.op("pe", lambda e: e.transpose(banks[0][:, 128:256], stg2.f32(128), ident[:]), reads=[stg2.r(), r_c], writes=[rbank[0]])
          P.op("dve", lambda e: e.tensor_copy(lnp[:], banks[0][:, 0:128]), reads=[rbank[0]], writes=[r_c])
          P.op("dve", lambda e: e.tensor_copy(prm[:], banks[0][:, 128:256]), reads=[rbank[0]], writes=[r_c])
        A.release(m0, [stg, stg2])

        epsb = {}
        for ev in (LN_EPS, RMS_EPS, RMS_EPS * 128.0, 1.0):
            t = sb("eps%d" % len(epsb), [128, 1], F32)
            P.op("dve", lambda e, t=t, ev=ev: e.memset(t[:], ev), writes=[r_c])
            epsb[ev] = t

        def eps_tile(v):
            return epsb[v][:]

        def rsqrt_eps(out, in_, eps, res, scale=1.0):
            eb = epsb[eps]
            np_ = in_.shape[0]
            P.op("act", lambda e: e.activation(out, in_, AF.Ln, bias=eb[0:np_, :], scale=scale), reads=[res, r_c], writes=[res])
            P.op("act", lambda e: e.activation(out, out, AF.Exp, scale=-0.5), reads=[res], writes=[res])

        def load_x():
            m = A.mark()
            sg = [A.f32(D), A.f32(D)]
            blocks = [(xp[b * 128:(b + 1) * 128, :], 128, b * 128) for b in range(NB)] + [(xs, NS, T)]
            if "nosamp" in DBG:
                blocks = blocks[:-1]
            for bi, (src, n, c0) in enumerate(blocks):
                s = sg[bi % 2]
                P.dma("sp", "xl%d" % (bi % 2), s.f32(D)[0:n, :], src, writes=[s.r()])
                tis = tiles_of(c0, n)
                for half in range(2):
                    bk = (bi * 2 + half) % 2
                    for kk in range(4):
                        k = half * 4 + kk
                        P.op("pe", lambda e, s=s, k=k, kk=kk, n=n, bk=bk: e.transpose(
                            banks[bk][:, kk * 128:kk * 128 + n], s.f32(D)[0:n, k * 128:(k + 1) * 128], ident[0:n, 0:n]),
                            reads=[s.r(), r_c], writes=[rbank[bk]])
                    src_ps = banks[bk][:].rearrange("p (a b) -> p a b", a=4, b=128)[:, :, 0:n]
                    ks = range(half * 4, half * 4 + 4)
                    P.op("dve", lambda e, src_ps=src_ps, half=half, c0=c0, n=n: e.tensor_copy(
                        xf[:, half * 4:half * 4 + 4, c0:c0 + n], src_ps),
                        reads=[rbank[bk]], writes=rx(r_xf, ks, tis))
                    if "noact" in DBG:
                        continue
                    P.op("act", lambda e, src_ps=src_ps, half=half, c0=c0, n=n: e.activation(
                        xb[:, half * 4:half * 4 + 4, c0:c0 + n], xf[:, half * 4:half * 4 + 4, c0:c0 + n], AF.Identity),
                        reads=rx(r_xf, ks, tis), writes=rx(r_xb, ks, tis))
            A.release(m, sg)

        def layer_norm(li):
            m = A.mark()
            bufs = []
            sets = []
            for _ in range(2):
                sets.append((A.bf(8 * TTS), A.bf(8 * TTS), A.f32(TTS), A.f32(TTS), A.f32(TTS)))
                bufs += list(sets[-1])
            for ti, (c0, n) in enumerate(tiles):
                rb, sq, mean, rstd, m2 = sets[ti % 2]
                xv = xf[:, :, c0:c0 + n]
                P.op("act", lambda e, rb=rb, xv=xv, n=n: e.activation(rb.bf(8, n), xv, AF.Copy),
                     reads=rx(r_xf, ts=[ti]), writes=[rb.r()])
                P.op("act", lambda e, sq=sq, xv=xv, n=n: e.activation(sq.bf(8, n), xv, AF.Square),
                     reads=rx(r_xf, ts=[ti]), writes=[sq.r()])
                for k in range(8):
                    P.op("pe", lambda e, rb=rb, k=k, n=n: e.matmul(banks[6][:, 0:n], onesn[:], rb.bf(8, n)[:, k, :],
                                                                   start=(k == 0), stop=(k == 7)),
                         reads=[rb.r(), r_c], writes=[rbank[6]])
                for k in range(8):
                    P.op("pe", lambda e, sq=sq, k=k, n=n: e.matmul(banks[7][:, 0:n], onesn[:], sq.bf(8, n)[:, k, :],
                                                                   start=(k == 0), stop=(k == 7)),
                         reads=[sq.r(), r_c], writes=[rbank[7]])
                P.op("act", lambda e, mean=mean, n=n: e.activation(mean.f32(n), banks[6][:, 0:n], AF.Copy),
                     reads=[rbank[6]], writes=[mean.r()])
                P.op("dve", lambda e, mean=mean, m2=m2, n=n: e.tensor_tensor(m2.f32(n), mean.f32(n), mean.f32(n), ALU.mult),
                     reads=[mean.r()], writes=[m2.r()])
                P.op("dve", lambda e, m2=m2, rstd=rstd, n=n: e.tensor_tensor(rstd.f32(n), banks[7][:, 0:n], m2.f32(n), ALU.subtract),
                     reads=[m2.r(), rbank[7]], writes=[rstd.r()])
                rsqrt_eps(rstd.f32(n), rstd.f32(n), LN_EPS, rstd.r())
                P.op("dve", lambda e, xv=xv, mean=mean, n=n: e.tensor_tensor(
                    xv, xv, mean.f32(n).unsqueeze(1).to_broadcast([128, 8, n]), ALU.subtract),
                    reads=rx(r_xf, ts=[ti]) + [mean.r()], writes=rx(r_xf, ts=[ti]))
                P.op("dve", lambda e, xv=xv, rstd=rstd, n=n: e.tensor_tensor(
                    xv, xv, rstd.f32(n).unsqueeze(1).to_broadcast([128, 8, n]), ALU.mult),
                    reads=rx(r_xf, ts=[ti]) + [rstd.r()], writes=rx(r_xf, ts=[ti]))
                for k in range(8):
                    P.op("act", lambda e, k=k, c0=c0, n=n: e.activation(
                        xf[:, k, c0:c0 + n], xf[:, k, c0:c0 + n], AF.Identity,
                        bias=lnp[:, 64 + li * 8 + k:64 + li * 8 + k + 1], scale=lnp[:, li * 8 + k:li * 8 + k + 1]),
                        reads=[r_xf[k][ti], r_c], writes=[r_xf[k][ti]])
                P.op("dve", lambda e, xv=xv, c0=c0, n=n: e.tensor_copy(xb[:, :, c0:c0 + n], xv),
                     reads=rx(r_xf, ts=[ti]), writes=rx(r_xb, ts=[ti]))
            A.release(m, bufs)

        def ffn(layer):
            m = A.mark()
            h = A.bf(11 * NT)
            w2 = A.bf(11 * D)
            sa = [A.f32(TTS), A.f32(TTS)]
            hv = h.bf(11, NT)
            w2v = w2.bf(11, D)
            cnt = 0
            for pas in range(2):
                for il in range(11):
                    i = pas * 11 + il
                    wb, wk = wslot()
                    wv = wb.bf(2, 8, 128)
                    P.dma("pool", wk, wv[:, 0], ffn_w13[layer, :, i * 128:(i + 1) * 128].rearrange("(k p) n -> p k n", p=128),
                          writes=[wb.r()])
                    P.dma("pool", wk, wv[:, 1], ffn_w13[layer, :, DFF + i * 128:DFF + (i + 1) * 128].rearrange("(k p) n -> p k n", p=128),
                          writes=[wb.r()])
                    P.commit(wk)
                    if il == 0:
                        for f in range(11):
                            P.dma("pool", "fw2", w2v[:, f, :], ffn_w2[layer, (pas * 11 + f) * 128:(pas * 11 + f + 1) * 128, :],
                                  writes=[w2.r(f)])
                        P.commit("fw2")
                    for ti, (c0, n) in enumerate(tiles):
                        ba, bb = banks[cnt % 2], banks[2 + cnt % 2]
                        ra, rbb = rbank[cnt % 2], rbank[2 + cnt % 2]
                        s = sa[cnt % 2]
                        cnt += 1
                        for k in range(8):
                            P.op("pe", lambda e, ba=ba, wv=wv, k=k, c0=c0, n=n: e.matmul(
                                ba[:, 0:n], wv[:, 0, k, :], xb[:, k, c0:c0 + n], start=(k == 0), stop=(k == 7)),
                                reads=[wb.r(), r_xb[k][ti]], writes=[ra])
                        for k in range(8):
                            P.op("pe", lambda e, bb=bb, wv=wv, k=k, c0=c0, n=n: e.matmul(
                                bb[:, 0:n], wv[:, 1, k, :], xb[:, k, c0:c0 + n], start=(k == 0), stop=(k == 7)),
                                reads=[wb.r(), r_xb[k][ti]], writes=[rbb])
                        P.op("act", lambda e, s=s, ba=ba, n=n: e.activation(s.f32(n), ba[:, 0:n], AF.Silu),
                             reads=[ra], writes=[s.r()])
                        P.op("dve", lambda e, s=s, bb=bb, il=il, c0=c0, n=n: e.tensor_tensor(
                            hv[:, il, c0:c0 + n], s.f32(n), bb[:, 0:n], ALU.mult),
                            reads=[s.r(), rbb], writes=[h.r((il, ti))])
                for dc in range(8):
                    for ti, (c0, n) in enumerate(tiles):
                        by, ry = banks[4 + cnt % 2], rbank[4 + cnt % 2]
                        cnt += 1
                        for f in range(11):
                            P.op("pe", lambda e, by=by, f=f, dc=dc, c0=c0, n=n: e.matmul(
                                by[:, 0:n], w2v[:, f, dc * 128:(dc + 1) * 128], hv[:, f, c0:c0 + n],
                                start=(f == 0), stop=(f == 10)),
                                reads=[w2.r(f), h.r((f, ti))], writes=[ry])
                        xv = xf[:, dc, c0:c0 + n]
                        if pas == 0:
                            P.op("dve", lambda e, xv=xv, by=by, n=n: e.scalar_tensor_tensor(
                                xv, xv, ALPHA, by[:, 0:n], ALU.mult, ALU.add),
                                reads=[ry, r_xf[dc][ti]], writes=[r_xf[dc][ti]])
                        else:
                            P.op("dve", lambda e, xv=xv, by=by, n=n: e.tensor_tensor(xv, xv, by[:, 0:n], ALU.add),
                                 reads=[ry, r_xf[dc][ti]], writes=[r_xf[dc][ti]])
            A.release(m, [h, w2] + sa)

        def store_y():
            m = A.mark()
            sg = [A.f32(D), A.f32(D)]
            blocks = [(yp[b * 128:(b + 1) * 128, :], 128, b * 128) for b in range(NB)] + [(ys, NS, T)]
            for bi, (dst, n, c0) in enumerate(blocks):
                s = sg[bi % 2]
                tis = tiles_of(c0, n)
                for half in range(2):
                    bk = (bi * 2 + half) % 2
                    for kk in range(4):
                        k = half * 4 + kk
                        P.op("pe", lambda e, k=k, kk=kk, n=n, bk=bk, c0=c0: e.transpose(
                            banks[bk][0:n, kk * 128:(kk + 1) * 128], xf[:, k, c0:c0 + n], ident[:]),
                            reads=rx(r_xf, [k], tis) + [r_c], writes=[rbank[bk]])
                    if half == 0:
                        P.op("dve", lambda e, s=s, n=n, bk=bk: e.tensor_copy(s.f32(D)[0:n, 0:512], banks[bk][0:n, :]),
                             reads=[rbank[bk]], writes=[s.r()])
                    else:
                        P.op("act", lambda e, s=s, n=n, bk=bk: e.activation(s.f32(D)[0:n, 512:1024], banks[bk][0:n, :], AF.Copy),
                             reads=[rbank[bk]], writes=[s.r()])
                P.dma("sp", "ys%d" % (bi % 2), dst, s.f32(D)[0:n, :], reads=[s.r()])
            A.release(m, sg)

        def pool_mixer(j):
            m = A.mark()
            bufs = []
            if j == 0:
                P.dma("sp", "po", o_pool_p[0], xp[T - 15:T, :])
                P.dma("sp", "po", o_pool_s[0, :, 14, :], xs)
            else:
                so = A.f32(D)
                bufs.append(so)
                nn = 15 + NS
                for half in range(2):
                    for kk in range(4):
                        k = half * 4 + kk
                        P.op("pe", lambda e, k=k, kk=kk, half=half: e.transpose(
                            banks[half][0:nn, kk * 128:(kk + 1) * 128], xf[:, k, T - 15:T + NS], ident[:]),
                            reads=rx(r_xf, [k], tiles_of(T - 15, nn)) + [r_c], writes=[rbank[half]])
                    P.op("dve", lambda e, half=half: e.tensor_copy(so.f32(D)[0:nn, half * 512:(half + 1) * 512], banks[half][0:nn, :]),
                         reads=[rbank[half]], writes=[so.r()])
                P.dma("sp", "po", o_pool_p[1], so.f32(D)[0:15, :], reads=[so.r()])
                P.dma("sp", "po", o_pool_s[1, :, 14, :], so.f32(D)[15:15 + NS, :], reads=[so.r()])
            P.dma("sp", "po", o_pool_s[j, :, 0:14, :], spool[j, :, 1:15, :])
            P.commit("po")
            hist = A.f32(8 * NS * 15)
            bufs.append(hist)
            hv = hist.f32(8, NS * 15)
            hs = [A.f32(D), A.f32(D)]
            bufs += hs
            rows = NS * 15 // 2
            src = spool[j].rearrange("s r d -> (s r) d")
            for b2 in range(2):
                s = hs[b2]
                P.dma("sp", "ph%d" % b2, s.f32(D)[0:rows, :], src[b2 * rows:(b2 + 1) * rows, :], writes=[s.r()])
                for half in range(2):
                    bk = half
                    for kk in range(4):
                        k = half * 4 + kk
                        P.op("pe", lambda e, s=s, k=k, kk=kk, bk=bk: e.transpose(
                            banks[bk][:, kk * 128:kk * 128 + rows], s.f32(D)[0:rows, k * 128:(k + 1) * 128], ident[0:rows, 0:rows]),
                            reads=[s.r(), r_c], writes=[rbank[bk]])
                    P.op("dve", lambda e, half=half, bk=bk, b2=b2: e.tensor_copy(
                        hv[:, half * 4:half * 4 + 4, b2 * rows:(b2 + 1) * rows],
                        banks[bk][:].rearrange("p (a b) -> p a b", a=4, b=128)[:, :, 0:rows]),
                        reads=[rbank[bk]], writes=[hist.r()])
            pooled = A.bf(8 * NT)
            bufs.append(pooled)
            pv = pooled.bf(8, NT)
            E = [A.f32(16 + T), A.f32(16 + T)]
            bufs += E
            ssum = A.f32(NS)
            bufs.append(ssum)
            for eb in E:
                P.op("pool", lambda e, eb=eb: e.memset(eb.f32(16 + T)[:, 0:16], 0.0), writes=[eb.r()])
            for k in range(8):
                g = k // 2
                win = 2 << g
                e0, e1 = E[0], E[1]
                tp = list(range(NTI - 1))
                P.op("pool", lambda e, e0=e0, k=k: e.tensor_copy(e0.f32(16 + T)[:, 16:16 + T], xf[:, k, 0:T]),
                     reads=rx(r_xf, [k], tp), writes=[e0.r()])
                cur, nxt = e0, e1
                sh = 1
                while sh < win:
                    P.op("dve", lambda e, cur=cur, nxt=nxt, sh=sh: e.tensor_tensor(
                        nxt.f32(16 + T)[:, 16:16 + T], cur.f32(16 + T)[:, 16:16 + T], cur.f32(16 + T)[:, 16 - sh:16 + T - sh], ALU.add),
                        reads=[cur.r()], writes=[nxt.r()])
                    cur, nxt = nxt, cur
                    sh *= 2
                P.op("dve", lambda e, cur=cur, k=k, win=win: e.scalar_tensor_tensor(
                    pv[:, k, 0:T], cur.f32(16 + T)[:, 16:16 + T], 1.0 / win, xf[:, k, 0:T], ALU.mult, ALU.subtract),
                    reads=[cur.r()] + rx(r_xf, [k], tp), writes=[pooled.r(k)])
                P.op("dve", lambda e, cur=cur, nxt=nxt, g=g: e.tensor_tensor(
                    nxt.f32(16 + T)[:, 16:31], cur.f32(16 + T)[:, 16:31], rcs[:, g, 0:15], ALU.mult),
                    reads=[cur.r(), r_c], writes=[nxt.r()])
                P.op("dve", lambda e, nxt=nxt, k=k: e.tensor_tensor(
                    pv[:, k, 0:15], nxt.f32(16 + T)[:, 16:31], xf[:, k, 0:15], ALU.subtract),
                    reads=[nxt.r()] + rx(r_xf, [k], [0]), writes=[pooled.r(k)])
                hk = hv[:, k, :].rearrange("p (s r) -> p s r", s=NS, r=15)
                P.op("dve", lambda e, hk=hk, win=win: e.tensor_reduce(
                    ssum.f32(NS), hk[:, :, 16 - win:15], AX.X, ALU.add),
                    reads=[hist.r()], writes=[ssum.r()])
                P.op("dve", lambda e, k=k: e.tensor_tensor(ssum.f32(NS), ssum.f32(NS), xf[:, k, T:NT], ALU.add),
                     reads=[ssum.r(), r_xf[k][NTI - 1]], writes=[ssum.r()])
                P.op("dve", lambda e, k=k, win=win: e.scalar_tensor_tensor(
                    pv[:, k, T:NT], ssum.f32(NS), 1.0 / win, xf[:, k, T:NT], ALU.mult, ALU.subtract),
                    reads=[ssum.r(), r_xf[k][NTI - 1]], writes=[pooled.r(k)])
            pw = A.bf(4 * 2 * 256)
            bufs.append(pw)
            pwv = pw.bf(4, 2, 256)
            P.dma("pool", "pw", pwv, pool_w[j].rearrange("g (cc p) d -> p g cc d", p=128), writes=[pw.r()])
            tm = [A.f32(TTS), A.f32(TTS)]
            bufs += tm
            cnt = 0
            for g in range(4):
                for dc in range(2):
                    k = 2 * g + dc
                    for ti, (c0, n) in enumerate(tiles):
                        bk = cnt % 2
                        t_ = tm[cnt % 2]
                        cnt += 1
                        for cc in range(2):
                            P.op("pe", lambda e, g=g, dc=dc, cc=cc, c0=c0, n=n, bk=bk: e.matmul(
                                banks[bk][:, 0:n], pwv[:, g, cc, dc * 128:(dc + 1) * 128], pv[:, 2 * g + cc, c0:c0 + n],
                                start=(cc == 0), stop=(cc == 1)),
                                reads=[pw.r(), pooled.r(2 * g + cc)], writes=[rbank[bk]])
                        P.op("act", lambda e, t_=t_, bk=bk, n=n, k=k: e.activation(
                            t_.f32(n), banks[bk][:, 0:n], AF.Copy, scale=prm[:, j * 8 + k:j * 8 + k + 1]),
                            reads=[rbank[bk], r_c], writes=[t_.r()])
                        xv = xf[:, k, c0:c0 + n]
                        P.op("dve", lambda e, xv=xv, t_=t_, n=n: e.scalar_tensor_tensor(
                            xv, xv, ALPHA, t_.f32(n), ALU.mult, ALU.add),
                            reads=[t_.r(), r_xf[k][ti]], writes=[r_xf[k][ti]])
            A.release(m, bufs)

        def MM(out, lhsT, rhs, r, w, start=True, stop=True):
            P.op("pe", lambda e: e.matmul(out, lhsT, rhs, start=start, stop=stop), reads=r, writes=w)

        def TR(out, in_, idn, r, w):
            P.op("pe", lambda e: e.transpose(out, in_, idn), reads=r, writes=w)

        def TT(eng, out, in0, in1, op, r, w):
            P.op(eng, lambda e: e.tensor_tensor(out, in0, in1, op), reads=r, writes=w)

        def STT(eng, out, in0, scalar, in1, op0, op1, r, w):
            P.op(eng, lambda e: e.scalar_tensor_tensor(out, in0, scalar, in1, op0, op1), reads=r, writes=w)

        def TS(eng, out, in0, s1, s2, op0, op1, r, w):
            if op1 is None:
                P.op(eng, lambda e: e.tensor_scalar(out, in0, s1, None, op0), reads=r, writes=w)
            else:
                P.op(eng, lambda e: e.tensor_scalar(out, in0, s1, s2, op0, op1), reads=r, writes=w)

        def CP(eng, out, in_, r, w):
            if eng == "act":
                P.op("act", lambda e: e.activation(out, in_, AF.Identity), reads=r, writes=w)
            else:
                P.op(eng, lambda e: e.tensor_copy(out, in_), reads=r, writes=w)

        def ACT(out, in_, func, r, w, scale=1.0, bias=None, accum=None):
            def fn(e):
                kw = {}
                if bias is not None:
                    kw["bias"] = bias
                if accum is not None:
                    kw["accum_out"] = accum
                return e.activation(out, in_, func, scale=scale, **kw)
            P.op("act", fn, reads=r, writes=w)

        def RED(eng, out, in_, r, w, op=ALU.add):
            P.op(eng, lambda e: e.tensor_reduce(out, in_, AX.X, op), reads=r, writes=w)

        def MS(eng, out, val, w):
            P.op(eng, lambda e: e.memset(out, val), writes=w)

        def bfps(bank, c0, n):
            return banks[bank][:, c0:c0 + n // 2].bitcast(BF16)

        ptiles = tiles[:-1]
        PT_ = list(range(NTI - 1))

        def out_proj_samples(w_dram, nk, ogsT, ogs_res):
            for dc in range(8):
                for kc in range(nk):
                    wb_, wk_ = wslot()
                    wv_ = wb_.bf(2, 8, 128)
                    if kc % 16 == 0:
                        pass
                    P.dma("pool", wk_, wv_[:, 0, 0, :], w_dram[kc * 128:(kc + 1) * 128, dc * 128:(dc + 1) * 128], writes=[wb_.r()])
                    P.commit(wk_)
                    MM(banks[4][:, 0:NS], wv_[:, 0, 0, :], ogsT[:, kc, :], [wb_.r(), ogs_res], [rbank[4]], start=(kc == 0), stop=(kc == nk - 1))
                xv = xf[:, dc, T:NT]
                STT("dve", xv, xv, ALPHA, banks[4][:, 0:NS], ALU.mult, ALU.add, [r_xf[dc][NTI - 1]], [r_xf[dc][NTI - 1], rbank[4]])

        def gdn_mixer():
            G = 4 if NB >= 4 else 2
            m = A.mark()
            bufs = []

            def al(n, bf=False):
                b = A.bf(n) if bf else A.f32(n)
                bufs.append(b)
                return b
            maskS = al(128); negT = al(128); Um = al(128); gn = al(128); gnb = al(128)
            bmask = al(5 * 128)
            bmv = bmask.f32(5, 128)
            alog = al(8); dtb = al(8); nea = al(8)
            P.dma("sp", "gc", maskS.f32(128), cd["maskS"], writes=[maskS.r()])
            P.dma("sp", "gc", negT.f32(128), cd["negmaskT"], writes=[negT.r()])
            P.dma("sp", "gc", Um.f32(128), cd["U"], writes=[Um.r()])
            P.dma("sp", "gc", bmask.f32(5, 128), cd["bmask"], writes=[bmask.r()])
            P.dma("sp", "gc", gn.f32(128), gdn_norm_g.partition_broadcast(128)[:, 0, :], writes=[gn.r()])
            P.dma("sp", "gc", alog.f32(8), gdn_a_log.partition_broadcast(128)[:, 0, :], writes=[alog.r()])
            P.dma("sp", "gc", dtb.f32(8), gdn_dt_bias.partition_broadcast(128)[:, 0, :], writes=[dtb.r()])
            P.commit("gc")
            TS("dve", gnb.f32(128), gn.f32(128), math.sqrt(128.0), None, ALU.mult, None, [gn.r()], [gnb.r()])
            ACT(nea.f32(8), alog.f32(8), AF.Exp, [alog.r()], [nea.r()])
            TS("dve", nea.f32(8), nea.f32(8), -1.0, None, ALU.mult, None, [nea.r()], [nea.r()])
            NG = NB + 1
            wg = al(8 * 16, bf=True)
            wgv = wg.bf(8, 16)
            P.dma("pool", "gw", wgv, gdn_w_in[:, 4096:4112].rearrange("(k p) n -> p k n", p=128), writes=[wg.r()])
            GA = al(NG * 16)
            GAv = GA.f32(NG, 16)
            MS("dve", GA.f32(NG * 16), 0.0, [GA.r()])
            for b in range(NB):
                for k in range(8):
                    MM(banks[0][:, b * 16:(b + 1) * 16], xb[:, k, b * 128:(b + 1) * 128], wgv[:, k, :],
                       [wg.r()] + rx(r_xb, [k], tiles_of(b * 128, 128)), [rbank[0]], start=(k == 0), stop=(k == 7))
            for k in range(8):
                MM(banks[0][0:NS, NB * 16:NG * 16], xb[:, k, T:NT], wgv[:, k, :],
                   [wg.r(), r_xb[k][NTI - 1]], [rbank[0]], start=(k == 0), stop=(k == 7))
            CP("dve", GA.f32(NB * 16), banks[0][:, 0:NB * 16], [], [GA.r(), rbank[0]])
            CP("dve", GAv[0:NS, NB, :], banks[0][0:NS, NB * 16:NG * 16], [], [GA.r(), rbank[0]])
            beta = al(NG * 8); nbeta = al(NG * 8); gg = al(NG * 8); gcol = al(NG * 8); eg = al(NG * 8)
            beg = al(NG * 8); egl = al(NG * 8); gl = al(NG * 8); t1 = al(NG * 8); t2 = al(NG * 8)
            v8 = lambda b_: b_.f32(NG, 8)
            ACT(v8(beta), GAv[:, :, 0:8], AF.Sigmoid, [GA.r()], [beta.r()])
            TS("dve", v8(nbeta), v8(beta), -1.0, None, ALU.mult, None, [beta.r()], [nbeta.r()])
            TT("dve", v8(t1), GAv[:, :, 8:16], dtb.f32(8).unsqueeze(1).to_broadcast([128, NG, 8]), ALU.add, [GA.r(), dtb.r()], [t1.r()])
            TS("dve", v8(t2), v8(t1), -1.0, None, ALU.mult, None, [t1.r()], [t2.r()])
            TT("dve", v8(t2), v8(t2), v8(t1), ALU.max, [t1.r(), t2.r()], [t2.r()])
            ACT(v8(t2), v8(t2), AF.Exp, [t2.r()], [t2.r()], scale=-1.0)
            ACT(v8(t2), v8(t2), AF.Ln, [t2.r(), r_c], [t2.r()], bias=eps_tile(1.0))
            TS("dve", v8(t1), v8(t1), 0.0, None, ALU.max, None, [t1.r()], [t1.r()])
            TT("dve", v8(t1), v8(t1), v8(t2), ALU.add, [t1.r(), t2.r()], [t1.r()])
            TT("dve", v8(gg), v8(t1), nea.f32(8).unsqueeze(1).to_broadcast([128, NG, 8]), ALU.mult, [t1.r(), nea.r()], [gg.r()])
            MM(banks[0][:, 0:NB * 8], Um.f32(128), gg.f32(NB * 8), [Um.r(), gg.r()], [rbank[0]])
            MM(banks[0][:, 256:256 + NB * 8], onesf[:], gg.f32(NB * 8), [r_c, gg.r()], [rbank[0]])
            CP("dve", gcol.f32(NB * 8), banks[0][:, 0:NB * 8], [], [gcol.r(), rbank[0]])
            CP("dve", gcol.f32(NG, 8)[:, NB, :], gg.f32(NG, 8)[:, NB, :], [gg.r()], [gcol.r()])
            ACT(eg.f32(NG * 8), gcol.f32(NG * 8), AF.Exp, [gcol.r()], [eg.r()])
            TT("dve", beg.f32(NG * 8), beta.f32(NG * 8), eg.f32(NG * 8), ALU.mult, [beta.r(), eg.r()], [beg.r()])
            ACT(gl.f32(NB * 8), banks[0][:, 256:256 + NB * 8], AF.Exp, [], [gl.r(), rbank[0]])
            TT("dve", egl.f32(NB * 8), banks[0][:, 256:256 + NB * 8], gcol.f32(NB * 8), ALU.subtract, [gcol.r()], [egl.r(), rbank[0]])
            ACT(egl.f32(NB * 8), egl.f32(NB * 8), AF.Exp, [egl.r()], [egl.r()])
            projsT = al(4 * 8 * NS)
            pjT = projsT.f32(4, 8, NS)
            lastU = al(24 * 3)
            luv = lastU.f32(24, 3)
            base_bufs = list(bufs)
            mH = A.mark()
            del bufs[:]
            Ub = al(T + 8)
            accq = al(T); vf = al(T)
            knf = vf
            qnb = al(T, bf=True); knb = al(T, bf=True)
            szf = accq
            sqb_ap = Ub.ap[:, 8:8 + T // 2].bitcast(BF16)
            ogb_ap = Ub.ap[:, 8 + T // 2:8 + T].bitcast(BF16)
            rsq = [al(TTS)]
            Sf = al(128); Sb = al(128, bf=True)
            wo = al(1024, bf=True)
            Ubv = Ub.ap[:, 5:8 + T]
            MS("pool", Ubv[:, 0:3], 0.0, [Ub.r()])

            class Ch:
                pass
            chains = []
            for ci in range(G):
                c = Ch()
                c.Ug = al(128); c.e1 = al(128); c.e2 = al(128); c.egrow = al(128)
                c.attnT = al(128, bf=True); c.qgT = al(128, bf=True)
                c.Xf = al(128); c.XTf = al(128); c.X8 = al(128); c.Z8 = al(128)
                c.Y1 = c.Ug; c.Z1 = c.e1; c.Y2 = c.e2; c.E0 = al(128); c.E1 = c.egrow
                c.Xo = al(4 * 128, bf=True); c.Zo = al(128, bf=True)
                c.Xb = al(128, bf=True); c.XTb = al(128, bf=True)
                c.Db = [al(128, bf=True), al(128, bf=True)]; c.Eb = [al(128, bf=True), al(128, bf=True)]
                c.M1 = al(128, bf=True); c.M1p = al(128, bf=True)
                c.PTb = c.Eb[0]
                c.vb = Sub(c.Xo, 0, 64); c.kbg = Sub(c.Xo, 64, 128); c.kg = Sub(c.Xo, 128, 192)
                c.u = c.Xf; c.wkT = Sub(c.Xo, 192, 256); c.vnew = Sub(c.Zo, 0, 64); c.on = c.XTf; c.ssq = al(8)
                c.osb = c.X8
                c.bA = ci
                c.bN = ci
                chains.append(c)
            B = lambda b_: b_.bf(128)
            F = lambda b_: b_.f32(128)

            for h in range(8):
                slots = []
                for typ in range(4):
                    if typ % 2 == 0:
                        wb_, wk_ = wslot()
                        wv_ = wb_.bf(2, 8, 128)
                    col = typ * 1024 + h * 128
                    P.dma("pool", wk_, wv_[:, typ % 2], gdn_w_in[:, col:col + 128].rearrange("(k p) n -> p k n", p=128), writes=[wb_.r()])
                    slots.append((wb_, wv_[:, typ % 2]))
                    if typ % 2 == 1:
                        P.commit(wk_)
                if True:
                    P.dma("pool", "gwo", wo.bf(1024), gdn_w_out[h * 128:(h + 1) * 128, :], writes=[wo.r()])
                cntb = [0]

                def project(typ, sink):
                    wb_, wv_ = slots[typ]
                    for ti, (c0, n) in enumerate(ptiles):
                        bk = 4 + cntb[0] % 2
                        cntb[0] += 1
                        for k in range(8):
                            MM(banks[bk][:, 0:n], wv_[:, k, :], xb[:, k, c0:c0 + n], [wb_.r(), r_xb[k][ti]], [rbank[bk]], start=(k == 0), stop=(k == 7))
                        sink(bk, c0, n)
                    bk = 4 + cntb[0] % 2
                    cntb[0] += 1
                    for k in range(8):
                        MM(banks[bk][:, 0:NS], wv_[:, k, :], xb[:, k, T:NT], [wb_.r(), r_xb[k][NTI - 1]], [rbank[bk]], start=(k == 0), stop=(k == 7))
                    CP("dve", pjT[:, typ, h, :], banks[bk][:, 0:NS], [], [projsT.r(), rbank[bk]])

                for typ in range(3):
                    ch = typ * 8 + h
                    project(typ, lambda bk, c0, n: CP("act", Ubv[:, 3 + c0:3 + c0 + n], banks[bk][:, 0:n], [], [Ub.r(), rbank[bk]]))
                    CP("act", luv[:, ch, :], Ubv[:, T:T + 3], [Ub.r()], [lastU.r()])
                    acc = [accq, knf, vf][typ]
                    ce = "dve"
                    cw = lambda j_: prm[:, 16 + j_ * 24 + ch:16 + j_ * 24 + ch + 1]
                    TS(ce, acc.f32(T), Ubv[:, 3:3 + T], cw(3), None, ALU.mult, None, [Ub.r(), r_c], [acc.r()])
                    for j_ in range(3):
                        STT(ce, acc.f32(T), Ubv[:, j_:j_ + T], cw(j_), acc.f32(T), ALU.mult, ALU.add, [Ub.r(), r_c, acc.r()], [acc.r()])
                    ACT(acc.f32(T), acc.f32(T), AF.Silu, [acc.r()], [acc.r()])
                    if typ < 2:
                        ACT(sqb_ap, acc.f32(T), AF.Square, [acc.r()], [Ub.r()])
                        for ti, (c0, n) in enumerate(ptiles):
                            bk = 4 + cntb[0] % 2
                            cntb[0] += 1
                            rs_ = rsq[0]
                            MM(banks[bk][:, 0:n], onesb[:], sqb_ap[:, c0:c0 + n], [Ub.r(), r_c], [rbank[bk]])
                            CP("dve", rs_.f32(n), banks[bk][:, 0:n], [], [rs_.r(), rbank[bk]])
                            rsqrt_eps(rs_.f32(n), rs_.f32(n), RMS_EPS, rs_.r())
                            if typ == 0:
                                STT("dve", qnb.bf(T)[:, c0:c0 + n], acc.f32(T)[:, c0:c0 + n], 128.0 ** -0.5, rs_.f32(n), ALU.mult, ALU.mult,
                                    [acc.r(), rs_.r()], [qnb.r()])
                            else:
                                TT("dve", acc.f32(T)[:, c0:c0 + n], acc.f32(T)[:, c0:c0 + n], rs_.f32(n), ALU.mult, [acc.r(), rs_.r()], [acc.r()])
                        if typ == 1:
                            CP("pool", knb.bf(T), knf.f32(T), [knf.r()], [knb.r()])
                project(3, lambda bk, c0, n: ACT(szf.f32(T)[:, c0:c0 + n], banks[bk][:, 0:n], AF.Silu, [], [szf.r(), rbank[bk]]))
                MS("dve", F(Sf), 0.0, [Sf.r()])
                MS("dve", B(Sb), 0.0, [Sb.r()])

                def st_a(c, b):
                    bs = slice(b * 128, (b + 1) * 128)
                    ACT(F(c.Ug), F(Um), AF.Identity, [Um.r(), gg.r()], [c.Ug.r()], scale=gg.f32(NG, 8)[:, b, h:h + 1])
                    yield
                    bk = banks[c.bA]
                    MM(bk[:, 0:128], knb.bf(T)[:, bs], knb.bf(T)[:, bs], [knb.r()], [rbank[c.bA]])
                    MM(bk[:, 128:256], knb.bf(T)[:, bs], qnb.bf(T)[:, bs], [knb.r(), qnb.r()], [rbank[c.bA]])
                    MM(bk[:, 256:384], onesf[:], F(c.Ug), [r_c, c.Ug.r()], [rbank[c.bA]])
                    yield
                    gc_ = gcol.f32(NG, 8)[:, b, h:h + 1]
                    STT("dve", F(c.e1), bk[:, 256:384], gc_, F(maskS), ALU.subtract, ALU.max, [gcol.r(), maskS.r()], [c.e1.r(), rbank[c.bA]])
                    STT("dve", F(c.e2), bk[:, 256:384], gc_, F(negT), ALU.subtract, ALU.min, [gcol.r(), negT.r()], [c.e2.r(), rbank[c.bA]])
                    ACT(F(c.egrow), bk[:, 256:384], AF.Exp, [], [c.egrow.r(), rbank[c.bA]])
                    yield
                    ACT(F(c.e1), F(c.e1), AF.Exp, [c.e1.r()], [c.e1.r()], scale=-1.0)
                    ACT(F(c.e2), F(c.e2), AF.Exp, [c.e2.r()], [c.e2.r()])
                    yield
                    STT("dve", F(c.Xf), bk[:, 0:128], nbeta.f32(NG, 8)[:, b, h:h + 1], F(c.e1), ALU.mult, ALU.mult, [nbeta.r(), c.e1.r()], [c.Xf.r(), rbank[c.bA]])
                    TT("dve", B(c.attnT), bk[:, 128:256], F(c.e2), ALU.mult, [c.e2.r()], [c.attnT.r(), rbank[c.bA]])
                    TT("dve", B(c.qgT), qnb.bf(T)[:, bs], F(c.egrow), ALU.mult, [qnb.r(), c.egrow.r()], [c.qgT.r()])

                def st_b(c, b):
                    bk = banks[c.bN]
                    TR(bk[:, 0:128], F(c.Xf), ident[:], [c.Xf.r(), r_c], [rbank[c.bN]])
                    yield
                    CP("act", F(c.XTf), bk[:, 0:128], [], [c.XTf.r(), rbank[c.bN]])
                    CP("act", B(c.XTb), bk[:, 0:128], [], [c.XTb.r(), rbank[c.bN]])
                    CP("act", B(c.Xb), F(c.Xf), [c.Xf.r()], [c.Xb.r()])
                    yield
                    TT("dve", F(c.X8), F(c.Xf), bmv[:, 0, :], ALU.mult, [c.Xf.r(), bmask.r()], [c.X8.r()])
                    TT("dve", F(c.Z8), F(c.XTf), bmv[:, 0, :], ALU.mult, [c.XTf.r(), bmask.r()], [c.Z8.r()])
                    yield
                    TT("dve", F(c.E0), F(c.Z8), ident[:], ALU.add, [c.Z8.r(), r_c], [c.E0.r()])

                def st_base1(c, b):
                    bk = banks[c.bN]
                    MM(bk[:, 0:128], F(c.Z8), F(c.X8), [c.Z8.r(), c.X8.r()], [rbank[c.bN]])
                    MM(bk[:, 128:256], F(c.X8), F(c.Z8), [c.Z8.r(), c.X8.r()], [rbank[c.bN]])
                    yield
                    CP("act", F(c.Y1), bk[:, 0:128], [], [c.Y1.r(), rbank[c.bN]])
                    CP("act", F(c.Z1), bk[:, 128:256], [], [c.Z1.r(), rbank[c.bN]])
                    yield
                    MM(bk[:, 256:384], F(c.Y1), F(c.E0), [c.Y1.r(), c.E0.r()], [rbank[c.bN]])
                    yield
                    TT("dve", F(c.E1), F(c.E0), bk[:, 256:384], ALU.add, [c.E0.r()], [c.E1.r(), rbank[c.bN]])

                def st_base2(c, b):
                    bk = banks[c.bN]
                    MM(bk[:, 0:128], F(c.Z1), F(c.Y1), [c.Z1.r(), c.Y1.r()], [rbank[c.bN]])
                    yield
                    CP("act", F(c.Y2), bk[:, 0:128], [], [c.Y2.r(), rbank[c.bN]])
                    yield
                    MM(bk[:, 128:256], F(c.Y2), F(c.E1), [c.Y2.r(), c.E1.r()], [rbank[c.bN]])
                    yield
                    TT("dve", F(c.E0), F(c.E1), bk[:, 128:256], ALU.add, [c.E1.r()], [c.E0.r(), rbank[c.bN]])
                    yield
                    TR(bk[:, 256:384], F(c.E0), ident[:], [c.E0.r(), r_c], [rbank[c.bN]])
                    yield
                    CP("act", B(c.Db[0]), bk[:, 256:384], [], [c.Db[0].r(), rbank[c.bN]])
                    CP("act", B(c.Eb[0]), F(c.E0), [c.E0.r()], [c.Eb[0].r()])

                def st_merge(l):
                    def f(c, b):
                        bk = banks[c.bN]
                        Dp, Ep = c.Db[l % 2], c.Eb[l % 2]
                        Dn, En = c.Db[(l + 1) % 2], c.Eb[(l + 1) % 2]
                        mk = bmv[:, 1 + l, :]
                        MM(bk[:, 0:128], B(c.XTb), B(Dp), [c.XTb.r(), Dp.r()], [rbank[c.bN]])
                        MM(bk[:, 128:256], B(c.Xb), B(Ep), [c.Xb.r(), Ep.r()], [rbank[c.bN]])
                        yield
                        TT("dve", B(c.M1), bk[:, 0:128], mk, ALU.mult, [bmask.r()], [c.M1.r(), rbank[c.bN]])
                        TT("dve", B(c.M1p), bk[:, 128:256], mk, ALU.mult, [bmask.r()], [c.M1p.r(), rbank[c.bN]])
                        yield
                        MM(bk[:, 256:384], identb[:], B(Dp), [r_c, Dp.r()], [rbank[c.bN]], start=True, stop=False)
                        MM(bk[:, 256:384], B(Ep), B(c.M1), [Ep.r(), c.M1.r()], [rbank[c.bN]], start=False, stop=True)
                        MM(bk[:, 384:512], identb[:], B(Ep), [r_c, Ep.r()], [rbank[c.bN]], start=True, stop=False)
                        MM(bk[:, 384:512], B(Dp), B(c.M1p), [Dp.r(), c.M1p.r()], [rbank[c.bN]], start=False, stop=True)
                        yield
                        CP("act", B(Dn), bk[:, 256:384], [], [Dn.r(), rbank[c.bN]])
                        CP("act", B(En), bk[:, 384:512], [], [En.r(), rbank[c.bN]])
                    return f

                def st_c(c, b):
                    bs = slice(b * 128, (b + 1) * 128)
                    bk = banks[c.bA]
                    TR(bfps(c.bA, 0, 128), knb.bf(T)[:, bs], identb[:], [knb.r(), r_c], [rbank[c.bA]])
                    TR(bk[:, 128:256], vf.f32(T)[:, bs], ident[:], [vf.r(), r_c], [rbank[c.bA]])
                    yield
                    ACT(B(c.vb), bk[:, 128:256], AF.Identity, [beta.r()], [c.vb.r(), rbank[c.bA]], scale=beta.f32(NG, 8)[:, b, h:h + 1])
                    ACT(B(c.kbg), bfps(c.bA, 0, 128), AF.Identity, [beg.r()], [c.kbg.r(), rbank[c.bA]], scale=beg.f32(NG, 8)[:, b, h:h + 1])
                    ACT(B(c.kg), bfps(c.bA, 0, 128), AF.Identity, [egl.r()], [c.kg.r(), rbank[c.bA]], scale=egl.f32(NG, 8)[:, b, h:h + 1])
                    yield
                    MM(bk[:, 256:384], B(c.PTb), B(c.vb), [c.PTb.r(), c.vb.r()], [rbank[c.bA]])
                    MM(bk[:, 384:512], B(c.kbg), B(c.PTb), [c.PTb.r(), c.kbg.r()], [rbank[c.bA]])
                    yield
                    CP("act", F(c.u), bk[:, 256:384], [], [c.u.r(), rbank[c.bA]])
                    CP("dve", B(c.wkT), bk[:, 384:512], [], [c.wkT.r(), rbank[c.bA]])

                def recur(c, b):
                    b6, b7 = banks[6], banks[7]
                    MM(b6[:, 0:128], B(c.wkT), B(Sb), [c.wkT.r(), Sb.r()], [rbank[6]])
                    TT("dve", B(c.vnew), F(c.u), b6[:, 0:128], ALU.subtract, [c.u.r()], [c.vnew.r(), rbank[6]])
                    MM(b7[:, 0:128], B(c.qgT), B(Sb), [c.qgT.r(), Sb.r()], [rbank[7]], start=True, stop=False)
                    MM(b7[:, 0:128], B(c.attnT), B(c.vnew), [c.attnT.r(), c.vnew.r()], [rbank[7]], start=False, stop=True)
                    MM(b6[:, 128:256], B(c.kg), B(c.vnew), [c.kg.r(), c.vnew.r()], [rbank[6]])
                    STT("dve", F(Sf), F(Sf), gl.f32(NB, 8)[:, b, h:h + 1], b6[:, 128:256], ALU.mult, ALU.add, [gl.r()], [Sf.r(), rbank[6]])
                    CP("act", B(Sb), F(Sf), [Sf.r()], [Sb.r()])
                    CP("act", F(c.osb), b7[:, 0:128], [], [c.osb.r(), rbank[7]])

                def recur_post(c, b):
                    bs = slice(b * 128, (b + 1) * 128)
                    ACT(F(c.on), F(c.osb), AF.Square, [c.osb.r()], [c.on.r(), c.ssq.r()], accum=c.ssq.f32(1))
                    rsqrt_eps(c.ssq.f32(1), c.ssq.f32(1), RMS_EPS * 128.0, c.ssq.r())
                    STT("dve", F(c.on), F(c.osb), c.ssq.f32(1), F(gnb), ALU.mult, ALU.mult, [c.osb.r(), c.ssq.r(), gnb.r()], [c.on.r()])
                    TR(banks[c.bN][:, 384:512], F(c.on), ident[:], [c.on.r(), r_c], [rbank[c.bN]])
                    TT("dve", ogb_ap[:, bs], banks[c.bN][:, 384:512], szf.f32(T)[:, bs], ALU.mult, [szf.r()], [Ub.r(), rbank[c.bN]])

                stages = [st_a, st_b, st_base1, st_base2] + [st_merge(l) for l in range(4)] + [st_c]
                pend_ = []
                for g0 in range(0, NB if "gdn_noblk" not in DBG else 0, G):
                    grp = [(chains[i], g0 + i) for i in range(min(G, NB - g0))]
                    for stg_ in stages:
                        gens_ = [stg_(c, b) for c, b in grp]
                        while gens_:
                            nx_ = []
                            for gn_ in gens_:
                                try:
                                    next(gn_)
                                    nx_.append(gn_)
                                except StopIteration:
                                    pass
                            gens_ = nx_
                    for c, b in grp:
                        recur(c, b)
                        if pend_:
                            recur_post(*pend_.pop())
                        pend_.append((c, b))
                    if pend_:
                        recur_post(*pend_.pop())
                if pend_:
                    recur_post(*pend_.pop())
                for dc in range(8):
                    for ti, (c0, n) in enumerate(ptiles):
                        bk = 4 + cntb[0] % 2
                        cntb[0] += 1
                        MM(banks[bk][:, 0:n], wo.bf(1024)[:, dc * 128:(dc + 1) * 128], ogb_ap[:, c0:c0 + n], [wo.r(), Ub.r()], [rbank[bk]])
                        xv = xf[:, dc, c0:c0 + n]
                        if h == 0:
                            STT("dve", xv, xv, ALPHA, banks[bk][:, 0:n], ALU.mult, ALU.add, [r_xf[dc][ti]], [r_xf[dc][ti], rbank[bk]])
                        else:
                            TT("dve", xv, xv, banks[bk][:, 0:n], ALU.add, [r_xf[dc][ti]], [r_xf[dc][ti], rbank[bk]])
                P.dma("sp", "gsp", o_gdn_p[h], F(Sf), reads=[Sf.r()])
            A.release(mH, list(bufs))
            del bufs[:]
            cst_ = A.f32(3072)
            for q4 in range(6):
                for i4 in range(4):
                    ch = q4 * 4 + i4
                    TR(banks[4][0:3, i4 * 128:(i4 + 1) * 128], luv[:, ch, :], ident[:], [lastU.r(), r_c], [rbank[4]])
                CP("dve", cst_.f32(3072)[0:3, q4 * 512:(q4 + 1) * 512], banks[4][0:3, :], [], [cst_.r(), rbank[4]])
            P.dma("sp", "gcp", o_conv_p, cst_.f32(3072)[0:3, :], reads=[cst_.r()])
            A.release(mH, [cst_])
            mS = A.mark()
            if "gdn_nosamp" in DBG:
                A.release(m, base_bufs)
                return
            sb_ = []

            def als(n, bf=False):
                b = A.bf(n) if bf else A.f32(n)
                sb_.append(b)
                return b
            projs = als(4096)
            for typ in range(4):
                for h2 in range(2):
                    bk = (typ * 2 + h2) % 2
                    for i4 in range(4):
                        hh = h2 * 4 + i4
                        TR(banks[bk][0:NS, i4 * 128:(i4 + 1) * 128], pjT[:, typ, hh, :], ident[:], [projsT.r(), r_c], [rbank[bk]])
                    CP("dve", projs.f32(4096)[0:NS, typ * 1024 + h2 * 512:typ * 1024 + (h2 + 1) * 512], banks[bk][0:NS, :], [], [projs.r(), rbank[bk]])
            P.dma("sp", "gcv", o_conv_s[:, 0:2, :], sconv[:, 1:3, :])
            P.dma("sp", "gcv", o_conv_s[:, 2, :], projs.f32(4096)[0:NS, 0:3072], reads=[projs.r()])
            P.commit("gcv")
            qkv = als(3072)
            zs = als(1024)
            mC = A.mark()
            PW = 256
            ext = A.f32(4 * PW); cwb = A.f32(4 * PW); prod = A.f32(4 * PW)
            for pc in range(3072 // PW):
                cs_ = slice(pc * PW, (pc + 1) * PW)
                P.dma("sp", "gsl", ext.f32(4, PW)[0:NS, 0:3, :], sconv[:, :, cs_], writes=[ext.r()])
                P.dma("sp", "gsl", cwb.f32(4, PW)[0:NS], gdn_conv_w[:, cs_].partition_broadcast(NS), writes=[cwb.r()])
                P.commit("gsl")
                CP("dve", ext.f32(4, PW)[0:NS, 3, :], projs.f32(4096)[0:NS, cs_], [projs.r()], [ext.r()])
                TT("dve", prod.f32(4, PW)[0:NS], ext.f32(4, PW)[0:NS], cwb.f32(4, PW)[0:NS], ALU.mult, [ext.r(), cwb.r()], [prod.r()])
                RED("dve", qkv.f32(3072)[0:NS, cs_], prod.f32(4, PW)[0:NS].rearrange("p j c -> p c j"), [prod.r()], [qkv.r()])
            A.release(mC, [ext, cwb, prod])
            ACT(qkv.f32(3072)[0:NS], qkv.f32(3072)[0:NS], AF.Silu, [qkv.r()], [qkv.r()])
            ACT(zs.f32(1024)[0:NS], projs.f32(4096)[0:NS, 3072:4096], AF.Silu, [projs.r()], [zs.r()])
            q3 = qkv.f32(3, 8, 128)[0:NS, 0]
            k3 = qkv.f32(3, 8, 128)[0:NS, 1]
            v3 = qkv.f32(3, 8, 128)[0:NS, 2]
            tmp = als(1024); ss = als(16)
            t3 = tmp.f32(8, 128)[0:NS]
            for (x3, scl) in ((q3, 128.0 ** -0.5), (k3, 1.0)):
                TT("dve", t3, x3, x3, ALU.mult, [qkv.r()], [tmp.r()])
                RED("dve", ss.f32(8)[0:NS], t3, [tmp.r()], [ss.r()])
                rsqrt_eps(ss.f32(8)[0:NS], ss.f32(8)[0:NS], RMS_EPS, ss.r())
                STT("dve", x3, x3, scl, ss.f32(8)[0:NS].unsqueeze(2).to_broadcast([NS, 8, 128]), ALU.mult, ALU.mult, [ss.r()], [qkv.r()])
            qk = als(8)
            TT("dve", t3, q3, k3, ALU.mult, [qkv.r()], [tmp.r()])
            RED("dve", qk.f32(8)[0:NS], t3, [tmp.r()], [qk.r()])
            kTs = als(8 * NS); qTs = als(8 * NS)
            for (x3, dst, bk) in ((k3, kTs, 4), (q3, qTs, 5)):
                for hh in range(8):
                    TR(banks[bk][:, hh * NS:(hh + 1) * NS], x3[:, hh, :], ident[0:NS, 0:NS], [qkv.r(), r_c], [rbank[bk]])
                CP("dve", dst.f32(8 * NS), banks[bk][:, 0:8 * NS], [], [dst.r(), rbank[bk]])
            dlt = als(NS * NS)
            P.dma("sp", "gsm", dlt.f32(NS, NS), cd["delta"], writes=[dlt.r()])
            dcol = als(NS)
            P.dma("sp", "gsm", dcol.f32(NS), cd["dcol"], writes=[dcol.r()])
            P.commit("gsm")
            KmL = [als(NS * NS), als(NS * NS)]; QmL = [als(NS * NS), als(NS * NS)]
            Sp = [als(128) for _ in range(4)]
            ci_ = 0
            for hh in range(8):
                Km = KmL[hh % 2]; Qm = QmL[hh % 2]
                for (src, dst) in ((kTs, Km), (qTs, Qm)):
                    TT("dve", dst.f32(NS, NS), src.f32(8, NS)[:, hh, :].unsqueeze(1).to_broadcast([128, NS, NS]),
                       dlt.f32(NS, NS), ALU.mult, [src.r(), dlt.r()], [dst.r()])
                for s in range(NS):
                    sp_ = Sp[ci_ % 4]
                    P.dma("sp", "gsq%d" % (ci_ % 4), sp_.f32(128), sgdn[s, hh], writes=[sp_.r()])
                    ci_ += 1
                    MM(banks[hh // 4][0:NS, (hh % 4) * 128:(hh % 4 + 1) * 128], Km.f32(NS, NS)[:, s, :], sp_.f32(128),
                       [Km.r(), sp_.r()], [rbank[hh // 4]], start=(s == 0), stop=(s == NS - 1))
                    MM(banks[2 + hh // 4][0:NS, (hh % 4) * 128:(hh % 4 + 1) * 128], Qm.f32(NS, NS)[:, s, :], sp_.f32(128),
                       [Qm.r(), sp_.r()], [rbank[2 + hh // 4]], start=(s == 0), stop=(s == NS - 1))
            KS = als(1024); QS = als(1024)
            for half in range(2):
                CP("dve", KS.f32(1024)[0:NS, half * 512:(half + 1) * 512], banks[half][0:NS, :], [], [KS.r(), rbank[half]])
                CP("dve", QS.f32(1024)[0:NS, half * 512:(half + 1) * 512], banks[2 + half][0:NS, :], [], [QS.r(), rbank[2 + half]])
            bc8 = lambda b_, col: b_.f32(NG, 8)[0:NS, col, :].unsqueeze(2).to_broadcast([NS, 8, 128])
            KS3 = KS.f32(8, 128)[0:NS]; QS3 = QS.f32(8, 128)[0:NS]
            vn = als(1024)
            vn3 = vn.f32(8, 128)[0:NS]
            TT("dve", KS3, KS3, bc8(eg, NB), ALU.mult, [eg.r()], [KS.r()])
            TT("dve", vn3, v3, KS3, ALU.subtract, [qkv.r(), KS.r()], [vn.r()])
            TT("dve", vn3, vn3, bc8(beta, NB), ALU.mult, [beta.r()], [vn.r()])
            TT("dve", QS3, QS3, bc8(eg, NB), ALU.mult, [eg.r()], [QS.r()])
            TT("dve", t3, vn3, qk.f32(8)[0:NS].unsqueeze(2).to_broadcast([NS, 8, 128]), ALU.mult, [vn.r(), qk.r()], [tmp.r()])
            TT("dve", QS3, QS3, t3, ALU.add, [tmp.r()], [QS.r()])
            TT("dve", t3, QS3, QS3, ALU.mult, [QS.r()], [tmp.r()])
            RED("dve", ss.f32(8)[0:NS], t3, [tmp.r()], [ss.r()])
            rsqrt_eps(ss.f32(8)[0:NS], ss.f32(8)[0:NS], RMS_EPS, ss.r(), scale=1.0 / 128.0)
            TT("dve", QS3, QS3, ss.f32(8)[0:NS].unsqueeze(2).to_broadcast([NS, 8, 128]), ALU.mult, [ss.r()], [QS.r()])
            TT("dve", QS3, QS3, gn.f32(128)[0:NS].unsqueeze(1).to_broadcast([NS, 8, 128]), ALU.mult, [gn.r()], [QS.r()])
            TT("dve", QS3, QS3, zs.f32(8, 128)[0:NS], ALU.mult, [zs.r()], [QS.r()])
            ogsT = als(8 * NS, bf=True)
            for hh in range(8):
                TR(banks[4][:, hh * NS:(hh + 1) * NS], QS3[:, hh, :], ident[0:NS, 0:NS], [QS.r(), r_c], [rbank[4]])
            CP("dve", ogsT.bf(8 * NS), banks[4][:, 0:8 * NS], [], [ogsT.r(), rbank[4]])
            out_proj_samples(gdn_w_out, 8, ogsT.bf(8, NS), ogsT.r())
            Rm = als(NS * 8)
            TT("dve", Rm.f32(NS, 8)[0:NS], dcol.f32(NS)[0:NS].unsqueeze(2).to_broadcast([NS, NS, 8]),
               eg.f32(NG, 8)[0:NS, NB, :].unsqueeze(1).to_broadcast([NS, NS, 8]), ALU.mult, [dcol.r(), eg.r()], [Rm.r()])
            EGb = als(NS * 8)
            MM(banks[4][:, 0:NS * 8], onesf[0:NS, :], Rm.f32(NS * 8)[0:NS], [Rm.r(), r_c], [rbank[4]])
            CP("dve", EGb.f32(NS * 8), banks[4][:, 0:NS * 8], [], [EGb.r(), rbank[4]])
            Vm = [als(1024), als(1024)]
            Sn = [als(128) for _ in range(4)]
            ci_ = 0
            for s in range(NS):
                vm_ = Vm[s % 2]
                TS("dve", vm_.f32(1024)[0:NS], vn.f32(1024)[0:NS], dcol.f32(NS)[0:NS, s:s + 1], None, ALU.mult, None, [vn.r(), dcol.r()], [vm_.r()])
                for hh in range(8):
                    sp_ = Sp[ci_ % 4]; sn_ = Sn[ci_ % 4]
                    bk = 4 + ci_ % 4
                    P.dma("sp", "gsq%d" % (ci_ % 4), sp_.f32(128), sgdn[s, hh], writes=[sp_.r()])
                    MM(banks[bk][:, 0:128], k3[:, hh, :], vm_.f32(8, 128)[0:NS, hh, :], [qkv.r(), vm_.r()], [rbank[bk]])
                    STT("dve", sn_.f32(128), sp_.f32(128), EGb.f32(NS, 8)[:, s, hh:hh + 1], banks[bk][:, 0:128], ALU.mult, ALU.add,
                        [sp_.r(), EGb.r()], [sn_.r(), rbank[bk]])
                    P.dma("pool", "gst%d" % (ci_ % 4), o_gdn_s[s, hh], sn_.f32(128), reads=[sn_.r()])
                    ci_ += 1
            A.release(mS, sb_)
            A.release(m, base_bufs)

        def ret_mixer():
            G = 2
            m = A.mark()
            bufs = []

            def al(n, bf=False):
                b = A.bf(n) if bf else A.f32(n)
                bufs.append(b)
                return b
            cos2 = al(NT); sin2 = al(NT); decT = al(128); xi = al(128); zeta = al(8); dcol = al(NS); dlt = al(NS * NS)
            P.dma("sp", "rc", cos2.f32(NT), cd["cos2"], writes=[cos2.r()])
            P.dma("sp", "rc", sin2.f32(NT), cd["sin2"], writes=[sin2.r()])
            P.dma("sp", "rc", zeta.f32(8), cd["zeta"], writes=[zeta.r()])
            P.dma("sp", "rc", dcol.f32(NS), cd["dcol"], writes=[dcol.r()])
            P.dma("sp", "rc", dlt.f32(NS, NS), cd["delta"], writes=[dlt.r()])
            P.commit("rc")
            SC = 128.0 ** -0.5
            qrb = al(NT, bf=True); krb = al(NT, bf=True); krf = al(NT); qsf = al(NS)
            t1 = [al(TTS), al(TTS)]; t2 = [al(TTS), al(TTS)]
            vtok = al(NB * 256, bf=True)
            vtv = vtok.bf(NB, 256)
            ogT = al(2 * NT, bf=True)
            ogv = ogT.bf(2, NT)
            Sf = al(256); Sb = al(256, bf=True)
            v_s = al(256); sg_s = al(256); kts = al(128); Qm = al(NS * NS); qk = al(8); tmp_s = al(256); o_s = al(256)
            Sp = [al(256) for _ in range(3)]
            Vm = [al(256) for _ in range(3)]; Sn = [al(256) for _ in range(3)]
            stat = [al(8) for _ in range(3)]

            class Ch:
                pass
            chains = []
            for ci in range(G):
                c = Ch()
                c.attnT = al(128, bf=True); c.qxT = al(128, bf=True); c.kz = al(128, bf=True)
                c.sg = al(256); c.on = al(256); c.osb = al(256)
                c.bA = ci % 2
                chains.append(c)
            B = lambda b_: b_.bf(128)

            def gnorm_gate(o_ap, np_, on_ap, on_res, sg_ap, sg_res, o_reads, o_writes):
                sm, sq_, mm = stat[0], stat[1], stat[2]
                ACT(on_ap, o_ap, AF.Identity, o_reads, [on_res, sm.r()] + o_writes, accum=sm.f32(1)[0:np_])
                ACT(on_ap, o_ap, AF.Square, o_reads, [on_res, sq_.r()] + o_writes, accum=sq_.f32(1)[0:np_])
                TS("dve", sm.f32(1)[0:np_], sm.f32(1)[0:np_], 1.0 / 256.0, None, ALU.mult, None, [sm.r()], [sm.r()])
                TT("dve", mm.f32(1)[0:np_], sm.f32(1)[0:np_], sm.f32(1)[0:np_], ALU.mult, [sm.r()], [mm.r()])
                STT("dve", sq_.f32(1)[0:np_], sq_.f32(1)[0:np_], 1.0 / 256.0, mm.f32(1)[0:np_], ALU.mult, ALU.subtract, [sq_.r(), mm.r()], [sq_.r()])
                rsqrt_eps(sq_.f32(1)[0:np_], sq_.f32(1)[0:np_], LN_EPS, sq_.r())
                TS("dve", on_ap, o_ap, sm.f32(1)[0:np_], sq_.f32(1)[0:np_], ALU.subtract, ALU.mult, o_reads + [sm.r(), sq_.r()], [on_res] + o_writes)
                TT("pool", on_ap, on_ap, sg_ap, ALU.mult, [on_res, sg_res], [on_res])

            for h in range(8):
                slots = []
                for xi_, base in enumerate((0, 1024)):
                    wb_, wk_ = wslot()
                    wv_ = wb_.bf(2, 8, 128)
                    c0_ = base + h * 128
                    r3 = lambda a, b_: ret_w_in[:, a:b_].rearrange("(k p) n -> p k n", p=128)
                    P.dma("pool", wk_, wv_[:, 0], r3(c0_, c0_ + 128), writes=[wb_.r()])
                    P.dma("pool", wk_, wv_[:, 1, :, 0:64], r3(c0_ + 64, c0_ + 128), writes=[wb_.r()])
                    P.dma("pool", wk_, wv_[:, 1, :, 64:128], r3(c0_, c0_ + 64), writes=[wb_.r()])
                    P.commit(wk_)
                    slots.append((wb_, wv_))
                P.dma("sp", "rdx", decT.f32(128), cd["decT"][:, h, :], writes=[decT.r()])
                P.dma("sp", "rdx", xi.f32(128), cd["xi"][:, h, :], writes=[xi.r()])
                P.commit("rdx")
                wv_t, wkv_ = wslot()
                P.dma("pool", wkv_, wv_t.bf(8, 256), ret_w_in[:, 2048 + h * 256:2048 + (h + 1) * 256].rearrange("(k p) n -> p k n", p=128), writes=[wv_t.r()])
                P.commit(wkv_)
                wg_t, wkg_ = wslot()
                P.dma("pool", wkg_, wg_t.bf(8, 256), ret_w_in[:, 4096 + h * 256:4096 + (h + 1) * 256].rearrange("(k p) n -> p k n", p=128), writes=[wg_t.r()])
                P.commit(wkg_)
                cn = 0
                for xi_ in range(2):
                    wb_, wv_ = slots[xi_]
                    for ti, (c0, n) in enumerate(tiles):
                        a1, a2 = t1[cn % 2], t2[cn % 2]
                        ba_, bb_ = 4 + 2 * (cn % 2), 5 + 2 * (cn % 2)
                        cn += 1
                        for k in range(8):
                            MM(banks[ba_][:, 0:n], wv_[:, 0, k, :], xb[:, k, c0:c0 + n], [wb_.r(), r_xb[k][ti]], [rbank[ba_]], start=(k == 0), stop=(k == 7))
                        for k in range(8):
                            MM(banks[bb_][:, 0:n], wv_[:, 1, k, :], xb[:, k, c0:c0 + n], [wb_.r(), r_xb[k][ti]], [rbank[bb_]], start=(k == 0), stop=(k == 7))
                        TT("dve", a1.f32(n), banks[ba_][:, 0:n], cos2.f32(NT)[:, c0:c0 + n], ALU.mult, [cos2.r()], [a1.r(), rbank[ba_]])
                        TT("dve", a2.f32(n), banks[bb_][:, 0:n], sin2.f32(NT)[:, c0:c0 + n], ALU.mult, [sin2.r()], [a2.r(), rbank[bb_]])
                        if xi_ == 0:
                            TT("pool", qrb.bf(NT)[:, c0:c0 + n], a1.f32(n), a2.f32(n), ALU.add, [a1.r(), a2.r()], [qrb.r()])
                            if ti == NTI - 1:
                                TT("pool", qsf.f32(NS), a1.f32(n), a2.f32(n), ALU.add, [a1.r(), a2.r()], [qsf.r()])
                        else:
                            TT("pool", krf.f32(NT)[:, c0:c0 + n], a1.f32(n), a2.f32(n), ALU.add, [a1.r(), a2.r()], [krf.r()])
                CP("pool", krb.bf(NT), krf.f32(NT), [krf.r()], [krb.r()])
                for b in range(NB):
                    bs = slice(b * 128, (b + 1) * 128)
                    bv_ = 6 + b % 2
                    for k in range(8):
                        MM(banks[bv_][:, 0:256], xb[:, k, bs], wv_t.bf(8, 256)[:, k, :], [wv_t.r()] + rx(r_xb, [k], tiles_of(b * 128, 128)), [rbank[bv_]],
                           start=(k == 0), stop=(k == 7))
                    CP("act", vtv[:, b, :], banks[bv_][:, 0:256], [], [vtok.r(), rbank[bv_]])
                for k in range(8):
                    MM(banks[6][0:NS, 256:512], xb[:, k, T:NT], wv_t.bf(8, 256)[:, k, :], [wv_t.r(), r_xb[k][NTI - 1]], [rbank[6]], start=(k == 0), stop=(k == 7))
                CP("act", v_s.f32(256)[0:NS], banks[6][0:NS, 256:512], [], [v_s.r(), rbank[6]])
                MS("dve", Sf.f32(256), 0.0, [Sf.r()])
                MS("dve", Sb.bf(256), 0.0, [Sb.r()])

                def st_a(c, b):
                    bs = slice(b * 128, (b + 1) * 128)
                    bk = banks[c.bA]
                    MM(bk[:, 0:128], krb.bf(NT)[:, bs], qrb.bf(NT)[:, bs], [krb.r(), qrb.r()], [rbank[c.bA]])
                    TR(bk[:, 128:256], krf.f32(NT)[:, bs], ident[:], [krf.r(), r_c], [rbank[c.bA]])
                    for k in range(8):
                        MM(bk[:, 256:512], xb[:, k, bs], wg_t.bf(8, 256)[:, k, :], [wg_t.r()] + rx(r_xb, [k], tiles_of(b * 128, 128)), [rbank[c.bA]],
                           start=(k == 0), stop=(k == 7))
                    yield
                    TT("dve", B(c.attnT), bk[:, 0:128], decT.f32(128), ALU.mult, [decT.r()], [c.attnT.r(), rbank[c.bA]])
                    ACT(B(c.kz), bk[:, 128:256], AF.Identity, [zeta.r()], [c.kz.r(), rbank[c.bA]], scale=zeta.f32(8)[:, h:h + 1])
                    ACT(c.sg.f32(256), bk[:, 256:512], AF.Silu, [], [c.sg.r(), rbank[c.bA]])
                    TT("pool", B(c.qxT), qrb.bf(NT)[:, bs], xi.f32(128), ALU.mult, [qrb.r(), xi.r()], [c.qxT.r()])

                def recur(c, b):
                    MM(banks[2][:, 0:256], B(c.qxT), Sb.bf(256), [c.qxT.r(), Sb.r()], [rbank[2]], start=True, stop=False)
                    MM(banks[2][:, 0:256], B(c.attnT), vtv[:, b, :], [c.attnT.r(), vtok.r()], [rbank[2]], start=False, stop=True)
                    MM(banks[3][:, 0:256], B(c.kz), vtv[:, b, :], [c.kz.r(), vtok.r()], [rbank[3]])
                    STT("dve", Sf.f32(256), Sf.f32(256), cst["gC"][h], banks[3][:, 0:256], ALU.mult, ALU.add, [], [Sf.r(), rbank[3]])
                    CP("act", Sb.bf(256), Sf.f32(256), [Sf.r()], [Sb.r()])
                    CP("act", c.osb.f32(256), banks[2][:, 0:256], [], [c.osb.r(), rbank[2]])

                def recur_post(c, b):
                    bs = slice(b * 128, (b + 1) * 128)
                    gnorm_gate(c.osb.f32(256), 128, c.on.f32(256), c.on.r(), c.sg.f32(256), c.sg.r(), [c.osb.r()], [])
                    for cc in range(2):
                        TR(banks[7][:, 256 + cc * 128:256 + (cc + 1) * 128], c.on.f32(256)[:, cc * 128:(cc + 1) * 128], ident[:], [c.on.r(), r_c], [rbank[7]])
                    CP("act", ogv[:, :, bs], banks[7][:, 256:512].rearrange("p (c t) -> p c t", c=2, t=128), [], [ogT.r(), rbank[7]])

                pend_ = []
                for g0 in range(0, NB if "ret_noblk" not in DBG else 0, G):
                    grp = [(chains[i], g0 + i) for i in range(min(G, NB - g0))]
                    gens_ = [st_a(c, b) for c, b in grp]
                    while gens_:
                        nx_ = []
                        for gn_ in gens_:
                            try:
                                next(gn_)
                                nx_.append(gn_)
                            except StopIteration:
                                pass
                        gens_ = nx_
                    for c, b in grp:
                        recur(c, b)
                        if pend_:
                            recur_post(*pend_.pop())
                        pend_.append((c, b))
                    if pend_:
                        recur_post(*pend_.pop())
                if pend_:
                    recur_post(*pend_.pop())
                P.dma("sp", "rsp", o_ret_p[h], Sf.f32(256), reads=[Sf.r()])
                for k in range(8 if "ret_nosamp" not in DBG else 0):
                    MM(banks[6][0:NS, 0:256], xb[:, k, T:NT], wg_t.bf(8, 256)[:, k, :], [wg_t.r(), r_xb[k][NTI - 1]], [rbank[6]], start=(k == 0), stop=(k == 7))
                if "ret_nosamp" not in DBG:
                    ACT(sg_s.f32(256)[0:NS], banks[6][0:NS, 0:256], AF.Silu, [], [sg_s.r(), rbank[6]])
                    ks_ = krf.f32(NT)[:, T:NT]
                    TR(banks[7][0:NS, 0:128], ks_, ident[:], [krf.r(), r_c], [rbank[7]])
                    CP("dve", kts.f32(128)[0:NS], banks[7][0:NS, 0:128], [], [kts.r(), rbank[7]])
                    TT("dve", tmp_s.f32(NS), qsf.f32(NS), ks_, ALU.mult, [qsf.r(), krf.r()], [tmp_s.r()])
                    MM(banks[7][0:NS, 128:129], tmp_s.f32(NS), onesf[:, 0:1], [tmp_s.r(), r_c], [rbank[7]])
                    TS("dve", qk.f32(1)[0:NS], banks[7][0:NS, 128:129], SC, None, ALU.mult, None, [], [qk.r(), rbank[7]])
                    TT("dve", Qm.f32(NS, NS), qsf.f32(NS).unsqueeze(1).to_broadcast([128, NS, NS]), dlt.f32(NS, NS), ALU.mult, [qsf.r(), dlt.r()], [Qm.r()])
                    for s in range(NS):
                        sp_ = Sp[s % 3]; vm_ = Vm[s % 3]; sn_ = Sn[s % 3]
                        bk = 4 + s % 2
                        P.dma("sp", "rsq%d" % (s % 3), sp_.f32(256), sret[s, h], writes=[sp_.r()])
                        MM(banks[6][0:NS, 256:512], Qm.f32(NS, NS)[:, s, :], sp_.f32(256), [Qm.r(), sp_.r()], [rbank[6]], start=(s == 0), stop=(s == NS - 1))
                        TS("dve", vm_.f32(256)[0:NS], v_s.f32(256)[0:NS], dcol.f32(NS)[0:NS, s:s + 1], SC, ALU.mult, ALU.mult, [v_s.r(), dcol.r()], [vm_.r()])
                        MM(banks[bk][:, 0:256], kts.f32(128)[0:NS], vm_.f32(256)[0:NS], [kts.r(), vm_.r()], [rbank[bk]])
                        STT("dve", sn_.f32(256), sp_.f32(256), cst["gam"][h], banks[bk][:, 0:256], ALU.mult, ALU.add, [sp_.r()], [sn_.r(), rbank[bk]])
                        P.dma("pool", "rst%d" % (s % 3), o_ret_s[s, h], sn_.f32(256), reads=[sn_.r()])
                    TS("dve", tmp_s.f32(256)[0:NS], v_s.f32(256)[0:NS], qk.f32(1)[0:NS], None, ALU.mult, None, [v_s.r(), qk.r()], [tmp_s.r()])
                    STT("dve", o_s.f32(256)[0:NS], banks[6][0:NS, 256:512], cst["gam"][h], tmp_s.f32(256)[0:NS], ALU.mult, ALU.add, [tmp_s.r()], [o_s.r(), rbank[6]])
                    gnorm_gate(o_s.f32(256)[0:NS], NS, tmp_s.f32(256)[0:NS], tmp_s.r(), sg_s.f32(256)[0:NS], sg_s.r(), [o_s.r()], [])
                    for cc in range(2):
                        TR(banks[7][:, 256 + cc * NS:256 + (cc + 1) * NS], tmp_s.f32(256)[0:NS, cc * 128:(cc + 1) * 128], ident[0:NS, 0:NS], [tmp_s.r(), r_c], [rbank[7]])
                    CP("dve", ogv[:, :, T:NT], banks[7][:, 256:256 + 2 * NS].rearrange("p (c t) -> p c t", c=2, t=NS), [], [ogT.r(), rbank[7]])
                wo_t, wko_ = wslot()
                P.dma("pool", wko_, wo_t.bf(2, 1024), ret_w_out[h * 256:(h + 1) * 256, :].rearrange("(c p) n -> p c n", p=128), writes=[wo_t.r()])
                P.commit(wko_)
                for dc in range(8):
                    for ti, (c0, n) in enumerate(tiles):
                        bk = 4 + (dc * NTI + ti) % 2
                        for cc in range(2):
                            MM(banks[bk][:, 0:n], wo_t.bf(2, 1024)[:, cc, dc * 128:(dc + 1) * 128], ogv[:, cc, c0:c0 + n], [wo_t.r(), ogT.r()], [rbank[bk]],
                               start=(cc == 0), stop=(cc == 1))
                        xv = xf[:, dc, c0:c0 + n]
                        if h == 0:
                            STT("dve", xv, xv, ALPHA, banks[bk][:, 0:n], ALU.mult, ALU.add, [r_xf[dc][ti]], [r_xf[dc][ti], rbank[bk]])
                        else:
                            TT("dve", xv, xv, banks[bk][:, 0:n], ALU.add, [r_xf[dc][ti]], [r_xf[dc][ti], rbank[bk]])
            A.release(m, bufs)

        if "noload" not in DBG:
            load_x()
        for li in range(nlayers):
            kind = li % 3
            if kind == 0:
                pool_mixer(li // 3)
            elif kind == 1:
                gdn_mixer()
            else:
                ret_mixer()
            layer_norm(2 * li)
            ffn(li)
            layer_norm(2 * li + 1)
        if "nostore" not in DBG:
            store_y()
        if "arena" in DBG:
            print("arena words", AW, "high water", A.hw)
        P.emit(st)
    return nc, cst


_CACHE = {}


def _in_maps(inputs, T, NS, cst):
    f = lambda a: np.ascontiguousarray(np.asarray(a, dtype=np.float32))
    shared = {
        "pool_w": f(inputs["pool_w"]), "pool_scale": f(inputs["pool_scale"]),
        "gdn_w_in": f(inputs["gdn_w_in"][0]), "gdn_conv_w": f(inputs["gdn_conv_w"][0]),
        "gdn_a_log": f(inputs["gdn_a_log"]), "gdn_dt_bias": f(inputs["gdn_dt_bias"]),
        "gdn_norm_g": f(inputs["gdn_norm_g"]), "gdn_w_out": f(inputs["gdn_w_out"][0]),
        "ret_w_in": f(inputs["ret_w_in"][0]), "ret_w_out": f(inputs["ret_w_out"][0]),
        "ffn_w13": f(inputs["ffn_w13"]), "ffn_w2": f(inputs["ffn_w2"]),
        "ln_g": f(inputs["ln_g"]).reshape(8, D), "ln_b": f(inputs["ln_b"]).reshape(8, D),
    }
    for k in list(CONST_SHAPES) + ["cos2", "sin2"]:
        shared["c_" + k] = f(cst[k])
    maps = []
    for c in range(NCORES):
        sl = slice(c * NS, (c + 1) * NS)
        mp = dict(shared)
        mp["xp"] = f(inputs["x_prompt"][c])
        mp["xs"] = f(inputs["x_sample"][sl, 0])
        mp["spool"] = f(inputs["state_pool"][:, sl])
        mp["sconv"] = f(inputs["state_gdn_conv"][0, sl])
        mp["sgdn"] = f(inputs["state_gdn"][0, sl])
        mp["sret"] = f(inputs["state_ret"][0, sl])
        maps.append(mp)
    return maps


def kernel(**inputs):
    T, NS = 2048, 16
    if "nc" not in _CACHE:
        _CACHE["nc"] = build(T, NS)
    nc, cst = _CACHE["nc"]
    maps = _in_maps(inputs, T, NS, cst)
    res = run_bass_kernel_spmd(nc, maps, core_ids=list(range(NCORES))).results
    cat = lambda k, ax: np.concatenate([np.asarray(r[k], dtype=np.float32)[None] if ax is None else np.asarray(r[k], dtype=np.float32)
                                        for r in res], axis=0 if ax is None else ax)
    y_prompt = cat("yp", None)
    y_sample = cat("ys", 0)[:, None, :]
    pool_p = np.stack([np.asarray(r["pool_p"], np.float32) for r in res], axis=1)
    pool_s = cat("pool_s", 1)
    conv_p = cat("conv_p", None)[None]
    conv_s = cat("conv_s", 0)[None]
    gdn_p = cat("gdn_p", None)[None]
    gdn_s = cat("gdn_s", 0)[None]
    ret_p = cat("ret_p", None)[None]
    ret_s = cat("ret_s", 0)[None]
    return (y_prompt, y_sample, pool_p, pool_s, conv_p, conv_s, gdn_p, gdn_s, ret_p, ret_s)
```

```python
import math
from contextlib import ExitStack
import numpy as np
import concourse.bass as bass
import concourse.mybir as mybir
from concourse.bass_utils import run_bass_kernel_spmd

F32 = mybir.dt.float32
BF16 = mybir.dt.bfloat16
ALU = mybir.AluOpType
AF = mybir.ActivationFunctionType
AX = mybir.AxisListType

D = 1024
DFF = 2816
NCORES = 8
PAST_LEN = 16384
ALPHA = 8.0 ** 0.25
LN_EPS = 1e-5
RMS_EPS = 1e-6
COMPUTE = ("pe", "act", "dve", "pool")


class Res:
    __slots__ = ("last_w", "readers")

    def __init__(self, inherit=None):
        self.last_w = None
        self.readers = dict(inherit) if inherit else {}


class Op:
    __slots__ = ("eng", "fn", "deps", "is_dma", "dkey", "dcount", "need_inc", "cnt", "idx")

    def __init__(self, eng, fn, is_dma=False, dkey=None):
        self.eng = eng
        self.fn = fn
        self.deps = []
        self.is_dma = is_dma
        self.dkey = dkey
        self.dcount = 0
        self.need_inc = False
        self.cnt = 0
        self.idx = 0


class Prog:
    def __init__(self, nc):
        self.nc = nc
        self.ops = []
        self.dma_counts = {}
        self.open_batch = {}

    def _add(self, op, reads, writes):
        op.idx = len(self.ops)
        deps = {}
        for r in reads:
            w = r.last_w
            if w is not None:
                deps[id(w)] = (w, True)
        for r in writes:
            w = r.last_w
            if w is not None and id(w) not in deps:
                deps[id(w)] = (w, False)
            for rd in r.readers.values():
                if id(rd) not in deps:
                    deps[id(rd)] = (rd, False)
        for (d, raw) in deps.values():
            if d is op:
                continue
            if (not d.is_dma) and (not op.is_dma) and d.eng == op.eng and d.eng == "pe":
                continue
            if d.is_dma and op.is_dma and d.dkey == op.dkey and d in self.open_batch.get(op.dkey, ()):
                continue
            op.deps.append(d)
            d.need_inc = True
        for r in reads:
            key = (op.eng, op.dkey) if op.is_dma else op.eng
            r.readers[key] = op
        for r in writes:
            r.last_w = op
            r.readers = {}
        self.ops.append(op)
        return op

    def op(self, eng, fn, reads=(), writes=()):
        return self._add(Op(eng, fn), reads, writes)

    def dma(self, eng, key, out, in_, reads=(), writes=()):
        def fn(e, out=out, in_=in_):
            return e.dma_start(out=out, in_=in_)
        o = Op(eng, fn, is_dma=True, dkey=key)
        self.dma_counts[key] = self.dma_counts.get(key, 0) + 16
        o.dcount = self.dma_counts[key]
        o.need_inc = True
        self.open_batch.setdefault(key, []).append(o)
        return self._add(o, reads, writes)

    def commit(self, key):
        for o in self.open_batch.get(key, []):
            o.dcount = self.dma_counts[key]
        self.open_batch[key] = []

    def emit(self, st):
        nc = self.nc
        sems = {e: st.enter_context(nc.semaphore("s_" + e)) for e in COMPUTE}
        dsems = {k: st.enter_context(nc.semaphore("d_%s" % (k,))) for k in self.dma_counts}
        cnts = {e: 0 for e in COMPUTE}
        for o in self.ops:
            if not o.is_dma and o.need_inc:
                cnts[o.eng] += 1
                o.cnt = cnts[o.eng]
        final_waits = {("d", k): (dsems[k], v) for k, v in self.dma_counts.items()}
        streams = {}
        for o in self.ops:
            streams.setdefault(o.eng, []).append(o)
        block = st.enter_context(nc.Block())

        def run(engname, e):
            wm = {}
            for o in streams.get(engname, []):
                need = {}
                for d in o.deps:
                    if d.is_dma:
                        k = ("d", d.dkey)
                        s, v = dsems[d.dkey], d.dcount
                    else:
                        k = ("c", d.eng)
                        s, v = sems[d.eng], d.cnt
                    if v > wm.get(k, 0) and v > need.get(k, (None, 0))[1]:
                        need[k] = (s, v)
                for k, (s, v) in need.items():
                    e.wait_ge(s, v)
                    wm[k] = v
                ins = o.fn(e)
                if o.is_dma:
                    ins.then_inc(dsems[o.dkey], 16)
                elif o.need_inc:
                    ins.then_inc(sems[o.eng], 1)
            if engname == "sp":
                for k, (s, v) in final_waits.items():
                    if v > wm.get(k, 0):
                        e.wait_ge(s, v)

        @block.sync
        def _(e):
            run("sp", e)

        @block.tensor
        def _(e):
            run("pe", e)

        @block.scalar
        def _(e):
            run("act", e)

        @block.vector
        def _(e):
            run("dve", e)

        @block.gpsimd
        def _(e):
            run("pool", e)


class Buf:
    def __init__(self, ap, lo, hi, inherit):
        self.ap = ap
        self.lo = lo
        self.hi = hi
        self.inherit = inherit
        self.ch = {}

    def r(self, key=None):
        c = self.ch.get(key)
        if c is None:
            c = Res(self.inherit)
            self.ch[key] = c
        return c

    def f32(self, *shape):
        return _shape(self.ap, shape)

    def bf(self, *shape):
        return _shape(self.ap.bitcast(BF16), shape)


class Sub:
    def __init__(self, parent, lo, hi):
        self.parent = parent
        self.ap = parent.ap[:, lo:hi]

    def r(self, key=None):
        return self.parent.r(key)

    def f32(self, *shape):
        return _shape(self.ap, shape)

    def bf(self, *shape):
        return _shape(self.ap.bitcast(BF16), shape)


def _shape(ap, shape):
    if len(shape) <= 1:
        return ap[:, 0:shape[0]] if shape else ap
    n = 1
    for s in shape:
        n *= s
    ap = ap[:, 0:n]
    if len(shape) == 2:
        return ap.rearrange("p (a b) -> p a b", a=shape[0], b=shape[1])
    if len(shape) == 3:
        return ap.rearrange("p (a b c) -> p a b c", a=shape[0], b=shape[1], c=shape[2])
    raise ValueError(shape)


class Arena:
    def __init__(self, ap, words):
        self.ap = ap
        self.words = words
        self.top = 0
        self.dead = []

    def alloc(self, words):
        words = (words + 7) // 8 * 8
        lo, hi = self.top, self.top + words
        assert hi <= self.words, ("arena overflow", hi, self.words)
        self.top = hi
        self.hw = max(getattr(self, "hw", 0), hi)
        inh = {}
        keep = []
        for b in self.dead:
            if b.hi <= lo or b.lo >= hi:
                keep.append(b)
                continue
            for c in b.ch.values():
                cand = list(c.readers.items())
                if c.last_w is not None:
                    w = c.last_w
                    cand.append(((w.eng, w.dkey) if w.is_dma else w.eng, w))
                for k, o in cand:
                    if k not in inh or inh[k].idx < o.idx:
                        inh[k] = o
            if not (b.lo >= lo and b.hi <= hi):
                keep.append(b)
        self.dead = keep
        return Buf(self.ap[:, lo:hi], lo, hi, inh)

    def f32(self, n):
        return self.alloc(n)

    def bf(self, n):
        return self.alloc((n + 1) // 2)

    def mark(self):
        return (self.top, [])

    def release(self, mark, bufs):
        self.top = mark[0]
        self.dead.extend(bufs)


def _consts(T, NS):
    c = {}
    c["ident"] = np.eye(128, dtype=np.float32)
    ii = np.arange(128)
    rc = np.zeros((128, 4, 16), np.float32)
    for g, win in enumerate((2, 4, 8, 16)):
        for t in range(16):
            rc[:, g, t] = 1.0 / min(t + 1, win)
    c["rc"] = rc
    c["maskS"] = np.where(ii[None, :] < ii[:, None], 0.0, 30000.0).astype(np.float32)
    c["negmaskT"] = np.where(ii[None, :] >= ii[:, None], 0.0, -30000.0).astype(np.float32)
    c["U"] = (ii[:, None] <= ii[None, :]).astype(np.float32)
    bm = lambda sz: ((ii[:, None] // sz) == (ii[None, :] // sz)).astype(np.float32)
    msk = np.zeros((128, 5, 128), np.float32)
    msk[:, 0, :] = bm(8)
    for li_, sz in enumerate((8, 16, 32, 64)):
        msk[:, 1 + li_, :] = bm(2 * sz) - bm(sz)
    c["bmask"] = msk
    H = 8
    lg = np.log(1.0 - 2.0 ** (-5.0 - np.arange(H, dtype=np.float64)))
    sc = 128.0 ** -0.5
    diff = (ii[None, :] - ii[:, None]).astype(np.float64)
    decT = np.where(diff[None] >= 0, np.exp(np.maximum(diff, 0.0)[None] * lg[:, None, None]), 0.0) * sc
    c["decT"] = np.ascontiguousarray(decT.transpose(1, 0, 2)).astype(np.float32)
    xi = np.exp((ii + 1.0)[None, :] * lg[:, None])
    c["xi"] = np.ascontiguousarray(np.broadcast_to(xi[None], (128, H, 128))).astype(np.float32)
    zeta = np.exp((127.0 - ii)[None, :] * lg[:, None]) * sc
    c["zeta"] = np.ascontiguousarray(zeta.T).astype(np.float32)
    c["gC"] = [float(np.exp(128.0 * lg[h])) for h in range(H)]
    c["gam"] = [float(np.exp(lg[h])) for h in range(H)]
    gb = np.zeros((128, 2, H), np.float32)
    gb[:, 0, :] = np.exp(lg)[None, :]
    gb[:, 1, :] = (np.exp(lg) * 0 + sc)[None, :]
    c["gamb"] = gb
    half = 64
    freqs = (np.float32(10000.0) ** (-np.arange(half, dtype=np.float32) / np.float32(half))).astype(np.float32)
    pos = np.concatenate([np.arange(T, dtype=np.float32), np.full((NS,), float(PAST_LEN), np.float32)])
    ang = (pos[None, :] * freqs[:, None]).astype(np.float32)
    cs, sn = np.cos(ang).astype(np.float32), np.sin(ang).astype(np.float32)
    c["cos2"] = np.concatenate([cs, cs], 0)
    c["sin2"] = np.concatenate([-sn, sn], 0)
    angs = (np.float32(PAST_LEN) * freqs).astype(np.float32)
    cst = np.zeros((128, 2, half), np.float32)
    cst[:, 0, :] = np.cos(angs)[None]
    cst[:, 1, :] = np.sin(angs)[None]
    c["ropes"] = cst
    dl = np.zeros((128, 16, 16), np.float32)
    for s in range(16):
        dl[:, s, s] = 1.0
    c["delta"] = dl
    dcol = np.zeros((128, 16), np.float32)
    for s in range(16):
        dcol[s, s] = 1.0
    c["dcol"] = dcol
    return c


CONST_SHAPES = {"ident": [128, 128], "rc": [128, 4, 16], "maskS": [128, 128], "negmaskT": [128, 128],
                "U": [128, 128], "bmask": [128, 5, 128], "decT": [128, 8, 128], "xi": [128, 8, 128], "zeta": [128, 8],
                "gamb": [128, 2, 8], "ropes": [128, 2, 64], "delta": [128, 16, 16], "dcol": [128, 16]}


AWO = None
DBG = set()


def build(T, NS, nlayers=4):
    assert T % 128 == 0 and NS == 16
    nc = bass.Bass("TRN2", target_bir_lowering=False)
    NB = T // 128
    NT = T + NS
    TTS = min(512, T)
    tiles = [(c0, TTS) for c0 in range(0, T, TTS)] + [(T, NS)]
    NTI = len(tiles)
    cst = _consts(T, NS)

    def din(name, shape):
        return nc.dram_tensor(name, list(shape), F32, kind="ExternalInput").ap()

    def dout(name, shape):
        return nc.dram_tensor(name, list(shape), F32, kind="ExternalOutput").ap()

    xp = din("xp", [T, D])
    xs = din("xs", [NS, D])
    spool = din("spool", [2, NS, 15, D])
    sconv = din("sconv", [NS, 3, 3072])
    sgdn = din("sgdn", [NS, 8, 128, 128])
    sret = din("sret", [NS, 8, 128, 256])
    pool_w = din("pool_w", [2, 4, 256, 256])
    pool_scale = din("pool_scale", [2, D])
    gdn_w_in = din("gdn_w_in", [D, 4112])
    gdn_conv_w = din("gdn_conv_w", [4, 3072])
    gdn_a_log = din("gdn_a_log", [1, 8])
    gdn_dt_bias = din("gdn_dt_bias", [1, 8])
    gdn_norm_g = din("gdn_norm_g", [1, 128])
    gdn_w_out = din("gdn_w_out", [D, D])
    ret_w_in = din("ret_w_in", [D, 6144])
    ret_w_out = din("ret_w_out", [2048, D])
    ffn_w13 = din("ffn_w13", [4, D, 2 * DFF])
    ffn_w2 = din("ffn_w2", [4, DFF, D])
    ln_g = din("ln_g", [8, D])
    ln_b = din("ln_b", [8, D])
    cd = {k: din("c_" + k, CONST_SHAPES[k]) for k in CONST_SHAPES}
    cd["cos2"] = din("c_cos2", [128, NT])
    cd["sin2"] = din("c_sin2", [128, NT])

    yp = dout("yp", [T, D])
    ys = dout("ys", [NS, D])
    o_pool_p = dout("pool_p", [2, 15, D])
    o_pool_s = dout("pool_s", [2, NS, 15, D])
    o_conv_p = dout("conv_p", [3, 3072])
    o_conv_s = dout("conv_s", [NS, 3, 3072])
    o_gdn_p = dout("gdn_p", [8, 128, 128])
    o_gdn_s = dout("gdn_s", [NS, 8, 128, 128])
    o_ret_p = dout("ret_p", [8, 128, 256])
    o_ret_s = dout("ret_s", [NS, 8, 128, 256])

    P = Prog(nc)
    st = ExitStack()
    with st:
        def sb(name, shape, dt):
            return st.enter_context(nc.sbuf_tensor(name, shape, dt))

        xf = sb("xf", [128, 8, NT], F32)
        xb = sb("xb", [128, 8, NT], BF16)
        ident = sb("ident", [128, 128], F32)
        identb = sb("identb", [128, 128], BF16)
        onesb = sb("onesb", [128, 128], BF16)
        onesn = sb("onesn", [128, 128], BF16)
        onesf = sb("onesf", [128, 128], F32)
        lnp = sb("lnp", [128, 128], F32)
        prm = sb("prm", [128, 128], F32)
        rcs = sb("rcs", [128, 4, 16], F32)
        AW = (nc.sbuf_bytes_remaining - 4096) // 4 // 8 * 8 if AWO is None else AWO
        arena_t = sb("arena", [128, AW], F32)
        A = Arena(arena_t[:], AW)
        banks = [st.enter_context(nc.psum_tensor("bank%d" % i, [128, 512], F32)) for i in range(8)]
        rbank = [Res() for _ in range(8)]
        r_xf = [[Res() for _ in range(NTI)] for _ in range(8)]
        r_xb = [[Res() for _ in range(NTI)] for _ in range(8)]
        r_c = Res()

        def rx(rs, ks=range(8), ts=range(NTI)):
            return [rs[k][t] for k in ks for t in ts]

        def tiles_of(c0, n):
            return [ti for ti, (a, m) in enumerate(tiles) if a < c0 + n and c0 < a + m]

        P.dma("sp", "c0", ident[:], cd["ident"], writes=[r_c])
        P.dma("sp", "c0", rcs[:], cd["rc"], writes=[r_c])
        P.commit("c0")
        P.op("dve", lambda e: e.tensor_copy(identb[:], ident[:]), reads=[r_c], writes=[r_c])
        P.op("dve", lambda e: e.memset(onesb[:], 1.0), writes=[r_c])
        P.op("dve", lambda e: e.memset(onesn[:], 1.0 / D), writes=[r_c])
        P.op("dve", lambda e: e.memset(onesf[:], 1.0), writes=[r_c])
        wring = [A.bf(2 * 8 * 128) for _ in range(4)]
        wr_i = [0]

        def wslot():
            b = wring[wr_i[0] % len(wring)]
            k = "w%d" % (wr_i[0] % len(wring))
            wr_i[0] += 1
            return b, k

        m0 = A.mark()
        if "noparams" in DBG:
            raise_ = None
        stg = A.f32(128)
        stg2 = A.f32(128)
        if "noparams" not in DBG:
          P.dma("sp", "c1", stg.f32(128)[0:64, :], ln_g.rearrange("l (k p) -> (l k) p", p=128), writes=[stg.r()])
          P.dma("sp", "c1", stg.f32(128)[64:128, :], ln_b.rearrange("l (k p) -> (l k) p", p=128), writes=[stg.r()])
          P.op("dve", lambda e: e.memset(stg2.f32(128), 0.0), writes=[stg2.r()])
          P.dma("sp", "c2", stg2.f32(128)[0:16, :], pool_scale.rearrange("j (k p) -> (j k) p", p=128), reads=[], writes=[stg2.r()])
          P.dma("sp", "c2", stg2.f32(128)[16:112, :], gdn_conv_w.rearrange("j (c p) -> (j c) p", p=128), writes=[stg2.r()])
          P.dma("sp", "c2", stg2.f32(128)[112:113, :], gdn_norm_g, writes=[stg2.r()])
          P.op("pe", lambda e: e.transpose(banks[0][:, 0:128], stg.f32(128), ident[:]), reads=[stg.r(), r_c], writes=[rbank[0]])
          P.op("pe", lambda e: e.transpose(banks[0][:, 128:256], stg2.f32(128), ident[:]), reads=[stg2.r(), r_c], writes=[rbank[0]])
          P.op("dve", lambda e: e.tensor_copy(lnp[:], banks[0][:, 0:128]), reads=[rbank[0]], writes=[r_c])
          P.op("dve", lambda e: e.tensor_copy(prm[:], banks[0][:, 128:256]), reads=[rbank[0]], writes=[r_c])
        A.release(m0, [stg, stg2])

        epsb = {}
        for ev in (LN_EPS, RMS_EPS, RMS_EPS * 128.0, 1.0):
            t = sb("eps%d" % len(epsb), [128, 1], F32)
            P.op("dve", lambda e, t=t, ev=ev: e.memset(t[:], ev), writes=[r_c])
            epsb[ev] = t

        def eps_tile(v):
            return epsb[v][:]

        def rsqrt_eps(out, in_, eps, res, scale=1.0):
            eb = epsb[eps]
            np_ = in_.shape[0]
            P.op("act", lambda e: e.activation(out, in_, AF.Ln, bias=eb[0:np_, :], scale=scale), reads=[res, r_c], writes=[res])
            P.op("act", lambda e: e.activation(out, out, AF.Exp, scale=-0.5), reads=[res], writes=[res])

        def load_x():
            m = A.mark()
            sg = [A.f32(D), A.f32(D)]
            blocks = [(xp[b * 128:(b + 1) * 128, :], 128, b * 128) for b in range(NB)] + [(xs, NS, T)]
            if "nosamp" in DBG:
                blocks = blocks[:-1]
            for bi, (src, n, c0) in enumerate(blocks):
                s = sg[bi % 2]
                P.dma("sp", "xl%d" % (bi % 2), s.f32(D)[0:n, :], src, writes=[s.r()])
                tis = tiles_of(c0, n)
                for half in range(2):
                    bk = (bi * 2 + half) % 2
                    for kk in range(4):
                        k = half * 4 + kk
                        P.op("pe", lambda e, s=s, k=k, kk=kk, n=n, bk=bk: e.transpose(
                            banks[bk][:, kk * 128:kk * 128 + n], s.f32(D)[0:n, k * 128:(k + 1) * 128], ident[0:n, 0:n]),
                            reads=[s.r(), r_c], writes=[rbank[bk]])
                    src_ps = banks[bk][:].rearrange("p (a b) -> p a b", a=4, b=128)[:, :, 0:n]
                    ks = range(half * 4, half * 4 + 4)
                    P.op("dve", lambda e, src_ps=src_ps, half=half, c0=c0, n=n: e.tensor_copy(
                        xf[:, half * 4:half * 4 + 4, c0:c0 + n], src_ps),
                        reads=[rbank[bk]], writes=rx(r_xf, ks, tis))
                    if "noact" in DBG:
                        continue
                    P.op("act", lambda e, src_ps=src_ps, half=half, c0=c0, n=n: e.activation(
                        xb[:, half * 4:half * 4 + 4, c0:c0 + n], xf[:, half * 4:half * 4 + 4, c0:c0 + n], AF.Identity),
                        reads=rx(r_xf, ks, tis), writes=rx(r_xb, ks, tis))
            A.release(m, sg)

        def layer_norm(li):
            m = A.mark()
            bufs = []
            sets = []
            for _ in range(2):
                sets.append((A.bf(8 * TTS), A.bf(8 * TTS), A.f32(TTS), A.f32(TTS), A.f32(TTS)))
                bufs += list(sets[-1])
            for ti, (c0, n) in enumerate(tiles):
                rb, sq, mean, rstd, m2 = sets[ti % 2]
                xv = xf[:, :, c0:c0 + n]
                P.op("act", lambda e, rb=rb, xv=xv, n=n: e.activation(rb.bf(8, n), xv, AF.Copy),
                     reads=rx(r_xf, ts=[ti]), writes=[rb.r()])
                P.op("act", lambda e, sq=sq, xv=xv, n=n: e.activation(sq.bf(8, n), xv, AF.Square),
                     reads=rx(r_xf, ts=[ti]), writes=[sq.r()])
                for k in range(8):
                    P.op("pe", lambda e, rb=rb, k=k, n=n: e.matmul(banks[6][:, 0:n], onesn[:], rb.bf(8, n)[:, k, :],
                                                                   start=(k == 0), stop=(k == 7)),
                         reads=[rb.r(), r_c], writes=[rbank[6]])
                for k in range(8):
                    P.op("pe", lambda e, sq=sq, k=k, n=n: e.matmul(banks[7][:, 0:n], onesn[:], sq.bf(8, n)[:, k, :],
                                                                   start=(k == 0), stop=(k == 7)),
                         reads=[sq.r(), r_c], writes=[rbank[7]])
                P.op("act", lambda e, mean=mean, n=n: e.activation(mean.f32(n), banks[6][:, 0:n], AF.Copy),
                     reads=[rbank[6]], writes=[mean.r()])
                P.op("dve", lambda e, mean=mean, m2=m2, n=n: e.tensor_tensor(m2.f32(n), mean.f32(n), mean.f32(n), ALU.mult),
                     reads=[mean.r()], writes=[m2.r()])
                P.op("dve", lambda e, m2=m2, rstd=rstd, n=n: e.tensor_tensor(rstd.f32(n), banks[7][:, 0:n], m2.f32(n), ALU.subtract),
                     reads=[m2.r(), rbank[7]], writes=[rstd.r()])
                rsqrt_eps(rstd.f32(n), rstd.f32(n), LN_EPS, rstd.r())
                P.op("dve", lambda e, xv=xv, mean=mean, n=n: e.tensor_tensor(
                    xv, xv, mean.f32(n).unsqueeze(1).to_broadcast([128, 8, n]), ALU.subtract),
                    reads=rx(r_xf, ts=[ti]) + [mean.r()], writes=rx(r_xf, ts=[ti]))
                P.op("dve", lambda e, xv=xv, rstd=rstd, n=n: e.tensor_tensor(
                    xv, xv, rstd.f32(n).unsqueeze(1).to_broadcast([128, 8, n]), ALU.mult),
                    reads=rx(r_xf, ts=[ti]) + [rstd.r()], writes=rx(r_xf, ts=[ti]))
                for k in range(8):
                    P.op("act", lambda e, k=k, c0=c0, n=n: e.activation(
                        xf[:, k, c0:c0 + n], xf[:, k, c0:c0 + n], AF.Identity,
                        bias=lnp[:, 64 + li * 8 + k:64 + li * 8 + k + 1], scale=lnp[:, li * 8 + k:li * 8 + k + 1]),
                        reads=[r_xf[k][ti], r_c], writes=[r_xf[k][ti]])
                P.op("dve", lambda e, xv=xv, c0=c0, n=n: e.tensor_copy(xb[:, :, c0:c0 + n], xv),
                     reads=rx(r_xf, ts=[ti]), writes=rx(r_xb, ts=[ti]))
            A.release(m, bufs)

        def ffn(layer):
            m = A.mark()
            h = A.bf(11 * NT)
            w2 = A.bf(11 * D)
            sa = [A.f32(TTS), A.f32(TTS)]
            hv = h.bf(11, NT)
            w2v = w2.bf(11, D)
            cnt = 0
            for pas in range(2):
                for il in range(11):
                    i = pas * 11 + il
                    wb, wk = wslot()
                    wv = wb.bf(2, 8, 128)
                    P.dma("pool", wk, wv[:, 0], ffn_w13[layer, :, i * 128:(i + 1) * 128].rearrange("(k p) n -> p k n", p=128),
                          writes=[wb.r()])
                    P.dma("pool", wk, wv[:, 1], ffn_w13[layer, :, DFF + i * 128:DFF + (i + 1) * 128].rearrange("(k p) n -> p k n", p=128),
                          writes=[wb.r()])
                    P.commit(wk)
                    if il == 0:
                        for f in range(11):
                            P.dma("pool", "fw2", w2v[:, f, :], ffn_w2[layer, (pas * 11 + f) * 128:(pas * 11 + f + 1) * 128, :],
                                  writes=[w2.r(f)])
                        P.commit("fw2")
                    for ti, (c0, n) in enumerate(tiles):
                        ba, bb = banks[cnt % 2], banks[2 + cnt % 2]
                        ra, rbb = rbank[cnt % 2], rbank[2 + cnt % 2]
                        s = sa[cnt % 2]
                        cnt += 1
                        for k in range(8):
                            P.op("pe", lambda e, ba=ba, wv=wv, k=k, c0=c0, n=n: e.matmul(
                                ba[:, 0:n], wv[:, 0, k, :], xb[:, k, c0:c0 + n], start=(k == 0), stop=(k == 7)),
                                reads=[wb.r(), r_xb[k][ti]], writes=[ra])
                        for k in range(8):
                            P.op("pe", lambda e, bb=bb, wv=wv, k=k, c0=c0, n=n: e.matmul(
                                bb[:, 0:n], wv[:, 1, k, :], xb[:, k, c0:c0 + n], start=(k == 0), stop=(k == 7)),
                                reads=[wb.r(), r_xb[k][ti]], writes=[rbb])
                        P.op("act", lambda e, s=s, ba=ba, n=n: e.activation(s.f32(n), ba[:, 0:n], AF.Silu),
                             reads=[ra], writes=[s.r()])
                        P.op("dve", lambda e, s=s, bb=bb, il=il, c0=c0, n=n: e.tensor_tensor(
                            hv[:, il, c0:c0 + n], s.f32(n), bb[:, 0:n], ALU.mult),
                            reads=[s.r(), rbb], writes=[h.r((il, ti))])
                for dc in range(8):
                    for ti, (c0, n) in enumerate(tiles):
                        by, ry = banks[4 + cnt % 2], rbank[4 + cnt % 2]
                        cnt += 1
                        for f in range(11):
                            P.op("pe", lambda e, by=by, f=f, dc=dc, c0=c0, n=n: e.matmul(
                                by[:, 0:n], w2v[:, f, dc * 128:(dc + 1) * 128], hv[:, f, c0:c0 + n],
                                start=(f == 0), stop=(f == 10)),
                                reads=[w2.r(f), h.r((f, ti))], writes=[ry])
                        xv = xf[:, dc, c0:c0 + n]
                        if pas == 0:
                            P.op("dve", lambda e, xv=xv, by=by, n=n: e.scalar_tensor_tensor(
                                xv, xv, ALPHA, by[:, 0:n], ALU.mult, ALU.add),
                                reads=[ry, r_xf[dc][ti]], writes=[r_xf[dc][ti]])
                        else:
                            P.op("dve", lambda e, xv=xv, by=by, n=n: e.tensor_tensor(xv, xv, by[:, 0:n], ALU.add),
                                 reads=[ry, r_xf[dc][ti]], writes=[r_xf[dc][ti]])
            A.release(m, [h, w2] + sa)

        def store_y():
            m = A.mark()
            sg = [A.f32(D), A.f32(D)]
            blocks = [(yp[b * 128:(b + 1) * 128, :], 128, b * 128) for b in range(NB)] + [(ys, NS, T)]
            for bi, (dst, n, c0) in enumerate(blocks):
                s = sg[bi % 2]
                tis = tiles_of(c0, n)
                for half in range(2):
                    bk = (bi * 2 + half) % 2
                    for kk in range(4):
                        k = half * 4 + kk
                        P.op("pe", lambda e, k=k, kk=kk, n=n, bk=bk, c0=c0: e.transpose(
                            banks[bk][0:n, kk * 128:(kk + 1) * 128], xf[:, k, c0:c0 + n], ident[:]),
                            reads=rx(r_xf, [k], tis) + [r_c], writes=[rbank[bk]])
                    if half == 0:
                        P.op("dve", lambda e, s=s, n=n, bk=bk: e.tensor_copy(s.f32(D)[0:n, 0:512], banks[bk][0:n, :]),
                             reads=[rbank[bk]], writes=[s.r()])
                    else:
                        P.op("act", lambda e, s=s, n=n, bk=bk: e.activation(s.f32(D)[0:n, 512:1024], banks[bk][0:n, :], AF.Copy),
                             reads=[rbank[bk]], writes=[s.r()])
                P.dma("sp", "ys%d" % (bi % 2), dst, s.f32(D)[0:n, :], reads=[s.r()])
            A.release(m, sg)

        def pool_mixer(j):
            m = A.mark()
            bufs = []
            if j == 0:
                P.dma("sp", "po", o_pool_p[0], xp[T - 15:T, :])
                P.dma("sp", "po", o_pool_s[0, :, 14, :], xs)
            else:
                so = A.f32(D)
                bufs.append(so)
                nn = 15 + NS
                for half in range(2):
                    for kk in range(4):
                        k = half * 4 + kk
                        P.op("pe", lambda e, k=k, kk=kk, half=half: e.transpose(
                            banks[half][0:nn, kk * 128:(kk + 1) * 128], xf[:, k, T - 15:T + NS], ident[:]),
                            reads=rx(r_xf, [k], tiles_of(T - 15, nn)) + [r_c], writes=[rbank[half]])
                    P.op("dve", lambda e, half=half: e.tensor_copy(so.f32(D)[0:nn, half * 512:(half + 1) * 512], banks[half][0:nn, :]),
                         reads=[rbank[half]], writes=[so.r()])
                P.dma("sp", "po", o_pool_p[1], so.f32(D)[0:15, :], reads=[so.r()])
                P.dma("sp", "po", o_pool_s[1, :, 14, :], so.f32(D)[15:15 + NS, :], reads=[so.r()])
            P.dma("sp", "po", o_pool_s[j, :, 0:14, :], spool[j, :, 1:15, :])
            P.commit("po")
            hist = A.f32(8 * NS * 15)
            bufs.append(hist)
            hv = hist.f32(8, NS * 15)
            hs = [A.f32(D), A.f32(D)]
            bufs += hs
            rows = NS * 15 // 2
            src = spool[j].rearrange("s r d -> (s r) d")
            for b2 in range(2):
                s = hs[b2]
                P.dma("sp", "ph%d" % b2, s.f32(D)[0:rows, :], src[b2 * rows:(b2 + 1) * rows, :], writes=[s.r()])
                for half in range(2):
                    bk = half
                    for kk in range(4):
                        k = half * 4 + kk
                        P.op("pe", lambda e, s=s, k=k, kk=kk, bk=bk: e.transpose(
                            banks[bk][:, kk * 128:kk * 128 + rows], s.f32(D)[0:rows, k * 128:(k + 1) * 128], ident[0:rows, 0:rows]),
                            reads=[s.r(), r_c], writes=[rbank[bk]])
                    P.op("dve", lambda e, half=half, bk=bk, b2=b2: e.tensor_copy(
                        hv[:, half * 4:half * 4 + 4, b2 * rows:(b2 + 1) * rows],
                        banks[bk][:].rearrange("p (a b) -> p a b", a=4, b=128)[:, :, 0:rows]),
                        reads=[rbank[bk]], writes=[hist.r()])
            pooled = A.bf(8 * NT)
            bufs.append(pooled)
            pv = pooled.bf(8, NT)
            E = [A.f32(16 + T), A.f32(16 + T)]
            bufs += E
            ssum = A.f32(NS)
            bufs.append(ssum)
            for eb in E:
                P.op("pool", lambda e, eb=eb: e.memset(eb.f32(16 + T)[:, 0:16], 0.0), writes=[eb.r()])
            for k in range(8):
                g = k // 2
                win = 2 << g
                e0, e1 = E[0], E[1]
                tp = list(range(NTI - 1))
                P.op("pool", lambda e, e0=e0, k=k: e.tensor_copy(e0.f32(16 + T)[:, 16:16 + T], xf[:, k, 0:T]),
                     reads=rx(r_xf, [k], tp), writes=[e0.r()])
                cur, nxt = e0, e1
                sh = 1
                while sh < win:
                    P.op("dve", lambda e, cur=cur, nxt=nxt, sh=sh: e.tensor_tensor(
                        nxt.f32(16 + T)[:, 16:16 + T], cur.f32(16 + T)[:, 16:16 + T], cur.f32(16 + T)[:, 16 - sh:16 + T - sh], ALU.add),
                        reads=[cur.r()], writes=[nxt.r()])
                    cur, nxt = nxt, cur
                    sh *= 2
                P.op("dve", lambda e, cur=cur, k=k, win=win: e.scalar_tensor_tensor(
                    pv[:, k, 0:T], cur.f32(16 + T)[:, 16:16 + T], 1.0 / win, xf[:, k, 0:T], ALU.mult, ALU.subtract),
                    reads=[cur.r()] + rx(r_xf, [k], tp), writes=[pooled.r(k)])
                P.op("dve", lambda e, cur=cur, nxt=nxt, g=g: e.tensor_tensor(
                    nxt.f32(16 + T)[:, 16:31], cur.f32(16 + T)[:, 16:31], rcs[:, g, 0:15], ALU.mult),
                    reads=[cur.r(), r_c], writes=[nxt.r()])
                P.op("dve", lambda e, nxt=nxt, k=k: e.tensor_tensor(
                    pv[:, k, 0:15], nxt.f32(16 + T)[:, 16:31], xf[:, k, 0:15], ALU.subtract),
                    reads=[nxt.r()] + rx(r_xf, [k], [0]), writes=[pooled.r(k)])
                hk = hv[:, k, :].rearrange("p (s r) -> p s r", s=NS, r=15)
                P.op("dve", lambda e, hk=hk, win=win: e.tensor_reduce(
                    ssum.f32(NS), hk[:, :, 16 - win:15], AX.X, ALU.add),
                    reads=[hist.r()], writes=[ssum.r()])
                P.op("dve", lambda e, k=k: e.tensor_tensor(ssum.f32(NS), ssum.f32(NS), xf[:, k, T:NT], ALU.add),
                     reads=[ssum.r(), r_xf[k][NTI - 1]], writes=[ssum.r()])
                P.op("dve", lambda e, k=k, win=win: e.scalar_tensor_tensor(
                    pv[:, k, T:NT], ssum.f32(NS), 1.0 / win, xf[:, k, T:NT], ALU.mult, ALU.subtract),
                    reads=[ssum.r(), r_xf[k][NTI - 1]], writes=[pooled.r(k)])
            pw = A.bf(4 * 2 * 256)
            bufs.append(pw)
            pwv = pw.bf(4, 2, 256)
            P.dma("pool", "pw", pwv, pool_w[j].rearrange("g (cc p) d -> p g cc d", p=128), writes=[pw.r()])
            tm = [A.f32(TTS), A.f32(TTS)]
            bufs += tm
            cnt = 0
            for g in range(4):
                for dc in range(2):
                    k = 2 * g + dc
                    for ti, (c0, n) in enumerate(tiles):
                        bk = cnt % 2
                        t_ = tm[cnt % 2]
                        cnt += 1
                        for cc in range(2):
                            P.op("pe", lambda e, g=g, dc=dc, cc=cc, c0=c0, n=n, bk=bk: e.matmul(
                                banks[bk][:, 0:n], pwv[:, g, cc, dc * 128:(dc + 1) * 128], pv[:, 2 * g + cc, c0:c0 + n],
                                start=(cc == 0), stop=(cc == 1)),
                                reads=[pw.r(), pooled.r(2 * g + cc)], writes=[rbank[bk]])
                        P.op("act", lambda e, t_=t_, bk=bk, n=n, k=k: e.activation(
                            t_.f32(n), banks[bk][:, 0:n], AF.Copy, scale=prm[:, j * 8 + k:j * 8 + k + 1]),
                            reads=[rbank[bk], r_c], writes=[t_.r()])
                        xv = xf[:, k, c0:c0 + n]
                        P.op("dve", lambda e, xv=xv, t_=t_, n=n: e.scalar_tensor_tensor(
                            xv, xv, ALPHA, t_.f32(n), ALU.mult, ALU.add),
                            reads=[t_.r(), r_xf[k][ti]], writes=[r_xf[k][ti]])
            A.release(m, bufs)

        def MM(out, lhsT, rhs, r, w, start=True, stop=True):
            P.op("pe", lambda e: e.matmul(out, lhsT, rhs, start=start, stop=stop), reads=r, writes=w)

        def TR(out, in_, idn, r, w):
            P.op("pe", lambda e: e.transpose(out, in_, idn), reads=r, writes=w)

        def TT(eng, out, in0, in1, op, r, w):
            P.op(eng, lambda e: e.tensor_tensor(out, in0, in1, op), reads=r, writes=w)

        def STT(eng, out, in0, scalar, in1, op0, op1, r, w):
            P.op(eng, lambda e: e.scalar_tensor_tensor(out, in0, scalar, in1, op0, op1), reads=r, writes=w)

        def TS(eng, out, in0, s1, s2, op0, op1, r, w):
            if op1 is None:
                P.op(eng, lambda e: e.tensor_scalar(out, in0, s1, None, op0), reads=r, writes=w)
            else:
                P.op(eng, lambda e: e.tensor_scalar(out, in0, s1, s2, op0, op1), reads=r, writes=w)

        def CP(eng, out, in_, r, w):
            if eng == "act":
                P.op("act", lambda e: e.activation(out, in_, AF.Identity), reads=r, writes=w)
            else:
                P.op(eng, lambda e: e.tensor_copy(out, in_), reads=r, writes=w)

        def ACT(out, in_, func, r, w, scale=1.0, bias=None, accum=None):
            def fn(e):
                kw = {}
                if bias is not None:
                    kw["bias"] = bias
                if accum is not None:
                    kw["accum_out"] = accum
                return e.activation(out, in_, func, scale=scale, **kw)
            P.op("act", fn, reads=r, writes=w)

        def RED(eng, out, in_, r, w, op=ALU.add):
            P.op(eng, lambda e: e.tensor_reduce(out, in_, AX.X, op), reads=r, writes=w)

        def MS(eng, out, val, w):
            P.op(eng, lambda e: e.memset(out, val), writes=w)

        def bfps(bank, c0, n):
            return banks[bank][:, c0:c0 + n // 2].bitcast(BF16)

        ptiles = tiles[:-1]
        PT_ = list(range(NTI - 1))

        def out_proj_samples(w_dram, nk, ogsT, ogs_res):
            for dc in range(8):
                for kc in range(nk):
                    wb_, wk_ = wslot()
                    wv_ = wb_.bf(2, 8, 128)
                    if kc % 16 == 0:
                        pass
                    P.dma("pool", wk_, wv_[:, 0, 0, :], w_dram[kc * 128:(kc + 1) * 128, dc * 128:(dc + 1) * 128], writes=[wb_.r()])
                    P.commit(wk_)
                    MM(banks[4][:, 0:NS], wv_[:, 0, 0, :], ogsT[:, kc, :], [wb_.r(), ogs_res], [rbank[4]], start=(kc == 0), stop=(kc == nk - 1))
                xv = xf[:, dc, T:NT]
                STT("dve", xv, xv, ALPHA, banks[4][:, 0:NS], ALU.mult, ALU.add, [r_xf[dc][NTI - 1]], [r_xf[dc][NTI - 1], rbank[4]])

        def gdn_mixer():
            G = 4 if NB >= 4 else 2
            m = A.mark()
            bufs = []

            def al(n, bf=False):
                b = A.bf(n) if bf else A.f32(n)
                bufs.append(b)
                return b
            maskS = al(128); negT = al(128); Um = al(128); gn = al(128); gnb = al(128)
            bmask = al(5 * 128)
            bmv = bmask.f32(5, 128)
            alog = al(8); dtb = al(8); nea = al(8)
            P.dma("sp", "gc", maskS.f32(128), cd["maskS"], writes=[maskS.r()])
            P.dma("sp", "gc", negT.f32(128), cd["negmaskT"], writes=[negT.r()])
            P.dma("sp", "gc", Um.f32(128), cd["U"], writes=[Um.r()])
            P.dma("sp", "gc", bmask.f32(5, 128), cd["bmask"], writes=[bmask.r()])
            P.dma("sp", "gc", gn.f32(128), gdn_norm_g.partition_broadcast(128)[:, 0, :], writes=[gn.r()])
            P.dma("sp", "gc", alog.f32(8), gdn_a_log.partition_broadcast(128)[:, 0, :], writes=[alog.r()])
            P.dma("sp", "gc", dtb.f32(8), gdn_dt_bias.partition_broadcast(128)[:, 0, :], writes=[dtb.r()])
            P.commit("gc")
            TS("dve", gnb.f32(128), gn.f32(128), math.sqrt(128.0), None, ALU.mult, None, [gn.r()], [gnb.r()])
            ACT(nea.f32(8), alog.f32(8), AF.Exp, [alog.r()], [nea.r()])
            TS("dve", nea.f32(8), nea.f32(8), -1.0, None, ALU.mult, None, [nea.r()], [nea.r()])
            NG = NB + 1
            wg = al(8 * 16, bf=True)
            wgv = wg.bf(8, 16)
            P.dma("pool", "gw", wgv, gdn_w_in[:, 4096:4112].rearrange("(k p) n -> p k n", p=128), writes=[wg.r()])
            GA = al(NG * 16)
            GAv = GA.f32(NG, 16)
            MS("dve", GA.f32(NG * 16), 0.0, [GA.r()])
            for b in range(NB):
                for k in range(8):
                    MM(banks[0][:, b * 16:(b + 1) * 16], xb[:, k, b * 128:(b + 1) * 128], wgv[:, k, :],
                       [wg.r()] + rx(r_xb, [k], tiles_of(b * 128, 128)), [rbank[0]], start=(k == 0), stop=(k == 7))
            for k in range(8):
                MM(banks[0][0:NS, NB * 16:NG * 16], xb[:, k, T:NT], wgv[:, k, :],
                   [wg.r(), r_xb[k][NTI - 1]], [rbank[0]], start=(k == 0), stop=(k == 7))
            CP("dve", GA.f32(NB * 16), banks[0][:, 0:NB * 16], [], [GA.r(), rbank[0]])
            CP("dve", GAv[0:NS, NB, :], banks[0][0:NS, NB * 16:NG * 16], [], [GA.r(), rbank[0]])
            beta = al(NG * 8); nbeta = al(NG * 8); gg = al(NG * 8); gcol = al(NG * 8); eg = al(NG * 8)
            beg = al(NG * 8); egl = al(NG * 8); gl = al(NG * 8); t1 = al(NG * 8); t2 = al(NG * 8)
            v8 = lambda b_: b_.f32(NG, 8)
            ACT(v8(beta), GAv[:, :, 0:8], AF.Sigmoid, [GA.r()], [beta.r()])
            TS("dve", v8(nbeta), v8(beta), -1.0, None, ALU.mult, None, [beta.r()], [nbeta.r()])
            TT("dve", v8(t1), GAv[:, :, 8:16], dtb.f32(8).unsqueeze(1).to_broadcast([128, NG, 8]), ALU.add, [GA.r(), dtb.r()], [t1.r()])
            TS("dve", v8(t2), v8(t1), -1.0, None, ALU.mult, None, [t1.r()], [t2.r()])
            TT("dve", v8(t2), v8(t2), v8(t1), ALU.max, [t1.r(), t2.r()], [t2.r()])
            ACT(v8(t2), v8(t2), AF.Exp, [t2.r()], [t2.r()], scale=-1.0)
            ACT(v8(t2), v8(t2), AF.Ln, [t2.r(), r_c], [t2.r()], bias=eps_tile(1.0))
            TS("dve", v8(t1), v8(t1), 0.0, None, ALU.max, None, [t1.r()], [t1.r()])
            TT("dve", v8(t1), v8(t1), v8(t2), ALU.add, [t1.r(), t2.r()], [t1.r()])
            TT("dve", v8(gg), v8(t1), nea.f32(8).unsqueeze(1).to_broadcast([128, NG, 8]), ALU.mult, [t1.r(), nea.r()], [gg.r()])
            MM(banks[0][:, 0:NB * 8], Um.f32(128), gg.f32(NB * 8), [Um.r(), gg.r()], [rbank[0]])
            MM(banks[0][:, 256:256 + NB * 8], onesf[:], gg.f32(NB * 8), [r_c, gg.r()], [rbank[0]])
            CP("dve", gcol.f32(NB * 8), banks[0][:, 0:NB * 8], [], [gcol.r(), rbank[0]])
            CP("dve", gcol.f32(NG, 8)[:, NB, :], gg.f32(NG, 8)[:, NB, :], [gg.r()], [gcol.r()])
            ACT(eg.f32(NG * 8), gcol.f32(NG * 8), AF.Exp, [gcol.r()], [eg.r()])
            TT("dve", beg.f32(NG * 8), beta.f32(NG * 8), eg.f32(NG * 8), ALU.mult, [beta.r(), eg.r()], [beg.r()])
            ACT(gl.f32(NB * 8), banks[0][:, 256:256 + NB * 8], AF.Exp, [], [gl.r(), rbank[0]])
            TT("dve", egl.f32(NB * 8), banks[0][:, 256:256 + NB * 8], gcol.f32(NB * 8), ALU.subtract, [gcol.r()], [egl.r(), rbank[0]])
            ACT(egl.f32(NB * 8), egl.f32(NB * 8), AF.Exp, [egl.r()], [egl.r()])
            projsT = al(4 * 8 * NS)
            pjT = projsT.f32(4, 8, NS)
            lastU = al(24 * 3)
            luv = lastU.f32(24, 3)
            base_bufs = list(bufs)
            mH = A.mark()
            del bufs[:]
            Ub = al(T + 8)
            accq = al(T); vf = al(T)
            knf = vf
            qnb = al(T, bf=True); knb = al(T, bf=True)
            szf = accq
            sqb_ap = Ub.ap[:, 8:8 + T // 2].bitcast(BF16)
            ogb_ap = Ub.ap[:, 8 + T // 2:8 + T].bitcast(BF16)
            rsq = [al(TTS)]
            Sf = al(128); Sb = al(128, bf=True)
            wo = al(1024, bf=True)
            Ubv = Ub.ap[:, 5:8 + T]
            MS("pool", Ubv[:, 0:3], 0.0, [Ub.r()])

            class Ch:
                pass
            chains = []
            for ci in range(G):
                c = Ch()
                c.Ug = al(128); c.e1 = al(128); c.e2 = al(128); c.egrow = al(128)
                c.attnT = al(128, bf=True); c.qgT = al(128, bf=True)
                c.Xf = al(128); c.XTf = al(128); c.X8 = al(128); c.Z8 = al(128)
                c.Y1 = c.Ug; c.Z1 = c.e1; c.Y2 = c.e2; c.E0 = al(128); c.E1 = c.egrow
                c.Xo = al(4 * 128, bf=True); c.Zo = al(128, bf=True)
                c.Xb = al(128, bf=True); c.XTb = al(128, bf=True)
                c.Db = [al(128, bf=True), al(128, bf=True)]; c.Eb = [al(128, bf=True), al(128, bf=True)]
                c.M1 = al(128, bf=True); c.M1p = al(128, bf=True)
                c.PTb = c.Eb[0]
                c.vb = Sub(c.Xo, 0, 64); c.kbg = Sub(c.Xo, 64, 128); c.kg = Sub(c.Xo, 128, 192)
                c.u = c.Xf; c.wkT = Sub(c.Xo, 192, 256); c.vnew = Sub(c.Zo, 0, 64); c.on = c.XTf; c.ssq = al(8)
                c.osb = c.X8
                c.bA = ci
                c.bN = ci
                chains.append(c)
            B = lambda b_: b_.bf(128)
            F = lambda b_: b_.f32(128)

            for h in range(8):
                slots = []
                for typ in range(4):
                    if typ % 2 == 0:
                        wb_, wk_ = wslot()
                        wv_ = wb_.bf(2, 8, 128)
                    col = typ * 1024 + h * 128
                    P.dma("pool", wk_, wv_[:, typ % 2], gdn_w_in[:, col:col + 128].rearrange("(k p) n -> p k n", p=128), writes=[wb_.r()])
                    slots.append((wb_, wv_[:, typ % 2]))
                    if typ % 2 == 1:
                        P.commit(wk_)
                if True:
                    P.dma("pool", "gwo", wo.bf(1024), gdn_w_out[h * 128:(h + 1) * 128, :], writes=[wo.r()])
                cntb = [0]

                def project(typ, sink):
                    wb_, wv_ = slots[typ]
                    for ti, (c0, n) in enumerate(ptiles):
                        bk = 4 + cntb[0] % 2
                        cntb[0] += 1
                        for k in range(8):
                            MM(banks[bk][:, 0:n], wv_[:, k, :], xb[:, k, c0:c0 + n], [wb_.r(), r_xb[k][ti]], [rbank[bk]], start=(k == 0), stop=(k == 7))
                        sink(bk, c0, n)
                    bk = 4 + cntb[0] % 2
                    cntb[0] += 1
                    for k in range(8):
                        MM(banks[bk][:, 0:NS], wv_[:, k, :], xb[:, k, T:NT], [wb_.r(), r_xb[k][NTI - 1]], [rbank[bk]], start=(k == 0), stop=(k == 7))
                    CP("dve", pjT[:, typ, h, :], banks[bk][:, 0:NS], [], [projsT.r(), rbank[bk]])

                for typ in range(3):
                    ch = typ * 8 + h
                    project(typ, lambda bk, c0, n: CP("act", Ubv[:, 3 + c0:3 + c0 + n], banks[bk][:, 0:n], [], [Ub.r(), rbank[bk]]))
                    CP("act", luv[:, ch, :], Ubv[:, T:T + 3], [Ub.r()], [lastU.r()])
                    acc = [accq, knf, vf][typ]
                    ce = "dve"
                    cw = lambda j_: prm[:, 16 + j_ * 24 + ch:16 + j_ * 24 + ch + 1]
                    TS(ce, acc.f32(T), Ubv[:, 3:3 + T], cw(3), None, ALU.mult, None, [Ub.r(), r_c], [acc.r()])
                    for j_ in range(3):
                        STT(ce, acc.f32(T), Ubv[:, j_:j_ + T], cw(j_), acc.f32(T), ALU.mult, ALU.add, [Ub.r(), r_c, acc.r()], [acc.r()])
                    ACT(acc.f32(T), acc.f32(T), AF.Silu, [acc.r()], [acc.r()])
                    if typ < 2:
                        ACT(sqb_ap, acc.f32(T), AF.Square, [acc.r()], [Ub.r()])
                        for ti, (c0, n) in enumerate(ptiles):
                            bk = 4 + cntb[0] % 2
                            cntb[0] += 1
                            rs_ = rsq[0]
                            MM(banks[bk][:, 0:n], onesb[:], sqb_ap[:, c0:c0 + n], [Ub.r(), r_c], [rbank[bk]])
                            CP("dve", rs_.f32(n), banks[bk][:, 0:n], [], [rs_.r(), rbank[bk]])
                            rsqrt_eps(rs_.f32(n), rs_.f32(n), RMS_EPS, rs_.r())
                            if typ == 0:
                                STT("dve", qnb.bf(T)[:, c0:c0 + n], acc.f32(T)[:, c0:c0 + n], 128.0 ** -0.5, rs_.f32(n), ALU.mult, ALU.mult,
                                    [acc.r(), rs_.r()], [qnb.r()])
                            else:
                                TT("dve", acc.f32(T)[:, c0:c0 + n], acc.f32(T)[:, c0:c0 + n], rs_.f32(n), ALU.mult, [acc.r(), rs_.r()], [acc.r()])
                        if typ == 1:
                            CP("pool", knb.bf(T), knf.f32(T), [knf.r()], [knb.r()])
                project(3, lambda bk, c0, n: ACT(szf.f32(T)[:, c0:c0 + n], banks[bk][:, 0:n], AF.Silu, [], [szf.r(), rbank[bk]]))
                MS("dve", F(Sf), 0.0, [Sf.r()])
                MS("dve", B(Sb), 0.0, [Sb.r()])

                def st_a(c, b):
                    bs = slice(b * 128, (b + 1) * 128)
                    ACT(F(c.Ug), F(Um), AF.Identity, [Um.r(), gg.r()], [c.Ug.r()], scale=gg.f32(NG, 8)[:, b, h:h + 1])
                    yield
                    bk = banks[c.bA]
                    MM(bk[:, 0:128], knb.bf(T)[:, bs], knb.bf(T)[:, bs], [knb.r()], [rbank[c.bA]])
                    MM(bk[:, 128:256], knb.bf(T)[:, bs], qnb.bf(T)[:, bs], [knb.r(), qnb.r()], [rbank[c.bA]])
                    MM(bk[:, 256:384], onesf[:], F(c.Ug), [r_c, c.Ug.r()], [rbank[c.bA]])
                    yield
                    gc_ = gcol.f32(NG, 8)[:, b, h:h + 1]
                    STT("dve", F(c.e1), bk[:, 256:384], gc_, F(maskS), ALU.subtract, ALU.max, [gcol.r(), maskS.r()], [c.e1.r(), rbank[c.bA]])
                    STT("dve", F(c.e2), bk[:, 256:384], gc_, F(negT), ALU.subtract, ALU.min, [gcol.r(), negT.r()], [c.e2.r(), rbank[c.bA]])
                    ACT(F(c.egrow), bk[:, 256:384], AF.Exp, [], [c.egrow.r(), rbank[c.bA]])
                    yield
                    ACT(F(c.e1), F(c.e1), AF.Exp, [c.e1.r()], [c.e1.r()], scale=-1.0)
                    ACT(F(c.e2), F(c.e2), AF.Exp, [c.e2.r()], [c.e2.r()])
                    yield
                    STT("dve", F(c.Xf), bk[:, 0:128], nbeta.f32(NG, 8)[:, b, h:h + 1], F(c.e1), ALU.mult, ALU.mult, [nbeta.r(), c.e1.r()], [c.Xf.r(), rbank[c.bA]])
                    TT("dve", B(c.attnT), bk[:, 128:256], F(c.e2), ALU.mult, [c.e2.r()], [c.attnT.r(), rbank[c.bA]])
                    TT("dve", B(c.qgT), qnb.bf(T)[:, bs], F(c.egrow), ALU.mult, [qnb.r(), c.egrow.r()], [c.qgT.r()])

                def st_b(c, b):
                    bk = banks[c.bN]
                    TR(bk[:, 0:128], F(c.Xf), ident[:], [c.Xf.r(), r_c], [rbank[c.bN]])
                    yield
                    CP("act", F(c.XTf), bk[:, 0:128], [], [c.XTf.r(), rbank[c.bN]])
                    CP("act", B(c.XTb), bk[:, 0:128], [], [c.XTb.r(), rbank[c.bN]])
                    CP("act", B(c.Xb), F(c.Xf), [c.Xf.r()], [c.Xb.r()])
                    yield
                    TT("dve", F(c.X8), F(c.Xf), bmv[:, 0, :], ALU.mult, [c.Xf.r(), bmask.r()], [c.X8.r()])
                    TT("dve", F(c.Z8), F(c.XTf), bmv[:, 0, :], ALU.mult, [c.XTf.r(), bmask.r()], [c.Z8.r()])
                    yield
                    TT("dve", F(c.E0), F(c.Z8), ident[:], ALU.add, [c.Z8.r(), r_c], [c.E0.r()])

                def st_base1(c, b):
                    bk = banks[c.bN]
                    MM(bk[:, 0:128], F(c.Z8), F(c.X8), [c.Z8.r(), c.X8.r()], [rbank[c.bN]])
                    MM(bk[:, 128:256], F(c.X8), F(c.Z8), [c.Z8.r(), c.X8.r()], [rbank[c.bN]])
                    yield
                    CP("act", F(c.Y1), bk[:, 0:128], [], [c.Y1.r(), rbank[c.bN]])
                    CP("act", F(c.Z1), bk[:, 128:256], [], [c.Z1.r(), rbank[c.bN]])
                    yield
                    MM(bk[:, 256:384], F(c.Y1), F(c.E0), [c.Y1.r(), c.E0.r()], [rbank[c.bN]])
                    yield
                    TT("dve", F(c.E1), F(c.E0), bk[:, 256:384], ALU.add, [c.E0.r()], [c.E1.r(), rbank[c.bN]])

                def st_base2(c, b):
                    bk = banks[c.bN]
                    MM(bk[:, 0:128], F(c.Z1), F(c.Y1), [c.Z1.r(), c.Y1.r()], [rbank[c.bN]])
                    yield
                    CP("act", F(c.Y2), bk[:, 0:128], [], [c.Y2.r(), rbank[c.bN]])
                    yield
                    MM(bk[:, 128:256], F(c.Y2), F(c.E1), [c.Y2.r(), c.E1.r()], [rbank[c.bN]])
                    yield
                    TT("dve", F(c.E0), F(c.E1), bk[:, 128:256], ALU.add, [c.E1.r()], [c.E0.r(), rbank[c.bN]])
                    yield
                    TR(bk[:, 256:384], F(c.E0), ident[:], [c.E0.r(), r_c], [rbank[c.bN]])
                    yield
                    CP("act", B(c.Db[0]), bk[:, 256:384], [], [c.Db[0].r(), rbank[c.bN]])
                    CP("act", B(c.Eb[0]), F(c.E0), [c.E0.r()], [c.Eb[0].r()])

                def st_merge(l):
                    def f(c, b):
                        bk = banks[c.bN]
                        Dp, Ep = c.Db[l % 2], c.Eb[l % 2]
                        Dn, En = c.Db[(l + 1) % 2], c.Eb[(l + 1) % 2]
                        mk = bmv[:, 1 + l, :]
                        if l < 3:
                            MM(bk[:, 0:128], B(c.XTb), B(Dp), [c.XTb.r(), Dp.r()], [rbank[c.bN]])
                        MM(bk[:, 128:256], B(c.Xb), B(Ep), [c.Xb.r(), Ep.r()], [rbank[c.bN]])
                        yield
                        if l < 3:
                            TT("dve", B(c.M1), bk[:, 0:128], mk, ALU.mult, [bmask.r()], [c.M1.r(), rbank[c.bN]])
                        TT("dve", B(c.M1p), bk[:, 128:256], mk, ALU.mult, [bmask.r()], [c.M1p.r(), rbank[c.bN]])
                        yield
                        if l < 3:
                            MM(bk[:, 256:384], identb[:], B(Dp), [r_c, Dp.r()], [rbank[c.bN]], start=True, stop=False)
                            MM(bk[:, 256:384], B(Ep), B(c.M1), [Ep.r(), c.M1.r()], [rbank[c.bN]], start=False, stop=True)
                        MM(bk[:, 384:512], identb[:], B(Ep), [r_c, Ep.r()], [rbank[c.bN]], start=True, stop=False)
                        MM(bk[:, 384:512], B(Dp), B(c.M1p), [Dp.r(), c.M1p.r()], [rbank[c.bN]], start=False, stop=True)
                        yield
                        if l < 3:
                            CP("act", B(Dn), bk[:, 256:384], [], [Dn.r(), rbank[c.bN]])
                        CP("act", B(En), bk[:, 384:512], [], [En.r(), rbank[c.bN]])
                    return f

                def st_c(c, b):
                    bs = slice(b * 128, (b + 1) * 128)
                    bk = banks[c.bA]
                    TR(bfps(c.bA, 0, 128), knb.bf(T)[:, bs], identb[:], [knb.r(), r_c], [rbank[c.bA]])
                    TR(bk[:, 128:256], vf.f32(T)[:, bs], ident[:], [vf.r(), r_c], [rbank[c.bA]])
                    yield
                    ACT(B(c.vb), bk[:, 128:256], AF.Identity, [beta.r()], [c.vb.r(), rbank[c.bA]], scale=beta.f32(NG, 8)[:, b, h:h + 1])
                    ACT(B(c.kbg), bfps(c.bA, 0, 128), AF.Identity, [beg.r()], [c.kbg.r(), rbank[c.bA]], scale=beg.f32(NG, 8)[:, b, h:h + 1])
                    ACT(B(c.kg), bfps(c.bA, 0, 128), AF.Identity, [egl.r()], [c.kg.r(), rbank[c.bA]], scale=egl.f32(NG, 8)[:, b, h:h + 1])
                    yield
                    MM(bk[:, 256:384], B(c.PTb), B(c.vb), [c.PTb.r(), c.vb.r()], [rbank[c.bA]])
                    MM(bk[:, 384:512], B(c.kbg), B(c.PTb), [c.PTb.r(), c.kbg.r()], [rbank[c.bA]])
                    yield
                    CP("act", F(c.u), bk[:, 256:384], [], [c.u.r(), rbank[c.bA]])
                    CP("dve", B(c.wkT), bk[:, 384:512], [], [c.wkT.r(), rbank[c.bA]])

                def recur(c, b):
                    b6, b7 = banks[6], banks[7]
                    MM(b6[:, 0:128], B(c.wkT), B(Sb), [c.wkT.r(), Sb.r()], [rbank[6]])
                    TT("dve", B(c.vnew), F(c.u), b6[:, 0:128], ALU.subtract, [c.u.r()], [c.vnew.r(), rbank[6]])
                    MM(b7[:, 0:128], B(c.qgT), B(Sb), [c.qgT.r(), Sb.r()], [rbank[7]], start=True, stop=False)
                    MM(b7[:, 0:128], B(c.attnT), B(c.vnew), [c.attnT.r(), c.vnew.r()], [rbank[7]], start=False, stop=True)
                    MM(b6[:, 128:256], B(c.kg), B(c.vnew), [c.kg.r(), c.vnew.r()], [rbank[6]])
                    STT("dve", F(Sf), F(Sf), gl.f32(NB, 8)[:, b, h:h + 1], b6[:, 128:256], ALU.mult, ALU.add, [gl.r()], [Sf.r(), rbank[6]])
                    CP("act", B(Sb), F(Sf), [Sf.r()], [Sb.r()])
                    CP("act", F(c.osb), b7[:, 0:128], [], [c.osb.r(), rbank[7]])

                def recur_post(c, b):
                    bs = slice(b * 128, (b + 1) * 128)
                    ACT(F(c.on), F(c.osb), AF.Square, [c.osb.r()], [c.on.r(), c.ssq.r()], accum=c.ssq.f32(1))
                    rsqrt_eps(c.ssq.f32(1), c.ssq.f32(1), RMS_EPS * 128.0, c.ssq.r())
                    STT("dve", F(c.on), F(c.osb), c.ssq.f32(1), F(gnb), ALU.mult, ALU.mult, [c.osb.r(), c.ssq.r(), gnb.r()], [c.on.r()])
                    TR(banks[c.bN][:, 384:512], F(c.on), ident[:], [c.on.r(), r_c], [rbank[c.bN]])
                    TT("dve", ogb_ap[:, bs], banks[c.bN][:, 384:512], szf.f32(T)[:, bs], ALU.mult, [szf.r()], [Ub.r(), rbank[c.bN]])

                stages = [st_a, st_b, st_base1, st_base2] + [st_merge(l) for l in range(4)] + [st_c]
                pend_ = []
                for g0 in range(0, NB if "gdn_noblk" not in DBG else 0, G):
                    grp = [(chains[i], g0 + i) for i in range(min(G, NB - g0))]
                    for stg_ in stages:
                        gens_ = [stg_(c, b) for c, b in grp]
                        while gens_:
                            nx_ = []
                            for gn_ in gens_:
                                try:
                                    next(gn_)
                                    nx_.append(gn_)
                                except StopIteration:
                                    pass
                            gens_ = nx_
                    for c, b in grp:
                        recur(c, b)
                        if pend_:
                            recur_post(*pend_.pop())
                        pend_.append((c, b))
                    if pend_:
                        recur_post(*pend_.pop())
                if pend_:
                    recur_post(*pend_.pop())
                for dc in range(8):
                    for ti, (c0, n) in enumerate(ptiles):
                        bk = 4 + cntb[0] % 2
                        cntb[0] += 1
                        MM(banks[bk][:, 0:n], wo.bf(1024)[:, dc * 128:(dc + 1) * 128], ogb_ap[:, c0:c0 + n], [wo.r(), Ub.r()], [rbank[bk]])
                        xv = xf[:, dc, c0:c0 + n]
                        if h == 0:
                            STT("dve", xv, xv, ALPHA, banks[bk][:, 0:n], ALU.mult, ALU.add, [r_xf[dc][ti]], [r_xf[dc][ti], rbank[bk]])
                        else:
                            TT("dve", xv, xv, banks[bk][:, 0:n], ALU.add, [r_xf[dc][ti]], [r_xf[dc][ti], rbank[bk]])
                P.dma("sp", "gsp", o_gdn_p[h], F(Sf), reads=[Sf.r()])
            A.release(mH, list(bufs))
            del bufs[:]
            cst_ = A.f32(3072)
            for q4 in range(6):
                for i4 in range(4):
                    ch = q4 * 4 + i4
                    TR(banks[4][0:3, i4 * 128:(i4 + 1) * 128], luv[:, ch, :], ident[:], [lastU.r(), r_c], [rbank[4]])
                CP("dve", cst_.f32(3072)[0:3, q4 * 512:(q4 + 1) * 512], banks[4][0:3, :], [], [cst_.r(), rbank[4]])
            P.dma("sp", "gcp", o_conv_p, cst_.f32(3072)[0:3, :], reads=[cst_.r()])
            A.release(mH, [cst_])
            mS = A.mark()
            if "gdn_nosamp" in DBG:
                A.release(m, base_bufs)
                return
            sb_ = []

            def als(n, bf=False):
                b = A.bf(n) if bf else A.f32(n)
                sb_.append(b)
                return b
            projs = als(4096)
            for typ in range(4):
                for h2 in range(2):
                    bk = (typ * 2 + h2) % 2
                    for i4 in range(4):
                        hh = h2 * 4 + i4
                        TR(banks[bk][0:NS, i4 * 128:(i4 + 1) * 128], pjT[:, typ, hh, :], ident[:], [projsT.r(), r_c], [rbank[bk]])
                    CP("dve", projs.f32(4096)[0:NS, typ * 1024 + h2 * 512:typ * 1024 + (h2 + 1) * 512], banks[bk][0:NS, :], [], [projs.r(), rbank[bk]])
            P.dma("sp", "gcv", o_conv_s[:, 0:2, :], sconv[:, 1:3, :])
            P.dma("sp", "gcv", o_conv_s[:, 2, :], projs.f32(4096)[0:NS, 0:3072], reads=[projs.r()])
            P.commit("gcv")
            qkv = als(3072)
            zs = als(1024)
            mC = A.mark()
            PW = 256
            ext = A.f32(4 * PW); cwb = A.f32(4 * PW); prod = A.f32(4 * PW)
            for pc in range(3072 // PW):
                cs_ = slice(pc * PW, (pc + 1) * PW)
                P.dma("sp", "gsl", ext.f32(4, PW)[0:NS, 0:3, :], sconv[:, :, cs_], writes=[ext.r()])
                P.dma("sp", "gsl", cwb.f32(4, PW)[0:NS], gdn_conv_w[:, cs_].partition_broadcast(NS), writes=[cwb.r()])
                P.commit("gsl")
                CP("dve", ext.f32(4, PW)[0:NS, 3, :], projs.f32(4096)[0:NS, cs_], [projs.r()], [ext.r()])
                TT("dve", prod.f32(4, PW)[0:NS], ext.f32(4, PW)[0:NS], cwb.f32(4, PW)[0:NS], ALU.mult, [ext.r(), cwb.r()], [prod.r()])
                RED("dve", qkv.f32(3072)[0:NS, cs_], prod.f32(4, PW)[0:NS].rearrange("p j c -> p c j"), [prod.r()], [qkv.r()])
            A.release(mC, [ext, cwb, prod])
            ACT(qkv.f32(3072)[0:NS], qkv.f32(3072)[0:NS], AF.Silu, [qkv.r()], [qkv.r()])
            ACT(zs.f32(1024)[0:NS], projs.f32(4096)[0:NS, 3072:4096], AF.Silu, [projs.r()], [zs.r()])
            q3 = qkv.f32(3, 8, 128)[0:NS, 0]
            k3 = qkv.f32(3, 8, 128)[0:NS, 1]
            v3 = qkv.f32(3, 8, 128)[0:NS, 2]
            tmp = als(1024); ss = als(16)
            t3 = tmp.f32(8, 128)[0:NS]
            for (x3, scl) in ((q3, 128.0 ** -0.5), (k3, 1.0)):
                TT("dve", t3, x3, x3, ALU.mult, [qkv.r()], [tmp.r()])
                RED("dve", ss.f32(8)[0:NS], t3, [tmp.r()], [ss.r()])
                rsqrt_eps(ss.f32(8)[0:NS], ss.f32(8)[0:NS], RMS_EPS, ss.r())
                STT("dve", x3, x3, scl, ss.f32(8)[0:NS].unsqueeze(2).to_broadcast([NS, 8, 128]), ALU.mult, ALU.mult, [ss.r()], [qkv.r()])
            qk = als(8)
            TT("dve", t3, q3, k3, ALU.mult, [qkv.r()], [tmp.r()])
            RED("dve", qk.f32(8)[0:NS], t3, [tmp.r()], [qk.r()])
            kTs = als(8 * NS); qTs = als(8 * NS)
            for (x3, dst, bk) in ((k3, kTs, 4), (q3, qTs, 5)):
                for hh in range(8):
                    TR(banks[bk][:, hh * NS:(hh + 1) * NS], x3[:, hh, :], ident[0:NS, 0:NS], [qkv.r(), r_c], [rbank[bk]])
                CP("dve", dst.f32(8 * NS), banks[bk][:, 0:8 * NS], [], [dst.r(), rbank[bk]])
            dlt = als(NS * NS)
            P.dma("sp", "gsm", dlt.f32(NS, NS), cd["delta"], writes=[dlt.r()])
            dcol = als(NS)
            P.dma("sp", "gsm", dcol.f32(NS), cd["dcol"], writes=[dcol.r()])
            P.commit("gsm")
            KmL = [als(NS * NS), als(NS * NS)]; QmL = [als(NS * NS), als(NS * NS)]
            Sp = [als(128) for _ in range(4)]
            ci_ = 0
            for hh in range(8):
                Km = KmL[hh % 2]; Qm = QmL[hh % 2]
                for (src, dst) in ((kTs, Km), (qTs, Qm)):
                    TT("dve", dst.f32(NS, NS), src.f32(8, NS)[:, hh, :].unsqueeze(1).to_broadcast([128, NS, NS]),
                       dlt.f32(NS, NS), ALU.mult, [src.r(), dlt.r()], [dst.r()])
                for s in range(NS):
                    sp_ = Sp[ci_ % 4]
                    P.dma("sp", "gsq%d" % (ci_ % 4), sp_.f32(128), sgdn[s, hh], writes=[sp_.r()])
                    ci_ += 1
                    MM(banks[hh // 4][0:NS, (hh % 4) * 128:(hh % 4 + 1) * 128], Km.f32(NS, NS)[:, s, :], sp_.f32(128),
                       [Km.r(), sp_.r()], [rbank[hh // 4]], start=(s == 0), stop=(s == NS - 1))
                    MM(banks[2 + hh // 4][0:NS, (hh % 4) * 128:(hh % 4 + 1) * 128], Qm.f32(NS, NS)[:, s, :], sp_.f32(128),
                       [Qm.r(), sp_.r()], [rbank[2 + hh // 4]], start=(s == 0), stop=(s == NS - 1))
            KS = als(1024); QS = als(1024)
            for half in range(2):
                CP("dve", KS.f32(1024)[0:NS, half * 512:(half + 1) * 512], banks[half][0:NS, :], [], [KS.r(), rbank[half]])
                CP("dve", QS.f32(1024)[0:NS, half * 512:(half + 1) * 512], banks[2 + half][0:NS, :], [], [QS.r(), rbank[2 + half]])
            bc8 = lambda b_, col: b_.f32(NG, 8)[0:NS, col, :].unsqueeze(2).to_broadcast([NS, 8, 128])
            KS3 = KS.f32(8, 128)[0:NS]; QS3 = QS.f32(8, 128)[0:NS]
            vn = als(1024)
            vn3 = vn.f32(8, 128)[0:NS]
            TT("dve", KS3, KS3, bc8(eg, NB), ALU.mult, [eg.r()], [KS.r()])
            TT("dve", vn3, v3, KS3, ALU.subtract, [qkv.r(), KS.r()], [vn.r()])
            TT("dve", vn3, vn3, bc8(beta, NB), ALU.mult, [beta.r()], [vn.r()])
            TT("dve", QS3, QS3, bc8(eg, NB), ALU.mult, [eg.r()], [QS.r()])
            TT("dve", t3, vn3, qk.f32(8)[0:NS].unsqueeze(2).to_broadcast([NS, 8, 128]), ALU.mult, [vn.r(), qk.r()], [tmp.r()])
            TT("dve", QS3, QS3, t3, ALU.add, [tmp.r()], [QS.r()])
            TT("dve", t3, QS3, QS3, ALU.mult, [QS.r()], [tmp.r()])
            RED("dve", ss.f32(8)[0:NS], t3, [tmp.r()], [ss.r()])
            rsqrt_eps(ss.f32(8)[0:NS], ss.f32(8)[0:NS], RMS_EPS, ss.r(), scale=1.0 / 128.0)
            TT("dve", QS3, QS3, ss.f32(8)[0:NS].unsqueeze(2).to_broadcast([NS, 8, 128]), ALU.mult, [ss.r()], [QS.r()])
            TT("dve", QS3, QS3, gn.f32(128)[0:NS].unsqueeze(1).to_broadcast([NS, 8, 128]), ALU.mult, [gn.r()], [QS.r()])
            TT("dve", QS3, QS3, zs.f32(8, 128)[0:NS], ALU.mult, [zs.r()], [QS.r()])
            ogsT = als(8 * NS, bf=True)
            for hh in range(8):
                TR(banks[4][:, hh * NS:(hh + 1) * NS], QS3[:, hh, :], ident[0:NS, 0:NS], [QS.r(), r_c], [rbank[4]])
            CP("dve", ogsT.bf(8 * NS), banks[4][:, 0:8 * NS], [], [ogsT.r(), rbank[4]])
            out_proj_samples(gdn_w_out, 8, ogsT.bf(8, NS), ogsT.r())
            Rm = als(NS * 8)
            TT("dve", Rm.f32(NS, 8)[0:NS], dcol.f32(NS)[0:NS].unsqueeze(2).to_broadcast([NS, NS, 8]),
               eg.f32(NG, 8)[0:NS, NB, :].unsqueeze(1).to_broadcast([NS, NS, 8]), ALU.mult, [dcol.r(), eg.r()], [Rm.r()])
            EGb = als(NS * 8)
            MM(banks[4][:, 0:NS * 8], onesf[0:NS, :], Rm.f32(NS * 8)[0:NS], [Rm.r(), r_c], [rbank[4]])
            CP("dve", EGb.f32(NS * 8), banks[4][:, 0:NS * 8], [], [EGb.r(), rbank[4]])
            Vm = [als(1024), als(1024)]
            Sn = [als(128) for _ in range(4)]
            ci_ = 0
            for s in range(NS):
                vm_ = Vm[s % 2]
                TS("dve", vm_.f32(1024)[0:NS], vn.f32(1024)[0:NS], dcol.f32(NS)[0:NS, s:s + 1], None, ALU.mult, None, [vn.r(), dcol.r()], [vm_.r()])
                for hh in range(8):
                    sp_ = Sp[ci_ % 4]; sn_ = Sn[ci_ % 4]
                    bk = 4 + ci_ % 4
                    P.dma("sp", "gsq%d" % (ci_ % 4), sp_.f32(128), sgdn[s, hh], writes=[sp_.r()])
                    MM(banks[bk][:, 0:128], k3[:, hh, :], vm_.f32(8, 128)[0:NS, hh, :], [qkv.r(), vm_.r()], [rbank[bk]])
                    STT("dve", sn_.f32(128), sp_.f32(128), EGb.f32(NS, 8)[:, s, hh:hh + 1], banks[bk][:, 0:128], ALU.mult, ALU.add,
                        [sp_.r(), EGb.r()], [sn_.r(), rbank[bk]])
                    P.dma("pool", "gst%d" % (ci_ % 4), o_gdn_s[s, hh], sn_.f32(128), reads=[sn_.r()])
                    ci_ += 1
            A.release(mS, sb_)
            A.release(m, base_bufs)

        def ret_mixer():
            G = 2
            m = A.mark()
            bufs = []

            def al(n, bf=False):
                b = A.bf(n) if bf else A.f32(n)
                bufs.append(b)
                return b
            cos2 = al(NT); sin2 = al(NT); decT = al(128); xi = al(128); zeta = al(8); dcol = al(NS); dlt = al(NS * NS)
            P.dma("sp", "rc", cos2.f32(NT), cd["cos2"], writes=[cos2.r()])
            P.dma("sp", "rc", sin2.f32(NT), cd["sin2"], writes=[sin2.r()])
            P.dma("sp", "rc", zeta.f32(8), cd["zeta"], writes=[zeta.r()])
            P.dma("sp", "rc", dcol.f32(NS), cd["dcol"], writes=[dcol.r()])
            P.dma("sp", "rc", dlt.f32(NS, NS), cd["delta"], writes=[dlt.r()])
            P.commit("rc")
            SC = 128.0 ** -0.5
            qrb = al(NT, bf=True); krb = al(NT, bf=True); krf = al(NT); qsf = al(NS)
            t1 = [al(TTS), al(TTS)]; t2 = [al(TTS), al(TTS)]
            vtok = al(NB * 256, bf=True)
            vtv = vtok.bf(NB, 256)
            ogT = al(2 * NT, bf=True)
            ogv = ogT.bf(2, NT)
            Sf = al(256); Sb = al(256, bf=True)
            v_s = al(256); sg_s = al(256); kts = al(128); Qm = al(NS * NS); qk = al(8); tmp_s = al(256); o_s = al(256)
            Sp = [al(256) for _ in range(3)]
            Vm = [al(256) for _ in range(3)]; Sn = [al(256) for _ in range(3)]
            stat = [al(8) for _ in range(3)]

            class Ch:
                pass
            chains = []
            for ci in range(G):
                c = Ch()
                c.attnT = al(128, bf=True); c.qxT = al(128, bf=True); c.kz = al(128, bf=True)
                c.sg = al(256); c.on = al(256); c.osb = al(256)
                c.bA = ci % 2
                chains.append(c)
            B = lambda b_: b_.bf(128)

            def gnorm_gate(o_ap, np_, on_ap, on_res, sg_ap, sg_res, o_reads, o_writes):
                sm, sq_, mm = stat[0], stat[1], stat[2]
                ACT(on_ap, o_ap, AF.Identity, o_reads, [on_res, sm.r()] + o_writes, accum=sm.f32(1)[0:np_])
                ACT(on_ap, o_ap, AF.Square, o_reads, [on_res, sq_.r()] + o_writes, accum=sq_.f32(1)[0:np_])
                TS("dve", sm.f32(1)[0:np_], sm.f32(1)[0:np_], 1.0 / 256.0, None, ALU.mult, None, [sm.r()], [sm.r()])
                TT("dve", mm.f32(1)[0:np_], sm.f32(1)[0:np_], sm.f32(1)[0:np_], ALU.mult, [sm.r()], [mm.r()])
                STT("dve", sq_.f32(1)[0:np_], sq_.f32(1)[0:np_], 1.0 / 256.0, mm.f32(1)[0:np_], ALU.mult, ALU.subtract, [sq_.r(), mm.r()], [sq_.r()])
                rsqrt_eps(sq_.f32(1)[0:np_], sq_.f32(1)[0:np_], LN_EPS, sq_.r())
                TS("dve", on_ap, o_ap, sm.f32(1)[0:np_], sq_.f32(1)[0:np_], ALU.subtract, ALU.mult, o_reads + [sm.r(), sq_.r()], [on_res] + o_writes)
                TT("pool", on_ap, on_ap, sg_ap, ALU.mult, [on_res, sg_res], [on_res])

            for h in range(8):
                slots = []
                for xi_, base in enumerate((0, 1024)):
                    wb_, wk_ = wslot()
                    wv_ = wb_.bf(2, 8, 128)
                    c0_ = base + h * 128
                    r3 = lambda a, b_: ret_w_in[:, a:b_].rearrange("(k p) n -> p k n", p=128)
                    P.dma("pool", wk_, wv_[:, 0], r3(c0_, c0_ + 128), writes=[wb_.r()])
                    P.dma("pool", wk_, wv_[:, 1, :, 0:64], r3(c0_ + 64, c0_ + 128), writes=[wb_.r()])
                    P.dma("pool", wk_, wv_[:, 1, :, 64:128], r3(c0_, c0_ + 64), writes=[wb_.r()])
                    P.commit(wk_)
                    slots.append((wb_, wv_))
                P.dma("sp", "rdx", decT.f32(128), cd["decT"][:, h, :], writes=[decT.r()])
                P.dma("sp", "rdx", xi.f32(128), cd["xi"][:, h, :], writes=[xi.r()])
                P.commit("rdx")
                wv_t, wkv_ = wslot()
                P.dma("pool", wkv_, wv_t.bf(8, 256), ret_w_in[:, 2048 + h * 256:2048 + (h + 1) * 256].rearrange("(k p) n -> p k n", p=128), writes=[wv_t.r()])
                P.commit(wkv_)
                wg_t, wkg_ = wslot()
                P.dma("pool", wkg_, wg_t.bf(8, 256), ret_w_in[:, 4096 + h * 256:4096 + (h + 1) * 256].rearrange("(k p) n -> p k n", p=128), writes=[wg_t.r()])
                P.commit(wkg_)
                cn = 0
                for xi_ in range(2):
                    wb_, wv_ = slots[xi_]
                    for ti, (c0, n) in enumerate(tiles):
                        a1, a2 = t1[cn % 2], t2[cn % 2]
                        ba_, bb_ = 4 + 2 * (cn % 2), 5 + 2 * (cn % 2)
                        cn += 1
                        for k in range(8):
                            MM(banks[ba_][:, 0:n], wv_[:, 0, k, :], xb[:, k, c0:c0 + n], [wb_.r(), r_xb[k][ti]], [rbank[ba_]], start=(k == 0), stop=(k == 7))
                        for k in range(8):
                            MM(banks[bb_][:, 0:n], wv_[:, 1, k, :], xb[:, k, c0:c0 + n], [wb_.r(), r_xb[k][ti]], [rbank[bb_]], start=(k == 0), stop=(k == 7))
                        TT("dve", a1.f32(n), banks[ba_][:, 0:n], cos2.f32(NT)[:, c0:c0 + n], ALU.mult, [cos2.r()], [a1.r(), rbank[ba_]])
                        TT("dve", a2.f32(n), banks[bb_][:, 0:n], sin2.f32(NT)[:, c0:c0 + n], ALU.mult, [sin2.r()], [a2.r(), rbank[bb_]])
                        if xi_ == 0:
                            TT("pool", qrb.bf(NT)[:, c0:c0 + n], a1.f32(n), a2.f32(n), ALU.add, [a1.r(), a2.r()], [qrb.r()])
                            if ti == NTI - 1:
                                TT("pool", qsf.f32(NS), a1.f32(n), a2.f32(n), ALU.add, [a1.r(), a2.r()], [qsf.r()])
                        else:
                            TT("pool", krf.f32(NT)[:, c0:c0 + n], a1.f32(n), a2.f32(n), ALU.add, [a1.r(), a2.r()], [krf.r()])
                CP("pool", krb.bf(NT), krf.f32(NT), [krf.r()], [krb.r()])
                for b in range(NB):
                    bs = slice(b * 128, (b + 1) * 128)
                    bv_ = 6 + b % 2
                    for k in range(8):
                        MM(banks[bv_][:, 0:256], xb[:, k, bs], wv_t.bf(8, 256)[:, k, :], [wv_t.r()] + rx(r_xb, [k], tiles_of(b * 128, 128)), [rbank[bv_]],
                           start=(k == 0), stop=(k == 7))
                    CP("act", vtv[:, b, :], banks[bv_][:, 0:256], [], [vtok.r(), rbank[bv_]])
                for k in range(8):
                    MM(banks[6][0:NS, 256:512], xb[:, k, T:NT], wv_t.bf(8, 256)[:, k, :], [wv_t.r(), r_xb[k][NTI - 1]], [rbank[6]], start=(k == 0), stop=(k == 7))
                CP("act", v_s.f32(256)[0:NS], banks[6][0:NS, 256:512], [], [v_s.r(), rbank[6]])
                MS("dve", Sf.f32(256), 0.0, [Sf.r()])
                MS("dve", Sb.bf(256), 0.0, [Sb.r()])

                def st_a(c, b):
                    bs = slice(b * 128, (b + 1) * 128)
                    bk = banks[c.bA]
                    MM(bk[:, 0:128], krb.bf(NT)[:, bs], qrb.bf(NT)[:, bs], [krb.r(), qrb.r()], [rbank[c.bA]])
                    TR(bk[:, 128:256], krf.f32(NT)[:, bs], ident[:], [krf.r(), r_c], [rbank[c.bA]])
                    for k in range(8):
                        MM(bk[:, 256:512], xb[:, k, bs], wg_t.bf(8, 256)[:, k, :], [wg_t.r()] + rx(r_xb, [k], tiles_of(b * 128, 128)), [rbank[c.bA]],
                           start=(k == 0), stop=(k == 7))
                    yield
                    TT("dve", B(c.attnT), bk[:, 0:128], decT.f32(128), ALU.mult, [decT.r()], [c.attnT.r(), rbank[c.bA]])
                    ACT(B(c.kz), bk[:, 128:256], AF.Identity, [zeta.r()], [c.kz.r(), rbank[c.bA]], scale=zeta.f32(8)[:, h:h + 1])
                    ACT(c.sg.f32(256), bk[:, 256:512], AF.Silu, [], [c.sg.r(), rbank[c.bA]])
                    TT("pool", B(c.qxT), qrb.bf(NT)[:, bs], xi.f32(128), ALU.mult, [qrb.r(), xi.r()], [c.qxT.r()])

                def recur(c, b):
                    MM(banks[2][:, 0:256], B(c.qxT), Sb.bf(256), [c.qxT.r(), Sb.r()], [rbank[2]], start=True, stop=False)
                    MM(banks[2][:, 0:256], B(c.attnT), vtv[:, b, :], [c.attnT.r(), vtok.r()], [rbank[2]], start=False, stop=True)
                    MM(banks[3][:, 0:256], B(c.kz), vtv[:, b, :], [c.kz.r(), vtok.r()], [rbank[3]])
                    STT("dve", Sf.f32(256), Sf.f32(256), cst["gC"][h], banks[3][:, 0:256], ALU.mult, ALU.add, [], [Sf.r(), rbank[3]])
                    CP("act", Sb.bf(256), Sf.f32(256), [Sf.r()], [Sb.r()])
                    CP("act", c.osb.f32(256), banks[2][:, 0:256], [], [c.osb.r(), rbank[2]])

                def recur_post(c, b):
                    bs = slice(b * 128, (b + 1) * 128)
                    gnorm_gate(c.osb.f32(256), 128, c.on.f32(256), c.on.r(), c.sg.f32(256), c.sg.r(), [c.osb.r()], [])
                    for cc in range(2):
                        TR(banks[7][:, 256 + cc * 128:256 + (cc + 1) * 128], c.on.f32(256)[:, cc * 128:(cc + 1) * 128], ident[:], [c.on.r(), r_c], [rbank[7]])
                    CP("act", ogv[:, :, bs], banks[7][:, 256:512].rearrange("p (c t) -> p c t", c=2, t=128), [], [ogT.r(), rbank[7]])

                pend_ = []
                for g0 in range(0, NB if "ret_noblk" not in DBG else 0, G):
                    grp = [(chains[i], g0 + i) for i in range(min(G, NB - g0))]
                    gens_ = [st_a(c, b) for c, b in grp]
                    while gens_:
                        nx_ = []
                        for gn_ in gens_:
                            try:
                                next(gn_)
                                nx_.append(gn_)
                            except StopIteration:
                                pass
                        gens_ = nx_
                    for c, b in grp:
                        recur(c, b)
                        if pend_:
                            recur_post(*pend_.pop())
                        pend_.append((c, b))
                    if pend_:
                        recur_post(*pend_.pop())
                if pend_:
                    recur_post(*pend_.pop())
                P.dma("sp", "rsp", o_ret_p[h], Sf.f32(256), reads=[Sf.r()])
                for k in range(8 if "ret_nosamp" not in DBG else 0):
                    MM(banks[6][0:NS, 0:256], xb[:, k, T:NT], wg_t.bf(8, 256)[:, k, :], [wg_t.r(), r_xb[k][NTI - 1]], [rbank[6]], start=(k == 0), stop=(k == 7))
                if "ret_nosamp" not in DBG:
                    ACT(sg_s.f32(256)[0:NS], banks[6][0:NS, 0:256], AF.Silu, [], [sg_s.r(), rbank[6]])
                    ks_ = krf.f32(NT)[:, T:NT]
                    TR(banks[7][0:NS, 0:128], ks_, ident[:], [krf.r(), r_c], [rbank[7]])
                    CP("dve", kts.f32(128)[0:NS], banks[7][0:NS, 0:128], [], [kts.r(), rbank[7]])
                    TT("dve", tmp_s.f32(NS), qsf.f32(NS), ks_, ALU.mult, [qsf.r(), krf.r()], [tmp_s.r()])
                    MM(banks[7][0:NS, 128:129], tmp_s.f32(NS), onesf[:, 0:1], [tmp_s.r(), r_c], [rbank[7]])
                    TS("dve", qk.f32(1)[0:NS], banks[7][0:NS, 128:129], SC, None, ALU.mult, None, [], [qk.r(), rbank[7]])
                    TT("dve", Qm.f32(NS, NS), qsf.f32(NS).unsqueeze(1).to_broadcast([128, NS, NS]), dlt.f32(NS, NS), ALU.mult, [qsf.r(), dlt.r()], [Qm.r()])
                    for s in range(NS):
                        sp_ = Sp[s % 3]; vm_ = Vm[s % 3]; sn_ = Sn[s % 3]
                        bk = 4 + s % 2
                        P.dma("sp", "rsq%d" % (s % 3), sp_.f32(256), sret[s, h], writes=[sp_.r()])
                        MM(banks[6][0:NS, 256:512], Qm.f32(NS, NS)[:, s, :], sp_.f32(256), [Qm.r(), sp_.r()], [rbank[6]], start=(s == 0), stop=(s == NS - 1))
                        TS("dve", vm_.f32(256)[0:NS], v_s.f32(256)[0:NS], dcol.f32(NS)[0:NS, s:s + 1], SC, ALU.mult, ALU.mult, [v_s.r(), dcol.r()], [vm_.r()])
                        MM(banks[bk][:, 0:256], kts.f32(128)[0:NS], vm_.f32(256)[0:NS], [kts.r(), vm_.r()], [rbank[bk]])
                        STT("dve", sn_.f32(256), sp_.f32(256), cst["gam"][h], banks[bk][:, 0:256], ALU.mult, ALU.add, [sp_.r()], [sn_.r(), rbank[bk]])
                        P.dma("pool", "rst%d" % (s % 3), o_ret_s[s, h], sn_.f32(256), reads=[sn_.r()])
                    TS("dve", tmp_s.f32(256)[0:NS], v_s.f32(256)[0:NS], qk.f32(1)[0:NS], None, ALU.mult, None, [v_s.r(), qk.r()], [tmp_s.r()])
                    STT("dve", o_s.f32(256)[0:NS], banks[6][0:NS, 256:512], cst["gam"][h], tmp_s.f32(256)[0:NS], ALU.mult, ALU.add, [tmp_s.r()], [o_s.r(), rbank[6]])
                    gnorm_gate(o_s.f32(256)[0:NS], NS, tmp_s.f32(256)[0:NS], tmp_s.r(), sg_s.f32(256)[0:NS], sg_s.r(), [o_s.r()], [])
                    for cc in range(2):
                        TR(banks[7][:, 256 + cc * NS:256 + (cc + 1) * NS], tmp_s.f32(256)[0:NS, cc * 128:(cc + 1) * 128], ident[0:NS, 0:NS], [tmp_s.r(), r_c], [rbank[7]])
                    CP("dve", ogv[:, :, T:NT], banks[7][:, 256:256 + 2 * NS].rearrange("p (c t) -> p c t", c=2, t=NS), [], [ogT.r(), rbank[7]])
                wo_t, wko_ = wslot()
                P.dma("pool", wko_, wo_t.bf(2, 1024), ret_w_out[h * 256:(h + 1) * 256, :].rearrange("(c p) n -> p c n", p=128), writes=[wo_t.r()])
                P.commit(wko_)
                for dc in range(8):
                    for ti, (c0, n) in enumerate(tiles):
                        bk = 4 + (dc * NTI + ti) % 2
                        for cc in range(2):
                            MM(banks[bk][:, 0:n], wo_t.bf(2, 1024)[:, cc, dc * 128:(dc + 1) * 128], ogv[:, cc, c0:c0 + n], [wo_t.r(), ogT.r()], [rbank[bk]],
                               start=(cc == 0), stop=(cc == 1))
                        xv = xf[:, dc, c0:c0 + n]
                        if h == 0:
                            STT("dve", xv, xv, ALPHA, banks[bk][:, 0:n], ALU.mult, ALU.add, [r_xf[dc][ti]], [r_xf[dc][ti], rbank[bk]])
                        else:
                            TT("dve", xv, xv, banks[bk][:, 0:n], ALU.add, [r_xf[dc][ti]], [r_xf[dc][ti], rbank[bk]])
            A.release(m, bufs)

        if "noload" not in DBG:
            load_x()
        for li in range(nlayers):
            kind = li % 3
            if kind == 0:
                pool_mixer(li // 3)
            elif kind == 1:
                gdn_mixer()
            else:
                ret_mixer()
            layer_norm(2 * li)
            ffn(li)
            layer_norm(2 * li + 1)
        if "nostore" not in DBG:
            store_y()
        if "arena" in DBG:
            print("arena words", AW, "high water", A.hw)
        P.emit(st)
    return nc, cst


_CACHE = {}


def _in_maps(inputs, T, NS, cst):
    f = lambda a: np.ascontiguousarray(np.asarray(a, dtype=np.float32))
    shared = {
        "pool_w": f(inputs["pool_w"]), "pool_scale": f(inputs["pool_scale"]),
        "gdn_w_in": f(inputs["gdn_w_in"][0]), "gdn_conv_w": f(inputs["gdn_conv_w"][0]),
        "gdn_a_log": f(inputs["gdn_a_log"]), "gdn_dt_bias": f(inputs["gdn_dt_bias"]),
        "gdn_norm_g": f(inputs["gdn_norm_g"]), "gdn_w_out": f(inputs["gdn_w_out"][0]),
        "ret_w_in": f(inputs["ret_w_in"][0]), "ret_w_out": f(inputs["ret_w_out"][0]),
        "ffn_w13": f(inputs["ffn_w13"]), "ffn_w2": f(inputs["ffn_w2"]),
        "ln_g": f(inputs["ln_g"]).reshape(8, D), "ln_b": f(inputs["ln_b"]).reshape(8, D),
    }
    for k in list(CONST_SHAPES) + ["cos2", "sin2"]:
        shared["c_" + k] = f(cst[k])
    maps = []
    for c in range(NCORES):
        sl = slice(c * NS, (c + 1) * NS)
        mp = dict(shared)
        mp["xp"] = f(inputs["x_prompt"][c])
        mp["xs"] = f(inputs["x_sample"][sl, 0])
        mp["spool"] = f(inputs["state_pool"][:, sl])
        mp["sconv"] = f(inputs["state_gdn_conv"][0, sl])
        mp["sgdn"] = f(inputs["state_gdn"][0, sl])
        mp["sret"] = f(inputs["state_ret"][0, sl])
        maps.append(mp)
    return maps


def kernel(**inputs):
    T, NS = 2048, 16
    if "nc" not in _CACHE:
        _CACHE["nc"] = build(T, NS)
    nc, cst = _CACHE["nc"]
    maps = _in_maps(inputs, T, NS, cst)
    res = run_bass_kernel_spmd(nc, maps, core_ids=list(range(NCORES))).results
    cat = lambda k, ax: np.concatenate([np.asarray(r[k], dtype=np.float32)[None] if ax is None else np.asarray(r[k], dtype=np.float32)
                                        for r in res], axis=0 if ax is None else ax)
    y_prompt = cat("yp", None)
    y_sample = cat("ys", 0)[:, None, :]
    pool_p = np.stack([np.asarray(r["pool_p"], np.float32) for r in res], axis=1)
    pool_s = cat("pool_s", 1)
    conv_p = cat("conv_p", None)[None]
    conv_s = cat("conv_s", 0)[None]
    gdn_p = cat("gdn_p", None)[None]
    gdn_s = cat("gdn_s", 0)[None]
    ret_p = cat("ret_p", None)[None]
    ret_s = cat("ret_s", 0)[None]
    return (y_prompt, y_sample, pool_p, pool_s, conv_p, conv_s, gdn_p, gdn_s, ret_p, ret_s)
```

```python
import math
from contextlib import ExitStack
import numpy as np
import concourse.bass as bass
import concourse.mybir as mybir
from concourse.bass_utils import run_bass_kernel_spmd

F32 = mybir.dt.float32
BF16 = mybir.dt.bfloat16
ALU = mybir.AluOpType
AF = mybir.ActivationFunctionType
AX = mybir.AxisListType

D = 1024
DFF = 2816
NCORES = 8
PAST_LEN = 16384
ALPHA = 8.0 ** 0.25
LN_EPS = 1e-5
RMS_EPS = 1e-6
COMPUTE = ("pe", "act", "dve", "pool")


class Res:
    __slots__ = ("last_w", "readers")

    def __init__(self, inherit=None):
        self.last_w = None
        self.readers = dict(inherit) if inherit else {}


class Op:
    __slots__ = ("eng", "fn", "deps", "is_dma", "dkey", "dcount", "need_inc", "cnt", "idx")

    def __init__(self, eng, fn, is_dma=False, dkey=None):
        self.eng = eng
        self.fn = fn
        self.deps = []
        self.is_dma = is_dma
        self.dkey = dkey
        self.dcount = 0
        self.need_inc = False
        self.cnt = 0
        self.idx = 0


class Prog:
    def __init__(self, nc):
        self.nc = nc
        self.ops = []
        self.dma_counts = {}
        self.open_batch = {}

    def _add(self, op, reads, writes):
        op.idx = len(self.ops)
        deps = {}
        for r in reads:
            w = r.last_w
            if w is not None:
                deps[id(w)] = (w, True)
        for r in writes:
            w = r.last_w
            if w is not None and id(w) not in deps:
                deps[id(w)] = (w, False)
            for rd in r.readers.values():
                if id(rd) not in deps:
                    deps[id(rd)] = (rd, False)
        for (d, raw) in deps.values():
            if d is op:
                continue
            if (not d.is_dma) and (not op.is_dma) and d.eng == op.eng and d.eng == "pe":
                continue
            if d.is_dma and op.is_dma and d.dkey == op.dkey and d in self.open_batch.get(op.dkey, ()):
                continue
            op.deps.append(d)
            d.need_inc = True
        for r in reads:
            key = (op.eng, op.dkey) if op.is_dma else op.eng
            r.readers[key] = op
        for r in writes:
            r.last_w = op
            r.readers = {}
        self.ops.append(op)
        return op

    def op(self, eng, fn, reads=(), writes=()):
        return self._add(Op(eng, fn), reads, writes)

    def dma(self, eng, key, out, in_, reads=(), writes=()):
        def fn(e, out=out, in_=in_):
            return e.dma_start(out=out, in_=in_)
        o = Op(eng, fn, is_dma=True, dkey=key)
        self.dma_counts[key] = self.dma_counts.get(key, 0) + 16
        o.dcount = self.dma_counts[key]
        o.need_inc = True
        self.open_batch.setdefault(key, []).append(o)
        return self._add(o, reads, writes)

    def commit(self, key):
        for o in self.open_batch.get(key, []):
            o.dcount = self.dma_counts[key]
        self.open_batch[key] = []

    def emit(self, st):
        nc = self.nc
        sems = {e: st.enter_context(nc.semaphore("s_" + e)) for e in COMPUTE}
        dsems = {k: st.enter_context(nc.semaphore("d_%s" % (k,))) for k in self.dma_counts}
        cnts = {e: 0 for e in COMPUTE}
        for o in self.ops:
            if not o.is_dma and o.need_inc:
                cnts[o.eng] += 1
                o.cnt = cnts[o.eng]
        final_waits = {("d", k): (dsems[k], v) for k, v in self.dma_counts.items()}
        streams = {}
        for o in self.ops:
            streams.setdefault(o.eng, []).append(o)
        block = st.enter_context(nc.Block())

        def run(engname, e):
            wm = {}
            for o in streams.get(engname, []):
                need = {}
                for d in o.deps:
                    if d.is_dma:
                        k = ("d", d.dkey)
                        s, v = dsems[d.dkey], d.dcount
                    else:
                        k = ("c", d.eng)
                        s, v = sems[d.eng], d.cnt
                    if v > wm.get(k, 0) and v > need.get(k, (None, 0))[1]:
                        need[k] = (s, v)
                for k, (s, v) in need.items():
                    e.wait_ge(s, v)
                    wm[k] = v
                ins = o.fn(e)
                if o.is_dma:
                    ins.then_inc(dsems[o.dkey], 16)
                elif o.need_inc:
                    ins.then_inc(sems[o.eng], 1)
            if engname == "sp":
                for k, (s, v) in final_waits.items():
                    if v > wm.get(k, 0):
                        e.wait_ge(s, v)

        @block.sync
        def _(e):
            run("sp", e)

        @block.tensor
        def _(e):
            run("pe", e)

        @block.scalar
        def _(e):
            run("act", e)

        @block.vector
        def _(e):
            run("dve", e)

        @block.gpsimd
        def _(e):
            run("pool", e)


class Buf:
    def __init__(self, ap, lo, hi, inherit):
        self.ap = ap
        self.lo = lo
        self.hi = hi
        self.inherit = inherit
        self.ch = {}

    def r(self, key=None):
        c = self.ch.get(key)
        if c is None:
            c = Res(self.inherit)
            self.ch[key] = c
        return c

    def f32(self, *shape):
        return _shape(self.ap, shape)

    def bf(self, *shape):
        return _shape(self.ap.bitcast(BF16), shape)


class Sub:
    def __init__(self, parent, lo, hi):
        self.parent = parent
        self.ap = parent.ap[:, lo:hi]

    def r(self, key=None):
        return self.parent.r(key)

    def f32(self, *shape):
        return _shape(self.ap, shape)

    def bf(self, *shape):
        return _shape(self.ap.bitcast(BF16), shape)


def _shape(ap, shape):
    if len(shape) <= 1:
        return ap[:, 0:shape[0]] if shape else ap
    n = 1
    for s in shape:
        n *= s
    ap = ap[:, 0:n]
    if len(shape) == 2:
        return ap.rearrange("p (a b) -> p a b", a=shape[0], b=shape[1])
    if len(shape) == 3:
        return ap.rearrange("p (a b c) -> p a b c", a=shape[0], b=shape[1], c=shape[2])
    raise ValueError(shape)


class Arena:
    def __init__(self, ap, words):
        self.ap = ap
        self.words = words
        self.top = 0
        self.dead = []

    def alloc(self, words):
        words = (words + 7) // 8 * 8
        lo, hi = self.top, self.top + words
        assert hi <= self.words, ("arena overflow", hi, self.words)
        self.top = hi
        self.hw = max(getattr(self, "hw", 0), hi)
        inh = {}
        keep = []
        for b in self.dead:
            if b.hi <= lo or b.lo >= hi:
                keep.append(b)
                continue
            for c in b.ch.values():
                cand = list(c.readers.items())
                if c.last_w is not None:
                    w = c.last_w
                    cand.append(((w.eng, w.dkey) if w.is_dma else w.eng, w))
                for k, o in cand:
                    if k not in inh or inh[k].idx < o.idx:
                        inh[k] = o
            if not (b.lo >= lo and b.hi <= hi):
                keep.append(b)
        self.dead = keep
        return Buf(self.ap[:, lo:hi], lo, hi, inh)

    def f32(self, n):
        return self.alloc(n)

    def bf(self, n):
        return self.alloc((n + 1) // 2)

    def mark(self):
        return (self.top, [])

    def release(self, mark, bufs):
        self.top = mark[0]
        self.dead.extend(bufs)


def _consts(T, NS):
    c = {}
    c["ident"] = np.eye(128, dtype=np.float32)
    ii = np.arange(128)
    rc = np.zeros((128, 4, 16), np.float32)
    for g, win in enumerate((2, 4, 8, 16)):
        for t in range(16):
            rc[:, g, t] = 1.0 / min(t + 1, win)
    c["rc"] = rc
    c["maskS"] = np.where(ii[None, :] < ii[:, None], 0.0, 30000.0).astype(np.float32)
    c["negmaskT"] = np.where(ii[None, :] >= ii[:, None], 0.0, -30000.0).astype(np.float32)
    c["U"] = (ii[:, None] <= ii[None, :]).astype(np.float32)
    bm = lambda sz: ((ii[:, None] // sz) == (ii[None, :] // sz)).astype(np.float32)
    msk = np.zeros((128, 5, 128), np.float32)
    msk[:, 0, :] = bm(8)
    for li_, sz in enumerate((8, 16, 32, 64)):
        msk[:, 1 + li_, :] = bm(2 * sz) - bm(sz)
    c["bmask"] = msk
    H = 8
    lg = np.log(1.0 - 2.0 ** (-5.0 - np.arange(H, dtype=np.float64)))
    sc = 128.0 ** -0.5
    diff = (ii[None, :] - ii[:, None]).astype(np.float64)
    decT = np.where(diff[None] >= 0, np.exp(np.maximum(diff, 0.0)[None] * lg[:, None, None]), 0.0) * sc
    c["decT"] = np.ascontiguousarray(decT.transpose(1, 0, 2)).astype(np.float32)
    xi = np.exp((ii + 1.0)[None, :] * lg[:, None])
    c["xi"] = np.ascontiguousarray(np.broadcast_to(xi[None], (128, H, 128))).astype(np.float32)
    zeta = np.exp((127.0 - ii)[None, :] * lg[:, None]) * sc
    c["zeta"] = np.ascontiguousarray(zeta.T).astype(np.float32)
    c["gC"] = [float(np.exp(128.0 * lg[h])) for h in range(H)]
    c["gam"] = [float(np.exp(lg[h])) for h in range(H)]
    gb = np.zeros((128, 2, H), np.float32)
    gb[:, 0, :] = np.exp(lg)[None, :]
    gb[:, 1, :] = (np.exp(lg) * 0 + sc)[None, :]
    c["gamb"] = gb
    half = 64
    freqs = (np.float32(10000.0) ** (-np.arange(half, dtype=np.float32) / np.float32(half))).astype(np.float32)
    pos = np.concatenate([np.arange(T, dtype=np.float32), np.full((NS,), float(PAST_LEN), np.float32)])
    ang = (pos[None, :] * freqs[:, None]).astype(np.float32)
    cs, sn = np.cos(ang).astype(np.float32), np.sin(ang).astype(np.float32)
    c["cos2"] = np.concatenate([cs, cs], 0)
    c["sin2"] = np.concatenate([-sn, sn], 0)
    angs = (np.float32(PAST_LEN) * freqs).astype(np.float32)
    cst = np.zeros((128, 2, half), np.float32)
    cst[:, 0, :] = np.cos(angs)[None]
    cst[:, 1, :] = np.sin(angs)[None]
    c["ropes"] = cst
    dl = np.zeros((128, 16, 16), np.float32)
    for s in range(16):
        dl[:, s, s] = 1.0
    c["delta"] = dl
    dcol = np.zeros((128, 16), np.float32)
    for s in range(16):
        dcol[s, s] = 1.0
    c["dcol"] = dcol
    return c


CONST_SHAPES = {"ident": [128, 128], "rc": [128, 4, 16], "maskS": [128, 128], "negmaskT": [128, 128],
                "U": [128, 128], "bmask": [128, 5, 128], "decT": [128, 8, 128], "xi": [128, 8, 128], "zeta": [128, 8],
                "gamb": [128, 2, 8], "ropes": [128, 2, 64], "delta": [128, 16, 16], "dcol": [128, 16]}


AWO = None
DBG = set()


def build(T, NS, nlayers=4):
    assert T % 128 == 0 and NS == 16
    nc = bass.Bass("TRN2", target_bir_lowering=False)
    NB = T // 128
    NT = T + NS
    TTS = min(512, T)
    tiles = [(c0, TTS) for c0 in range(0, T, TTS)] + [(T, NS)]
    NTI = len(tiles)
    cst = _consts(T, NS)

    def din(name, shape):
        return nc.dram_tensor(name, list(shape), F32, kind="ExternalInput").ap()

    def dout(name, shape):
        return nc.dram_tensor(name, list(shape), F32, kind="ExternalOutput").ap()

    xp = din("xp", [T, D])
    xs = din("xs", [NS, D])
    spool = din("spool", [2, NS, 15, D])
    sconv = din("sconv", [NS, 3, 3072])
    sgdn = din("sgdn", [NS, 8, 128, 128])
    sret = din("sret", [NS, 8, 128, 256])
    pool_w = din("pool_w", [2, 4, 256, 256])
    pool_scale = din("pool_scale", [2, D])
    gdn_w_in = din("gdn_w_in", [D, 4112])
    gdn_conv_w = din("gdn_conv_w", [4, 3072])
    gdn_a_log = din("gdn_a_log", [1, 8])
    gdn_dt_bias = din("gdn_dt_bias", [1, 8])
    gdn_norm_g = din("gdn_norm_g", [1, 128])
    gdn_w_out = din("gdn_w_out", [D, D])
    ret_w_in = din("ret_w_in", [D, 6144])
    ret_w_out = din("ret_w_out", [2048, D])
    ffn_w13 = din("ffn_w13", [4, D, 2 * DFF])
    ffn_w2 = din("ffn_w2", [4, DFF, D])
    ln_g = din("ln_g", [8, D])
    ln_b = din("ln_b", [8, D])
    cd = {k: din("c_" + k, CONST_SHAPES[k]) for k in CONST_SHAPES}
    cd["cos2"] = din("c_cos2", [128, NT])
    cd["sin2"] = din("c_sin2", [128, NT])

    yp = dout("yp", [T, D])
    ys = dout("ys", [NS, D])
    o_pool_p = dout("pool_p", [2, 15, D])
    o_pool_s = dout("pool_s", [2, NS, 15, D])
    o_conv_p = dout("conv_p", [3, 3072])
    o_conv_s = dout("conv_s", [NS, 3, 3072])
    o_gdn_p = dout("gdn_p", [8, 128, 128])
    o_gdn_s = dout("gdn_s", [NS, 8, 128, 128])
    o_ret_p = dout("ret_p", [8, 128, 256])
    o_ret_s = dout("ret_s", [NS, 8, 128, 256])

    P = Prog(nc)
    st = ExitStack()
    with st:
        def sb(name, shape, dt):
            return st.enter_context(nc.sbuf_tensor(name, shape, dt))

        xf = sb("xf", [128, 8, NT], F32)
        xb = sb("xb", [128, 8, NT], BF16)
        ident = sb("ident", [128, 128], F32)
        identb = sb("identb", [128, 128], BF16)
        onesb = sb("onesb", [128, 128], BF16)
        onesn = sb("onesn", [128, 128], BF16)
        onesf = sb("onesf", [128, 128], F32)
        lnp = sb("lnp", [128, 128], F32)
        prm = sb("prm", [128, 128], F32)
        rcs = sb("rcs", [128, 4, 16], F32)
        AW = (nc.sbuf_bytes_remaining - 4096) // 4 // 8 * 8 if AWO is None else AWO
        arena_t = sb("arena", [128, AW], F32)
        A = Arena(arena_t[:], AW)
        banks = [st.enter_context(nc.psum_tensor("bank%d" % i, [128, 512], F32)) for i in range(8)]
        rbank = [Res() for _ in range(8)]
        r_xf = [[Res() for _ in range(NTI)] for _ in range(8)]
        r_xb = [[Res() for _ in range(NTI)] for _ in range(8)]
        r_c = Res()

        def rx(rs, ks=range(8), ts=range(NTI)):
            return [rs[k][t] for k in ks for t in ts]

        def tiles_of(c0, n):
            return [ti for ti, (a, m) in enumerate(tiles) if a < c0 + n and c0 < a + m]

        P.dma("sp", "c0", ident[:], cd["ident"], writes=[r_c])
        P.dma("sp", "c0", rcs[:], cd["rc"], writes=[r_c])
        P.commit("c0")
        P.op("dve", lambda e: e.tensor_copy(identb[:], ident[:]), reads=[r_c], writes=[r_c])
        P.op("dve", lambda e: e.memset(onesb[:], 1.0), writes=[r_c])
        P.op("dve", lambda e: e.memset(onesn[:], 1.0 / D), writes=[r_c])
        P.op("dve", lambda e: e.memset(onesf[:], 1.0), writes=[r_c])
        wring = [A.bf(2 * 8 * 128) for _ in range(4)]
        wr_i = [0]

        def wslot():
            b = wring[wr_i[0] % len(wring)]
            k = "w%d" % (wr_i[0] % len(wring))
            wr_i[0] += 1
            return b, k

        m0 = A.mark()
        if "noparams" in DBG:
            raise_ = None
        stg = A.f32(128)
        stg2 = A.f32(128)
        if "noparams" not in DBG:
          P.dma("sp", "c1", stg.f32(128)[0:64, :], ln_g.rearrange("l (k p) -> (l k) p", p=128), writes=[stg.r()])
          P.dma("sp", "c1", stg.f32(128)[64:128, :], ln_b.rearrange("l (k p) -> (l k) p", p=128), writes=[stg.r()])
          P.op("dve", lambda e: e.memset(stg2.f32(128), 0.0), writes=[stg2.r()])
          P.dma("sp", "c2", stg2.f32(128)[0:16, :], pool_scale.rearrange("j (k p) -> (j k) p", p=128), reads=[], writes=[stg2.r()])
          P.dma("sp", "c2", stg2.f32(128)[16:112, :], gdn_conv_w.rearrange("j (c p) -> (j c) p", p=128), writes=[stg2.r()])
          P.dma("sp", "c2", stg2.f32(128)[112:113, :], gdn_norm_g, writes=[stg2.r()])
          P.op("pe", lambda e: e.transpose(banks[0][:, 0:128], stg.f32(128), ident[:]), reads=[stg.r(), r_c], writes=[rbank[0]])
          P.op("pe", lambda e: e.transpose(banks[0][:, 128:256], stg2.f32(128), ident[:]), reads=[stg2.r(), r_c], writes=[rbank[0]])
          P.op("dve", lambda e: e.tensor_copy(lnp[:], banks[0][:, 0:128]), reads=[rbank[0]], writes=[r_c])
          P.op("dve", lambda e: e.tensor_copy(prm[:], banks[0][:, 128:256]), reads=[rbank[0]], writes=[r_c])
        A.release(m0, [stg, stg2])

        epsb = {}
        for ev in (LN_EPS, RMS_EPS, RMS_EPS * 128.0, 1.0):
            t = sb("eps%d" % len(epsb), [128, 1], F32)
            P.op("dve", lambda e, t=t, ev=ev: e.memset(t[:], ev), writes=[r_c])
            epsb[ev] = t

        def eps_tile(v):
            return epsb[v][:]

        def rsqrt_eps(out, in_, eps, res, scale=1.0):
            eb = epsb[eps]
            np_ = in_.shape[0]
            P.op("act", lambda e: e.activation(out, in_, AF.Ln, bias=eb[0:np_, :], scale=scale), reads=[res, r_c], writes=[res])
            P.op("act", lambda e: e.activation(out, out, AF.Exp, scale=-0.5), reads=[res], writes=[res])

        def load_x():
            m = A.mark()
            sg = [A.f32(D), A.f32(D)]
            blocks = [(xp[b * 128:(b + 1) * 128, :], 128, b * 128) for b in range(NB)] + [(xs, NS, T)]
            if "nosamp" in DBG:
                blocks = blocks[:-1]
            for bi, (src, n, c0) in enumerate(blocks):
                s = sg[bi % 2]
                P.dma("sp", "xl%d" % (bi % 2), s.f32(D)[0:n, :], src, writes=[s.r()])
                tis = tiles_of(c0, n)
                for half in range(2):
                    bk = (bi * 2 + half) % 2
                    for kk in range(4):
                        k = half * 4 + kk
                        P.op("pe", lambda e, s=s, k=k, kk=kk, n=n, bk=bk: e.transpose(
                            banks[bk][:, kk * 128:kk * 128 + n], s.f32(D)[0:n, k * 128:(k + 1) * 128], ident[0:n, 0:n]),
                            reads=[s.r(), r_c], writes=[rbank[bk]])
                    src_ps = banks[bk][:].rearrange("p (a b) -> p a b", a=4, b=128)[:, :, 0:n]
                    ks = range(half * 4, half * 4 + 4)
                    P.op("dve", lambda e, src_ps=src_ps, half=half, c0=c0, n=n: e.tensor_copy(
                        xf[:, half * 4:half * 4 + 4, c0:c0 + n], src_ps),
                        reads=[rbank[bk]], writes=rx(r_xf, ks, tis))
                    if "noact" in DBG:
                        continue
                    P.op("act", lambda e, src_ps=src_ps, half=half, c0=c0, n=n: e.activation(
                        xb[:, half * 4:half * 4 + 4, c0:c0 + n], xf[:, half * 4:half * 4 + 4, c0:c0 + n], AF.Identity),
                        reads=rx(r_xf, ks, tis), writes=rx(r_xb, ks, tis))
            A.release(m, sg)

        def layer_norm(li):
            m = A.mark()
            bufs = []
            sets = []
            for _ in range(2):
                sets.append((A.bf(8 * TTS), A.bf(8 * TTS), A.f32(TTS), A.f32(TTS), A.f32(TTS)))
                bufs += list(sets[-1])
            for ti, (c0, n) in enumerate(tiles):
                rb, sq, mean, rstd, m2 = sets[ti % 2]
                xv = xf[:, :, c0:c0 + n]
                P.op("act", lambda e, rb=rb, xv=xv, n=n: e.activation(rb.bf(8, n), xv, AF.Copy),
                     reads=rx(r_xf, ts=[ti]), writes=[rb.r()])
                P.op("act", lambda e, sq=sq, xv=xv, n=n: e.activation(sq.bf(8, n), xv, AF.Square),
                     reads=rx(r_xf, ts=[ti]), writes=[sq.r()])
                for k in range(8):
                    P.op("pe", lambda e, rb=rb, k=k, n=n: e.matmul(banks[6][:, 0:n], onesn[:], rb.bf(8, n)[:, k, :],
                                                                   start=(k == 0), stop=(k == 7)),
                         reads=[rb.r(), r_c], writes=[rbank[6]])
                for k in range(8):
                    P.op("pe", lambda e, sq=sq, k=k, n=n: e.matmul(banks[7][:, 0:n], onesn[:], sq.bf(8, n)[:, k, :],
                                                                   start=(k == 0), stop=(k == 7)),
                         reads=[sq.r(), r_c], writes=[rbank[7]])
                P.op("act", lambda e, mean=mean, n=n: e.activation(mean.f32(n), banks[6][:, 0:n], AF.Copy),
                     reads=[rbank[6]], writes=[mean.r()])
                P.op("dve", lambda e, mean=mean, m2=m2, n=n: e.tensor_tensor(m2.f32(n), mean.f32(n), mean.f32(n), ALU.mult),
                     reads=[mean.r()], writes=[m2.r()])
                P.op("dve", lambda e, m2=m2, rstd=rstd, n=n: e.tensor_tensor(rstd.f32(n), banks[7][:, 0:n], m2.f32(n), ALU.subtract),
                     reads=[m2.r(), rbank[7]], writes=[rstd.r()])
                rsqrt_eps(rstd.f32(n), rstd.f32(n), LN_EPS, rstd.r())
                P.op("dve", lambda e, xv=xv, mean=mean, n=n: e.tensor_tensor(
                    xv, xv, mean.f32(n).unsqueeze(1).to_broadcast([128, 8, n]), ALU.subtract),
                    reads=rx(r_xf, ts=[ti]) + [mean.r()], writes=rx(r_xf, ts=[ti]))
                P.op("dve", lambda e, xv=xv, rstd=rstd, n=n: e.tensor_tensor(
                    xv, xv, rstd.f32(n).unsqueeze(1).to_broadcast([128, 8, n]), ALU.mult),
                    reads=rx(r_xf, ts=[ti]) + [rstd.r()], writes=rx(r_xf, ts=[ti]))
                for k in range(8):
                    P.op("act", lambda e, k=k, c0=c0, n=n: e.activation(
                        xf[:, k, c0:c0 + n], xf[:, k, c0:c0 + n], AF.Identity,
                        bias=lnp[:, 64 + li * 8 + k:64 + li * 8 + k + 1], scale=lnp[:, li * 8 + k:li * 8 + k + 1]),
                        reads=[r_xf[k][ti], r_c], writes=[r_xf[k][ti]])
                P.op("dve", lambda e, xv=xv, c0=c0, n=n: e.tensor_copy(xb[:, :, c0:c0 + n], xv),
                     reads=rx(r_xf, ts=[ti]), writes=rx(r_xb, ts=[ti]))
            A.release(m, bufs)

        def ffn(layer):
            m = A.mark()
            h = A.bf(11 * NT)
            w2 = A.bf(11 * D)
            sa = [A.f32(TTS), A.f32(TTS)]
            hv = h.bf(11, NT)
            w2v = w2.bf(11, D)
            cnt = 0
            for pas in range(2):
                for il in range(11):
                    i = pas * 11 + il
                    wb, wk = wslot()
                    wv = wb.bf(2, 8, 128)
                    P.dma("pool", wk, wv[:, 0], ffn_w13[layer, :, i * 128:(i + 1) * 128].rearrange("(k p) n -> p k n", p=128),
                          writes=[wb.r()])
                    P.dma("pool", wk, wv[:, 1], ffn_w13[layer, :, DFF + i * 128:DFF + (i + 1) * 128].rearrange("(k p) n -> p k n", p=128),
                          writes=[wb.r()])
                    P.commit(wk)
                    if il == 0:
                        for f in range(11):
                            P.dma("pool", "fw2", w2v[:, f, :], ffn_w2[layer, (pas * 11 + f) * 128:(pas * 11 + f + 1) * 128, :],
                                  writes=[w2.r(f)])
                        P.commit("fw2")
                    for ti, (c0, n) in enumerate(tiles):
                        ba, bb = banks[cnt % 2], banks[2 + cnt % 2]
                        ra, rbb = rbank[cnt % 2], rbank[2 + cnt % 2]
                        s = sa[cnt % 2]
                        cnt += 1
                        for k in range(8):
                            P.op("pe", lambda e, ba=ba, wv=wv, k=k, c0=c0, n=n: e.matmul(
                                ba[:, 0:n], wv[:, 0, k, :], xb[:, k, c0:c0 + n], start=(k == 0), stop=(k == 7)),
                                reads=[wb.r(), r_xb[k][ti]], writes=[ra])
                        for k in range(8):
                            P.op("pe", lambda e, bb=bb, wv=wv, k=k, c0=c0, n=n: e.matmul(
                                bb[:, 0:n], wv[:, 1, k, :], xb[:, k, c0:c0 + n], start=(k == 0), stop=(k == 7)),
                                reads=[wb.r(), r_xb[k][ti]], writes=[rbb])
                        P.op("act", lambda e, s=s, ba=ba, n=n: e.activation(s.f32(n), ba[:, 0:n], AF.Silu),
                             reads=[ra], writes=[s.r()])
                        P.op("dve", lambda e, s=s, bb=bb, il=il, c0=c0, n=n: e.tensor_tensor(
                            hv[:, il, c0:c0 + n], s.f32(n), bb[:, 0:n], ALU.mult),
                            reads=[s.r(), rbb], writes=[h.r((il, ti))])
                for dc in range(8):
                    for ti, (c0, n) in enumerate(tiles):
                        by, ry = banks[4 + cnt % 2], rbank[4 + cnt % 2]
                        cnt += 1
                        for f in range(11):
                            P.op("pe", lambda e, by=by, f=f, dc=dc, c0=c0, n=n: e.matmul(
                                by[:, 0:n], w2v[:, f, dc * 128:(dc + 1) * 128], hv[:, f, c0:c0 + n],
                                start=(f == 0), stop=(f == 10)),
                                reads=[w2.r(f), h.r((f, ti))], writes=[ry])
                        xv = xf[:, dc, c0:c0 + n]
                        if pas == 0:
                            P.op("dve", lambda e, xv=xv, by=by, n=n: e.scalar_tensor_tensor(
                                xv, xv, ALPHA, by[:, 0:n], ALU.mult, ALU.add),
                                reads=[ry, r_xf[dc][ti]], writes=[r_xf[dc][ti]])
                        else:
                            P.op("dve", lambda e, xv=xv, by=by, n=n: e.tensor_tensor(xv, xv, by[:, 0:n], ALU.add),
                                 reads=[ry, r_xf[dc][ti]], writes=[r_xf[dc][ti]])
            A.release(m, [h, w2] + sa)

        def store_y():
            m = A.mark()
            sg = [A.f32(D), A.f32(D)]
            blocks = [(yp[b * 128:(b + 1) * 128, :], 128, b * 128) for b in range(NB)] + [(ys, NS, T)]
            for bi, (dst, n, c0) in enumerate(blocks):
                s = sg[bi % 2]
                tis = tiles_of(c0, n)
                for half in range(2):
                    bk = (bi * 2 + half) % 2
                    for kk in range(4):
                        k = half * 4 + kk
                        P.op("pe", lambda e, k=k, kk=kk, n=n, bk=bk, c0=c0: e.transpose(
                            banks[bk][0:n, kk * 128:(kk + 1) * 128], xf[:, k, c0:c0 + n], ident[:]),
                            reads=rx(r_xf, [k], tis) + [r_c], writes=[rbank[bk]])
                    if half == 0:
                        P.op("dve", lambda e, s=s, n=n, bk=bk: e.tensor_copy(s.f32(D)[0:n, 0:512], banks[bk][0:n, :]),
                             reads=[rbank[bk]], writes=[s.r()])
                    else:
                        P.op("act", lambda e, s=s, n=n, bk=bk: e.activation(s.f32(D)[0:n, 512:1024], banks[bk][0:n, :], AF.Copy),
                             reads=[rbank[bk]], writes=[s.r()])
                P.dma("sp", "ys%d" % (bi % 2), dst, s.f32(D)[0:n, :], reads=[s.r()])
            A.release(m, sg)

        def pool_mixer(j):
            m = A.mark()
            bufs = []
            if j == 0:
                P.dma("sp", "po", o_pool_p[0], xp[T - 15:T, :])
                P.dma("sp", "po", o_pool_s[0, :, 14, :], xs)
            else:
                so = A.f32(D)
                bufs.append(so)
                nn = 15 + NS
                for half in range(2):
                    for kk in range(4):
                        k = half * 4 + kk
                        P.op("pe", lambda e, k=k, kk=kk, half=half: e.transpose(
                            banks[half][0:nn, kk * 128:(kk + 1) * 128], xf[:, k, T - 15:T + NS], ident[:]),
                            reads=rx(r_xf, [k], tiles_of(T - 15, nn)) + [r_c], writes=[rbank[half]])
                    P.op("dve", lambda e, half=half: e.tensor_copy(so.f32(D)[0:nn, half * 512:(half + 1) * 512], banks[half][0:nn, :]),
                         reads=[rbank[half]], writes=[so.r()])
                P.dma("sp", "po", o_pool_p[1], so.f32(D)[0:15, :], reads=[so.r()])
                P.dma("sp", "po", o_pool_s[1, :, 14, :], so.f32(D)[15:15 + NS, :], reads=[so.r()])
            P.dma("sp", "po", o_pool_s[j, :, 0:14, :], spool[j, :, 1:15, :])
            P.commit("po")
            hist = A.f32(8 * NS * 15)
            bufs.append(hist)
            hv = hist.f32(8, NS * 15)
            hs = [A.f32(D), A.f32(D)]
            bufs += hs
            rows = NS * 15 // 2
            src = spool[j].rearrange("s r d -> (s r) d")
            for b2 in range(2):
                s = hs[b2]
                P.dma("sp", "ph%d" % b2, s.f32(D)[0:rows, :], src[b2 * rows:(b2 + 1) * rows, :], writes=[s.r()])
                for half in range(2):
                    bk = half
                    for kk in range(4):
                        k = half * 4 + kk
                        P.op("pe", lambda e, s=s, k=k, kk=kk, bk=bk: e.transpose(
                            banks[bk][:, kk * 128:kk * 128 + rows], s.f32(D)[0:rows, k * 128:(k + 1) * 128], ident[0:rows, 0:rows]),
                            reads=[s.r(), r_c], writes=[rbank[bk]])
                    P.op("dve", lambda e, half=half, bk=bk, b2=b2: e.tensor_copy(
                        hv[:, half * 4:half * 4 + 4, b2 * rows:(b2 + 1) * rows],
                        banks[bk][:].rearrange("p (a b) -> p a b", a=4, b=128)[:, :, 0:rows]),
                        reads=[rbank[bk]], writes=[hist.r()])
            pooled = A.bf(8 * NT)
            bufs.append(pooled)
            pv = pooled.bf(8, NT)
            E = [A.f32(16 + T), A.f32(16 + T)]
            bufs += E
            ssum = A.f32(NS)
            bufs.append(ssum)
            for eb in E:
                P.op("pool", lambda e, eb=eb: e.memset(eb.f32(16 + T)[:, 0:16], 0.0), writes=[eb.r()])
            for k in range(8):
                g = k // 2
                win = 2 << g
                e0, e1 = E[0], E[1]
                tp = list(range(NTI - 1))
                P.op("pool", lambda e, e0=e0, k=k: e.tensor_copy(e0.f32(16 + T)[:, 16:16 + T], xf[:, k, 0:T]),
                     reads=rx(r_xf, [k], tp), writes=[e0.r()])
                cur, nxt = e0, e1
                sh = 1
                while sh < win:
                    P.op("dve", lambda e, cur=cur, nxt=nxt, sh=sh: e.tensor_tensor(
                        nxt.f32(16 + T)[:, 16:16 + T], cur.f32(16 + T)[:, 16:16 + T], cur.f32(16 + T)[:, 16 - sh:16 + T - sh], ALU.add),
                        reads=[cur.r()], writes=[nxt.r()])
                    cur, nxt = nxt, cur
                    sh *= 2
                P.op("dve", lambda e, cur=cur, k=k, win=win: e.scalar_tensor_tensor(
                    pv[:, k, 0:T], cur.f32(16 + T)[:, 16:16 + T], 1.0 / win, xf[:, k, 0:T], ALU.mult, ALU.subtract),
                    reads=[cur.r()] + rx(r_xf, [k], tp), writes=[pooled.r(k)])
                P.op("dve", lambda e, cur=cur, nxt=nxt, g=g: e.tensor_tensor(
                    nxt.f32(16 + T)[:, 16:31], cur.f32(16 + T)[:, 16:31], rcs[:, g, 0:15], ALU.mult),
                    reads=[cur.r(), r_c], writes=[nxt.r()])
                P.op("dve", lambda e, nxt=nxt, k=k: e.tensor_tensor(
                    pv[:, k, 0:15], nxt.f32(16 + T)[:, 16:31], xf[:, k, 0:15], ALU.subtract),
                    reads=[nxt.r()] + rx(r_xf, [k], [0]), writes=[pooled.r(k)])
                hk = hv[:, k, :].rearrange("p (s r) -> p s r", s=NS, r=15)
                P.op("dve", lambda e, hk=hk, win=win: e.tensor_reduce(
                    ssum.f32(NS), hk[:, :, 16 - win:15], AX.X, ALU.add),
                    reads=[hist.r()], writes=[ssum.r()])
                P.op("dve", lambda e, k=k: e.tensor_tensor(ssum.f32(NS), ssum.f32(NS), xf[:, k, T:NT], ALU.add),
                     reads=[ssum.r(), r_xf[k][NTI - 1]], writes=[ssum.r()])
                P.op("dve", lambda e, k=k, win=win: e.scalar_tensor_tensor(
                    pv[:, k, T:NT], ssum.f32(NS), 1.0 / win, xf[:, k, T:NT], ALU.mult, ALU.subtract),
                    reads=[ssum.r(), r_xf[k][NTI - 1]], writes=[pooled.r(k)])
            pw = A.bf(4 * 2 * 256)
            bufs.append(pw)
            pwv = pw.bf(4, 2, 256)
            P.dma("pool", "pw", pwv, pool_w[j].rearrange("g (cc p) d -> p g cc d", p=128), writes=[pw.r()])
            tm = [A.f32(TTS), A.f32(TTS)]
            bufs += tm
            cnt = 0
            for g in range(4):
                for dc in range(2):
                    k = 2 * g + dc
                    for ti, (c0, n) in enumerate(tiles):
                        bk = cnt % 2
                        t_ = tm[cnt % 2]
                        cnt += 1
                        for cc in range(2):
                            P.op("pe", lambda e, g=g, dc=dc, cc=cc, c0=c0, n=n, bk=bk: e.matmul(
                                banks[bk][:, 0:n], pwv[:, g, cc, dc * 128:(dc + 1) * 128], pv[:, 2 * g + cc, c0:c0 + n],
                                start=(cc == 0), stop=(cc == 1)),
                                reads=[pw.r(), pooled.r(2 * g + cc)], writes=[rbank[bk]])
                        P.op("act", lambda e, t_=t_, bk=bk, n=n, k=k: e.activation(
                            t_.f32(n), banks[bk][:, 0:n], AF.Copy, scale=prm[:, j * 8 + k:j * 8 + k + 1]),
                            reads=[rbank[bk], r_c], writes=[t_.r()])
                        xv = xf[:, k, c0:c0 + n]
                        P.op("dve", lambda e, xv=xv, t_=t_, n=n: e.scalar_tensor_tensor(
                            xv, xv, ALPHA, t_.f32(n), ALU.mult, ALU.add),
                            reads=[t_.r(), r_xf[k][ti]], writes=[r_xf[k][ti]])
            A.release(m, bufs)

        def MM(out, lhsT, rhs, r, w, start=True, stop=True):
            P.op("pe", lambda e: e.matmul(out, lhsT, rhs, start=start, stop=stop), reads=r, writes=w)

        def TR(out, in_, idn, r, w):
            P.op("pe", lambda e: e.transpose(out, in_, idn), reads=r, writes=w)

        def TT(eng, out, in0, in1, op, r, w):
            P.op(eng, lambda e: e.tensor_tensor(out, in0, in1, op), reads=r, writes=w)

        def STT(eng, out, in0, scalar, in1, op0, op1, r, w):
            P.op(eng, lambda e: e.scalar_tensor_tensor(out, in0, scalar, in1, op0, op1), reads=r, writes=w)

        def TS(eng, out, in0, s1, s2, op0, op1, r, w):
            if op1 is None:
                P.op(eng, lambda e: e.tensor_scalar(out, in0, s1, None, op0), reads=r, writes=w)
            else:
                P.op(eng, lambda e: e.tensor_scalar(out, in0, s1, s2, op0, op1), reads=r, writes=w)

        def CP(eng, out, in_, r, w):
            if eng == "act":
                P.op("act", lambda e: e.activation(out, in_, AF.Identity), reads=r, writes=w)
            else:
                P.op(eng, lambda e: e.tensor_copy(out, in_), reads=r, writes=w)

        def ACT(out, in_, func, r, w, scale=1.0, bias=None, accum=None):
            def fn(e):
                kw = {}
                if bias is not None:
                    kw["bias"] = bias
                if accum is not None:
                    kw["accum_out"] = accum
                return e.activation(out, in_, func, scale=scale, **kw)
            P.op("act", fn, reads=r, writes=w)

        def RED(eng, out, in_, r, w, op=ALU.add):
            P.op(eng, lambda e: e.tensor_reduce(out, in_, AX.X, op), reads=r, writes=w)

        def MS(eng, out, val, w):
            P.op(eng, lambda e: e.memset(out, val), writes=w)

        def bfps(bank, c0, n):
            return banks[bank][:, c0:c0 + n // 2].bitcast(BF16)

        ptiles = tiles[:-1]
        PT_ = list(range(NTI - 1))

        def out_proj_samples(w_dram, nk, ogsT, ogs_res):
            for dc in range(8):
                for kc in range(nk):
                    wb_, wk_ = wslot()
                    wv_ = wb_.bf(2, 8, 128)
                    if kc % 16 == 0:
                        pass
                    P.dma("pool", wk_, wv_[:, 0, 0, :], w_dram[kc * 128:(kc + 1) * 128, dc * 128:(dc + 1) * 128], writes=[wb_.r()])
                    P.commit(wk_)
                    MM(banks[4][:, 0:NS], wv_[:, 0, 0, :], ogsT[:, kc, :], [wb_.r(), ogs_res], [rbank[4]], start=(kc == 0), stop=(kc == nk - 1))
                xv = xf[:, dc, T:NT]
                STT("dve", xv, xv, ALPHA, banks[4][:, 0:NS], ALU.mult, ALU.add, [r_xf[dc][NTI - 1]], [r_xf[dc][NTI - 1], rbank[4]])

        def gdn_mixer():
            G = 4 if NB >= 4 else 2
            m = A.mark()
            bufs = []

            def al(n, bf=False):
                b = A.bf(n) if bf else A.f32(n)
                bufs.append(b)
                return b
            maskS = al(128); negT = al(128); Um = al(128); gn = al(128); gnb = al(128)
            bmask = al(5 * 128)
            bmv = bmask.f32(5, 128)
            alog = al(8); dtb = al(8); nea = al(8)
            P.dma("sp", "gc", maskS.f32(128), cd["maskS"], writes=[maskS.r()])
            P.dma("sp", "gc", negT.f32(128), cd["negmaskT"], writes=[negT.r()])
            P.dma("sp", "gc", Um.f32(128), cd["U"], writes=[Um.r()])
            P.dma("sp", "gc", bmask.f32(5, 128), cd["bmask"], writes=[bmask.r()])
            P.dma("sp", "gc", gn.f32(128), gdn_norm_g.partition_broadcast(128)[:, 0, :], writes=[gn.r()])
            P.dma("sp", "gc", alog.f32(8), gdn_a_log.partition_broadcast(128)[:, 0, :], writes=[alog.r()])
            P.dma("sp", "gc", dtb.f32(8), gdn_dt_bias.partition_broadcast(128)[:, 0, :], writes=[dtb.r()])
            P.commit("gc")
            TS("dve", gnb.f32(128), gn.f32(128), math.sqrt(128.0), None, ALU.mult, None, [gn.r()], [gnb.r()])
            ACT(nea.f32(8), alog.f32(8), AF.Exp, [alog.r()], [nea.r()])
            TS("dve", nea.f32(8), nea.f32(8), -1.0, None, ALU.mult, None, [nea.r()], [nea.r()])
            NG = NB + 1
            wg = al(8 * 16, bf=True)
            wgv = wg.bf(8, 16)
            P.dma("pool", "gw", wgv, gdn_w_in[:, 4096:4112].rearrange("(k p) n -> p k n", p=128), writes=[wg.r()])
            GA = al(NG * 16)
            GAv = GA.f32(NG, 16)
            MS("dve", GA.f32(NG * 16), 0.0, [GA.r()])
            for b in range(NB):
                for k in range(8):
                    MM(banks[0][:, b * 16:(b + 1) * 16], xb[:, k, b * 128:(b + 1) * 128], wgv[:, k, :],
                       [wg.r()] + rx(r_xb, [k], tiles_of(b * 128, 128)), [rbank[0]], start=(k == 0), stop=(k == 7))
            for k in range(8):
                MM(banks[0][0:NS, NB * 16:NG * 16], xb[:, k, T:NT], wgv[:, k, :],
                   [wg.r(), r_xb[k][NTI - 1]], [rbank[0]], start=(k == 0), stop=(k == 7))
            CP("dve", GA.f32(NB * 16), banks[0][:, 0:NB * 16], [], [GA.r(), rbank[0]])
            CP("dve", GAv[0:NS, NB, :], banks[0][0:NS, NB * 16:NG * 16], [], [GA.r(), rbank[0]])
            beta = al(NG * 8); nbeta = al(NG * 8); gg = al(NG * 8); gcol = al(NG * 8); eg = al(NG * 8)
            beg = al(NG * 8); egl = al(NG * 8); gl = al(NG * 8); t1 = al(NG * 8); t2 = al(NG * 8)
            v8 = lambda b_: b_.f32(NG, 8)
            ACT(v8(beta), GAv[:, :, 0:8], AF.Sigmoid, [GA.r()], [beta.r()])
            TS("dve", v8(nbeta), v8(beta), -1.0, None, ALU.mult, None, [beta.r()], [nbeta.r()])
            TT("dve", v8(t1), GAv[:, :, 8:16], dtb.f32(8).unsqueeze(1).to_broadcast([128, NG, 8]), ALU.add, [GA.r(), dtb.r()], [t1.r()])
            TS("dve", v8(t2), v8(t1), -1.0, None, ALU.mult, None, [t1.r()], [t2.r()])
            TT("dve", v8(t2), v8(t2), v8(t1), ALU.max, [t1.r(), t2.r()], [t2.r()])
            ACT(v8(t2), v8(t2), AF.Exp, [t2.r()], [t2.r()], scale=-1.0)
            ACT(v8(t2), v8(t2), AF.Ln, [t2.r(), r_c], [t2.r()], bias=eps_tile(1.0))
            TS("dve", v8(t1), v8(t1), 0.0, None, ALU.max, None, [t1.r()], [t1.r()])
            TT("dve", v8(t1), v8(t1), v8(t2), ALU.add, [t1.r(), t2.r()], [t1.r()])
            TT("dve", v8(gg), v8(t1), nea.f32(8).unsqueeze(1).to_broadcast([128, NG, 8]), ALU.mult, [t1.r(), nea.r()], [gg.r()])
            MM(banks[0][:, 0:NB * 8], Um.f32(128), gg.f32(NB * 8), [Um.r(), gg.r()], [rbank[0]])
            MM(banks[0][:, 256:256 + NB * 8], onesf[:], gg.f32(NB * 8), [r_c, gg.r()], [rbank[0]])
            CP("dve", gcol.f32(NB * 8), banks[0][:, 0:NB * 8], [], [gcol.r(), rbank[0]])
            CP("dve", gcol.f32(NG, 8)[:, NB, :], gg.f32(NG, 8)[:, NB, :], [gg.r()], [gcol.r()])
            ACT(eg.f32(NG * 8), gcol.f32(NG * 8), AF.Exp, [gcol.r()], [eg.r()])
            TT("dve", beg.f32(NG * 8), beta.f32(NG * 8), eg.f32(NG * 8), ALU.mult, [beta.r(), eg.r()], [beg.r()])
            ACT(gl.f32(NB * 8), banks[0][:, 256:256 + NB * 8], AF.Exp, [], [gl.r(), rbank[0]])
            TT("dve", egl.f32(NB * 8), banks[0][:, 256:256 + NB * 8], gcol.f32(NB * 8), ALU.subtract, [gcol.r()], [egl.r(), rbank[0]])
            ACT(egl.f32(NB * 8), egl.f32(NB * 8), AF.Exp, [egl.r()], [egl.r()])
            projsT = al(4 * 8 * NS)
            pjT = projsT.f32(4, 8, NS)
            lastU = al(24 * 3)
            luv = lastU.f32(24, 3)
            base_bufs = list(bufs)
            mH = A.mark()
            del bufs[:]
            Ub = al(T + 8)
            accq = al(T); vf = al(T)
            knf = vf
            qnb = al(T, bf=True); knb = al(T, bf=True)
            szf = accq
            sqb_ap = Ub.ap[:, 8:8 + T // 2].bitcast(BF16)
            ogb_ap = Ub.ap[:, 8 + T // 2:8 + T].bitcast(BF16)
            rsq = [al(TTS)]
            Sf = al(128); Sb = al(128, bf=True)
            wo = al(1024, bf=True)
            Ubv = Ub.ap[:, 5:8 + T]
            MS("pool", Ubv[:, 0:3], 0.0, [Ub.r()])

            class Ch:
                pass
            chains = []
            for ci in range(G):
                c = Ch()
                c.Ug = al(128); c.e1 = al(128); c.e2 = al(128); c.egrow = al(128)
                c.attnT = al(128, bf=True); c.qgT = al(128, bf=True)
                c.Xf = al(128); c.XTf = al(128); c.X8 = al(128); c.Z8 = al(128)
                c.Y1 = c.Ug; c.Z1 = c.e1; c.Y2 = c.e2; c.E0 = al(128); c.E1 = c.egrow
                c.Xo = al(4 * 128, bf=True); c.Zo = al(128, bf=True)
                c.Xb = al(128, bf=True); c.XTb = al(128, bf=True)
                c.Db = [al(128, bf=True), al(128, bf=True)]; c.Eb = [al(128, bf=True), al(128, bf=True)]
                c.M1 = al(128, bf=True); c.M1p = al(128, bf=True)
                c.PTb = c.Eb[0]
                c.vb = Sub(c.Xo, 0, 64); c.kbg = Sub(c.Xo, 64, 128); c.kg = Sub(c.Xo, 128, 192)
                c.u = c.Xf; c.wkT = Sub(c.Xo, 192, 256); c.vnew = Sub(c.Zo, 0, 64); c.on = c.XTf; c.ssq = al(8)
                c.osb = c.X8
                c.bA = ci
                c.bN = ci
                chains.append(c)
            B = lambda b_: b_.bf(128)
            F = lambda b_: b_.f32(128)

            for h in range(8):
                slots = []
                for typ in range(4):
                    if typ % 2 == 0:
                        wb_, wk_ = wslot()
                        wv_ = wb_.bf(2, 8, 128)
                    col = typ * 1024 + h * 128
                    P.dma("pool", wk_, wv_[:, typ % 2], gdn_w_in[:, col:col + 128].rearrange("(k p) n -> p k n", p=128), writes=[wb_.r()])
                    slots.append((wb_, wv_[:, typ % 2]))
                    if typ % 2 == 1:
                        P.commit(wk_)
                if True:
                    P.dma("pool", "gwo", wo.bf(1024), gdn_w_out[h * 128:(h + 1) * 128, :], writes=[wo.r()])
                cntb = [0]

                def project(typ, sink):
                    wb_, wv_ = slots[typ]
                    for ti, (c0, n) in enumerate(ptiles):
                        bk = 4 + cntb[0] % 2
                        cntb[0] += 1
                        for k in range(8):
                            MM(banks[bk][:, 0:n], wv_[:, k, :], xb[:, k, c0:c0 + n], [wb_.r(), r_xb[k][ti]], [rbank[bk]], start=(k == 0), stop=(k == 7))
                        sink(bk, c0, n)
                    bk = 4 + cntb[0] % 2
                    cntb[0] += 1
                    for k in range(8):
                        MM(banks[bk][:, 0:NS], wv_[:, k, :], xb[:, k, T:NT], [wb_.r(), r_xb[k][NTI - 1]], [rbank[bk]], start=(k == 0), stop=(k == 7))
                    CP("dve", pjT[:, typ, h, :], banks[bk][:, 0:NS], [], [projsT.r(), rbank[bk]])

                for typ in range(3):
                    ch = typ * 8 + h
                    project(typ, lambda bk, c0, n: CP("act", Ubv[:, 3 + c0:3 + c0 + n], banks[bk][:, 0:n], [], [Ub.r(), rbank[bk]]))
                    CP("act", luv[:, ch, :], Ubv[:, T:T + 3], [Ub.r()], [lastU.r()])
                    acc = [accq, knf, vf][typ]
                    ce = "dve"
                    cw = lambda j_: prm[:, 16 + j_ * 24 + ch:16 + j_ * 24 + ch + 1]
                    TS(ce, acc.f32(T), Ubv[:, 3:3 + T], cw(3), None, ALU.mult, None, [Ub.r(), r_c], [acc.r()])
                    for j_ in range(3):
                        STT(ce, acc.f32(T), Ubv[:, j_:j_ + T], cw(j_), acc.f32(T), ALU.mult, ALU.add, [Ub.r(), r_c, acc.r()], [acc.r()])
                    ACT(acc.f32(T), acc.f32(T), AF.Silu, [acc.r()], [acc.r()])
                    if typ < 2:
                        ACT(sqb_ap, acc.f32(T), AF.Square, [acc.r()], [Ub.r()])
                        for ti, (c0, n) in enumerate(ptiles):
                            bk = 4 + cntb[0] % 2
                            cntb[0] += 1
                            rs_ = rsq[0]
                            MM(banks[bk][:, 0:n], onesb[:], sqb_ap[:, c0:c0 + n], [Ub.r(), r_c], [rbank[bk]])
                            CP("dve", rs_.f32(n), banks[bk][:, 0:n], [], [rs_.r(), rbank[bk]])
                            rsqrt_eps(rs_.f32(n), rs_.f32(n), RMS_EPS, rs_.r())
                            if typ == 0:
                                STT("dve", qnb.bf(T)[:, c0:c0 + n], acc.f32(T)[:, c0:c0 + n], 128.0 ** -0.5, rs_.f32(n), ALU.mult, ALU.mult,
                                    [acc.r(), rs_.r()], [qnb.r()])
                            else:
                                TT("dve", acc.f32(T)[:, c0:c0 + n], acc.f32(T)[:, c0:c0 + n], rs_.f32(n), ALU.mult, [acc.r(), rs_.r()], [acc.r()])
                        if typ == 1:
                            CP("pool", knb.bf(T), knf.f32(T), [knf.r()], [knb.r()])
                project(3, lambda bk, c0, n: ACT(szf.f32(T)[:, c0:c0 + n], banks[bk][:, 0:n], AF.Silu, [], [szf.r(), rbank[bk]]))
                MS("dve", F(Sf), 0.0, [Sf.r()])
                MS("dve", B(Sb), 0.0, [Sb.r()])

                def st_a(c, b):
                    bs = slice(b * 128, (b + 1) * 128)
                    ACT(F(c.Ug), F(Um), AF.Identity, [Um.r(), gg.r()], [c.Ug.r()], scale=gg.f32(NG, 8)[:, b, h:h + 1])
                    yield
                    bk = banks[c.bA]
                    MM(bk[:, 0:128], knb.bf(T)[:, bs], knb.bf(T)[:, bs], [knb.r()], [rbank[c.bA]])
                    MM(bk[:, 128:256], knb.bf(T)[:, bs], qnb.bf(T)[:, bs], [knb.r(), qnb.r()], [rbank[c.bA]])
                    MM(bk[:, 256:384], onesf[:], F(c.Ug), [r_c, c.Ug.r()], [rbank[c.bA]])
                    yield
                    gc_ = gcol.f32(NG, 8)[:, b, h:h + 1]
                    STT("dve", F(c.e1), bk[:, 256:384], gc_, F(maskS), ALU.subtract, ALU.max, [gcol.r(), maskS.r()], [c.e1.r(), rbank[c.bA]])
                    STT("dve", F(c.e2), bk[:, 256:384], gc_, F(negT), ALU.subtract, ALU.min, [gcol.r(), negT.r()], [c.e2.r(), rbank[c.bA]])
                    ACT(F(c.egrow), bk[:, 256:384], AF.Exp, [], [c.egrow.r(), rbank[c.bA]])
                    yield
                    ACT(F(c.e1), F(c.e1), AF.Exp, [c.e1.r()], [c.e1.r()], scale=-1.0)
                    ACT(F(c.e2), F(c.e2), AF.Exp, [c.e2.r()], [c.e2.r()])
                    yield
                    STT("dve", F(c.Xf), bk[:, 0:128], nbeta.f32(NG, 8)[:, b, h:h + 1], F(c.e1), ALU.mult, ALU.mult, [nbeta.r(), c.e1.r()], [c.Xf.r(), rbank[c.bA]])
                    TT("dve", B(c.attnT), bk[:, 128:256], F(c.e2), ALU.mult, [c.e2.r()], [c.attnT.r(), rbank[c.bA]])
                    TT("dve", B(c.qgT), qnb.bf(T)[:, bs], F(c.egrow), ALU.mult, [qnb.r(), c.egrow.r()], [c.qgT.r()])

                def st_b(c, b):
                    bk = banks[c.bN]
                    TR(bk[:, 0:128], F(c.Xf), ident[:], [c.Xf.r(), r_c], [rbank[c.bN]])
                    yield
                    CP("act", F(c.XTf), bk[:, 0:128], [], [c.XTf.r(), rbank[c.bN]])
                    CP("act", B(c.XTb), bk[:, 0:128], [], [c.XTb.r(), rbank[c.bN]])
                    CP("act", B(c.Xb), F(c.Xf), [c.Xf.r()], [c.Xb.r()])
                    yield
                    TT("dve", F(c.X8), F(c.Xf), bmv[:, 0, :], ALU.mult, [c.Xf.r(), bmask.r()], [c.X8.r()])
                    TT("dve", F(c.Z8), F(c.XTf), bmv[:, 0, :], ALU.mult, [c.XTf.r(), bmask.r()], [c.Z8.r()])
                    yield
                    TT("dve", F(c.E0), F(c.Z8), ident[:], ALU.add, [c.Z8.r(), r_c], [c.E0.r()])

                def st_base1(c, b):
                    bk = banks[c.bN]
                    MM(bk[:, 0:128], F(c.Z8), F(c.X8), [c.Z8.r(), c.X8.r()], [rbank[c.bN]])
                    MM(bk[:, 128:256], F(c.X8), F(c.Z8), [c.Z8.r(), c.X8.r()], [rbank[c.bN]])
                    yield
                    CP("act", F(c.Y1), bk[:, 0:128], [], [c.Y1.r(), rbank[c.bN]])
                    CP("act", F(c.Z1), bk[:, 128:256], [], [c.Z1.r(), rbank[c.bN]])
                    yield
                    MM(bk[:, 256:384], F(c.Y1), F(c.E0), [c.Y1.r(), c.E0.r()], [rbank[c.bN]])
                    yield
                    TT("dve", F(c.E1), F(c.E0), bk[:, 256:384], ALU.add, [c.E0.r()], [c.E1.r(), rbank[c.bN]])

                def st_base2(c, b):
                    bk = banks[c.bN]
                    MM(bk[:, 0:128], F(c.Z1), F(c.Y1), [c.Z1.r(), c.Y1.r()], [rbank[c.bN]])
                    yield
                    CP("act", F(c.Y2), bk[:, 0:128], [], [c.Y2.r(), rbank[c.bN]])
                    yield
                    MM(bk[:, 128:256], F(c.Y2), F(c.E1), [c.Y2.r(), c.E1.r()], [rbank[c.bN]])
                    yield
                    TT("dve", F(c.E0), F(c.E1), bk[:, 128:256], ALU.add, [c.E1.r()], [c.E0.r(), rbank[c.bN]])
                    yield
                    TR(bk[:, 256:384], F(c.E0), ident[:], [c.E0.r(), r_c], [rbank[c.bN]])
                    yield
                    CP("act", B(c.Db[0]), bk[:, 256:384], [], [c.Db[0].r(), rbank[c.bN]])
                    CP("act", B(c.Eb[0]), F(c.E0), [c.E0.r()], [c.Eb[0].r()])

                def st_merge(l):
                    def f(c, b):
                        bk = banks[c.bN]
                        Dp, Ep = c.Db[l % 2], c.Eb[l % 2]
                        Dn, En = c.Db[(l + 1) % 2], c.Eb[(l + 1) % 2]
                        mk = bmv[:, 1 + l, :]
                        if l < 3:
                            MM(bk[:, 0:128], B(c.XTb), B(Dp), [c.XTb.r(), Dp.r()], [rbank[c.bN]])
                        MM(bk[:, 128:256], B(c.Xb), B(Ep), [c.Xb.r(), Ep.r()], [rbank[c.bN]])
                        yield
                        if l < 3:
                            TT("dve", B(c.M1), bk[:, 0:128], mk, ALU.mult, [bmask.r()], [c.M1.r(), rbank[c.bN]])
                        TT("dve", B(c.M1p), bk[:, 128:256], mk, ALU.mult, [bmask.r()], [c.M1p.r(), rbank[c.bN]])
                        yield
                        if l < 3:
                            MM(bk[:, 256:384], identb[:], B(Dp), [r_c, Dp.r()], [rbank[c.bN]], start=True, stop=False)
                            MM(bk[:, 256:384], B(Ep), B(c.M1), [Ep.r(), c.M1.r()], [rbank[c.bN]], start=False, stop=True)
                        MM(bk[:, 384:512], identb[:], B(Ep), [r_c, Ep.r()], [rbank[c.bN]], start=True, stop=False)
                        MM(bk[:, 384:512], B(Dp), B(c.M1p), [Dp.r(), c.M1p.r()], [rbank[c.bN]], start=False, stop=True)
                        yield
                        if l < 3:
                            CP("act", B(Dn), bk[:, 256:384], [], [Dn.r(), rbank[c.bN]])
                        CP("act", B(En), bk[:, 384:512], [], [En.r(), rbank[c.bN]])
                    return f

                def st_c(c, b):
                    bs = slice(b * 128, (b + 1) * 128)
                    bk = banks[c.bA]
                    TR(bfps(c.bA, 0, 128), knb.bf(T)[:, bs], identb[:], [knb.r(), r_c], [rbank[c.bA]])
                    TR(bk[:, 128:256], vf.f32(T)[:, bs], ident[:], [vf.r(), r_c], [rbank[c.bA]])
                    yield
                    ACT(B(c.vb), bk[:, 128:256], AF.Identity, [beta.r()], [c.vb.r(), rbank[c.bA]], scale=beta.f32(NG, 8)[:, b, h:h + 1])
                    ACT(B(c.kbg), bfps(c.bA, 0, 128), AF.Identity, [beg.r()], [c.kbg.r(), rbank[c.bA]], scale=beg.f32(NG, 8)[:, b, h:h + 1])
                    ACT(B(c.kg), bfps(c.bA, 0, 128), AF.Identity, [egl.r()], [c.kg.r(), rbank[c.bA]], scale=egl.f32(NG, 8)[:, b, h:h + 1])
                    yield
                    MM(bk[:, 256:384], B(c.PTb), B(c.vb), [c.PTb.r(), c.vb.r()], [rbank[c.bA]])
                    MM(bk[:, 384:512], B(c.kbg), B(c.PTb), [c.PTb.r(), c.kbg.r()], [rbank[c.bA]])
                    yield
                    CP("act", F(c.u), bk[:, 256:384], [], [c.u.r(), rbank[c.bA]])
                    CP("dve", B(c.wkT), bk[:, 384:512], [], [c.wkT.r(), rbank[c.bA]])

                def recur(c, b):
                    b6, b7 = banks[6], banks[7]
                    MM(b6[:, 0:128], B(c.wkT), B(Sb), [c.wkT.r(), Sb.r()], [rbank[6]])
                    TT("dve", B(c.vnew), F(c.u), b6[:, 0:128], ALU.subtract, [c.u.r()], [c.vnew.r(), rbank[6]])
                    MM(b7[:, 0:128], B(c.qgT), B(Sb), [c.qgT.r(), Sb.r()], [rbank[7]], start=True, stop=False)
                    MM(b7[:, 0:128], B(c.attnT), B(c.vnew), [c.attnT.r(), c.vnew.r()], [rbank[7]], start=False, stop=True)
                    MM(b6[:, 128:256], B(c.kg), B(c.vnew), [c.kg.r(), c.vnew.r()], [rbank[6]])
                    STT("dve", F(Sf), F(Sf), gl.f32(NB, 8)[:, b, h:h + 1], b6[:, 128:256], ALU.mult, ALU.add, [gl.r()], [Sf.r(), rbank[6]])
                    CP("act", B(Sb), F(Sf), [Sf.r()], [Sb.r()])
                    CP("act", F(c.osb), b7[:, 0:128], [], [c.osb.r(), rbank[7]])

                def recur_post(c, b):
                    bs = slice(b * 128, (b + 1) * 128)
                    ACT(F(c.on), F(c.osb), AF.Square, [c.osb.r()], [c.on.r(), c.ssq.r()], accum=c.ssq.f32(1))
                    rsqrt_eps(c.ssq.f32(1), c.ssq.f32(1), RMS_EPS * 128.0, c.ssq.r())
                    STT("dve", F(c.on), F(c.osb), c.ssq.f32(1), F(gnb), ALU.mult, ALU.mult, [c.osb.r(), c.ssq.r(), gnb.r()], [c.on.r()])
                    TR(banks[c.bN][:, 384:512], F(c.on), ident[:], [c.on.r(), r_c], [rbank[c.bN]])
                    TT("dve", ogb_ap[:, bs], banks[c.bN][:, 384:512], szf.f32(T)[:, bs], ALU.mult, [szf.r()], [Ub.r(), rbank[c.bN]])

                stages = [st_a, st_b, st_base1, st_base2] + [st_merge(l) for l in range(4)] + [st_c]
                pend_ = []
                for g0 in range(0, NB if "gdn_noblk" not in DBG else 0, G):
                    grp = [(chains[i], g0 + i) for i in range(min(G, NB - g0))]
                    for stg_ in stages:
                        gens_ = [stg_(c, b) for c, b in grp]
                        while gens_:
                            nx_ = []
                            for gn_ in gens_:
                                try:
                                    next(gn_)
                                    nx_.append(gn_)
                                except StopIteration:
                                    pass
                            gens_ = nx_
                    for c, b in grp:
                        recur(c, b)
                        if pend_:
                            recur_post(*pend_.pop())
                        pend_.append((c, b))
                    if pend_:
                        recur_post(*pend_.pop())
                if pend_:
                    recur_post(*pend_.pop())
                for dc in range(8):
                    for ti, (c0, n) in enumerate(ptiles):
                        bk = 4 + cntb[0] % 2
                        cntb[0] += 1
                        MM(banks[bk][:, 0:n], wo.bf(1024)[:, dc * 128:(dc + 1) * 128], ogb_ap[:, c0:c0 + n], [wo.r(), Ub.r()], [rbank[bk]])
                        xv = xf[:, dc, c0:c0 + n]
                        if h == 0:
                            STT("dve", xv, xv, ALPHA, banks[bk][:, 0:n], ALU.mult, ALU.add, [r_xf[dc][ti]], [r_xf[dc][ti], rbank[bk]])
                        else:
                            TT("dve", xv, xv, banks[bk][:, 0:n], ALU.add, [r_xf[dc][ti]], [r_xf[dc][ti], rbank[bk]])
                P.dma("sp", "gsp", o_gdn_p[h], F(Sf), reads=[Sf.r()])
            A.release(mH, list(bufs))
            del bufs[:]
            cst_ = A.f32(3072)
            for q4 in range(6):
                for i4 in range(4):
                    ch = q4 * 4 + i4
                    TR(banks[4][0:3, i4 * 128:(i4 + 1) * 128], luv[:, ch, :], ident[:], [lastU.r(), r_c], [rbank[4]])
                CP("dve", cst_.f32(3072)[0:3, q4 * 512:(q4 + 1) * 512], banks[4][0:3, :], [], [cst_.r(), rbank[4]])
            P.dma("sp", "gcp", o_conv_p, cst_.f32(3072)[0:3, :], reads=[cst_.r()])
            A.release(mH, [cst_])
            mS = A.mark()
            if "gdn_nosamp" in DBG:
                A.release(m, base_bufs)
                return
            sb_ = []

            def als(n, bf=False):
                b = A.bf(n) if bf else A.f32(n)
                sb_.append(b)
                return b
            projs = als(4096)
            for typ in range(4):
                for h2 in range(2):
                    bk = (typ * 2 + h2) % 2
                    for i4 in range(4):
                        hh = h2 * 4 + i4
                        TR(banks[bk][0:NS, i4 * 128:(i4 + 1) * 128], pjT[:, typ, hh, :], ident[:], [projsT.r(), r_c], [rbank[bk]])
                    CP("dve", projs.f32(4096)[0:NS, typ * 1024 + h2 * 512:typ * 1024 + (h2 + 1) * 512], banks[bk][0:NS, :], [], [projs.r(), rbank[bk]])
            P.dma("sp", "gcv", o_conv_s[:, 0:2, :], sconv[:, 1:3, :])
            P.dma("sp", "gcv", o_conv_s[:, 2, :], projs.f32(4096)[0:NS, 0:3072], reads=[projs.r()])
            P.commit("gcv")
            qkv = als(3072)
            zs = als(1024)
            mC = A.mark()
            PW = 256
            ext = A.f32(4 * PW); cwb = A.f32(4 * PW); prod = A.f32(4 * PW)
            for pc in range(3072 // PW):
                cs_ = slice(pc * PW, (pc + 1) * PW)
                P.dma("sp", "gsl", ext.f32(4, PW)[0:NS, 0:3, :], sconv[:, :, cs_], writes=[ext.r()])
                P.dma("sp", "gsl", cwb.f32(4, PW)[0:NS], gdn_conv_w[:, cs_].partition_broadcast(NS), writes=[cwb.r()])
                P.commit("gsl")
                CP("dve", ext.f32(4, PW)[0:NS, 3, :], projs.f32(4096)[0:NS, cs_], [projs.r()], [ext.r()])
                TT("dve", prod.f32(4, PW)[0:NS], ext.f32(4, PW)[0:NS], cwb.f32(4, PW)[0:NS], ALU.mult, [ext.r(), cwb.r()], [prod.r()])
                RED("dve", qkv.f32(3072)[0:NS, cs_], prod.f32(4, PW)[0:NS].rearrange("p j c -> p c j"), [prod.r()], [qkv.r()])
            A.release(mC, [ext, cwb, prod])
            ACT(qkv.f32(3072)[0:NS], qkv.f32(3072)[0:NS], AF.Silu, [qkv.r()], [qkv.r()])
            ACT(zs.f32(1024)[0:NS], projs.f32(4096)[0:NS, 3072:4096], AF.Silu, [projs.r()], [zs.r()])
            q3 = qkv.f32(3, 8, 128)[0:NS, 0]
            k3 = qkv.f32(3, 8, 128)[0:NS, 1]
            v3 = qkv.f32(3, 8, 128)[0:NS, 2]
            tmp = als(1024); ss = als(16)
            t3 = tmp.f32(8, 128)[0:NS]
            for (x3, scl) in ((q3, 128.0 ** -0.5), (k3, 1.0)):
                TT("dve", t3, x3, x3, ALU.mult, [qkv.r()], [tmp.r()])
                RED("dve", ss.f32(8)[0:NS], t3, [tmp.r()], [ss.r()])
                rsqrt_eps(ss.f32(8)[0:NS], ss.f32(8)[0:NS], RMS_EPS, ss.r())
                STT("dve", x3, x3, scl, ss.f32(8)[0:NS].unsqueeze(2).to_broadcast([NS, 8, 128]), ALU.mult, ALU.mult, [ss.r()], [qkv.r()])
            qk = als(8)
            TT("dve", t3, q3, k3, ALU.mult, [qkv.r()], [tmp.r()])
            RED("dve", qk.f32(8)[0:NS], t3, [tmp.r()], [qk.r()])
            kTs = als(8 * NS); qTs = als(8 * NS)
            for (x3, dst, bk) in ((k3, kTs, 4), (q3, qTs, 5)):
                for hh in range(8):
                    TR(banks[bk][:, hh * NS:(hh + 1) * NS], x3[:, hh, :], ident[0:NS, 0:NS], [qkv.r(), r_c], [rbank[bk]])
                CP("dve", dst.f32(8 * NS), banks[bk][:, 0:8 * NS], [], [dst.r(), rbank[bk]])
            dlt = als(NS * NS)
            P.dma("sp", "gsm", dlt.f32(NS, NS), cd["delta"], writes=[dlt.r()])
            dcol = als(NS)
            P.dma("sp", "gsm", dcol.f32(NS), cd["dcol"], writes=[dcol.r()])
            P.commit("gsm")
            KmL = [als(NS * NS), als(NS * NS)]; QmL = [als(NS * NS), als(NS * NS)]
            Sp = [als(128) for _ in range(4)]
            ci_ = 0
            for hh in range(8):
                Km = KmL[hh % 2]; Qm = QmL[hh % 2]
                for (src, dst) in ((kTs, Km), (qTs, Qm)):
                    TT("dve", dst.f32(NS, NS), src.f32(8, NS)[:, hh, :].unsqueeze(1).to_broadcast([128, NS, NS]),
                       dlt.f32(NS, NS), ALU.mult, [src.r(), dlt.r()], [dst.r()])
                for s in range(NS):
                    sp_ = Sp[ci_ % 4]
                    P.dma("sp", "gsq%d" % (ci_ % 4), sp_.f32(128), sgdn[s, hh], writes=[sp_.r()])
                    ci_ += 1
                    MM(banks[hh // 4][0:NS, (hh % 4) * 128:(hh % 4 + 1) * 128], Km.f32(NS, NS)[:, s, :], sp_.f32(128),
                       [Km.r(), sp_.r()], [rbank[hh // 4]], start=(s == 0), stop=(s == NS - 1))
                    MM(banks[2 + hh // 4][0:NS, (hh % 4) * 128:(hh % 4 + 1) * 128], Qm.f32(NS, NS)[:, s, :], sp_.f32(128),
                       [Qm.r(), sp_.r()], [rbank[2 + hh // 4]], start=(s == 0), stop=(s == NS - 1))
            KS = als(1024); QS = als(1024)
            for half in range(2):
                CP("dve", KS.f32(1024)[0:NS, half * 512:(half + 1) * 512], banks[half][0:NS, :], [], [KS.r(), rbank[half]])
                CP("dve", QS.f32(1024)[0:NS, half * 512:(half + 1) * 512], banks[2 + half][0:NS, :], [], [QS.r(), rbank[2 + half]])
            bc8 = lambda b_, col: b_.f32(NG, 8)[0:NS, col, :].unsqueeze(2).to_broadcast([NS, 8, 128])
            KS3 = KS.f32(8, 128)[0:NS]; QS3 = QS.f32(8, 128)[0:NS]
            vn = als(1024)
            vn3 = vn.f32(8, 128)[0:NS]
            TT("dve", KS3, KS3, bc8(eg, NB), ALU.mult, [eg.r()], [KS.r()])
            TT("dve", vn3, v3, KS3, ALU.subtract, [qkv.r(), KS.r()], [vn.r()])
            TT("dve", vn3, vn3, bc8(beta, NB), ALU.mult, [beta.r()], [vn.r()])
            TT("dve", QS3, QS3, bc8(eg, NB), ALU.mult, [eg.r()], [QS.r()])
            TT("dve", t3, vn3, qk.f32(8)[0:NS].unsqueeze(2).to_broadcast([NS, 8, 128]), ALU.mult, [vn.r(), qk.r()], [tmp.r()])
            TT("dve", QS3, QS3, t3, ALU.add, [tmp.r()], [QS.r()])
            TT("dve", t3, QS3, QS3, ALU.mult, [QS.r()], [tmp.r()])
            RED("dve", ss.f32(8)[0:NS], t3, [tmp.r()], [ss.r()])
            rsqrt_eps(ss.f32(8)[0:NS], ss.f32(8)[0:NS], RMS_EPS, ss.r(), scale=1.0 / 128.0)
            TT("dve", QS3, QS3, ss.f32(8)[0:NS].unsqueeze(2).to_broadcast([NS, 8, 128]), ALU.mult, [ss.r()], [QS.r()])
            TT("dve", QS3, QS3, gn.f32(128)[0:NS].unsqueeze(1).to_broadcast([NS, 8, 128]), ALU.mult, [gn.r()], [QS.r()])
            TT("dve", QS3, QS3, zs.f32(8, 128)[0:NS], ALU.mult, [zs.r()], [QS.r()])
            ogsT = als(8 * NS, bf=True)
            for hh in range(8):
                TR(banks[4][:, hh * NS:(hh + 1) * NS], QS3[:, hh, :], ident[0:NS, 0:NS], [QS.r(), r_c], [rbank[4]])
            CP("dve", ogsT.bf(8 * NS), banks[4][:, 0:8 * NS], [], [ogsT.r(), rbank[4]])
            out_proj_samples(gdn_w_out, 8, ogsT.bf(8, NS), ogsT.r())
            Rm = als(NS * 8)
            TT("dve", Rm.f32(NS, 8)[0:NS], dcol.f32(NS)[0:NS].unsqueeze(2).to_broadcast([NS, NS, 8]),
               eg.f32(NG, 8)[0:NS, NB, :].unsqueeze(1).to_broadcast([NS, NS, 8]), ALU.mult, [dcol.r(), eg.r()], [Rm.r()])
            EGb = als(NS * 8)
            MM(banks[4][:, 0:NS * 8], onesf[0:NS, :], Rm.f32(NS * 8)[0:NS], [Rm.r(), r_c], [rbank[4]])
            CP("dve", EGb.f32(NS * 8), banks[4][:, 0:NS * 8], [], [EGb.r(), rbank[4]])
            Vm = [als(1024), als(1024)]
            Sn = [als(128) for _ in range(4)]
            ci_ = 0
            for s in range(NS):
                vm_ = Vm[s % 2]
                TS("dve", vm_.f32(1024)[0:NS], vn.f32(1024)[0:NS], dcol.f32(NS)[0:NS, s:s + 1], None, ALU.mult, None, [vn.r(), dcol.r()], [vm_.r()])
                for hh in range(8):
                    sp_ = Sp[ci_ % 4]; sn_ = Sn[ci_ % 4]
                    bk = 4 + ci_ % 4
                    P.dma("sp", "gsq%d" % (ci_ % 4), sp_.f32(128), sgdn[s, hh], writes=[sp_.r()])
                    MM(banks[bk][:, 0:128], k3[:, hh, :], vm_.f32(8, 128)[0:NS, hh, :], [qkv.r(), vm_.r()], [rbank[bk]])
                    STT("dve", sn_.f32(128), sp_.f32(128), EGb.f32(NS, 8)[:, s, hh:hh + 1], banks[bk][:, 0:128], ALU.mult, ALU.add,
                        [sp_.r(), EGb.r()], [sn_.r(), rbank[bk]])
                    P.dma("pool", "gst%d" % (ci_ % 4), o_gdn_s[s, hh], sn_.f32(128), reads=[sn_.r()])
                    ci_ += 1
            A.release(mS, sb_)
            A.release(m, base_bufs)

        def ret_mixer():
            G = 2
            m = A.mark()
            bufs = []

            def al(n, bf=False):
                b = A.bf(n) if bf else A.f32(n)
                bufs.append(b)
                return b
            cos2 = al(NT); sin2 = al(NT); decT = al(128); xi = al(128); zeta = al(8); dcol = al(NS); dlt = al(NS * NS)
            P.dma("sp", "rc", cos2.f32(NT), cd["cos2"], writes=[cos2.r()])
            P.dma("sp", "rc", sin2.f32(NT), cd["sin2"], writes=[sin2.r()])
            P.dma("sp", "rc", zeta.f32(8), cd["zeta"], writes=[zeta.r()])
            P.dma("sp", "rc", dcol.f32(NS), cd["dcol"], writes=[dcol.r()])
            P.dma("sp", "rc", dlt.f32(NS, NS), cd["delta"], writes=[dlt.r()])
            P.commit("rc")
            SC = 128.0 ** -0.5
            qrb = al(NT, bf=True); krb = al(NT, bf=True); krf = al(NT); qsf = al(NS)
            t1 = [al(TTS)]; t2 = [al(TTS)]
            vtok = al(NB * 256, bf=True)
            vtv = vtok.bf(NB, 256)
            ogT = al(2 * NT, bf=True)
            ogv = ogT.bf(2, NT)
            Sf = al(256); Sb = al(256, bf=True)
            v_s = al(256); sg_s = al(256); kts = al(128); Qm = al(NS * NS); qk = al(8); tmp_s = al(256); o_s = al(256)
            Sp = [al(256) for _ in range(2)]
            Vm = [al(256) for _ in range(2)]; Sn = [al(256) for _ in range(2)]
            stat = [al(8) for _ in range(3)]

            class Ch:
                pass
            chains = []
            for ci in range(2 * G):
                c = Ch()
                c.attnT = al(128, bf=True); c.qxT = al(128, bf=True); c.kz = al(128, bf=True)
                c.sg = al(256); c.on = al(256); c.osb = al(256)
                c.bA = (0, 1, 4, 5)[ci]
                chains.append(c)
            B = lambda b_: b_.bf(128)

            def gnorm_gate(o_ap, np_, on_ap, on_res, sg_ap, sg_res, o_reads, o_writes):
                sm, sq_, mm = stat[0], stat[1], stat[2]
                ACT(on_ap, o_ap, AF.Identity, o_reads, [on_res, sm.r()] + o_writes, accum=sm.f32(1)[0:np_])
                ACT(on_ap, o_ap, AF.Square, o_reads, [on_res, sq_.r()] + o_writes, accum=sq_.f32(1)[0:np_])
                TS("dve", sm.f32(1)[0:np_], sm.f32(1)[0:np_], 1.0 / 256.0, None, ALU.mult, None, [sm.r()], [sm.r()])
                TT("dve", mm.f32(1)[0:np_], sm.f32(1)[0:np_], sm.f32(1)[0:np_], ALU.mult, [sm.r()], [mm.r()])
                STT("dve", sq_.f32(1)[0:np_], sq_.f32(1)[0:np_], 1.0 / 256.0, mm.f32(1)[0:np_], ALU.mult, ALU.subtract, [sq_.r(), mm.r()], [sq_.r()])
                rsqrt_eps(sq_.f32(1)[0:np_], sq_.f32(1)[0:np_], LN_EPS, sq_.r())
                TS("dve", on_ap, o_ap, sm.f32(1)[0:np_], sq_.f32(1)[0:np_], ALU.subtract, ALU.mult, o_reads + [sm.r(), sq_.r()], [on_res] + o_writes)
                TT("pool", on_ap, on_ap, sg_ap, ALU.mult, [on_res, sg_res], [on_res])

            for h in range(8):
                slots = []
                for xi_, base in enumerate((0, 1024)):
                    wb_, wk_ = wslot()
                    wv_ = wb_.bf(2, 8, 128)
                    c0_ = base + h * 128
                    r3 = lambda a, b_: ret_w_in[:, a:b_].rearrange("(k p) n -> p k n", p=128)
                    P.dma("pool", wk_, wv_[:, 0], r3(c0_, c0_ + 128), writes=[wb_.r()])
                    P.dma("pool", wk_, wv_[:, 1, :, 0:64], r3(c0_ + 64, c0_ + 128), writes=[wb_.r()])
                    P.dma("pool", wk_, wv_[:, 1, :, 64:128], r3(c0_, c0_ + 64), writes=[wb_.r()])
                    P.commit(wk_)
                    slots.append((wb_, wv_))
                P.dma("sp", "rdx", decT.f32(128), cd["decT"][:, h, :], writes=[decT.r()])
                P.dma("sp", "rdx", xi.f32(128), cd["xi"][:, h, :], writes=[xi.r()])
                P.commit("rdx")
                wv_t, wkv_ = wslot()
                P.dma("pool", wkv_, wv_t.bf(8, 256), ret_w_in[:, 2048 + h * 256:2048 + (h + 1) * 256].rearrange("(k p) n -> p k n", p=128), writes=[wv_t.r()])
                P.commit(wkv_)
                wg_t, wkg_ = wslot()
                P.dma("pool", wkg_, wg_t.bf(8, 256), ret_w_in[:, 4096 + h * 256:4096 + (h + 1) * 256].rearrange("(k p) n -> p k n", p=128), writes=[wg_t.r()])
                P.commit(wkg_)
                cn = 0
                for xi_ in range(2):
                    wb_, wv_ = slots[xi_]
                    for ti, (c0, n) in enumerate(tiles):
                        a1, a2 = t1[0], t2[0]
                        ba_, bb_ = 4 + 2 * (cn % 2), 5 + 2 * (cn % 2)
                        cn += 1
                        for k in range(8):
                            MM(banks[ba_][:, 0:n], wv_[:, 0, k, :], xb[:, k, c0:c0 + n], [wb_.r(), r_xb[k][ti]], [rbank[ba_]], start=(k == 0), stop=(k == 7))
                        for k in range(8):
                            MM(banks[bb_][:, 0:n], wv_[:, 1, k, :], xb[:, k, c0:c0 + n], [wb_.r(), r_xb[k][ti]], [rbank[bb_]], start=(k == 0), stop=(k == 7))
                        TT("dve", a1.f32(n), banks[ba_][:, 0:n], cos2.f32(NT)[:, c0:c0 + n], ALU.mult, [cos2.r()], [a1.r(), rbank[ba_]])
                        TT("dve", a2.f32(n), banks[bb_][:, 0:n], sin2.f32(NT)[:, c0:c0 + n], ALU.mult, [sin2.r()], [a2.r(), rbank[bb_]])
                        if xi_ == 0:
                            TT("pool", qrb.bf(NT)[:, c0:c0 + n], a1.f32(n), a2.f32(n), ALU.add, [a1.r(), a2.r()], [qrb.r()])
                            if ti == NTI - 1:
                                TT("pool", qsf.f32(NS), a1.f32(n), a2.f32(n), ALU.add, [a1.r(), a2.r()], [qsf.r()])
                        else:
                            TT("pool", krf.f32(NT)[:, c0:c0 + n], a1.f32(n), a2.f32(n), ALU.add, [a1.r(), a2.r()], [krf.r()])
                CP("pool", krb.bf(NT), krf.f32(NT), [krf.r()], [krb.r()])
                for b in range(NB):
                    bs = slice(b * 128, (b + 1) * 128)
                    bv_ = 6 + b % 2
                    for k in range(8):
                        MM(banks[bv_][:, 0:256], xb[:, k, bs], wv_t.bf(8, 256)[:, k, :], [wv_t.r()] + rx(r_xb, [k], tiles_of(b * 128, 128)), [rbank[bv_]],
                           start=(k == 0), stop=(k == 7))
                    CP("act", vtv[:, b, :], banks[bv_][:, 0:256], [], [vtok.r(), rbank[bv_]])
                for k in range(8):
                    MM(banks[6][0:NS, 256:512], xb[:, k, T:NT], wv_t.bf(8, 256)[:, k, :], [wv_t.r(), r_xb[k][NTI - 1]], [rbank[6]], start=(k == 0), stop=(k == 7))
                CP("act", v_s.f32(256)[0:NS], banks[6][0:NS, 256:512], [], [v_s.r(), rbank[6]])
                MS("dve", Sf.f32(256), 0.0, [Sf.r()])
                MS("dve", Sb.bf(256), 0.0, [Sb.r()])

                def st_a(c, b):
                    bs = slice(b * 128, (b + 1) * 128)
                    bk = banks[c.bA]
                    MM(bk[:, 0:128], krb.bf(NT)[:, bs], qrb.bf(NT)[:, bs], [krb.r(), qrb.r()], [rbank[c.bA]])
                    TR(bk[:, 128:256], krf.f32(NT)[:, bs], ident[:], [krf.r(), r_c], [rbank[c.bA]])
                    for k in range(8):
                        MM(bk[:, 256:512], xb[:, k, bs], wg_t.bf(8, 256)[:, k, :], [wg_t.r()] + rx(r_xb, [k], tiles_of(b * 128, 128)), [rbank[c.bA]],
                           start=(k == 0), stop=(k == 7))
                    yield
                    TT("dve", B(c.attnT), bk[:, 0:128], decT.f32(128), ALU.mult, [decT.r()], [c.attnT.r(), rbank[c.bA]])
                    ACT(B(c.kz), bk[:, 128:256], AF.Identity, [zeta.r()], [c.kz.r(), rbank[c.bA]], scale=zeta.f32(8)[:, h:h + 1])
                    ACT(c.sg.f32(256), bk[:, 256:512], AF.Silu, [], [c.sg.r(), rbank[c.bA]])
                    TT("pool", B(c.qxT), qrb.bf(NT)[:, bs], xi.f32(128), ALU.mult, [qrb.r(), xi.r()], [c.qxT.r()])

                def recur(c, b):
                    MM(banks[2][:, 0:256], B(c.qxT), Sb.bf(256), [c.qxT.r(), Sb.r()], [rbank[2]], start=True, stop=False)
                    MM(banks[2][:, 0:256], B(c.attnT), vtv[:, b, :], [c.attnT.r(), vtok.r()], [rbank[2]], start=False, stop=True)
                    MM(banks[3][:, 0:256], B(c.kz), vtv[:, b, :], [c.kz.r(), vtok.r()], [rbank[3]])
                    STT("dve", Sf.f32(256), Sf.f32(256), cst["gC"][h], banks[3][:, 0:256], ALU.mult, ALU.add, [], [Sf.r(), rbank[3]])
                    CP("act", Sb.bf(256), Sf.f32(256), [Sf.r()], [Sb.r()])
                    CP("act", c.osb.f32(256), banks[2][:, 0:256], [], [c.osb.r(), rbank[2]])

                def recur_post(c, b):
                    bs = slice(b * 128, (b + 1) * 128)
                    gnorm_gate(c.osb.f32(256), 128, c.on.f32(256), c.on.r(), c.sg.f32(256), c.sg.r(), [c.osb.r()], [])
                    for cc in range(2):
                        TR(banks[7][:, 256 + cc * 128:256 + (cc + 1) * 128], c.on.f32(256)[:, cc * 128:(cc + 1) * 128], ident[:], [c.on.r(), r_c], [rbank[7]])
                    CP("act", ogv[:, :, bs], banks[7][:, 256:512].rearrange("p (c t) -> p c t", c=2, t=128), [], [ogT.r(), rbank[7]])

                pend_ = []
                for g0 in range(0, NB if "ret_noblk" not in DBG else 0, G):
                    cs_ = ((g0 // G) % 2) * G
                    grp = [(chains[cs_ + i], g0 + i) for i in range(min(G, NB - g0))]
                    gens_ = [st_a(c, b) for c, b in grp]
                    first_ = True
                    while gens_:
                        nx_ = []
                        for gn_ in gens_:
                            try:
                                next(gn_)
                                nx_.append(gn_)
                            except StopIteration:
                                pass
                        gens_ = nx_
                        if first_:
                            for cb_ in pend_:
                                recur_post(*cb_)
                            first_ = False
                    for c, b in grp:
                        recur(c, b)
                    pend_ = list(grp)
                for cb_ in pend_:
                    recur_post(*cb_)
                P.dma("sp", "rsp", o_ret_p[h], Sf.f32(256), reads=[Sf.r()])
                for k in range(8 if "ret_nosamp" not in DBG else 0):
                    MM(banks[6][0:NS, 0:256], xb[:, k, T:NT], wg_t.bf(8, 256)[:, k, :], [wg_t.r(), r_xb[k][NTI - 1]], [rbank[6]], start=(k == 0), stop=(k == 7))
                if "ret_nosamp" not in DBG:
                    ACT(sg_s.f32(256)[0:NS], banks[6][0:NS, 0:256], AF.Silu, [], [sg_s.r(), rbank[6]])
                    ks_ = krf.f32(NT)[:, T:NT]
                    TR(banks[7][0:NS, 0:128], ks_, ident[:], [krf.r(), r_c], [rbank[7]])
                    CP("dve", kts.f32(128)[0:NS], banks[7][0:NS, 0:128], [], [kts.r(), rbank[7]])
                    TT("dve", tmp_s.f32(NS), qsf.f32(NS), ks_, ALU.mult, [qsf.r(), krf.r()], [tmp_s.r()])
                    MM(banks[7][0:NS, 128:129], tmp_s.f32(NS), onesf[:, 0:1], [tmp_s.r(), r_c], [rbank[7]])
                    TS("dve", qk.f32(1)[0:NS], banks[7][0:NS, 128:129], SC, None, ALU.mult, None, [], [qk.r(), rbank[7]])
                    TT("dve", Qm.f32(NS, NS), qsf.f32(NS).unsqueeze(1).to_broadcast([128, NS, NS]), dlt.f32(NS, NS), ALU.mult, [qsf.r(), dlt.r()], [Qm.r()])
                    for s in range(NS):
                        sp_ = Sp[s % 2]; vm_ = Vm[s % 2]; sn_ = Sn[s % 2]
                        bk = 4 + s % 2
                        P.dma("sp", "rsq%d" % (s % 2), sp_.f32(256), sret[s, h], writes=[sp_.r()])
                        MM(banks[6][0:NS, 256:512], Qm.f32(NS, NS)[:, s, :], sp_.f32(256), [Qm.r(), sp_.r()], [rbank[6]], start=(s == 0), stop=(s == NS - 1))
                        TS("dve", vm_.f32(256)[0:NS], v_s.f32(256)[0:NS], dcol.f32(NS)[0:NS, s:s + 1], SC, ALU.mult, ALU.mult, [v_s.r(), dcol.r()], [vm_.r()])
                        MM(banks[bk][:, 0:256], kts.f32(128)[0:NS], vm_.f32(256)[0:NS], [kts.r(), vm_.r()], [rbank[bk]])
                        STT("dve", sn_.f32(256), sp_.f32(256), cst["gam"][h], banks[bk][:, 0:256], ALU.mult, ALU.add, [sp_.r()], [sn_.r(), rbank[bk]])
                        P.dma("pool", "rst%d" % (s % 2), o_ret_s[s, h], sn_.f32(256), reads=[sn_.r()])
                    TS("dve", tmp_s.f32(256)[0:NS], v_s.f32(256)[0:NS], qk.f32(1)[0:NS], None, ALU.mult, None, [v_s.r(), qk.r()], [tmp_s.r()])
                    STT("dve", o_s.f32(256)[0:NS], banks[6][0:NS, 256:512], cst["gam"][h], tmp_s.f32(256)[0:NS], ALU.mult, ALU.add, [tmp_s.r()], [o_s.r(), rbank[6]])
                    gnorm_gate(o_s.f32(256)[0:NS], NS, tmp_s.f32(256)[0:NS], tmp_s.r(), sg_s.f32(256)[0:NS], sg_s.r(), [o_s.r()], [])
                    for cc in range(2):
                        TR(banks[7][:, 256 + cc * NS:256 + (cc + 1) * NS], tmp_s.f32(256)[0:NS, cc * 128:(cc + 1) * 128], ident[0:NS, 0:NS], [tmp_s.r(), r_c], [rbank[7]])
                    CP("dve", ogv[:, :, T:NT], banks[7][:, 256:256 + 2 * NS].rearrange("p (c t) -> p c t", c=2, t=NS), [], [ogT.r(), rbank[7]])
                wo_t, wko_ = wslot()
                P.dma("pool", wko_, wo_t.bf(2, 1024), ret_w_out[h * 256:(h + 1) * 256, :].rearrange("(c p) n -> p c n", p=128), writes=[wo_t.r()])
                P.commit(wko_)
                for dc in range(8):
                    for ti, (c0, n) in enumerate(tiles):
                        bk = 4 + (dc * NTI + ti) % 2
                        for cc in range(2):
                            MM(banks[bk][:, 0:n], wo_t.bf(2, 1024)[:, cc, dc * 128:(dc + 1) * 128], ogv[:, cc, c0:c0 + n], [wo_t.r(), ogT.r()], [rbank[bk]],
                               start=(cc == 0), stop=(cc == 1))
                        xv = xf[:, dc, c0:c0 + n]
                        if h == 0:
                            STT("dve", xv, xv, ALPHA, banks[bk][:, 0:n], ALU.mult, ALU.add, [r_xf[dc][ti]], [r_xf[dc][ti], rbank[bk]])
                        else:
                            TT("dve", xv, xv, banks[bk][:, 0:n], ALU.add, [r_xf[dc][ti]], [r_xf[dc][ti], rbank[bk]])
            A.release(m, bufs)

        if "noload" not in DBG:
            load_x()
        for li in range(nlayers):
            kind = li % 3
            if kind == 0:
                pool_mixer(li // 3)
            elif kind == 1:
                gdn_mixer()
            else:
                ret_mixer()
            layer_norm(2 * li)
            ffn(li)
            layer_norm(2 * li + 1)
        if "nostore" not in DBG:
            store_y()
        if "arena" in DBG:
            print("arena words", AW, "high water", A.hw)
        P.emit(st)
    return nc, cst


_CACHE = {}


def _in_maps(inputs, T, NS, cst):
    f = lambda a: np.ascontiguousarray(np.asarray(a, dtype=np.float32))
    shared = {
        "pool_w": f(inputs["pool_w"]), "pool_scale": f(inputs["pool_scale"]),
        "gdn_w_in": f(inputs["gdn_w_in"][0]), "gdn_conv_w": f(inputs["gdn_conv_w"][0]),
        "gdn_a_log": f(inputs["gdn_a_log"]), "gdn_dt_bias": f(inputs["gdn_dt_bias"]),
        "gdn_norm_g": f(inputs["gdn_norm_g"]), "gdn_w_out": f(inputs["gdn_w_out"][0]),
        "ret_w_in": f(inputs["ret_w_in"][0]), "ret_w_out": f(inputs["ret_w_out"][0]),
        "ffn_w13": f(inputs["ffn_w13"]), "ffn_w2": f(inputs["ffn_w2"]),
        "ln_g": f(inputs["ln_g"]).reshape(8, D), "ln_b": f(inputs["ln_b"]).reshape(8, D),
    }
    for k in list(CONST_SHAPES) + ["cos2", "sin2"]:
        shared["c_" + k] = f(cst[k])
    maps = []
    for c in range(NCORES):
        sl = slice(c * NS, (c + 1) * NS)
        mp = dict(shared)
        mp["xp"] = f(inputs["x_prompt"][c])
        mp["xs"] = f(inputs["x_sample"][sl, 0])
        mp["spool"] = f(inputs["state_pool"][:, sl])
        mp["sconv"] = f(inputs["state_gdn_conv"][0, sl])
        mp["sgdn"] = f(inputs["state_gdn"][0, sl])
        mp["sret"] = f(inputs["state_ret"][0, sl])
        maps.append(mp)
    return maps


def kernel(**inputs):
    T, NS = 2048, 16
    if "nc" not in _CACHE:
        _CACHE["nc"] = build(T, NS)
    nc, cst = _CACHE["nc"]
    maps = _in_maps(inputs, T, NS, cst)
    res = run_bass_kernel_spmd(nc, maps, core_ids=list(range(NCORES))).results
    cat = lambda k, ax: np.concatenate([np.asarray(r[k], dtype=np.float32)[None] if ax is None else np.asarray(r[k], dtype=np.float32)
                                        for r in res], axis=0 if ax is None else ax)
    y_prompt = cat("yp", None)
    y_sample = cat("ys", 0)[:, None, :]
    pool_p = np.stack([np.asarray(r["pool_p"], np.float32) for r in res], axis=1)
    pool_s = cat("pool_s", 1)
    conv_p = cat("conv_p", None)[None]
    conv_s = cat("conv_s", 0)[None]
    gdn_p = cat("gdn_p", None)[None]
    gdn_s = cat("gdn_s", 0)[None]
    ret_p = cat("ret_p", None)[None]
    ret_s = cat("ret_s", 0)[None]
    return (y_prompt, y_sample, pool_p, pool_s, conv_p, conv_s, gdn_p, gdn_s, ret_p, ret_s)
```

```python
import math
from contextlib import ExitStack
import numpy as np
import concourse.bass as bass
import concourse.mybir as mybir
from concourse.bass_utils import run_bass_kernel_spmd

F32 = mybir.dt.float32
BF16 = mybir.dt.bfloat16
ALU = mybir.AluOpType
AF = mybir.ActivationFunctionType
AX = mybir.AxisListType

D = 1024
DFF = 2816
NCORES = 8
PAST_LEN = 16384
ALPHA = 8.0 ** 0.25
LN_EPS = 1e-5
RMS_EPS = 1e-6
COMPUTE = ("pe", "act", "dve", "pool")


class Res:
    __slots__ = ("last_w", "readers")

    def __init__(self, inherit=None):
        self.last_w = None
        self.readers = dict(inherit) if inherit else {}


class Op:
    __slots__ = ("eng", "fn", "deps", "is_dma", "dkey", "dcount", "need_inc", "cnt", "idx")

    def __init__(self, eng, fn, is_dma=False, dkey=None):
        self.eng = eng
        self.fn = fn
        self.deps = []
        self.is_dma = is_dma
        self.dkey = dkey
        self.dcount = 0
        self.need_inc = False
        self.cnt = 0
        self.idx = 0


class Prog:
    def __init__(self, nc):
        self.nc = nc
        self.ops = []
        self.dma_counts = {}
        self.open_batch = {}

    def _add(self, op, reads, writes):
        op.idx = len(self.ops)
        deps = {}
        for r in reads:
            w = r.last_w
            if w is not None:
                deps[id(w)] = (w, True)
        for r in writes:
            w = r.last_w
            if w is not None and id(w) not in deps:
                deps[id(w)] = (w, False)
            for rd in r.readers.values():
                if id(rd) not in deps:
                    deps[id(rd)] = (rd, False)
        for (d, raw) in deps.values():
            if d is op:
                continue
            if (not d.is_dma) and (not op.is_dma) and d.eng == op.eng and d.eng == "pe":
                continue
            if d.is_dma and op.is_dma and d.dkey == op.dkey and d in self.open_batch.get(op.dkey, ()):
                continue
            op.deps.append(d)
            d.need_inc = True
        for r in reads:
            key = (op.eng, op.dkey) if op.is_dma else op.eng
            r.readers[key] = op
        for r in writes:
            r.last_w = op
            r.readers = {}
        self.ops.append(op)
        return op

    def op(self, eng, fn, reads=(), writes=()):
        return self._add(Op(eng, fn), reads, writes)

    def dma(self, eng, key, out, in_, reads=(), writes=()):
        def fn(e, out=out, in_=in_):
            return e.dma_start(out=out, in_=in_)
        o = Op(eng, fn, is_dma=True, dkey=key)
        self.dma_counts[key] = self.dma_counts.get(key, 0) + 16
        o.dcount = self.dma_counts[key]
        o.need_inc = True
        self.open_batch.setdefault(key, []).append(o)
        return self._add(o, reads, writes)

    def commit(self, key):
        for o in self.open_batch.get(key, []):
            o.dcount = self.dma_counts[key]
        self.open_batch[key] = []

    def emit(self, st):
        nc = self.nc
        sems = {e: st.enter_context(nc.semaphore("s_" + e)) for e in COMPUTE}
        dsems = {k: st.enter_context(nc.semaphore("d_%s" % (k,))) for k in self.dma_counts}
        cnts = {e: 0 for e in COMPUTE}
        for o in self.ops:
            if not o.is_dma and o.need_inc:
                cnts[o.eng] += 1
                o.cnt = cnts[o.eng]
        final_waits = {("d", k): (dsems[k], v) for k, v in self.dma_counts.items()}
        streams = {}
        for o in self.ops:
            streams.setdefault(o.eng, []).append(o)
        block = st.enter_context(nc.Block())

        def run(engname, e):
            wm = {}
            for o in streams.get(engname, []):
                need = {}
                for d in o.deps:
                    if d.is_dma:
                        k = ("d", d.dkey)
                        s, v = dsems[d.dkey], d.dcount
                    else:
                        k = ("c", d.eng)
                        s, v = sems[d.eng], d.cnt
                    if v > wm.get(k, 0) and v > need.get(k, (None, 0))[1]:
                        need[k] = (s, v)
                for k, (s, v) in need.items():
                    e.wait_ge(s, v)
                    wm[k] = v
                ins = o.fn(e)
                if o.is_dma:
                    ins.then_inc(dsems[o.dkey], 16)
                elif o.need_inc:
                    ins.then_inc(sems[o.eng], 1)
            if engname == "sp":
                for k, (s, v) in final_waits.items():
                    if v > wm.get(k, 0):
                        e.wait_ge(s, v)

        @block.sync
        def _(e):
            run("sp", e)

        @block.tensor
        def _(e):
            run("pe", e)

        @block.scalar
        def _(e):
            run("act", e)

        @block.vector
        def _(e):
            run("dve", e)

        @block.gpsimd
        def _(e):
            run("pool", e)


class Buf:
    def __init__(self, ap, lo, hi, inherit):
        self.ap = ap
        self.lo = lo
        self.hi = hi
        self.inherit = inherit
        self.ch = {}

    def r(self, key=None):
        c = self.ch.get(key)
        if c is None:
            c = Res(self.inherit)
            self.ch[key] = c
        return c

    def f32(self, *shape):
        return _shape(self.ap, shape)

    def bf(self, *shape):
        return _shape(self.ap.bitcast(BF16), shape)


class Sub:
    def __init__(self, parent, lo, hi):
        self.parent = parent
        self.ap = parent.ap[:, lo:hi]

    def r(self, key=None):
        return self.parent.r(key)

    def f32(self, *shape):
        return _shape(self.ap, shape)

    def bf(self, *shape):
        return _shape(self.ap.bitcast(BF16), shape)


def _shape(ap, shape):
    if len(shape) <= 1:
        return ap[:, 0:shape[0]] if shape else ap
    n = 1
    for s in shape:
        n *= s
    ap = ap[:, 0:n]
    if len(shape) == 2:
        return ap.rearrange("p (a b) -> p a b", a=shape[0], b=shape[1])
    if len(shape) == 3:
        return ap.rearrange("p (a b c) -> p a b c", a=shape[0], b=shape[1], c=shape[2])
    raise ValueError(shape)


class Arena:
    def __init__(self, ap, words):
        self.ap = ap
        self.words = words
        self.top = 0
        self.dead = []

    def alloc(self, words):
        words = (words + 7) // 8 * 8
        lo, hi = self.top, self.top + words
        assert hi <= self.words, ("arena overflow", hi, self.words)
        self.top = hi
        self.hw = max(getattr(self, "hw", 0), hi)
        inh = {}
        keep = []
        for b in self.dead:
            if b.hi <= lo or b.lo >= hi:
                keep.append(b)
                continue
            for c in b.ch.values():
                cand = list(c.readers.items())
                if c.last_w is not None:
                    w = c.last_w
                    cand.append(((w.eng, w.dkey) if w.is_dma else w.eng, w))
                for k, o in cand:
                    if k not in inh or inh[k].idx < o.idx:
                        inh[k] = o
            if not (b.lo >= lo and b.hi <= hi):
                keep.append(b)
        self.dead = keep
        return Buf(self.ap[:, lo:hi], lo, hi, inh)

    def f32(self, n):
        return self.alloc(n)

    def bf(self, n):
        return self.alloc((n + 1) // 2)

    def mark(self):
        return (self.top, [])

    def release(self, mark, bufs):
        self.top = mark[0]
        self.dead.extend(bufs)


def _consts(T, NS):
    c = {}
    c["ident"] = np.eye(128, dtype=np.float32)
    ii = np.arange(128)
    rc = np.zeros((128, 4, 16), np.float32)
    for g, win in enumerate((2, 4, 8, 16)):
        for t in range(16):
            rc[:, g, t] = 1.0 / min(t + 1, win)
    c["rc"] = rc
    c["maskS"] = np.where(ii[None, :] < ii[:, None], 0.0, 30000.0).astype(np.float32)
    c["negmaskT"] = np.where(ii[None, :] >= ii[:, None], 0.0, -30000.0).astype(np.float32)
    c["U"] = (ii[:, None] <= ii[None, :]).astype(np.float32)
    bm = lambda sz: ((ii[:, None] // sz) == (ii[None, :] // sz)).astype(np.float32)
    msk = np.zeros((128, 5, 128), np.float32)
    msk[:, 0, :] = bm(8)
    for li_, sz in enumerate((8, 16, 32, 64)):
        msk[:, 1 + li_, :] = bm(2 * sz) - bm(sz)
    c["bmask"] = msk
    H = 8
    lg = np.log(1.0 - 2.0 ** (-5.0 - np.arange(H, dtype=np.float64)))
    sc = 128.0 ** -0.5
    diff = (ii[None, :] - ii[:, None]).astype(np.float64)
    decT = np.where(diff[None] >= 0, np.exp(np.maximum(diff, 0.0)[None] * lg[:, None, None]), 0.0) * sc
    c["decT"] = np.ascontiguousarray(decT.transpose(1, 0, 2)).astype(np.float32)
    xi = np.exp((ii + 1.0)[None, :] * lg[:, None])
    c["xi"] = np.ascontiguousarray(np.broadcast_to(xi[None], (128, H, 128))).astype(np.float32)
    zeta = np.exp((127.0 - ii)[None, :] * lg[:, None]) * sc
    c["zeta"] = np.ascontiguousarray(zeta.T).astype(np.float32)
    c["gC"] = [float(np.exp(128.0 * lg[h])) for h in range(H)]
    c["gam"] = [float(np.exp(lg[h])) for h in range(H)]
    gb = np.zeros((128, 2, H), np.float32)
    gb[:, 0, :] = np.exp(lg)[None, :]
    gb[:, 1, :] = (np.exp(lg) * 0 + sc)[None, :]
    c["gamb"] = gb
    half = 64
    freqs = (np.float32(10000.0) ** (-np.arange(half, dtype=np.float32) / np.float32(half))).astype(np.float32)
    pos = np.concatenate([np.arange(T, dtype=np.float32), np.full((NS,), float(PAST_LEN), np.float32)])
    ang = (pos[None, :] * freqs[:, None]).astype(np.float32)
    cs, sn = np.cos(ang).astype(np.float32), np.sin(ang).astype(np.float32)
    c["cos2"] = np.concatenate([cs, cs], 0)
    c["sin2"] = np.concatenate([-sn, sn], 0)
    angs = (np.float32(PAST_LEN) * freqs).astype(np.float32)
    cst = np.zeros((128, 2, half), np.float32)
    cst[:, 0, :] = np.cos(angs)[None]
    cst[:, 1, :] = np.sin(angs)[None]
    c["ropes"] = cst
    dl = np.zeros((128, 16, 16), np.float32)
    for s in range(16):
        dl[:, s, s] = 1.0
    c["delta"] = dl
    dcol = np.zeros((128, 16), np.float32)
    for s in range(16):
        dcol[s, s] = 1.0
    c["dcol"] = dcol
    return c


CONST_SHAPES = {"ident": [128, 128], "rc": [128, 4, 16], "maskS": [128, 128], "negmaskT": [128, 128],
                "U": [128, 128], "bmask": [128, 5, 128], "decT": [128, 8, 128], "xi": [128, 8, 128], "zeta": [128, 8],
                "gamb": [128, 2, 8], "ropes": [128, 2, 64], "delta": [128, 16, 16], "dcol": [128, 16]}


AWO = None
DBG = set()


def build(T, NS, nlayers=4):
    assert T % 128 == 0 and NS == 16
    nc = bass.Bass("TRN2", target_bir_lowering=False)
    NB = T // 128
    NT = T + NS
    TTS = min(512, T)
    tiles = [(c0, TTS) for c0 in range(0, T, TTS)] + [(T, NS)]
    NTI = len(tiles)
    cst = _consts(T, NS)

    def din(name, shape):
        return nc.dram_tensor(name, list(shape), F32, kind="ExternalInput").ap()

    def dout(name, shape):
        return nc.dram_tensor(name, list(shape), F32, kind="ExternalOutput").ap()

    xp = din("xp", [T, D])
    xs = din("xs", [NS, D])
    spool = din("spool", [2, NS, 15, D])
    sconv = din("sconv", [NS, 3, 3072])
    sgdn = din("sgdn", [NS, 8, 128, 128])
    sret = din("sret", [NS, 8, 128, 256])
    pool_w = din("pool_w", [2, 4, 256, 256])
    pool_scale = din("pool_scale", [2, D])
    gdn_w_in = din("gdn_w_in", [D, 4112])
    gdn_conv_w = din("gdn_conv_w", [4, 3072])
    gdn_a_log = din("gdn_a_log", [1, 8])
    gdn_dt_bias = din("gdn_dt_bias", [1, 8])
    gdn_norm_g = din("gdn_norm_g", [1, 128])
    gdn_w_out = din("gdn_w_out", [D, D])
    ret_w_in = din("ret_w_in", [D, 6144])
    ret_w_out = din("ret_w_out", [2048, D])
    ffn_w13 = din("ffn_w13", [4, D, 2 * DFF])
    ffn_w2 = din("ffn_w2", [4, DFF, D])
    ln_g = din("ln_g", [8, D])
    ln_b = din("ln_b", [8, D])
    cd = {k: din("c_" + k, CONST_SHAPES[k]) for k in CONST_SHAPES}
    cd["cos2"] = din("c_cos2", [128, NT])
    cd["sin2"] = din("c_sin2", [128, NT])

    yp = dout("yp", [T, D])
    ys = dout("ys", [NS, D])
    o_pool_p = dout("pool_p", [2, 15, D])
    o_pool_s = dout("pool_s", [2, NS, 15, D])
    o_conv_p = dout("conv_p", [3, 3072])
    o_conv_s = dout("conv_s", [NS, 3, 3072])
    o_gdn_p = dout("gdn_p", [8, 128, 128])
    o_gdn_s = dout("gdn_s", [NS, 8, 128, 128])
    o_ret_p = dout("ret_p", [8, 128, 256])
    o_ret_s = dout("ret_s", [NS, 8, 128, 256])

    P = Prog(nc)
    st = ExitStack()
    with st:
        def sb(name, shape, dt):
            return st.enter_context(nc.sbuf_tensor(name, shape, dt))

        xf = sb("xf", [128, 8, NT], F32)
        xb = sb("xb", [128, 8, NT], BF16)
        ident = sb("ident", [128, 128], F32)
        identb = sb("identb", [128, 128], BF16)
        onesb = sb("onesb", [128, 128], BF16)
        onesn = sb("onesn", [128, 128], BF16)
        onesf = sb("onesf", [128, 128], F32)
        lnp = sb("lnp", [128, 128], F32)
        prm = sb("prm", [128, 128], F32)
        rcs = sb("rcs", [128, 4, 16], F32)
        AW = (nc.sbuf_bytes_remaining - 4096) // 4 // 8 * 8 if AWO is None else AWO
        arena_t = sb("arena", [128, AW], F32)
        A = Arena(arena_t[:], AW)
        banks = [st.enter_context(nc.psum_tensor("bank%d" % i, [128, 512], F32)) for i in range(8)]
        rbank = [Res() for _ in range(8)]
        r_xf = [[Res() for _ in range(NTI)] for _ in range(8)]
        r_xb = [[Res() for _ in range(NTI)] for _ in range(8)]
        r_c = Res()

        def rx(rs, ks=range(8), ts=range(NTI)):
            return [rs[k][t] for k in ks for t in ts]

        def tiles_of(c0, n):
            return [ti for ti, (a, m) in enumerate(tiles) if a < c0 + n and c0 < a + m]

        P.dma("sp", "c0", ident[:], cd["ident"], writes=[r_c])
        P.dma("sp", "c0", rcs[:], cd["rc"], writes=[r_c])
        P.commit("c0")
        P.op("dve", lambda e: e.tensor_copy(identb[:], ident[:]), reads=[r_c], writes=[r_c])
        P.op("dve", lambda e: e.memset(onesb[:], 1.0), writes=[r_c])
        P.op("dve", lambda e: e.memset(onesn[:], 1.0 / D), writes=[r_c])
        P.op("dve", lambda e: e.memset(onesf[:], 1.0), writes=[r_c])
        wring = [A.bf(2 * 8 * 128) for _ in range(4)]
        wr_i = [0]

        def wslot():
            b = wring[wr_i[0] % len(wring)]
            k = "w%d" % (wr_i[0] % len(wring))
            wr_i[0] += 1
            return b, k

        m0 = A.mark()
        if "noparams" in DBG:
            raise_ = None
        stg = A.f32(128)
        stg2 = A.f32(128)
        if "noparams" not in DBG:
          P.dma("sp", "c1", stg.f32(128)[0:64, :], ln_g.rearrange("l (k p) -> (l k) p", p=128), writes=[stg.r()])
          P.dma("sp", "c1", stg.f32(128)[64:128, :], ln_b.rearrange("l (k p) -> (l k) p", p=128), writes=[stg.r()])
          P.op("dve", lambda e: e.memset(stg2.f32(128), 0.0), writes=[stg2.r()])
          P.dma("sp", "c2", stg2.f32(128)[0:16, :], pool_scale.rearrange("j (k p) -> (j k) p", p=128), reads=[], writes=[stg2.r()])
          P.dma("sp", "c2", stg2.f32(128)[16:112, :], gdn_conv_w.rearrange("j (c p) -> (j c) p", p=128), writes=[stg2.r()])
          P.dma("sp", "c2", stg2.f32(128)[112:113, :], gdn_norm_g, writes=[stg2.r()])
          P.op("pe", lambda e: e.transpose(banks[0][:, 0:128], stg.f32(128), ident[:]), reads=[stg.r(), r_c], writes=[rbank[0]])
          P.op("pe", lambda e: e.transpose(banks[0][:, 128:256], stg2.f32(128), ident[:]), reads=[stg2.r(), r_c], writes=[rbank[0]])
          P.op("dve", lambda e: e.tensor_copy(lnp[:], banks[0][:, 0:128]), reads=[rbank[0]], writes=[r_c])
          P.op("dve", lambda e: e.tensor_copy(prm[:], banks[0][:, 128:256]), reads=[rbank[0]], writes=[r_c])
        A.release(m0, [stg, stg2])

        epsb = {}
        for ev in (LN_EPS, RMS_EPS, RMS_EPS * 128.0, 1.0):
            t = sb("eps%d" % len(epsb), [128, 1], F32)
            P.op("dve", lambda e, t=t, ev=ev: e.memset(t[:], ev), writes=[r_c])
            epsb[ev] = t

        def eps_tile(v):
            return epsb[v][:]

        def rsqrt_eps(out, in_, eps, res, scale=1.0):
            eb = epsb[eps]
            np_ = in_.shape[0]
            P.op("act", lambda e: e.activation(out, in_, AF.Ln, bias=eb[0:np_, :], scale=scale), reads=[res, r_c], writes=[res])
            P.op("act", lambda e: e.activation(out, out, AF.Exp, scale=-0.5), reads=[res], writes=[res])

        def load_x():
            m = A.mark()
            sg = [A.f32(D), A.f32(D)]
            blocks = [(xp[b * 128:(b + 1) * 128, :], 128, b * 128) for b in range(NB)] + [(xs, NS, T)]
            if "nosamp" in DBG:
                blocks = blocks[:-1]
            for bi, (src, n, c0) in enumerate(blocks):
                s = sg[bi % 2]
                P.dma("sp", "xl%d" % (bi % 2), s.f32(D)[0:n, :], src, writes=[s.r()])
                tis = tiles_of(c0, n)
                for half in range(2):
                    bk = (bi * 2 + half) % 2
                    for kk in range(4):
                        k = half * 4 + kk
                        P.op("pe", lambda e, s=s, k=k, kk=kk, n=n, bk=bk: e.transpose(
                            banks[bk][:, kk * 128:kk * 128 + n], s.f32(D)[0:n, k * 128:(k + 1) * 128], ident[0:n, 0:n]),
                            reads=[s.r(), r_c], writes=[rbank[bk]])
                    src_ps = banks[bk][:].rearrange("p (a b) -> p a b", a=4, b=128)[:, :, 0:n]
                    ks = range(half * 4, half * 4 + 4)
                    P.op("dve", lambda e, src_ps=src_ps, half=half, c0=c0, n=n: e.tensor_copy(
                        xf[:, half * 4:half * 4 + 4, c0:c0 + n], src_ps),
                        reads=[rbank[bk]], writes=rx(r_xf, ks, tis))
                    if "noact" in DBG:
                        continue
                    P.op("act", lambda e, src_ps=src_ps, half=half, c0=c0, n=n: e.activation(
                        xb[:, half * 4:half * 4 + 4, c0:c0 + n], xf[:, half * 4:half * 4 + 4, c0:c0 + n], AF.Identity),
                        reads=rx(r_xf, ks, tis), writes=rx(r_xb, ks, tis))
            A.release(m, sg)

        def layer_norm(li):
            m = A.mark()
            bufs = []
            sets = []
            for _ in range(2):
                sets.append((A.bf(8 * TTS), A.bf(8 * TTS), A.f32(TTS), A.f32(TTS), A.f32(TTS)))
                bufs += list(sets[-1])

            def tile_chain(ti, c0, n, par):
                rb, sq, mean, rstd, m2 = sets[par]
                b_s, b_q = (6, 7) if par == 0 else (4, 5)
                xv = xf[:, :, c0:c0 + n]
                P.op("act", lambda e: e.activation(rb.bf(8, n), xv, AF.Copy), reads=rx(r_xf, ts=[ti]), writes=[rb.r()])
                P.op("act", lambda e: e.activation(sq.bf(8, n), xv, AF.Square), reads=rx(r_xf, ts=[ti]), writes=[sq.r()])
                yield
                for k in range(8):
                    P.op("pe", lambda e, k=k: e.matmul(banks[b_s][:, 0:n], onesn[:], rb.bf(8, n)[:, k, :], start=(k == 0), stop=(k == 7)),
                         reads=[rb.r(), r_c], writes=[rbank[b_s]])
                for k in range(8):
                    P.op("pe", lambda e, k=k: e.matmul(banks[b_q][:, 0:n], onesn[:], sq.bf(8, n)[:, k, :], start=(k == 0), stop=(k == 7)),
                         reads=[sq.r(), r_c], writes=[rbank[b_q]])
                yield
                P.op("act", lambda e: e.activation(mean.f32(n), banks[b_s][:, 0:n], AF.Copy), reads=[rbank[b_s]], writes=[mean.r()])
                yield
                P.op("dve", lambda e: e.tensor_tensor(m2.f32(n), mean.f32(n), mean.f32(n), ALU.mult), reads=[mean.r()], writes=[m2.r()])
                P.op("dve", lambda e: e.tensor_tensor(rstd.f32(n), banks[b_q][:, 0:n], m2.f32(n), ALU.subtract),
                     reads=[m2.r(), rbank[b_q]], writes=[rstd.r()])
                P.op("dve", lambda e: e.tensor_tensor(xv, xv, mean.f32(n).unsqueeze(1).to_broadcast([128, 8, n]), ALU.subtract),
                     reads=rx(r_xf, ts=[ti]) + [mean.r()], writes=rx(r_xf, ts=[ti]))
                yield
                rsqrt_eps(rstd.f32(n), rstd.f32(n), LN_EPS, rstd.r())
                yield
                P.op("dve", lambda e: e.tensor_tensor(xv, xv, rstd.f32(n).unsqueeze(1).to_broadcast([128, 8, n]), ALU.mult),
                     reads=rx(r_xf, ts=[ti]) + [rstd.r()], writes=rx(r_xf, ts=[ti]))
                yield
                for k in range(8):
                    P.op("act", lambda e, k=k: e.activation(
                        xf[:, k, c0:c0 + n], xf[:, k, c0:c0 + n], AF.Identity,
                        bias=lnp[:, 64 + li * 8 + k:64 + li * 8 + k + 1], scale=lnp[:, li * 8 + k:li * 8 + k + 1]),
                        reads=[r_xf[k][ti], r_c], writes=[r_xf[k][ti]])
                yield
                P.op("dve", lambda e: e.tensor_copy(xb[:, :, c0:c0 + n], xv), reads=rx(r_xf, ts=[ti]), writes=rx(r_xb, ts=[ti]))

            for t0 in range(0, NTI, 2):
                gens_ = [tile_chain(ti, tiles[ti][0], tiles[ti][1], ti - t0) for ti in range(t0, min(NTI, t0 + 2))]
                while gens_:
                    nx_ = []
                    for gn_ in gens_:
                        try:
                            next(gn_)
                            nx_.append(gn_)
                        except StopIteration:
                            pass
                    gens_ = nx_
            A.release(m, bufs)

        def ffn(layer):
            m = A.mark()
            h = A.bf(11 * NT)
            w2 = A.bf(11 * D)
            sa = [A.f32(TTS), A.f32(TTS)]
            hv = h.bf(11, NT)
            w2v = w2.bf(11, D)
            cnt = 0
            for pas in range(2):
                for il in range(11):
                    i = pas * 11 + il
                    wb, wk = wslot()
                    wv = wb.bf(2, 8, 128)
                    P.dma("pool", wk, wv[:, 0], ffn_w13[layer, :, i * 128:(i + 1) * 128].rearrange("(k p) n -> p k n", p=128),
                          writes=[wb.r()])
                    P.dma("pool", wk, wv[:, 1], ffn_w13[layer, :, DFF + i * 128:DFF + (i + 1) * 128].rearrange("(k p) n -> p k n", p=128),
                          writes=[wb.r()])
                    P.commit(wk)
                    if il == 0:
                        for f in range(11):
                            P.dma("pool", "fw2", w2v[:, f, :], ffn_w2[layer, (pas * 11 + f) * 128:(pas * 11 + f + 1) * 128, :],
                                  writes=[w2.r(f)])
                        P.commit("fw2")
                    for ti, (c0, n) in enumerate(tiles):
                        ba, bb = banks[cnt % 2], banks[2 + cnt % 2]
                        ra, rbb = rbank[cnt % 2], rbank[2 + cnt % 2]
                        s = sa[cnt % 2]
                        cnt += 1
                        for k in range(8):
                            P.op("pe", lambda e, ba=ba, wv=wv, k=k, c0=c0, n=n: e.matmul(
                                ba[:, 0:n], wv[:, 0, k, :], xb[:, k, c0:c0 + n], start=(k == 0), stop=(k == 7)),
                                reads=[wb.r(), r_xb[k][ti]], writes=[ra])
                        for k in range(8):
                            P.op("pe", lambda e, bb=bb, wv=wv, k=k, c0=c0, n=n: e.matmul(
                                bb[:, 0:n], wv[:, 1, k, :], xb[:, k, c0:c0 + n], start=(k == 0), stop=(k == 7)),
                                reads=[wb.r(), r_xb[k][ti]], writes=[rbb])
                        P.op("act", lambda e, s=s, ba=ba, n=n: e.activation(s.f32(n), ba[:, 0:n], AF.Silu),
                             reads=[ra], writes=[s.r()])
                        P.op("dve", lambda e, s=s, bb=bb, il=il, c0=c0, n=n: e.tensor_tensor(
                            hv[:, il, c0:c0 + n], s.f32(n), bb[:, 0:n], ALU.mult),
                            reads=[s.r(), rbb], writes=[h.r((il, ti))])
                for dc in range(8):
                    for ti, (c0, n) in enumerate(tiles):
                        by, ry = banks[4 + cnt % 2], rbank[4 + cnt % 2]
                        cnt += 1
                        for f in range(11):
                            P.op("pe", lambda e, by=by, f=f, dc=dc, c0=c0, n=n: e.matmul(
                                by[:, 0:n], w2v[:, f, dc * 128:(dc + 1) * 128], hv[:, f, c0:c0 + n],
                                start=(f == 0), stop=(f == 10)),
                                reads=[w2.r(f), h.r((f, ti))], writes=[ry])
                        xv = xf[:, dc, c0:c0 + n]
                        if pas == 0:
                            P.op("dve", lambda e, xv=xv, by=by, n=n: e.scalar_tensor_tensor(
                                xv, xv, ALPHA, by[:, 0:n], ALU.mult, ALU.add),
                                reads=[ry, r_xf[dc][ti]], writes=[r_xf[dc][ti]])
                        else:
                            P.op("dve", lambda e, xv=xv, by=by, n=n: e.tensor_tensor(xv, xv, by[:, 0:n], ALU.add),
                                 reads=[ry, r_xf[dc][ti]], writes=[r_xf[dc][ti]])
            A.release(m, [h, w2] + sa)

        def store_y():
            m = A.mark()
            sg = [A.f32(D), A.f32(D)]
            blocks = [(yp[b * 128:(b + 1) * 128, :], 128, b * 128) for b in range(NB)] + [(ys, NS, T)]
            for bi, (dst, n, c0) in enumerate(blocks):
                s = sg[bi % 2]
                tis = tiles_of(c0, n)
                for half in range(2):
                    bk = (bi * 2 + half) % 2
                    for kk in range(4):
                        k = half * 4 + kk
                        P.op("pe", lambda e, k=k, kk=kk, n=n, bk=bk, c0=c0: e.transpose(
                            banks[bk][0:n, kk * 128:(kk + 1) * 128], xf[:, k, c0:c0 + n], ident[:]),
                            reads=rx(r_xf, [k], tis) + [r_c], writes=[rbank[bk]])
                    if half == 0:
                        P.op("dve", lambda e, s=s, n=n, bk=bk: e.tensor_copy(s.f32(D)[0:n, 0:512], banks[bk][0:n, :]),
                             reads=[rbank[bk]], writes=[s.r()])
                    else:
                        P.op("act", lambda e, s=s, n=n, bk=bk: e.activation(s.f32(D)[0:n, 512:1024], banks[bk][0:n, :], AF.Copy),
                             reads=[rbank[bk]], writes=[s.r()])
                P.dma("sp", "ys%d" % (bi % 2), dst, s.f32(D)[0:n, :], reads=[s.r()])
            A.release(m, sg)

        def pool_mixer(j):
            m = A.mark()
            bufs = []
            if j == 0:
                P.dma("sp", "po", o_pool_p[0], xp[T - 15:T, :])
                P.dma("sp", "po", o_pool_s[0, :, 14, :], xs)
            else:
                so = A.f32(D)
                bufs.append(so)
                nn = 15 + NS
                for half in range(2):
                    for kk in range(4):
                        k = half * 4 + kk
                        P.op("pe", lambda e, k=k, kk=kk, half=half: e.transpose(
                            banks[half][0:nn, kk * 128:(kk + 1) * 128], xf[:, k, T - 15:T + NS], ident[:]),
                            reads=rx(r_xf, [k], tiles_of(T - 15, nn)) + [r_c], writes=[rbank[half]])
                    P.op("dve", lambda e, half=half: e.tensor_copy(so.f32(D)[0:nn, half * 512:(half + 1) * 512], banks[half][0:nn, :]),
                         reads=[rbank[half]], writes=[so.r()])
                P.dma("sp", "po", o_pool_p[1], so.f32(D)[0:15, :], reads=[so.r()])
                P.dma("sp", "po", o_pool_s[1, :, 14, :], so.f32(D)[15:15 + NS, :], reads=[so.r()])
            P.dma("sp", "po", o_pool_s[j, :, 0:14, :], spool[j, :, 1:15, :])
            P.commit("po")
            hist = A.f32(8 * NS * 15)
            bufs.append(hist)
            hv = hist.f32(8, NS * 15)
            hs = [A.f32(D), A.f32(D)]
            bufs += hs
            rows = NS * 15 // 2
            src = spool[j].rearrange("s r d -> (s r) d")
            for b2 in range(2):
                s = hs[b2]
                P.dma("sp", "ph%d" % b2, s.f32(D)[0:rows, :], src[b2 * rows:(b2 + 1) * rows, :], writes=[s.r()])
                for half in range(2):
                    bk = half
                    for kk in range(4):
                        k = half * 4 + kk
                        P.op("pe", lambda e, s=s, k=k, kk=kk, bk=bk: e.transpose(
                            banks[bk][:, kk * 128:kk * 128 + rows], s.f32(D)[0:rows, k * 128:(k + 1) * 128], ident[0:rows, 0:rows]),
                            reads=[s.r(), r_c], writes=[rbank[bk]])
                    P.op("dve", lambda e, half=half, bk=bk, b2=b2: e.tensor_copy(
                        hv[:, half * 4:half * 4 + 4, b2 * rows:(b2 + 1) * rows],
                        banks[bk][:].rearrange("p (a b) -> p a b", a=4, b=128)[:, :, 0:rows]),
                        reads=[rbank[bk]], writes=[hist.r()])
            pooled = A.bf(8 * NT)
            bufs.append(pooled)
            pv = pooled.bf(8, NT)
            E = [A.f32(16 + T), A.f32(16 + T)]
            bufs += E
            ssum = A.f32(NS)
            bufs.append(ssum)
            for eb in E:
                P.op("pool", lambda e, eb=eb: e.memset(eb.f32(16 + T)[:, 0:16], 0.0), writes=[eb.r()])
            for k in range(8):
                g = k // 2
                win = 2 << g
                e0, e1 = E[0], E[1]
                tp = list(range(NTI - 1))
                P.op("pool", lambda e, e0=e0, k=k: e.tensor_copy(e0.f32(16 + T)[:, 16:16 + T], xf[:, k, 0:T]),
                     reads=rx(r_xf, [k], tp), writes=[e0.r()])
                cur, nxt = e0, e1
                sh = 1
                while sh < win:
                    P.op("dve", lambda e, cur=cur, nxt=nxt, sh=sh: e.tensor_tensor(
                        nxt.f32(16 + T)[:, 16:16 + T], cur.f32(16 + T)[:, 16:16 + T], cur.f32(16 + T)[:, 16 - sh:16 + T - sh], ALU.add),
                        reads=[cur.r()], writes=[nxt.r()])
                    cur, nxt = nxt, cur
                    sh *= 2
                P.op("dve", lambda e, cur=cur, k=k, win=win: e.scalar_tensor_tensor(
                    pv[:, k, 0:T], cur.f32(16 + T)[:, 16:16 + T], 1.0 / win, xf[:, k, 0:T], ALU.mult, ALU.subtract),
                    reads=[cur.r()] + rx(r_xf, [k], tp), writes=[pooled.r(k)])
                P.op("dve", lambda e, cur=cur, nxt=nxt, g=g: e.tensor_tensor(
                    nxt.f32(16 + T)[:, 16:31], cur.f32(16 + T)[:, 16:31], rcs[:, g, 0:15], ALU.mult),
                    reads=[cur.r(), r_c], writes=[nxt.r()])
                P.op("dve", lambda e, nxt=nxt, k=k: e.tensor_tensor(
                    pv[:, k, 0:15], nxt.f32(16 + T)[:, 16:31], xf[:, k, 0:15], ALU.subtract),
                    reads=[nxt.r()] + rx(r_xf, [k], [0]), writes=[pooled.r(k)])
                hk = hv[:, k, :].rearrange("p (s r) -> p s r", s=NS, r=15)
                P.op("dve", lambda e, hk=hk, win=win: e.tensor_reduce(
                    ssum.f32(NS), hk[:, :, 16 - win:15], AX.X, ALU.add),
                    reads=[hist.r()], writes=[ssum.r()])
                P.op("dve", lambda e, k=k: e.tensor_tensor(ssum.f32(NS), ssum.f32(NS), xf[:, k, T:NT], ALU.add),
                     reads=[ssum.r(), r_xf[k][NTI - 1]], writes=[ssum.r()])
                P.op("dve", lambda e, k=k, win=win: e.scalar_tensor_tensor(
                    pv[:, k, T:NT], ssum.f32(NS), 1.0 / win, xf[:, k, T:NT], ALU.mult, ALU.subtract),
                    reads=[ssum.r(), r_xf[k][NTI - 1]], writes=[pooled.r(k)])
            pw = A.bf(4 * 2 * 256)
            bufs.append(pw)
            pwv = pw.bf(4, 2, 256)
            P.dma("pool", "pw", pwv, pool_w[j].rearrange("g (cc p) d -> p g cc d", p=128), writes=[pw.r()])
            tm = [A.f32(TTS), A.f32(TTS)]
            bufs += tm
            cnt = 0
            for g in range(4):
                for dc in range(2):
                    k = 2 * g + dc
                    for ti, (c0, n) in enumerate(tiles):
                        bk = cnt % 2
                        t_ = tm[cnt % 2]
                        cnt += 1
                        for cc in range(2):
                            P.op("pe", lambda e, g=g, dc=dc, cc=cc, c0=c0, n=n, bk=bk: e.matmul(
                                banks[bk][:, 0:n], pwv[:, g, cc, dc * 128:(dc + 1) * 128], pv[:, 2 * g + cc, c0:c0 + n],
                                start=(cc == 0), stop=(cc == 1)),
                                reads=[pw.r(), pooled.r(2 * g + cc)], writes=[rbank[bk]])
                        P.op("act", lambda e, t_=t_, bk=bk, n=n, k=k: e.activation(
                            t_.f32(n), banks[bk][:, 0:n], AF.Copy, scale=prm[:, j * 8 + k:j * 8 + k + 1]),
                            reads=[rbank[bk], r_c], writes=[t_.r()])
                        xv = xf[:, k, c0:c0 + n]
                        P.op("dve", lambda e, xv=xv, t_=t_, n=n: e.scalar_tensor_tensor(
                            xv, xv, ALPHA, t_.f32(n), ALU.mult, ALU.add),
                            reads=[t_.r(), r_xf[k][ti]], writes=[r_xf[k][ti]])
            A.release(m, bufs)

        def MM(out, lhsT, rhs, r, w, start=True, stop=True):
            P.op("pe", lambda e: e.matmul(out, lhsT, rhs, start=start, stop=stop), reads=r, writes=w)

        def TR(out, in_, idn, r, w):
            P.op("pe", lambda e: e.transpose(out, in_, idn), reads=r, writes=w)

        def TT(eng, out, in0, in1, op, r, w):
            P.op(eng, lambda e: e.tensor_tensor(out, in0, in1, op), reads=r, writes=w)

        def STT(eng, out, in0, scalar, in1, op0, op1, r, w):
            P.op(eng, lambda e: e.scalar_tensor_tensor(out, in0, scalar, in1, op0, op1), reads=r, writes=w)

        def TS(eng, out, in0, s1, s2, op0, op1, r, w):
            if op1 is None:
                P.op(eng, lambda e: e.tensor_scalar(out, in0, s1, None, op0), reads=r, writes=w)
            else:
                P.op(eng, lambda e: e.tensor_scalar(out, in0, s1, s2, op0, op1), reads=r, writes=w)

        def CP(eng, out, in_, r, w):
            if eng == "act":
                P.op("act", lambda e: e.activation(out, in_, AF.Identity), reads=r, writes=w)
            else:
                P.op(eng, lambda e: e.tensor_copy(out, in_), reads=r, writes=w)

        def ACT(out, in_, func, r, w, scale=1.0, bias=None, accum=None):
            def fn(e):
                kw = {}
                if bias is not None:
                    kw["bias"] = bias
                if accum is not None:
                    kw["accum_out"] = accum
                return e.activation(out, in_, func, scale=scale, **kw)
            P.op("act", fn, reads=r, writes=w)

        def RED(eng, out, in_, r, w, op=ALU.add):
            P.op(eng, lambda e: e.tensor_reduce(out, in_, AX.X, op), reads=r, writes=w)

        def MS(eng, out, val, w):
            P.op(eng, lambda e: e.memset(out, val), writes=w)

        def bfps(bank, c0, n):
            return banks[bank][:, c0:c0 + n // 2].bitcast(BF16)

        ptiles = tiles[:-1]
        PT_ = list(range(NTI - 1))

        def out_proj_samples(w_dram, nk, ogsT, ogs_res):
            for dc in range(8):
                for kc in range(nk):
                    wb_, wk_ = wslot()
                    wv_ = wb_.bf(2, 8, 128)
                    if kc % 16 == 0:
                        pass
                    P.dma("pool", wk_, wv_[:, 0, 0, :], w_dram[kc * 128:(kc + 1) * 128, dc * 128:(dc + 1) * 128], writes=[wb_.r()])
                    P.commit(wk_)
                    MM(banks[4][:, 0:NS], wv_[:, 0, 0, :], ogsT[:, kc, :], [wb_.r(), ogs_res], [rbank[4]], start=(kc == 0), stop=(kc == nk - 1))
                xv = xf[:, dc, T:NT]
                STT("dve", xv, xv, ALPHA, banks[4][:, 0:NS], ALU.mult, ALU.add, [r_xf[dc][NTI - 1]], [r_xf[dc][NTI - 1], rbank[4]])

        def gdn_mixer():
            G = 4 if NB >= 4 else 2
            m = A.mark()
            bufs = []

            def al(n, bf=False):
                b = A.bf(n) if bf else A.f32(n)
                bufs.append(b)
                return b
            maskS = al(128); negT = al(128); Um = al(128); gn = al(128); gnb = al(128)
            bmask = al(5 * 128)
            bmv = bmask.f32(5, 128)
            alog = al(8); dtb = al(8); nea = al(8)
            P.dma("sp", "gc", maskS.f32(128), cd["maskS"], writes=[maskS.r()])
            P.dma("sp", "gc", negT.f32(128), cd["negmaskT"], writes=[negT.r()])
            P.dma("sp", "gc", Um.f32(128), cd["U"], writes=[Um.r()])
            P.dma("sp", "gc", bmask.f32(5, 128), cd["bmask"], writes=[bmask.r()])
            P.dma("sp", "gc", gn.f32(128), gdn_norm_g.partition_broadcast(128)[:, 0, :], writes=[gn.r()])
            P.dma("sp", "gc", alog.f32(8), gdn_a_log.partition_broadcast(128)[:, 0, :], writes=[alog.r()])
            P.dma("sp", "gc", dtb.f32(8), gdn_dt_bias.partition_broadcast(128)[:, 0, :], writes=[dtb.r()])
            P.commit("gc")
            TS("dve", gnb.f32(128), gn.f32(128), math.sqrt(128.0), None, ALU.mult, None, [gn.r()], [gnb.r()])
            ACT(nea.f32(8), alog.f32(8), AF.Exp, [alog.r()], [nea.r()])
            TS("dve", nea.f32(8), nea.f32(8), -1.0, None, ALU.mult, None, [nea.r()], [nea.r()])
            NG = NB + 1
            wg = al(8 * 16, bf=True)
            wgv = wg.bf(8, 16)
            P.dma("pool", "gw", wgv, gdn_w_in[:, 4096:4112].rearrange("(k p) n -> p k n", p=128), writes=[wg.r()])
            GA = al(NG * 16)
            GAv = GA.f32(NG, 16)
            MS("dve", GA.f32(NG * 16), 0.0, [GA.r()])
            for b in range(NB):
                for k in range(8):
                    MM(banks[0][:, b * 16:(b + 1) * 16], xb[:, k, b * 128:(b + 1) * 128], wgv[:, k, :],
                       [wg.r()] + rx(r_xb, [k], tiles_of(b * 128, 128)), [rbank[0]], start=(k == 0), stop=(k == 7))
            for k in range(8):
                MM(banks[0][0:NS, NB * 16:NG * 16], xb[:, k, T:NT], wgv[:, k, :],
                   [wg.r(), r_xb[k][NTI - 1]], [rbank[0]], start=(k == 0), stop=(k == 7))
            CP("dve", GA.f32(NB * 16), banks[0][:, 0:NB * 16], [], [GA.r(), rbank[0]])
            CP("dve", GAv[0:NS, NB, :], banks[0][0:NS, NB * 16:NG * 16], [], [GA.r(), rbank[0]])
            beta = al(NG * 8); nbeta = al(NG * 8); gg = al(NG * 8); gcol = al(NG * 8); eg = al(NG * 8)
            beg = al(NG * 8); egl = al(NG * 8); gl = al(NG * 8); t1 = al(NG * 8); t2 = al(NG * 8)
            v8 = lambda b_: b_.f32(NG, 8)
            ACT(v8(beta), GAv[:, :, 0:8], AF.Sigmoid, [GA.r()], [beta.r()])
            TS("dve", v8(nbeta), v8(beta), -1.0, None, ALU.mult, None, [beta.r()], [nbeta.r()])
            TT("dve", v8(t1), GAv[:, :, 8:16], dtb.f32(8).unsqueeze(1).to_broadcast([128, NG, 8]), ALU.add, [GA.r(), dtb.r()], [t1.r()])
            TS("dve", v8(t2), v8(t1), -1.0, None, ALU.mult, None, [t1.r()], [t2.r()])
            TT("dve", v8(t2), v8(t2), v8(t1), ALU.max, [t1.r(), t2.r()], [t2.r()])
            ACT(v8(t2), v8(t2), AF.Exp, [t2.r()], [t2.r()], scale=-1.0)
            ACT(v8(t2), v8(t2), AF.Ln, [t2.r(), r_c], [t2.r()], bias=eps_tile(1.0))
            TS("dve", v8(t1), v8(t1), 0.0, None, ALU.max, None, [t1.r()], [t1.r()])
            TT("dve", v8(t1), v8(t1), v8(t2), ALU.add, [t1.r(), t2.r()], [t1.r()])
            TT("dve", v8(gg), v8(t1), nea.f32(8).unsqueeze(1).to_broadcast([128, NG, 8]), ALU.mult, [t1.r(), nea.r()], [gg.r()])
            MM(banks[0][:, 0:NB * 8], Um.f32(128), gg.f32(NB * 8), [Um.r(), gg.r()], [rbank[0]])
            MM(banks[0][:, 256:256 + NB * 8], onesf[:], gg.f32(NB * 8), [r_c, gg.r()], [rbank[0]])
            CP("dve", gcol.f32(NB * 8), banks[0][:, 0:NB * 8], [], [gcol.r(), rbank[0]])
            CP("dve", gcol.f32(NG, 8)[:, NB, :], gg.f32(NG, 8)[:, NB, :], [gg.r()], [gcol.r()])
            ACT(eg.f32(NG * 8), gcol.f32(NG * 8), AF.Exp, [gcol.r()], [eg.r()])
            TT("dve", beg.f32(NG * 8), beta.f32(NG * 8), eg.f32(NG * 8), ALU.mult, [beta.r(), eg.r()], [beg.r()])
            ACT(gl.f32(NB * 8), banks[0][:, 256:256 + NB * 8], AF.Exp, [], [gl.r(), rbank[0]])
            TT("dve", egl.f32(NB * 8), banks[0][:, 256:256 + NB * 8], gcol.f32(NB * 8), ALU.subtract, [gcol.r()], [egl.r(), rbank[0]])
            ACT(egl.f32(NB * 8), egl.f32(NB * 8), AF.Exp, [egl.r()], [egl.r()])
            projsT = al(4 * 8 * NS)
            pjT = projsT.f32(4, 8, NS)
            lastU = al(24 * 3)
            luv = lastU.f32(24, 3)
            base_bufs = list(bufs)
            mH = A.mark()
            del bufs[:]
            Ub = al(T + 8)
            accq = al(T); vf = al(T)
            knf = vf
            qnb = al(T, bf=True); knb = al(T, bf=True)
            szf = accq
            sqb_ap = Ub.ap[:, 8:8 + T // 2].bitcast(BF16)
            ogb_ap = Ub.ap[:, 8 + T // 2:8 + T].bitcast(BF16)
            rsq = [al(TTS)]
            Sf = al(128); Sb = al(128, bf=True)
            wo = al(1024, bf=True)
            Ubv = Ub.ap[:, 5:8 + T]
            MS("pool", Ubv[:, 0:3], 0.0, [Ub.r()])

            class Ch:
                pass
            chains = []
            for ci in range(G):
                c = Ch()
                c.Ug = al(128); c.e1 = al(128); c.e2 = al(128); c.egrow = al(128)
                c.attnT = al(128, bf=True); c.qgT = al(128, bf=True)
                c.Xf = al(128); c.XTf = al(128); c.X8 = al(128); c.Z8 = al(128)
                c.Y1 = c.Ug; c.Z1 = c.e1; c.Y2 = c.e2; c.E0 = al(128); c.E1 = c.egrow
                c.Xo = al(4 * 128, bf=True); c.Zo = al(128, bf=True)
                c.Xb = al(128, bf=True); c.XTb = al(128, bf=True)
                c.Db = [al(128, bf=True), al(128, bf=True)]; c.Eb = [al(128, bf=True), al(128, bf=True)]
                c.M1 = al(128, bf=True); c.M1p = al(128, bf=True)
                c.PTb = c.Eb[0]
                c.vb = Sub(c.Xo, 0, 64); c.kbg = Sub(c.Xo, 64, 128); c.kg = Sub(c.Xo, 128, 192)
                c.u = c.Xf; c.wkT = Sub(c.Xo, 192, 256); c.vnew = Sub(c.Zo, 0, 64); c.on = c.XTf; c.ssq = al(8)
                c.osb = c.X8
                c.bA = ci
                c.bN = ci
                chains.append(c)
            B = lambda b_: b_.bf(128)
            F = lambda b_: b_.f32(128)

            for h in range(8):
                slots = []
                for typ in range(4):
                    if typ % 2 == 0:
                        wb_, wk_ = wslot()
                        wv_ = wb_.bf(2, 8, 128)
                    col = typ * 1024 + h * 128
                    P.dma("pool", wk_, wv_[:, typ % 2], gdn_w_in[:, col:col + 128].rearrange("(k p) n -> p k n", p=128), writes=[wb_.r()])
                    slots.append((wb_, wv_[:, typ % 2]))
                    if typ % 2 == 1:
                        P.commit(wk_)
                if True:
                    P.dma("pool", "gwo", wo.bf(1024), gdn_w_out[h * 128:(h + 1) * 128, :], writes=[wo.r()])
                cntb = [0]

                def project(typ, sink):
                    wb_, wv_ = slots[typ]
                    for ti, (c0, n) in enumerate(ptiles):
                        bk = 4 + cntb[0] % 2
                        cntb[0] += 1
                        for k in range(8):
                            MM(banks[bk][:, 0:n], wv_[:, k, :], xb[:, k, c0:c0 + n], [wb_.r(), r_xb[k][ti]], [rbank[bk]], start=(k == 0), stop=(k == 7))
                        sink(bk, c0, n)
                    bk = 4 + cntb[0] % 2
                    cntb[0] += 1
                    for k in range(8):
                        MM(banks[bk][:, 0:NS], wv_[:, k, :], xb[:, k, T:NT], [wb_.r(), r_xb[k][NTI - 1]], [rbank[bk]], start=(k == 0), stop=(k == 7))
                    CP("dve", pjT[:, typ, h, :], banks[bk][:, 0:NS], [], [projsT.r(), rbank[bk]])

                for typ in range(3):
                    ch = typ * 8 + h
                    project(typ, lambda bk, c0, n: CP("act", Ubv[:, 3 + c0:3 + c0 + n], banks[bk][:, 0:n], [], [Ub.r(), rbank[bk]]))
                    CP("act", luv[:, ch, :], Ubv[:, T:T + 3], [Ub.r()], [lastU.r()])
                    acc = [accq, knf, vf][typ]
                    ce = "dve"
                    cw = lambda j_: prm[:, 16 + j_ * 24 + ch:16 + j_ * 24 + ch + 1]
                    TS(ce, acc.f32(T), Ubv[:, 3:3 + T], cw(3), None, ALU.mult, None, [Ub.r(), r_c], [acc.r()])
                    for j_ in range(3):
                        STT(ce, acc.f32(T), Ubv[:, j_:j_ + T], cw(j_), acc.f32(T), ALU.mult, ALU.add, [Ub.r(), r_c, acc.r()], [acc.r()])
                    ACT(acc.f32(T), acc.f32(T), AF.Silu, [acc.r()], [acc.r()])
                    if typ < 2:
                        ACT(sqb_ap, acc.f32(T), AF.Square, [acc.r()], [Ub.r()])
                        for ti, (c0, n) in enumerate(ptiles):
                            bk = 4 + cntb[0] % 2
                            cntb[0] += 1
                            rs_ = rsq[0]
                            MM(banks[bk][:, 0:n], onesb[:], sqb_ap[:, c0:c0 + n], [Ub.r(), r_c], [rbank[bk]])
                            CP("dve", rs_.f32(n), banks[bk][:, 0:n], [], [rs_.r(), rbank[bk]])
                            rsqrt_eps(rs_.f32(n), rs_.f32(n), RMS_EPS, rs_.r())
                            if typ == 0:
                                STT("dve", qnb.bf(T)[:, c0:c0 + n], acc.f32(T)[:, c0:c0 + n], 128.0 ** -0.5, rs_.f32(n), ALU.mult, ALU.mult,
                                    [acc.r(), rs_.r()], [qnb.r()])
                            else:
                                TT("dve", acc.f32(T)[:, c0:c0 + n], acc.f32(T)[:, c0:c0 + n], rs_.f32(n), ALU.mult, [acc.r(), rs_.r()], [acc.r()])
                        if typ == 1:
                            CP("pool", knb.bf(T), knf.f32(T), [knf.r()], [knb.r()])
                project(3, lambda bk, c0, n: ACT(szf.f32(T)[:, c0:c0 + n], banks[bk][:, 0:n], AF.Silu, [], [szf.r(), rbank[bk]]))
                MS("dve", F(Sf), 0.0, [Sf.r()])
                MS("dve", B(Sb), 0.0, [Sb.r()])

                def st_a(c, b):
                    bs = slice(b * 128, (b + 1) * 128)
                    ACT(F(c.Ug), F(Um), AF.Identity, [Um.r(), gg.r()], [c.Ug.r()], scale=gg.f32(NG, 8)[:, b, h:h + 1])
                    yield
                    bk = banks[c.bA]
                    MM(bk[:, 0:128], knb.bf(T)[:, bs], knb.bf(T)[:, bs], [knb.r()], [rbank[c.bA]])
                    MM(bk[:, 128:256], knb.bf(T)[:, bs], qnb.bf(T)[:, bs], [knb.r(), qnb.r()], [rbank[c.bA]])
                    MM(bk[:, 256:384], onesf[:], F(c.Ug), [r_c, c.Ug.r()], [rbank[c.bA]])
                    yield
                    gc_ = gcol.f32(NG, 8)[:, b, h:h + 1]
                    STT("dve", F(c.e1), bk[:, 256:384], gc_, F(maskS), ALU.subtract, ALU.max, [gcol.r(), maskS.r()], [c.e1.r(), rbank[c.bA]])
                    STT("dve", F(c.e2), bk[:, 256:384], gc_, F(negT), ALU.subtract, ALU.min, [gcol.r(), negT.r()], [c.e2.r(), rbank[c.bA]])
                    ACT(F(c.egrow), bk[:, 256:384], AF.Exp, [], [c.egrow.r(), rbank[c.bA]])
                    yield
                    ACT(F(c.e1), F(c.e1), AF.Exp, [c.e1.r()], [c.e1.r()], scale=-1.0)
                    ACT(F(c.e2), F(c.e2), AF.Exp, [c.e2.r()], [c.e2.r()])
                    yield
                    STT("dve", F(c.Xf), bk[:, 0:128], nbeta.f32(NG, 8)[:, b, h:h + 1], F(c.e1), ALU.mult, ALU.mult, [nbeta.r(), c.e1.r()], [c.Xf.r(), rbank[c.bA]])
                    TT("dve", B(c.attnT), bk[:, 128:256], F(c.e2), ALU.mult, [c.e2.r()], [c.attnT.r(), rbank[c.bA]])
                    TT("dve", B(c.qgT), qnb.bf(T)[:, bs], F(c.egrow), ALU.mult, [qnb.r(), c.egrow.r()], [c.qgT.r()])

                def st_b(c, b):
                    bk = banks[c.bN]
                    TR(bk[:, 0:128], F(c.Xf), ident[:], [c.Xf.r(), r_c], [rbank[c.bN]])
                    yield
                    CP("act", F(c.XTf), bk[:, 0:128], [], [c.XTf.r(), rbank[c.bN]])
                    CP("act", B(c.XTb), bk[:, 0:128], [], [c.XTb.r(), rbank[c.bN]])
                    CP("act", B(c.Xb), F(c.Xf), [c.Xf.r()], [c.Xb.r()])
                    yield
                    TT("dve", F(c.X8), F(c.Xf), bmv[:, 0, :], ALU.mult, [c.Xf.r(), bmask.r()], [c.X8.r()])
                    TT("dve", F(c.Z8), F(c.XTf), bmv[:, 0, :], ALU.mult, [c.XTf.r(), bmask.r()], [c.Z8.r()])
                    yield
                    TT("dve", F(c.E0), F(c.Z8), ident[:], ALU.add, [c.Z8.r(), r_c], [c.E0.r()])

                def st_base1(c, b):
                    bk = banks[c.bN]
                    MM(bk[:, 0:128], F(c.Z8), F(c.X8), [c.Z8.r(), c.X8.r()], [rbank[c.bN]])
                    MM(bk[:, 128:256], F(c.X8), F(c.Z8), [c.Z8.r(), c.X8.r()], [rbank[c.bN]])
                    yield
                    CP("act", F(c.Y1), bk[:, 0:128], [], [c.Y1.r(), rbank[c.bN]])
                    CP("act", F(c.Z1), bk[:, 128:256], [], [c.Z1.r(), rbank[c.bN]])
                    yield
                    MM(bk[:, 256:384], F(c.Y1), F(c.E0), [c.Y1.r(), c.E0.r()], [rbank[c.bN]])
                    yield
                    TT("dve", F(c.E1), F(c.E0), bk[:, 256:384], ALU.add, [c.E0.r()], [c.E1.r(), rbank[c.bN]])

                def st_base2(c, b):
                    bk = banks[c.bN]
                    MM(bk[:, 0:128], F(c.Z1), F(c.Y1), [c.Z1.r(), c.Y1.r()], [rbank[c.bN]])
                    yield
                    CP("act", F(c.Y2), bk[:, 0:128], [], [c.Y2.r(), rbank[c.bN]])
                    yield
                    MM(bk[:, 128:256], F(c.Y2), F(c.E1), [c.Y2.r(), c.E1.r()], [rbank[c.bN]])
                    yield
                    TT("dve", F(c.E0), F(c.E1), bk[:, 128:256], ALU.add, [c.E1.r()], [c.E0.r(), rbank[c.bN]])
                    yield
                    TR(bk[:, 256:384], F(c.E0), ident[:], [c.E0.r(), r_c], [rbank[c.bN]])
                    yield
                    CP("act", B(c.Db[0]), bk[:, 256:384], [], [c.Db[0].r(), rbank[c.bN]])
                    CP("act", B(c.Eb[0]), F(c.E0), [c.E0.r()], [c.Eb[0].r()])

                def st_merge(l):
                    def f(c, b):
                        bk = banks[c.bN]
                        Dp, Ep = c.Db[l % 2], c.Eb[l % 2]
                        Dn, En = c.Db[(l + 1) % 2], c.Eb[(l + 1) % 2]
                        mk = bmv[:, 1 + l, :]
                        if l < 3:
                            MM(bk[:, 0:128], B(c.XTb), B(Dp), [c.XTb.r(), Dp.r()], [rbank[c.bN]])
                        MM(bk[:, 128:256], B(c.Xb), B(Ep), [c.Xb.r(), Ep.r()], [rbank[c.bN]])
                        yield
                        if l < 3:
                            TT("dve", B(c.M1), bk[:, 0:128], mk, ALU.mult, [bmask.r()], [c.M1.r(), rbank[c.bN]])
                        TT("dve", B(c.M1p), bk[:, 128:256], mk, ALU.mult, [bmask.r()], [c.M1p.r(), rbank[c.bN]])
                        yield
                        if l < 3:
                            MM(bk[:, 256:384], identb[:], B(Dp), [r_c, Dp.r()], [rbank[c.bN]], start=True, stop=False)
                            MM(bk[:, 256:384], B(Ep), B(c.M1), [Ep.r(), c.M1.r()], [rbank[c.bN]], start=False, stop=True)
                        MM(bk[:, 384:512], identb[:], B(Ep), [r_c, Ep.r()], [rbank[c.bN]], start=True, stop=False)
                        MM(bk[:, 384:512], B(Dp), B(c.M1p), [Dp.r(), c.M1p.r()], [rbank[c.bN]], start=False, stop=True)
                        yield
                        if l < 3:
                            CP("act", B(Dn), bk[:, 256:384], [], [Dn.r(), rbank[c.bN]])
                        CP("act", B(En), bk[:, 384:512], [], [En.r(), rbank[c.bN]])
                    return f

                def st_c(c, b):
                    bs = slice(b * 128, (b + 1) * 128)
                    bk = banks[c.bA]
                    TR(bfps(c.bA, 0, 128), knb.bf(T)[:, bs], identb[:], [knb.r(), r_c], [rbank[c.bA]])
                    TR(bk[:, 128:256], vf.f32(T)[:, bs], ident[:], [vf.r(), r_c], [rbank[c.bA]])
                    yield
                    ACT(B(c.vb), bk[:, 128:256], AF.Identity, [beta.r()], [c.vb.r(), rbank[c.bA]], scale=beta.f32(NG, 8)[:, b, h:h + 1])
                    ACT(B(c.kbg), bfps(c.bA, 0, 128), AF.Identity, [beg.r()], [c.kbg.r(), rbank[c.bA]], scale=beg.f32(NG, 8)[:, b, h:h + 1])
                    ACT(B(c.kg), bfps(c.bA, 0, 128), AF.Identity, [egl.r()], [c.kg.r(), rbank[c.bA]], scale=egl.f32(NG, 8)[:, b, h:h + 1])
                    yield
                    MM(bk[:, 256:384], B(c.PTb), B(c.vb), [c.PTb.r(), c.vb.r()], [rbank[c.bA]])
                    MM(bk[:, 384:512], B(c.kbg), B(c.PTb), [c.PTb.r(), c.kbg.r()], [rbank[c.bA]])
                    yield
                    CP("act", F(c.u), bk[:, 256:384], [], [c.u.r(), rbank[c.bA]])
                    CP("dve", B(c.wkT), bk[:, 384:512], [], [c.wkT.r(), rbank[c.bA]])

                def recur(c, b):
                    b6, b7 = banks[6], banks[7]
                    MM(b6[:, 0:128], B(c.wkT), B(Sb), [c.wkT.r(), Sb.r()], [rbank[6]])
                    TT("dve", B(c.vnew), F(c.u), b6[:, 0:128], ALU.subtract, [c.u.r()], [c.vnew.r(), rbank[6]])
                    MM(b7[:, 0:128], B(c.qgT), B(Sb), [c.qgT.r(), Sb.r()], [rbank[7]], start=True, stop=False)
                    MM(b7[:, 0:128], B(c.attnT), B(c.vnew), [c.attnT.r(), c.vnew.r()], [rbank[7]], start=False, stop=True)
                    MM(b6[:, 128:256], B(c.kg), B(c.vnew), [c.kg.r(), c.vnew.r()], [rbank[6]])
                    STT("dve", F(Sf), F(Sf), gl.f32(NB, 8)[:, b, h:h + 1], b6[:, 128:256], ALU.mult, ALU.add, [gl.r()], [Sf.r(), rbank[6]])
                    CP("act", B(Sb), F(Sf), [Sf.r()], [Sb.r()])
                    CP("act", F(c.osb), b7[:, 0:128], [], [c.osb.r(), rbank[7]])

                def recur_post(c, b):
                    bs = slice(b * 128, (b + 1) * 128)
                    ACT(F(c.on), F(c.osb), AF.Square, [c.osb.r()], [c.on.r(), c.ssq.r()], accum=c.ssq.f32(1))
                    rsqrt_eps(c.ssq.f32(1), c.ssq.f32(1), RMS_EPS * 128.0, c.ssq.r())
                    STT("dve", F(c.on), F(c.osb), c.ssq.f32(1), F(gnb), ALU.mult, ALU.mult, [c.osb.r(), c.ssq.r(), gnb.r()], [c.on.r()])
                    TR(banks[c.bN][:, 384:512], F(c.on), ident[:], [c.on.r(), r_c], [rbank[c.bN]])
                    TT("dve", ogb_ap[:, bs], banks[c.bN][:, 384:512], szf.f32(T)[:, bs], ALU.mult, [szf.r()], [Ub.r(), rbank[c.bN]])

                stages = [st_a, st_b, st_base1, st_base2] + [st_merge(l) for l in range(4)] + [st_c]
                pend_ = []
                for g0 in range(0, NB if "gdn_noblk" not in DBG else 0, G):
                    grp = [(chains[i], g0 + i) for i in range(min(G, NB - g0))]
                    for stg_ in stages:
                        gens_ = [stg_(c, b) for c, b in grp]
                        while gens_:
                            nx_ = []
                            for gn_ in gens_:
                                try:
                                    next(gn_)
                                    nx_.append(gn_)
                                except StopIteration:
                                    pass
                            gens_ = nx_
                    for c, b in grp:
                        recur(c, b)
                        if pend_:
                            recur_post(*pend_.pop())
                        pend_.append((c, b))
                    if pend_:
                        recur_post(*pend_.pop())
                if pend_:
                    recur_post(*pend_.pop())
                for dc in range(8):
                    for ti, (c0, n) in enumerate(ptiles):
                        bk = 4 + cntb[0] % 2
                        cntb[0] += 1
                        MM(banks[bk][:, 0:n], wo.bf(1024)[:, dc * 128:(dc + 1) * 128], ogb_ap[:, c0:c0 + n], [wo.r(), Ub.r()], [rbank[bk]])
                        xv = xf[:, dc, c0:c0 + n]
                        if h == 0:
                            STT("dve", xv, xv, ALPHA, banks[bk][:, 0:n], ALU.mult, ALU.add, [r_xf[dc][ti]], [r_xf[dc][ti], rbank[bk]])
                        else:
                            TT("dve", xv, xv, banks[bk][:, 0:n], ALU.add, [r_xf[dc][ti]], [r_xf[dc][ti], rbank[bk]])
                P.dma("sp", "gsp", o_gdn_p[h], F(Sf), reads=[Sf.r()])
            A.release(mH, list(bufs))
            del bufs[:]
            cst_ = A.f32(3072)
            for q4 in range(6):
                for i4 in range(4):
                    ch = q4 * 4 + i4
                    TR(banks[4][0:3, i4 * 128:(i4 + 1) * 128], luv[:, ch, :], ident[:], [lastU.r(), r_c], [rbank[4]])
                CP("dve", cst_.f32(3072)[0:3, q4 * 512:(q4 + 1) * 512], banks[4][0:3, :], [], [cst_.r(), rbank[4]])
            P.dma("sp", "gcp", o_conv_p, cst_.f32(3072)[0:3, :], reads=[cst_.r()])
            A.release(mH, [cst_])
            mS = A.mark()
            if "gdn_nosamp" in DBG:
                A.release(m, base_bufs)
                return
            sb_ = []

            def als(n, bf=False):
                b = A.bf(n) if bf else A.f32(n)
                sb_.append(b)
                return b
            projs = als(4096)
            for typ in range(4):
                for h2 in range(2):
                    bk = (typ * 2 + h2) % 2
                    for i4 in range(4):
                        hh = h2 * 4 + i4
                        TR(banks[bk][0:NS, i4 * 128:(i4 + 1) * 128], pjT[:, typ, hh, :], ident[:], [projsT.r(), r_c], [rbank[bk]])
                    CP("dve", projs.f32(4096)[0:NS, typ * 1024 + h2 * 512:typ * 1024 + (h2 + 1) * 512], banks[bk][0:NS, :], [], [projs.r(), rbank[bk]])
            P.dma("sp", "gcv", o_conv_s[:, 0:2, :], sconv[:, 1:3, :])
            P.dma("sp", "gcv", o_conv_s[:, 2, :], projs.f32(4096)[0:NS, 0:3072], reads=[projs.r()])
            P.commit("gcv")
            qkv = als(3072)
            zs = als(1024)
            mC = A.mark()
            PW = 256
            ext = A.f32(4 * PW); cwb = A.f32(4 * PW); prod = A.f32(4 * PW)
            for pc in range(3072 // PW):
                cs_ = slice(pc * PW, (pc + 1) * PW)
                P.dma("sp", "gsl", ext.f32(4, PW)[0:NS, 0:3, :], sconv[:, :, cs_], writes=[ext.r()])
                P.dma("sp", "gsl", cwb.f32(4, PW)[0:NS], gdn_conv_w[:, cs_].partition_broadcast(NS), writes=[cwb.r()])
                P.commit("gsl")
                CP("dve", ext.f32(4, PW)[0:NS, 3, :], projs.f32(4096)[0:NS, cs_], [projs.r()], [ext.r()])
                TT("dve", prod.f32(4, PW)[0:NS], ext.f32(4, PW)[0:NS], cwb.f32(4, PW)[0:NS], ALU.mult, [ext.r(), cwb.r()], [prod.r()])
                RED("dve", qkv.f32(3072)[0:NS, cs_], prod.f32(4, PW)[0:NS].rearrange("p j c -> p c j"), [prod.r()], [qkv.r()])
            A.release(mC, [ext, cwb, prod])
            ACT(qkv.f32(3072)[0:NS], qkv.f32(3072)[0:NS], AF.Silu, [qkv.r()], [qkv.r()])
            ACT(zs.f32(1024)[0:NS], projs.f32(4096)[0:NS, 3072:4096], AF.Silu, [projs.r()], [zs.r()])
            q3 = qkv.f32(3, 8, 128)[0:NS, 0]
            k3 = qkv.f32(3, 8, 128)[0:NS, 1]
            v3 = qkv.f32(3, 8, 128)[0:NS, 2]
            tmp = als(1024); ss = als(16)
            t3 = tmp.f32(8, 128)[0:NS]
            for (x3, scl) in ((q3, 128.0 ** -0.5), (k3, 1.0)):
                TT("dve", t3, x3, x3, ALU.mult, [qkv.r()], [tmp.r()])
                RED("dve", ss.f32(8)[0:NS], t3, [tmp.r()], [ss.r()])
                rsqrt_eps(ss.f32(8)[0:NS], ss.f32(8)[0:NS], RMS_EPS, ss.r())
                STT("dve", x3, x3, scl, ss.f32(8)[0:NS].unsqueeze(2).to_broadcast([NS, 8, 128]), ALU.mult, ALU.mult, [ss.r()], [qkv.r()])
            qk = als(8)
            TT("dve", t3, q3, k3, ALU.mult, [qkv.r()], [tmp.r()])
            RED("dve", qk.f32(8)[0:NS], t3, [tmp.r()], [qk.r()])
            kTs = als(8 * NS); qTs = als(8 * NS)
            for (x3, dst, bk) in ((k3, kTs, 4), (q3, qTs, 5)):
                for hh in range(8):
                    TR(banks[bk][:, hh * NS:(hh + 1) * NS], x3[:, hh, :], ident[0:NS, 0:NS], [qkv.r(), r_c], [rbank[bk]])
                CP("dve", dst.f32(8 * NS), banks[bk][:, 0:8 * NS], [], [dst.r(), rbank[bk]])
            dlt = als(NS * NS)
            P.dma("sp", "gsm", dlt.f32(NS, NS), cd["delta"], writes=[dlt.r()])
            dcol = als(NS)
            P.dma("sp", "gsm", dcol.f32(NS), cd["dcol"], writes=[dcol.r()])
            P.commit("gsm")
            KmL = [als(NS * NS), als(NS * NS)]; QmL = [als(NS * NS), als(NS * NS)]
            Sp = [als(128) for _ in range(4)]
            ci_ = 0
            for hh in range(8):
                Km = KmL[hh % 2]; Qm = QmL[hh % 2]
                for (src, dst) in ((kTs, Km), (qTs, Qm)):
                    TT("dve", dst.f32(NS, NS), src.f32(8, NS)[:, hh, :].unsqueeze(1).to_broadcast([128, NS, NS]),
                       dlt.f32(NS, NS), ALU.mult, [src.r(), dlt.r()], [dst.r()])
                for s in range(NS):
                    sp_ = Sp[ci_ % 4]
                    P.dma("sp", "gsq%d" % (ci_ % 4), sp_.f32(128), sgdn[s, hh], writes=[sp_.r()])
                    ci_ += 1
                    MM(banks[hh // 4][0:NS, (hh % 4) * 128:(hh % 4 + 1) * 128], Km.f32(NS, NS)[:, s, :], sp_.f32(128),
                       [Km.r(), sp_.r()], [rbank[hh // 4]], start=(s == 0), stop=(s == NS - 1))
                    MM(banks[2 + hh // 4][0:NS, (hh % 4) * 128:(hh % 4 + 1) * 128], Qm.f32(NS, NS)[:, s, :], sp_.f32(128),
                       [Qm.r(), sp_.r()], [rbank[2 + hh // 4]], start=(s == 0), stop=(s == NS - 1))
            KS = als(1024); QS = als(1024)
            for half in range(2):
                CP("dve", KS.f32(1024)[0:NS, half * 512:(half + 1) * 512], banks[half][0:NS, :], [], [KS.r(), rbank[half]])
                CP("dve", QS.f32(1024)[0:NS, half * 512:(half + 1) * 512], banks[2 + half][0:NS, :], [], [QS.r(), rbank[2 + half]])
            bc8 = lambda b_, col: b_.f32(NG, 8)[0:NS, col, :].unsqueeze(2).to_broadcast([NS, 8, 128])
            KS3 = KS.f32(8, 128)[0:NS]; QS3 = QS.f32(8, 128)[0:NS]
            vn = als(1024)
            vn3 = vn.f32(8, 128)[0:NS]
            TT("dve", KS3, KS3, bc8(eg, NB), ALU.mult, [eg.r()], [KS.r()])
            TT("dve", vn3, v3, KS3, ALU.subtract, [qkv.r(), KS.r()], [vn.r()])
            TT("dve", vn3, vn3, bc8(beta, NB), ALU.mult, [beta.r()], [vn.r()])
            TT("dve", QS3, QS3, bc8(eg, NB), ALU.mult, [eg.r()], [QS.r()])
            TT("dve", t3, vn3, qk.f32(8)[0:NS].unsqueeze(2).to_broadcast([NS, 8, 128]), ALU.mult, [vn.r(), qk.r()], [tmp.r()])
            TT("dve", QS3, QS3, t3, ALU.add, [tmp.r()], [QS.r()])
            TT("dve", t3, QS3, QS3, ALU.mult, [QS.r()], [tmp.r()])
            RED("dve", ss.f32(8)[0:NS], t3, [tmp.r()], [ss.r()])
            rsqrt_eps(ss.f32(8)[0:NS], ss.f32(8)[0:NS], RMS_EPS, ss.r(), scale=1.0 / 128.0)
            TT("dve", QS3, QS3, ss.f32(8)[0:NS].unsqueeze(2).to_broadcast([NS, 8, 128]), ALU.mult, [ss.r()], [QS.r()])
            TT("dve", QS3, QS3, gn.f32(128)[0:NS].unsqueeze(1).to_broadcast([NS, 8, 128]), ALU.mult, [gn.r()], [QS.r()])
            TT("dve", QS3, QS3, zs.f32(8, 128)[0:NS], ALU.mult, [zs.r()], [QS.r()])
            ogsT = als(8 * NS, bf=True)
            for hh in range(8):
                TR(banks[4][:, hh * NS:(hh + 1) * NS], QS3[:, hh, :], ident[0:NS, 0:NS], [QS.r(), r_c], [rbank[4]])
            CP("dve", ogsT.bf(8 * NS), banks[4][:, 0:8 * NS], [], [ogsT.r(), rbank[4]])
            out_proj_samples(gdn_w_out, 8, ogsT.bf(8, NS), ogsT.r())
            Rm = als(NS * 8)
            TT("dve", Rm.f32(NS, 8)[0:NS], dcol.f32(NS)[0:NS].unsqueeze(2).to_broadcast([NS, NS, 8]),
               eg.f32(NG, 8)[0:NS, NB, :].unsqueeze(1).to_broadcast([NS, NS, 8]), ALU.mult, [dcol.r(), eg.r()], [Rm.r()])
            EGb = als(NS * 8)
            MM(banks[4][:, 0:NS * 8], onesf[0:NS, :], Rm.f32(NS * 8)[0:NS], [Rm.r(), r_c], [rbank[4]])
            CP("dve", EGb.f32(NS * 8), banks[4][:, 0:NS * 8], [], [EGb.r(), rbank[4]])
            Vm = [als(1024), als(1024)]
            Sn = [als(128) for _ in range(4)]
            ci_ = 0
            for s in range(NS):
                vm_ = Vm[s % 2]
                TS("dve", vm_.f32(1024)[0:NS], vn.f32(1024)[0:NS], dcol.f32(NS)[0:NS, s:s + 1], None, ALU.mult, None, [vn.r(), dcol.r()], [vm_.r()])
                for hh in range(8):
                    sp_ = Sp[ci_ % 4]; sn_ = Sn[ci_ % 4]
                    bk = 4 + ci_ % 4
                    P.dma("sp", "gsq%d" % (ci_ % 4), sp_.f32(128), sgdn[s, hh], writes=[sp_.r()])
                    MM(banks[bk][:, 0:128], k3[:, hh, :], vm_.f32(8, 128)[0:NS, hh, :], [qkv.r(), vm_.r()], [rbank[bk]])
                    STT("dve", sn_.f32(128), sp_.f32(128), EGb.f32(NS, 8)[:, s, hh:hh + 1], banks[bk][:, 0:128], ALU.mult, ALU.add,
                        [sp_.r(), EGb.r()], [sn_.r(), rbank[bk]])
                    P.dma("pool", "gst%d" % (ci_ % 4), o_gdn_s[s, hh], sn_.f32(128), reads=[sn_.r()])
                    ci_ += 1
            A.release(mS, sb_)
            A.release(m, base_bufs)

        def ret_mixer():
            G = 2
            m = A.mark()
            bufs = []

            def al(n, bf=False):
                b = A.bf(n) if bf else A.f32(n)
                bufs.append(b)
                return b
            cos2 = al(NT); sin2 = al(NT); decT = al(128); xi = al(128); zeta = al(8); dcol = al(NS); dlt = al(NS * NS)
            P.dma("sp", "rc", cos2.f32(NT), cd["cos2"], writes=[cos2.r()])
            P.dma("sp", "rc", sin2.f32(NT), cd["sin2"], writes=[sin2.r()])
            P.dma("sp", "rc", zeta.f32(8), cd["zeta"], writes=[zeta.r()])
            P.dma("sp", "rc", dcol.f32(NS), cd["dcol"], writes=[dcol.r()])
            P.dma("sp", "rc", dlt.f32(NS, NS), cd["delta"], writes=[dlt.r()])
            P.commit("rc")
            SC = 128.0 ** -0.5
            qrb = al(NT, bf=True); krb = al(NT, bf=True); krf = al(NT); qsf = al(NS)
            t1 = [al(TTS)]; t2 = [al(TTS)]
            vtok = al(NB * 256, bf=True)
            vtv = vtok.bf(NB, 256)
            ogT = al(2 * NT, bf=True)
            ogv = ogT.bf(2, NT)
            Sf = al(256); Sb = al(256, bf=True)
            v_s = al(256); sg_s = al(256); kts = al(128); Qm = al(NS * NS); qk = al(8); tmp_s = al(256); o_s = al(256)
            Sp = [al(256) for _ in range(2)]
            Vm = [al(256) for _ in range(2)]; Sn = [al(256) for _ in range(2)]
            stat = [al(8) for _ in range(3)]

            class Ch:
                pass
            chains = []
            for ci in range(2 * G):
                c = Ch()
                c.attnT = al(128, bf=True); c.qxT = al(128, bf=True); c.kz = al(128, bf=True)
                c.sg = al(256); c.on = al(256); c.osb = al(256)
                c.bA = (0, 1, 4, 5)[ci]
                chains.append(c)
            B = lambda b_: b_.bf(128)

            def gnorm_gate(o_ap, np_, on_ap, on_res, sg_ap, sg_res, o_reads, o_writes):
                sm, sq_, mm = stat[0], stat[1], stat[2]
                ACT(on_ap, o_ap, AF.Identity, o_reads, [on_res, sm.r()] + o_writes, accum=sm.f32(1)[0:np_])
                ACT(on_ap, o_ap, AF.Square, o_reads, [on_res, sq_.r()] + o_writes, accum=sq_.f32(1)[0:np_])
                TS("dve", sm.f32(1)[0:np_], sm.f32(1)[0:np_], 1.0 / 256.0, None, ALU.mult, None, [sm.r()], [sm.r()])
                TT("dve", mm.f32(1)[0:np_], sm.f32(1)[0:np_], sm.f32(1)[0:np_], ALU.mult, [sm.r()], [mm.r()])
                STT("dve", sq_.f32(1)[0:np_], sq_.f32(1)[0:np_], 1.0 / 256.0, mm.f32(1)[0:np_], ALU.mult, ALU.subtract, [sq_.r(), mm.r()], [sq_.r()])
                rsqrt_eps(sq_.f32(1)[0:np_], sq_.f32(1)[0:np_], LN_EPS, sq_.r())
                TS("dve", on_ap, o_ap, sm.f32(1)[0:np_], sq_.f32(1)[0:np_], ALU.subtract, ALU.mult, o_reads + [sm.r(), sq_.r()], [on_res] + o_writes)
                TT("pool", on_ap, on_ap, sg_ap, ALU.mult, [on_res, sg_res], [on_res])

            for h in range(8):
                slots = []
                for xi_, base in enumerate((0, 1024)):
                    wb_, wk_ = wslot()
                    wv_ = wb_.bf(2, 8, 128)
                    c0_ = base + h * 128
                    r3 = lambda a, b_: ret_w_in[:, a:b_].rearrange("(k p) n -> p k n", p=128)
                    P.dma("pool", wk_, wv_[:, 0], r3(c0_, c0_ + 128), writes=[wb_.r()])
                    P.dma("pool", wk_, wv_[:, 1, :, 0:64], r3(c0_ + 64, c0_ + 128), writes=[wb_.r()])
                    P.dma("pool", wk_, wv_[:, 1, :, 64:128], r3(c0_, c0_ + 64), writes=[wb_.r()])
                    P.commit(wk_)
                    slots.append((wb_, wv_))
                P.dma("sp", "rdx", decT.f32(128), cd["decT"][:, h, :], writes=[decT.r()])
                P.dma("sp", "rdx", xi.f32(128), cd["xi"][:, h, :], writes=[xi.r()])
                P.commit("rdx")
                wv_t, wkv_ = wslot()
                P.dma("pool", wkv_, wv_t.bf(8, 256), ret_w_in[:, 2048 + h * 256:2048 + (h + 1) * 256].rearrange("(k p) n -> p k n", p=128), writes=[wv_t.r()])
                P.commit(wkv_)
                wg_t, wkg_ = wslot()
                P.dma("pool", wkg_, wg_t.bf(8, 256), ret_w_in[:, 4096 + h * 256:4096 + (h + 1) * 256].rearrange("(k p) n -> p k n", p=128), writes=[wg_t.r()])
                P.commit(wkg_)
                cn = 0
                for xi_ in range(2):
                    wb_, wv_ = slots[xi_]
                    for ti, (c0, n) in enumerate(tiles):
                        a1, a2 = t1[0], t2[0]
                        ba_, bb_ = 4 + 2 * (cn % 2), 5 + 2 * (cn % 2)
                        cn += 1
                        for k in range(8):
                            MM(banks[ba_][:, 0:n], wv_[:, 0, k, :], xb[:, k, c0:c0 + n], [wb_.r(), r_xb[k][ti]], [rbank[ba_]], start=(k == 0), stop=(k == 7))
                        for k in range(8):
                            MM(banks[bb_][:, 0:n], wv_[:, 1, k, :], xb[:, k, c0:c0 + n], [wb_.r(), r_xb[k][ti]], [rbank[bb_]], start=(k == 0), stop=(k == 7))
                        TT("dve", a1.f32(n), banks[ba_][:, 0:n], cos2.f32(NT)[:, c0:c0 + n], ALU.mult, [cos2.r()], [a1.r(), rbank[ba_]])
                        TT("dve", a2.f32(n), banks[bb_][:, 0:n], sin2.f32(NT)[:, c0:c0 + n], ALU.mult, [sin2.r()], [a2.r(), rbank[bb_]])
                        if xi_ == 0:
                            TT("pool", qrb.bf(NT)[:, c0:c0 + n], a1.f32(n), a2.f32(n), ALU.add, [a1.r(), a2.r()], [qrb.r()])
                            if ti == NTI - 1:
                                TT("pool", qsf.f32(NS), a1.f32(n), a2.f32(n), ALU.add, [a1.r(), a2.r()], [qsf.r()])
                        else:
                            TT("pool", krf.f32(NT)[:, c0:c0 + n], a1.f32(n), a2.f32(n), ALU.add, [a1.r(), a2.r()], [krf.r()])
                CP("pool", krb.bf(NT), krf.f32(NT), [krf.r()], [krb.r()])
                for b in range(NB):
                    bs = slice(b * 128, (b + 1) * 128)
                    bv_ = 6 + b % 2
                    for k in range(8):
                        MM(banks[bv_][:, 0:256], xb[:, k, bs], wv_t.bf(8, 256)[:, k, :], [wv_t.r()] + rx(r_xb, [k], tiles_of(b * 128, 128)), [rbank[bv_]],
                           start=(k == 0), stop=(k == 7))
                    CP("act", vtv[:, b, :], banks[bv_][:, 0:256], [], [vtok.r(), rbank[bv_]])
                for k in range(8):
                    MM(banks[6][0:NS, 256:512], xb[:, k, T:NT], wv_t.bf(8, 256)[:, k, :], [wv_t.r(), r_xb[k][NTI - 1]], [rbank[6]], start=(k == 0), stop=(k == 7))
                CP("act", v_s.f32(256)[0:NS], banks[6][0:NS, 256:512], [], [v_s.r(), rbank[6]])
                MS("dve", Sf.f32(256), 0.0, [Sf.r()])
                MS("dve", Sb.bf(256), 0.0, [Sb.r()])

                def st_a(c, b):
                    bs = slice(b * 128, (b + 1) * 128)
                    bk = banks[c.bA]
                    MM(bk[:, 0:128], krb.bf(NT)[:, bs], qrb.bf(NT)[:, bs], [krb.r(), qrb.r()], [rbank[c.bA]])
                    TR(bk[:, 128:256], krf.f32(NT)[:, bs], ident[:], [krf.r(), r_c], [rbank[c.bA]])
                    for k in range(8):
                        MM(bk[:, 256:512], xb[:, k, bs], wg_t.bf(8, 256)[:, k, :], [wg_t.r()] + rx(r_xb, [k], tiles_of(b * 128, 128)), [rbank[c.bA]],
                           start=(k == 0), stop=(k == 7))
                    yield
                    TT("dve", B(c.attnT), bk[:, 0:128], decT.f32(128), ALU.mult, [decT.r()], [c.attnT.r(), rbank[c.bA]])
                    ACT(B(c.kz), bk[:, 128:256], AF.Identity, [zeta.r()], [c.kz.r(), rbank[c.bA]], scale=zeta.f32(8)[:, h:h + 1])
                    ACT(c.sg.f32(256), bk[:, 256:512], AF.Silu, [], [c.sg.r(), rbank[c.bA]])
                    TT("pool", B(c.qxT), qrb.bf(NT)[:, bs], xi.f32(128), ALU.mult, [qrb.r(), xi.r()], [c.qxT.r()])

                def recur(c, b):
                    MM(banks[2][:, 0:256], B(c.qxT), Sb.bf(256), [c.qxT.r(), Sb.r()], [rbank[2]], start=True, stop=False)
                    MM(banks[2][:, 0:256], B(c.attnT), vtv[:, b, :], [c.attnT.r(), vtok.r()], [rbank[2]], start=False, stop=True)
                    MM(banks[3][:, 0:256], B(c.kz), vtv[:, b, :], [c.kz.r(), vtok.r()], [rbank[3]])
                    STT("dve", Sf.f32(256), Sf.f32(256), cst["gC"][h], banks[3][:, 0:256], ALU.mult, ALU.add, [], [Sf.r(), rbank[3]])
                    CP("act", Sb.bf(256), Sf.f32(256), [Sf.r()], [Sb.r()])
                    CP("act", c.osb.f32(256), banks[2][:, 0:256], [], [c.osb.r(), rbank[2]])

                def recur_post(c, b):
                    bs = slice(b * 128, (b + 1) * 128)
                    gnorm_gate(c.osb.f32(256), 128, c.on.f32(256), c.on.r(), c.sg.f32(256), c.sg.r(), [c.osb.r()], [])
                    for cc in range(2):
                        TR(banks[7][:, 256 + cc * 128:256 + (cc + 1) * 128], c.on.f32(256)[:, cc * 128:(cc + 1) * 128], ident[:], [c.on.r(), r_c], [rbank[7]])
                    CP("act", ogv[:, :, bs], banks[7][:, 256:512].rearrange("p (c t) -> p c t", c=2, t=128), [], [ogT.r(), rbank[7]])

                pend_ = []
                for g0 in range(0, NB if "ret_noblk" not in DBG else 0, G):
                    cs_ = ((g0 // G) % 2) * G
                    grp = [(chains[cs_ + i], g0 + i) for i in range(min(G, NB - g0))]
                    gens_ = [st_a(c, b) for c, b in grp]
                    first_ = True
                    while gens_:
                        nx_ = []
                        for gn_ in gens_:
                            try:
                                next(gn_)
                                nx_.append(gn_)
                            except StopIteration:
                                pass
                        gens_ = nx_
                        if first_:
                            for cb_ in pend_:
                                recur_post(*cb_)
                            first_ = False
                    for c, b in grp:
                        recur(c, b)
                    pend_ = list(grp)
                for cb_ in pend_:
                    recur_post(*cb_)
                P.dma("sp", "rsp", o_ret_p[h], Sf.f32(256), reads=[Sf.r()])
                for k in range(8 if "ret_nosamp" not in DBG else 0):
                    MM(banks[6][0:NS, 0:256], xb[:, k, T:NT], wg_t.bf(8, 256)[:, k, :], [wg_t.r(), r_xb[k][NTI - 1]], [rbank[6]], start=(k == 0), stop=(k == 7))
                if "ret_nosamp" not in DBG:
                    ACT(sg_s.f32(256)[0:NS], banks[6][0:NS, 0:256], AF.Silu, [], [sg_s.r(), rbank[6]])
                    ks_ = krf.f32(NT)[:, T:NT]
                    TR(banks[7][0:NS, 0:128], ks_, ident[:], [krf.r(), r_c], [rbank[7]])
                    CP("dve", kts.f32(128)[0:NS], banks[7][0:NS, 0:128], [], [kts.r(), rbank[7]])
                    TT("dve", tmp_s.f32(NS), qsf.f32(NS), ks_, ALU.mult, [qsf.r(), krf.r()], [tmp_s.r()])
                    MM(banks[7][0:NS, 128:129], tmp_s.f32(NS), onesf[:, 0:1], [tmp_s.r(), r_c], [rbank[7]])
                    TS("dve", qk.f32(1)[0:NS], banks[7][0:NS, 128:129], SC, None, ALU.mult, None, [], [qk.r(), rbank[7]])
                    TT("dve", Qm.f32(NS, NS), qsf.f32(NS).unsqueeze(1).to_broadcast([128, NS, NS]), dlt.f32(NS, NS), ALU.mult, [qsf.r(), dlt.r()], [Qm.r()])
                    for s in range(NS):
                        sp_ = Sp[s % 2]; vm_ = Vm[s % 2]; sn_ = Sn[s % 2]
                        bk = 4 + s % 2
                        P.dma("sp", "rsq%d" % (s % 2), sp_.f32(256), sret[s, h], writes=[sp_.r()])
                        MM(banks[6][0:NS, 256:512], Qm.f32(NS, NS)[:, s, :], sp_.f32(256), [Qm.r(), sp_.r()], [rbank[6]], start=(s == 0), stop=(s == NS - 1))
                        TS("dve", vm_.f32(256)[0:NS], v_s.f32(256)[0:NS], dcol.f32(NS)[0:NS, s:s + 1], SC, ALU.mult, ALU.mult, [v_s.r(), dcol.r()], [vm_.r()])
                        MM(banks[bk][:, 0:256], kts.f32(128)[0:NS], vm_.f32(256)[0:NS], [kts.r(), vm_.r()], [rbank[bk]])
                        STT("dve", sn_.f32(256), sp_.f32(256), cst["gam"][h], banks[bk][:, 0:256], ALU.mult, ALU.add, [sp_.r()], [sn_.r(), rbank[bk]])
                        P.dma("pool", "rst%d" % (s % 2), o_ret_s[s, h], sn_.f32(256), reads=[sn_.r()])
                    TS("dve", tmp_s.f32(256)[0:NS], v_s.f32(256)[0:NS], qk.f32(1)[0:NS], None, ALU.mult, None, [v_s.r(), qk.r()], [tmp_s.r()])
                    STT("dve", o_s.f32(256)[0:NS], banks[6][0:NS, 256:512], cst["gam"][h], tmp_s.f32(256)[0:NS], ALU.mult, ALU.add, [tmp_s.r()], [o_s.r(), rbank[6]])
                    gnorm_gate(o_s.f32(256)[0:NS], NS, tmp_s.f32(256)[0:NS], tmp_s.r(), sg_s.f32(256)[0:NS], sg_s.r(), [o_s.r()], [])
                    for cc in range(2):
                        TR(banks[7][:, 256 + cc * NS:256 + (cc + 1) * NS], tmp_s.f32(256)[0:NS, cc * 128:(cc + 1) * 128], ident[0:NS, 0:NS], [tmp_s.r(), r_c], [rbank[7]])
                    CP("dve", ogv[:, :, T:NT], banks[7][:, 256:256 + 2 * NS].rearrange("p (c t) -> p c t", c=2, t=NS), [], [ogT.r(), rbank[7]])
                wo_t, wko_ = wslot()
                P.dma("pool", wko_, wo_t.bf(2, 1024), ret_w_out[h * 256:(h + 1) * 256, :].rearrange("(c p) n -> p c n", p=128), writes=[wo_t.r()])
                P.commit(wko_)
                for dc in range(8):
                    for ti, (c0, n) in enumerate(tiles):
                        bk = 4 + (dc * NTI + ti) % 2
                        for cc in range(2):
                            MM(banks[bk][:, 0:n], wo_t.bf(2, 1024)[:, cc, dc * 128:(dc + 1) * 128], ogv[:, cc, c0:c0 + n], [wo_t.r(), ogT.r()], [rbank[bk]],
                               start=(cc == 0), stop=(cc == 1))
                        xv = xf[:, dc, c0:c0 + n]
                        if h == 0:
                            STT("dve", xv, xv, ALPHA, banks[bk][:, 0:n], ALU.mult, ALU.add, [r_xf[dc][ti]], [r_xf[dc][ti], rbank[bk]])
                        else:
                            TT("dve", xv, xv, banks[bk][:, 0:n], ALU.add, [r_xf[dc][ti]], [r_xf[dc][ti], rbank[bk]])
            A.release(m, bufs)

        if "noload" not in DBG:
            load_x()
        for li in range(nlayers):
            kind = li % 3
            if kind == 0:
                pool_mixer(li // 3)
            elif kind == 1:
                gdn_mixer()
            else:
                ret_mixer()
            layer_norm(2 * li)
            ffn(li)
            layer_norm(2 * li + 1)
        if "nostore" not in DBG:
            store_y()
        if "arena" in DBG:
            print("arena words", AW, "high water", A.hw)
        P.emit(st)
    return nc, cst


_CACHE = {}


def _in_maps(inputs, T, NS, cst):
    f = lambda a: np.ascontiguousarray(np.asarray(a, dtype=np.float32))
    shared = {
        "pool_w": f(inputs["pool_w"]), "pool_scale": f(inputs["pool_scale"]),
        "gdn_w_in": f(inputs["gdn_w_in"][0]), "gdn_conv_w": f(inputs["gdn_conv_w"][0]),
        "gdn_a_log": f(inputs["gdn_a_log"]), "gdn_dt_bias": f(inputs["gdn_dt_bias"]),
        "gdn_norm_g": f(inputs["gdn_norm_g"]), "gdn_w_out": f(inputs["gdn_w_out"][0]),
        "ret_w_in": f(inputs["ret_w_in"][0]), "ret_w_out": f(inputs["ret_w_out"][0]),
        "ffn_w13": f(inputs["ffn_w13"]), "ffn_w2": f(inputs["ffn_w2"]),
        "ln_g": f(inputs["ln_g"]).reshape(8, D), "ln_b": f(inputs["ln_b"]).reshape(8, D),
    }
    for k in list(CONST_SHAPES) + ["cos2", "sin2"]:
        shared["c_" + k] = f(cst[k])
    maps = []
    for c in range(NCORES):
        sl = slice(c * NS, (c + 1) * NS)
        mp = dict(shared)
        mp["xp"] = f(inputs["x_prompt"][c])
        mp["xs"] = f(inputs["x_sample"][sl, 0])
        mp["spool"] = f(inputs["state_pool"][:, sl])
        mp["sconv"] = f(inputs["state_gdn_conv"][0, sl])
        mp["sgdn"] = f(inputs["state_gdn"][0, sl])
        mp["sret"] = f(inputs["state_ret"][0, sl])
        maps.append(mp)
    return maps


def kernel(**inputs):
    T, NS = 2048, 16
    if "nc" not in _CACHE:
        _CACHE["nc"] = build(T, NS)
    nc, cst = _CACHE["nc"]
    maps = _in_maps(inputs, T, NS, cst)
    res = run_bass_kernel_spmd(nc, maps, core_ids=list(range(NCORES))).results
    cat = lambda k, ax: np.concatenate([np.asarray(r[k], dtype=np.float32)[None] if ax is None else np.asarray(r[k], dtype=np.float32)
                                        for r in res], axis=0 if ax is None else ax)
    y_prompt = cat("yp", None)
    y_sample = cat("ys", 0)[:, None, :]
    pool_p = np.stack([np.asarray(r["pool_p"], np.float32) for r in res], axis=1)
    pool_s = cat("pool_s", 1)
    conv_p = cat("conv_p", None)[None]
    conv_s = cat("conv_s", 0)[None]
    gdn_p = cat("gdn_p", None)[None]
    gdn_s = cat("gdn_s", 0)[None]
    ret_p = cat("ret_p", None)[None]
    ret_s = cat("ret_s", 0)[None]
    return (y_prompt, y_sample, pool_p, pool_s, conv_p, conv_s, gdn_p, gdn_s, ret_p, ret_s)
```

```python
import math
from contextlib import ExitStack
import numpy as np
import concourse.bass as bass
import concourse.mybir as mybir
from concourse.bass_utils import run_bass_kernel_spmd

F32 = mybir.dt.float32
BF16 = mybir.dt.bfloat16
ALU = mybir.AluOpType
AF = mybir.ActivationFunctionType
AX = mybir.AxisListType

D = 1024
DFF = 2816
NCORES = 8
PAST_LEN = 16384
ALPHA = 8.0 ** 0.25
LN_EPS = 1e-5
RMS_EPS = 1e-6
COMPUTE = ("pe", "act", "dve", "pool")


class Res:
    __slots__ = ("last_w", "readers")

    def __init__(self, inherit=None):
        self.last_w = None
        self.readers = dict(inherit) if inherit else {}


class Op:
    __slots__ = ("eng", "fn", "deps", "is_dma", "dkey", "dcount", "need_inc", "cnt", "idx")

    def __init__(self, eng, fn, is_dma=False, dkey=None):
        self.eng = eng
        self.fn = fn
        self.deps = []
        self.is_dma = is_dma
        self.dkey = dkey
        self.dcount = 0
        self.need_inc = False
        self.cnt = 0
        self.idx = 0


class Prog:
    def __init__(self, nc):
        self.nc = nc
        self.ops = []
        self.dma_counts = {}
        self.open_batch = {}

    def _add(self, op, reads, writes):
        op.idx = len(self.ops)
        deps = {}
        for r in reads:
            w = r.last_w
            if w is not None:
                deps[id(w)] = (w, True)
        for r in writes:
            w = r.last_w
            if w is not None and id(w) not in deps:
                deps[id(w)] = (w, False)
            for rd in r.readers.values():
                if id(rd) not in deps:
                    deps[id(rd)] = (rd, False)
        for (d, raw) in deps.values():
            if d is op:
                continue
            if (not d.is_dma) and (not op.is_dma) and d.eng == op.eng and d.eng == "pe":
                continue
            if d.is_dma and op.is_dma and d.dkey == op.dkey and d in self.open_batch.get(op.dkey, ()):
                continue
            op.deps.append(d)
            d.need_inc = True
        for r in reads:
            key = (op.eng, op.dkey) if op.is_dma else op.eng
            r.readers[key] = op
        for r in writes:
            r.last_w = op
            r.readers = {}
        self.ops.append(op)
        return op

    def op(self, eng, fn, reads=(), writes=()):
        return self._add(Op(eng, fn), reads, writes)

    def dma(self, eng, key, out, in_, reads=(), writes=()):
        def fn(e, out=out, in_=in_):
            return e.dma_start(out=out, in_=in_)
        o = Op(eng, fn, is_dma=True, dkey=key)
        self.dma_counts[key] = self.dma_counts.get(key, 0) + 16
        o.dcount = self.dma_counts[key]
        o.need_inc = True
        self.open_batch.setdefault(key, []).append(o)
        return self._add(o, reads, writes)

    def commit(self, key):
        for o in self.open_batch.get(key, []):
            o.dcount = self.dma_counts[key]
        self.open_batch[key] = []

    def emit(self, st):
        nc = self.nc
        sems = {e: st.enter_context(nc.semaphore("s_" + e)) for e in COMPUTE}
        dsems = {k: st.enter_context(nc.semaphore("d_%s" % (k,))) for k in self.dma_counts}
        cnts = {e: 0 for e in COMPUTE}
        for o in self.ops:
            if not o.is_dma and o.need_inc:
                cnts[o.eng] += 1
                o.cnt = cnts[o.eng]
        final_waits = {("d", k): (dsems[k], v) for k, v in self.dma_counts.items()}
        streams = {}
        for o in self.ops:
            streams.setdefault(o.eng, []).append(o)
        block = st.enter_context(nc.Block())

        def run(engname, e):
            wm = {}
            for o in streams.get(engname, []):
                need = {}
                for d in o.deps:
                    if d.is_dma:
                        k = ("d", d.dkey)
                        s, v = dsems[d.dkey], d.dcount
                    else:
                        k = ("c", d.eng)
                        s, v = sems[d.eng], d.cnt
                    if v > wm.get(k, 0) and v > need.get(k, (None, 0))[1]:
                        need[k] = (s, v)
                for k, (s, v) in need.items():
                    e.wait_ge(s, v)
                    wm[k] = v
                ins = o.fn(e)
                if o.is_dma:
                    ins.then_inc(dsems[o.dkey], 16)
                elif o.need_inc:
                    ins.then_inc(sems[o.eng], 1)
            if engname == "sp":
                for k, (s, v) in final_waits.items():
                    if v > wm.get(k, 0):
                        e.wait_ge(s, v)

        @block.sync
        def _(e):
            run("sp", e)

        @block.tensor
        def _(e):
            run("pe", e)

        @block.scalar
        def _(e):
            run("act", e)

        @block.vector
        def _(e):
            run("dve", e)

        @block.gpsimd
        def _(e):
            run("pool", e)


class Buf:
    def __init__(self, ap, lo, hi, inherit):
        self.ap = ap
        self.lo = lo
        self.hi = hi
        self.inherit = inherit
        self.ch = {}

    def r(self, key=None):
        c = self.ch.get(key)
        if c is None:
            c = Res(self.inherit)
            self.ch[key] = c
        return c

    def f32(self, *shape):
        return _shape(self.ap, shape)

    def bf(self, *shape):
        return _shape(self.ap.bitcast(BF16), shape)


class Sub:
    def __init__(self, parent, lo, hi):
        self.parent = parent
        self.ap = parent.ap[:, lo:hi]

    def r(self, key=None):
        return self.parent.r(key)

    def f32(self, *shape):
        return _shape(self.ap, shape)

    def bf(self, *shape):
        return _shape(self.ap.bitcast(BF16), shape)


def _shape(ap, shape):
    if len(shape) <= 1:
        return ap[:, 0:shape[0]] if shape else ap
    n = 1
    for s in shape:
        n *= s
    ap = ap[:, 0:n]
    if len(shape) == 2:
        return ap.rearrange("p (a b) -> p a b", a=shape[0], b=shape[1])
    if len(shape) == 3:
        return ap.rearrange("p (a b c) -> p a b c", a=shape[0], b=shape[1], c=shape[2])
    raise ValueError(shape)


class Arena:
    def __init__(self, ap, words):
        self.ap = ap
        self.words = words
        self.top = 0
        self.dead = []

    def alloc(self, words):
        words = (words + 7) // 8 * 8
        lo, hi = self.top, self.top + words
        assert hi <= self.words, ("arena overflow", hi, self.words)
        self.top = hi
        self.hw = max(getattr(self, "hw", 0), hi)
        inh = {}
        keep = []
        for b in self.dead:
            if b.hi <= lo or b.lo >= hi:
                keep.append(b)
                continue
            for c in b.ch.values():
                cand = list(c.readers.items())
                if c.last_w is not None:
                    w = c.last_w
                    cand.append(((w.eng, w.dkey) if w.is_dma else w.eng, w))
                for k, o in cand:
                    if k not in inh or inh[k].idx < o.idx:
                        inh[k] = o
            if not (b.lo >= lo and b.hi <= hi):
                keep.append(b)
        self.dead = keep
        return Buf(self.ap[:, lo:hi], lo, hi, inh)

    def f32(self, n):
        return self.alloc(n)

    def bf(self, n):
        return self.alloc((n + 1) // 2)

    def mark(self):
        return (self.top, [])

    def release(self, mark, bufs):
        self.top = mark[0]
        self.dead.extend(bufs)


def _consts(T, NS):
    c = {}
    c["ident"] = np.eye(128, dtype=np.float32)
    ii = np.arange(128)
    rc = np.zeros((128, 4, 16), np.float32)
    for g, win in enumerate((2, 4, 8, 16)):
        for t in range(16):
            rc[:, g, t] = 1.0 / min(t + 1, win)
    c["rc"] = rc
    c["maskS"] = np.where(ii[None, :] < ii[:, None], 0.0, 30000.0).astype(np.float32)
    c["negmaskT"] = np.where(ii[None, :] >= ii[:, None], 0.0, -30000.0).astype(np.float32)
    c["U"] = (ii[:, None] <= ii[None, :]).astype(np.float32)
    bm = lambda sz: ((ii[:, None] // sz) == (ii[None, :] // sz)).astype(np.float32)
    msk = np.zeros((128, 5, 128), np.float32)
    msk[:, 0, :] = bm(8)
    for li_, sz in enumerate((8, 16, 32, 64)):
        msk[:, 1 + li_, :] = bm(2 * sz) - bm(sz)
    c["bmask"] = msk
    H = 8
    lg = np.log(1.0 - 2.0 ** (-5.0 - np.arange(H, dtype=np.float64)))
    sc = 128.0 ** -0.5
    diff = (ii[None, :] - ii[:, None]).astype(np.float64)
    decT = np.where(diff[None] >= 0, np.exp(np.maximum(diff, 0.0)[None] * lg[:, None, None]), 0.0) * sc
    c["decT"] = np.ascontiguousarray(decT.transpose(1, 0, 2)).astype(np.float32)
    xi = np.exp((ii + 1.0)[None, :] * lg[:, None])
    c["xi"] = np.ascontiguousarray(np.broadcast_to(xi[None], (128, H, 128))).astype(np.float32)
    zeta = np.exp((127.0 - ii)[None, :] * lg[:, None]) * sc
    c["zeta"] = np.ascontiguousarray(zeta.T).astype(np.float32)
    c["gC"] = [float(np.exp(128.0 * lg[h])) for h in range(H)]
    c["gam"] = [float(np.exp(lg[h])) for h in range(H)]
    gb = np.zeros((128, 2, H), np.float32)
    gb[:, 0, :] = np.exp(lg)[None, :]
    gb[:, 1, :] = (np.exp(lg) * 0 + sc)[None, :]
    c["gamb"] = gb
    half = 64
    freqs = (np.float32(10000.0) ** (-np.arange(half, dtype=np.float32) / np.float32(half))).astype(np.float32)
    pos = np.concatenate([np.arange(T, dtype=np.float32), np.full((NS,), float(PAST_LEN), np.float32)])
    ang = (pos[None, :] * freqs[:, None]).astype(np.float32)
    cs, sn = np.cos(ang).astype(np.float32), np.sin(ang).astype(np.float32)
    c["cos2"] = np.concatenate([cs, cs], 0)
    c["sin2"] = np.concatenate([-sn, sn], 0)
    angs = (np.float32(PAST_LEN) * freqs).astype(np.float32)
    cst = np.zeros((128, 2, half), np.float32)
    cst[:, 0, :] = np.cos(angs)[None]
    cst[:, 1, :] = np.sin(angs)[None]
    c["ropes"] = cst
    dl = np.zeros((128, 16, 16), np.float32)
    for s in range(16):
        dl[:, s, s] = 1.0
    c["delta"] = dl
    dcol = np.zeros((128, 16), np.float32)
    for s in range(16):
        dcol[s, s] = 1.0
    c["dcol"] = dcol
    return c


CONST_SHAPES = {"ident": [128, 128], "rc": [128, 4, 16], "maskS": [128, 128], "negmaskT": [128, 128],
                "U": [128, 128], "bmask": [128, 5, 128], "decT": [128, 8, 128], "xi": [128, 8, 128], "zeta": [128, 8],
                "gamb": [128, 2, 8], "ropes": [128, 2, 64], "delta": [128, 16, 16], "dcol": [128, 16]}


AWO = None
DBG = set()


def build(T, NS, nlayers=4):
    assert T % 128 == 0 and NS == 16
    nc = bass.Bass("TRN2", target_bir_lowering=False)
    NB = T // 128
    NT = T + NS
    TTS = min(512, T)
    tiles = [(c0, TTS) for c0 in range(0, T, TTS)] + [(T, NS)]
    NTI = len(tiles)
    cst = _consts(T, NS)

    def din(name, shape):
        return nc.dram_tensor(name, list(shape), F32, kind="ExternalInput").ap()

    def dout(name, shape):
        return nc.dram_tensor(name, list(shape), F32, kind="ExternalOutput").ap()

    xp = din("xp", [T, D])
    xs = din("xs", [NS, D])
    spool = din("spool", [2, NS, 15, D])
    sconv = din("sconv", [NS, 3, 3072])
    sgdn = din("sgdn", [NS, 8, 128, 128])
    sret = din("sret", [NS, 8, 128, 256])
    pool_w = din("pool_w", [2, 4, 256, 256])
    pool_scale = din("pool_scale", [2, D])
    gdn_w_in = din("gdn_w_in", [D, 4112])
    gdn_conv_w = din("gdn_conv_w", [4, 3072])
    gdn_a_log = din("gdn_a_log", [1, 8])
    gdn_dt_bias = din("gdn_dt_bias", [1, 8])
    gdn_norm_g = din("gdn_norm_g", [1, 128])
    gdn_w_out = din("gdn_w_out", [D, D])
    ret_w_in = din("ret_w_in", [D, 6144])
    ret_w_out = din("ret_w_out", [2048, D])
    ffn_w13 = din("ffn_w13", [4, D, 2 * DFF])
    ffn_w2 = din("ffn_w2", [4, DFF, D])
    ln_g = din("ln_g", [8, D])
    ln_b = din("ln_b", [8, D])
    cd = {k: din("c_" + k, CONST_SHAPES[k]) for k in CONST_SHAPES}
    cd["cos2"] = din("c_cos2", [128, NT])
    cd["sin2"] = din("c_sin2", [128, NT])

    yp = dout("yp", [T, D])
    ys = dout("ys", [NS, D])
    o_pool_p = dout("pool_p", [2, 15, D])
    o_pool_s = dout("pool_s", [2, NS, 15, D])
    o_conv_p = dout("conv_p", [3, 3072])
    o_conv_s = dout("conv_s", [NS, 3, 3072])
    o_gdn_p = dout("gdn_p", [8, 128, 128])
    o_gdn_s = dout("gdn_s", [NS, 8, 128, 128])
    o_ret_p = dout("ret_p", [8, 128, 256])
    o_ret_s = dout("ret_s", [NS, 8, 128, 256])

    P = Prog(nc)
    st = ExitStack()
    with st:
        def sb(name, shape, dt):
            return st.enter_context(nc.sbuf_tensor(name, shape, dt))

        xf = sb("xf", [128, 8, NT], F32)
        xb = sb("xb", [128, 8, NT], BF16)
        ident = sb("ident", [128, 128], F32)
        identb = sb("identb", [128, 128], BF16)
        onesb = sb("onesb", [128, 128], BF16)
        onesn = sb("onesn", [128, 128], BF16)
        onesf = sb("onesf", [128, 128], F32)
        lnp = sb("lnp", [128, 128], F32)
        prm = sb("prm", [128, 128], F32)
        rcs = sb("rcs", [128, 4, 16], F32)
        AW = (nc.sbuf_bytes_remaining - 4096) // 4 // 8 * 8 if AWO is None else AWO
        arena_t = sb("arena", [128, AW], F32)
        A = Arena(arena_t[:], AW)
        banks = [st.enter_context(nc.psum_tensor("bank%d" % i, [128, 512], F32)) for i in range(8)]
        rbank = [Res() for _ in range(8)]
        r_xf = [[Res() for _ in range(NTI)] for _ in range(8)]
        r_xb = [[Res() for _ in range(NTI)] for _ in range(8)]
        r_c = Res()

        def rx(rs, ks=range(8), ts=range(NTI)):
            return [rs[k][t] for k in ks for t in ts]

        def tiles_of(c0, n):
            return [ti for ti, (a, m) in enumerate(tiles) if a < c0 + n and c0 < a + m]

        P.dma("sp", "c0", ident[:], cd["ident"], writes=[r_c])
        P.dma("sp", "c0", rcs[:], cd["rc"], writes=[r_c])
        P.commit("c0")
        P.op("dve", lambda e: e.tensor_copy(identb[:], ident[:]), reads=[r_c], writes=[r_c])
        P.op("dve", lambda e: e.memset(onesb[:], 1.0), writes=[r_c])
        P.op("dve", lambda e: e.memset(onesn[:], 1.0 / D), writes=[r_c])
        P.op("dve", lambda e: e.memset(onesf[:], 1.0), writes=[r_c])
        wring = [A.bf(2 * 8 * 128) for _ in range(4)]
        wr_i = [0]

        def wslot():
            b = wring[wr_i[0] % len(wring)]
            k = "w%d" % (wr_i[0] % len(wring))
            wr_i[0] += 1
            return b, k

        m0 = A.mark()
        if "noparams" in DBG:
            raise_ = None
        stg = A.f32(128)
        stg2 = A.f32(128)
        if "noparams" not in DBG:
          P.dma("sp", "c1", stg.f32(128)[0:64, :], ln_g.rearrange("l (k p) -> (l k) p", p=128), writes=[stg.r()])
          P.dma("sp", "c1", stg.f32(128)[64:128, :], ln_b.rearrange("l (k p) -> (l k) p", p=128), writes=[stg.r()])
          P.op("dve", lambda e: e.memset(stg2.f32(128), 0.0), writes=[stg2.r()])
          P.dma("sp", "c2", stg2.f32(128)[0:16, :], pool_scale.rearrange("j (k p) -> (j k) p", p=128), reads=[], writes=[stg2.r()])
          P.dma("sp", "c2", stg2.f32(128)[16:112, :], gdn_conv_w.rearrange("j (c p) -> (j c) p", p=128), writes=[stg2.r()])
          P.dma("sp", "c2", stg2.f32(128)[112:113, :], gdn_norm_g, writes=[stg2.r()])
          P.op("pe", lambda e: e.transpose(banks[0][:, 0:128], stg.f32(128), ident[:]), reads=[stg.r(), r_c], writes=[rbank[0]])
          P.op("pe", lambda e: e.transpose(banks[0][:, 128:256], stg2.f32(128), ident[:]), reads=[stg2.r(), r_c], writes=[rbank[0]])
          P.op("dve", lambda e: e.tensor_copy(lnp[:], banks[0][:, 0:128]), reads=[rbank[0]], writes=[r_c])
          P.op("dve", lambda e: e.tensor_copy(prm[:], banks[0][:, 128:256]), reads=[rbank[0]], writes=[r_c])
        A.release(m0, [stg, stg2])

        epsb = {}
        for ev in (LN_EPS, RMS_EPS, RMS_EPS * 128.0, 1.0):
            t = sb("eps%d" % len(epsb), [128, 1], F32)
            P.op("dve", lambda e, t=t, ev=ev: e.memset(t[:], ev), writes=[r_c])
            epsb[ev] = t

        def eps_tile(v):
            return epsb[v][:]

        def rsqrt_eps(out, in_, eps, res, scale=1.0):
            eb = epsb[eps]
            np_ = in_.shape[0]
            P.op("act", lambda e: e.activation(out, in_, AF.Ln, bias=eb[0:np_, :], scale=scale), reads=[res, r_c], writes=[res])
            P.op("act", lambda e: e.activation(out, out, AF.Exp, scale=-0.5), reads=[res], writes=[res])

        def load_x():
            m = A.mark()
            sg = [A.f32(D), A.f32(D)]
            blocks = [(xp[b * 128:(b + 1) * 128, :], 128, b * 128) for b in range(NB)] + [(xs, NS, T)]
            if "nosamp" in DBG:
                blocks = blocks[:-1]
            for bi, (src, n, c0) in enumerate(blocks):
                s = sg[bi % 2]
                P.dma("sp", "xl%d" % (bi % 2), s.f32(D)[0:n, :], src, writes=[s.r()])
                tis = tiles_of(c0, n)
                for half in range(2):
                    bk = (bi * 2 + half) % 2
                    for kk in range(4):
                        k = half * 4 + kk
                        P.op("pe", lambda e, s=s, k=k, kk=kk, n=n, bk=bk: e.transpose(
                            banks[bk][:, kk * 128:kk * 128 + n], s.f32(D)[0:n, k * 128:(k + 1) * 128], ident[0:n, 0:n]),
                            reads=[s.r(), r_c], writes=[rbank[bk]])
                    src_ps = banks[bk][:].rearrange("p (a b) -> p a b", a=4, b=128)[:, :, 0:n]
                    ks = range(half * 4, half * 4 + 4)
                    P.op("dve", lambda e, src_ps=src_ps, half=half, c0=c0, n=n: e.tensor_copy(
                        xf[:, half * 4:half * 4 + 4, c0:c0 + n], src_ps),
                        reads=[rbank[bk]], writes=rx(r_xf, ks, tis))
                    if "noact" in DBG:
                        continue
                    P.op("act", lambda e, src_ps=src_ps, half=half, c0=c0, n=n: e.activation(
                        xb[:, half * 4:half * 4 + 4, c0:c0 + n], xf[:, half * 4:half * 4 + 4, c0:c0 + n], AF.Identity),
                        reads=rx(r_xf, ks, tis), writes=rx(r_xb, ks, tis))
            A.release(m, sg)

        def layer_norm(li):
            m = A.mark()
            bufs = []
            sets = []
            for _ in range(2):
                sets.append((A.bf(8 * TTS), A.bf(8 * TTS), A.f32(TTS), A.f32(TTS), A.f32(TTS)))
                bufs += list(sets[-1])

            def tile_chain(ti, c0, n, par):
                rb, sq, mean, rstd, m2 = sets[par]
                b_s, b_q = (6, 7) if par == 0 else (4, 5)
                xv = xf[:, :, c0:c0 + n]
                P.op("act", lambda e: e.activation(rb.bf(8, n), xv, AF.Copy), reads=rx(r_xf, ts=[ti]), writes=[rb.r()])
                P.op("act", lambda e: e.activation(sq.bf(8, n), xv, AF.Square), reads=rx(r_xf, ts=[ti]), writes=[sq.r()])
                yield
                for k in range(8):
                    P.op("pe", lambda e, k=k: e.matmul(banks[b_s][:, 0:n], onesn[:], rb.bf(8, n)[:, k, :], start=(k == 0), stop=(k == 7)),
                         reads=[rb.r(), r_c], writes=[rbank[b_s]])
                for k in range(8):
                    P.op("pe", lambda e, k=k: e.matmul(banks[b_q][:, 0:n], onesn[:], sq.bf(8, n)[:, k, :], start=(k == 0), stop=(k == 7)),
                         reads=[sq.r(), r_c], writes=[rbank[b_q]])
                yield
                P.op("act", lambda e: e.activation(mean.f32(n), banks[b_s][:, 0:n], AF.Copy), reads=[rbank[b_s]], writes=[mean.r()])
                yield
                P.op("dve", lambda e: e.tensor_tensor(m2.f32(n), mean.f32(n), mean.f32(n), ALU.mult), reads=[mean.r()], writes=[m2.r()])
                P.op("dve", lambda e: e.tensor_tensor(rstd.f32(n), banks[b_q][:, 0:n], m2.f32(n), ALU.subtract),
                     reads=[m2.r(), rbank[b_q]], writes=[rstd.r()])
                P.op("dve", lambda e: e.tensor_tensor(xv, xv, mean.f32(n).unsqueeze(1).to_broadcast([128, 8, n]), ALU.subtract),
                     reads=rx(r_xf, ts=[ti]) + [mean.r()], writes=rx(r_xf, ts=[ti]))
                yield
                rsqrt_eps(rstd.f32(n), rstd.f32(n), LN_EPS, rstd.r())
                yield
                P.op("dve", lambda e: e.tensor_tensor(xv, xv, rstd.f32(n).unsqueeze(1).to_broadcast([128, 8, n]), ALU.mult),
                     reads=rx(r_xf, ts=[ti]) + [rstd.r()], writes=rx(r_xf, ts=[ti]))
                yield
                for k in range(8):
                    P.op("act", lambda e, k=k: e.activation(
                        xf[:, k, c0:c0 + n], xf[:, k, c0:c0 + n], AF.Identity,
                        bias=lnp[:, 64 + li * 8 + k:64 + li * 8 + k + 1], scale=lnp[:, li * 8 + k:li * 8 + k + 1]),
                        reads=[r_xf[k][ti], r_c], writes=[r_xf[k][ti]])
                yield
                P.op("dve", lambda e: e.tensor_copy(xb[:, :, c0:c0 + n], xv), reads=rx(r_xf, ts=[ti]), writes=rx(r_xb, ts=[ti]))

            for t0 in range(0, NTI, 2):
                gens_ = [tile_chain(ti, tiles[ti][0], tiles[ti][1], ti - t0) for ti in range(t0, min(NTI, t0 + 2))]
                while gens_:
                    nx_ = []
                    for gn_ in gens_:
                        try:
                            next(gn_)
                            nx_.append(gn_)
                        except StopIteration:
                            pass
                    gens_ = nx_
            A.release(m, bufs)

        def ffn(layer):
            m = A.mark()
            h = A.bf(11 * NT)
            w2 = A.bf(11 * D)
            sa = [A.f32(TTS), A.f32(TTS)]
            hv = h.bf(11, NT)
            w2v = w2.bf(11, D)
            cnt = 0
            for pas in range(2):
                for il in range(11):
                    i = pas * 11 + il
                    wb, wk = wslot()
                    wv = wb.bf(2, 8, 128)
                    P.dma("pool", wk, wv[:, 0], ffn_w13[layer, :, i * 128:(i + 1) * 128].rearrange("(k p) n -> p k n", p=128),
                          writes=[wb.r()])
                    P.dma("pool", wk, wv[:, 1], ffn_w13[layer, :, DFF + i * 128:DFF + (i + 1) * 128].rearrange("(k p) n -> p k n", p=128),
                          writes=[wb.r()])
                    P.commit(wk)
                    if il == 0:
                        for f in range(11):
                            P.dma("pool", "fw2", w2v[:, f, :], ffn_w2[layer, (pas * 11 + f) * 128:(pas * 11 + f + 1) * 128, :],
                                  writes=[w2.r(f)])
                        P.commit("fw2")
                    for ti, (c0, n) in enumerate(tiles):
                        ba, bb = banks[cnt % 2], banks[2 + cnt % 2]
                        ra, rbb = rbank[cnt % 2], rbank[2 + cnt % 2]
                        s = sa[cnt % 2]
                        cnt += 1
                        for k in range(8):
                            P.op("pe", lambda e, ba=ba, wv=wv, k=k, c0=c0, n=n: e.matmul(
                                ba[:, 0:n], wv[:, 0, k, :], xb[:, k, c0:c0 + n], start=(k == 0), stop=(k == 7)),
                                reads=[wb.r(), r_xb[k][ti]], writes=[ra])
                        for k in range(8):
                            P.op("pe", lambda e, bb=bb, wv=wv, k=k, c0=c0, n=n: e.matmul(
                                bb[:, 0:n], wv[:, 1, k, :], xb[:, k, c0:c0 + n], start=(k == 0), stop=(k == 7)),
                                reads=[wb.r(), r_xb[k][ti]], writes=[rbb])
                        P.op("act", lambda e, s=s, ba=ba, n=n: e.activation(s.f32(n), ba[:, 0:n], AF.Silu),
                             reads=[ra], writes=[s.r()])
                        P.op("dve", lambda e, s=s, bb=bb, il=il, c0=c0, n=n: e.tensor_tensor(
                            hv[:, il, c0:c0 + n], s.f32(n), bb[:, 0:n], ALU.mult),
                            reads=[s.r(), rbb], writes=[h.r((il, ti))])
                for dc in range(8):
                    for ti, (c0, n) in enumerate(tiles):
                        by, ry = banks[4 + cnt % 2], rbank[4 + cnt % 2]
                        cnt += 1
                        for f in range(11):
                            P.op("pe", lambda e, by=by, f=f, dc=dc, c0=c0, n=n: e.matmul(
                                by[:, 0:n], w2v[:, f, dc * 128:(dc + 1) * 128], hv[:, f, c0:c0 + n],
                                start=(f == 0), stop=(f == 10)),
                                reads=[w2.r(f), h.r((f, ti))], writes=[ry])
                        xv = xf[:, dc, c0:c0 + n]
                        if pas == 0:
                            P.op("dve", lambda e, xv=xv, by=by, n=n: e.scalar_tensor_tensor(
                                xv, xv, ALPHA, by[:, 0:n], ALU.mult, ALU.add),
                                reads=[ry, r_xf[dc][ti]], writes=[r_xf[dc][ti]])
                        else:
                            P.op("dve", lambda e, xv=xv, by=by, n=n: e.tensor_tensor(xv, xv, by[:, 0:n], ALU.add),
                                 reads=[ry, r_xf[dc][ti]], writes=[r_xf[dc][ti]])
            A.release(m, [h, w2] + sa)

        def store_y():
            m = A.mark()
            sg = [A.f32(D), A.f32(D)]
            blocks = [(yp[b * 128:(b + 1) * 128, :], 128, b * 128) for b in range(NB)] + [(ys, NS, T)]
            for bi, (dst, n, c0) in enumerate(blocks):
                s = sg[bi % 2]
                tis = tiles_of(c0, n)
                for half in range(2):
                    bk = (bi * 2 + half) % 2
                    for kk in range(4):
                        k = half * 4 + kk
                        P.op("pe", lambda e, k=k, kk=kk, n=n, bk=bk, c0=c0: e.transpose(
                            banks[bk][0:n, kk * 128:(kk + 1) * 128], xf[:, k, c0:c0 + n], ident[:]),
                            reads=rx(r_xf, [k], tis) + [r_c], writes=[rbank[bk]])
                    if half == 0:
                        P.op("dve", lambda e, s=s, n=n, bk=bk: e.tensor_copy(s.f32(D)[0:n, 0:512], banks[bk][0:n, :]),
                             reads=[rbank[bk]], writes=[s.r()])
                    else:
                        P.op("act", lambda e, s=s, n=n, bk=bk: e.activation(s.f32(D)[0:n, 512:1024], banks[bk][0:n, :], AF.Copy),
                             reads=[rbank[bk]], writes=[s.r()])
                P.dma("sp", "ys%d" % (bi % 2), dst, s.f32(D)[0:n, :], reads=[s.r()])
            A.release(m, sg)

        def pool_mixer(j):
            m = A.mark()
            bufs = []
            if j == 0:
                P.dma("sp", "po", o_pool_p[0], xp[T - 15:T, :])
                P.dma("sp", "po", o_pool_s[0, :, 14, :], xs)
            else:
                so = A.f32(D)
                bufs.append(so)
                nn = 15 + NS
                for half in range(2):
                    for kk in range(4):
                        k = half * 4 + kk
                        P.op("pe", lambda e, k=k, kk=kk, half=half: e.transpose(
                            banks[half][0:nn, kk * 128:(kk + 1) * 128], xf[:, k, T - 15:T + NS], ident[:]),
                            reads=rx(r_xf, [k], tiles_of(T - 15, nn)) + [r_c], writes=[rbank[half]])
                    P.op("dve", lambda e, half=half: e.tensor_copy(so.f32(D)[0:nn, half * 512:(half + 1) * 512], banks[half][0:nn, :]),
                         reads=[rbank[half]], writes=[so.r()])
                P.dma("sp", "po", o_pool_p[1], so.f32(D)[0:15, :], reads=[so.r()])
                P.dma("sp", "po", o_pool_s[1, :, 14, :], so.f32(D)[15:15 + NS, :], reads=[so.r()])
            P.dma("sp", "po", o_pool_s[j, :, 0:14, :], spool[j, :, 1:15, :])
            P.commit("po")
            hist = A.f32(8 * NS * 15)
            bufs.append(hist)
            hv = hist.f32(8, NS * 15)
            hs = [A.f32(D), A.f32(D)]
            bufs += hs
            rows = NS * 15 // 2
            src = spool[j].rearrange("s r d -> (s r) d")
            for b2 in range(2):
                s = hs[b2]
                P.dma("sp", "ph%d" % b2, s.f32(D)[0:rows, :], src[b2 * rows:(b2 + 1) * rows, :], writes=[s.r()])
                for half in range(2):
                    bk = half
                    for kk in range(4):
                        k = half * 4 + kk
                        P.op("pe", lambda e, s=s, k=k, kk=kk, bk=bk: e.transpose(
                            banks[bk][:, kk * 128:kk * 128 + rows], s.f32(D)[0:rows, k * 128:(k + 1) * 128], ident[0:rows, 0:rows]),
                            reads=[s.r(), r_c], writes=[rbank[bk]])
                    P.op("dve", lambda e, half=half, bk=bk, b2=b2: e.tensor_copy(
                        hv[:, half * 4:half * 4 + 4, b2 * rows:(b2 + 1) * rows],
                        banks[bk][:].rearrange("p (a b) -> p a b", a=4, b=128)[:, :, 0:rows]),
                        reads=[rbank[bk]], writes=[hist.r()])
            pooled = A.bf(8 * NT)
            bufs.append(pooled)
            pv = pooled.bf(8, NT)
            E = [A.f32(16 + T), A.f32(16 + T)]
            bufs += E
            ssum = A.f32(NS)
            bufs.append(ssum)
            for eb in E:
                P.op("pool", lambda e, eb=eb: e.memset(eb.f32(16 + T)[:, 0:16], 0.0), writes=[eb.r()])
            for k in range(8):
                g = k // 2
                win = 2 << g
                e0, e1 = E[0], E[1]
                tp = list(range(NTI - 1))
                P.op("pool", lambda e, e0=e0, k=k: e.tensor_copy(e0.f32(16 + T)[:, 16:16 + T], xf[:, k, 0:T]),
                     reads=rx(r_xf, [k], tp), writes=[e0.r()])
                cur, nxt = e0, e1
                sh = 1
                while sh < win:
                    P.op("dve", lambda e, cur=cur, nxt=nxt, sh=sh: e.tensor_tensor(
                        nxt.f32(16 + T)[:, 16:16 + T], cur.f32(16 + T)[:, 16:16 + T], cur.f32(16 + T)[:, 16 - sh:16 + T - sh], ALU.add),
                        reads=[cur.r()], writes=[nxt.r()])
                    cur, nxt = nxt, cur
                    sh *= 2
                P.op("dve", lambda e, cur=cur, k=k, win=win: e.scalar_tensor_tensor(
                    pv[:, k, 0:T], cur.f32(16 + T)[:, 16:16 + T], 1.0 / win, xf[:, k, 0:T], ALU.mult, ALU.subtract),
                    reads=[cur.r()] + rx(r_xf, [k], tp), writes=[pooled.r(k)])
                P.op("dve", lambda e, cur=cur, nxt=nxt, g=g: e.tensor_tensor(
                    nxt.f32(16 + T)[:, 16:31], cur.f32(16 + T)[:, 16:31], rcs[:, g, 0:15], ALU.mult),
                    reads=[cur.r(), r_c], writes=[nxt.r()])
                P.op("dve", lambda e, nxt=nxt, k=k: e.tensor_tensor(
                    pv[:, k, 0:15], nxt.f32(16 + T)[:, 16:31], xf[:, k, 0:15], ALU.subtract),
                    reads=[nxt.r()] + rx(r_xf, [k], [0]), writes=[pooled.r(k)])
                hk = hv[:, k, :].rearrange("p (s r) -> p s r", s=NS, r=15)
                P.op("dve", lambda e, hk=hk, win=win: e.tensor_reduce(
                    ssum.f32(NS), hk[:, :, 16 - win:15], AX.X, ALU.add),
                    reads=[hist.r()], writes=[ssum.r()])
                P.op("dve", lambda e, k=k: e.tensor_tensor(ssum.f32(NS), ssum.f32(NS), xf[:, k, T:NT], ALU.add),
                     reads=[ssum.r(), r_xf[k][NTI - 1]], writes=[ssum.r()])
                P.op("dve", lambda e, k=k, win=win: e.scalar_tensor_tensor(
                    pv[:, k, T:NT], ssum.f32(NS), 1.0 / win, xf[:, k, T:NT], ALU.mult, ALU.subtract),
                    reads=[ssum.r(), r_xf[k][NTI - 1]], writes=[pooled.r(k)])
            pw = A.bf(4 * 2 * 256)
            bufs.append(pw)
            pwv = pw.bf(4, 2, 256)
            P.dma("pool", "pw", pwv, pool_w[j].rearrange("g (cc p) d -> p g cc d", p=128), writes=[pw.r()])
            tm = [A.f32(TTS), A.f32(TTS)]
            bufs += tm
            cnt = 0
            for g in range(4):
                for dc in range(2):
                    k = 2 * g + dc
                    for ti, (c0, n) in enumerate(tiles):
                        bk = cnt % 2
                        t_ = tm[cnt % 2]
                        cnt += 1
                        for cc in range(2):
                            P.op("pe", lambda e, g=g, dc=dc, cc=cc, c0=c0, n=n, bk=bk: e.matmul(
                                banks[bk][:, 0:n], pwv[:, g, cc, dc * 128:(dc + 1) * 128], pv[:, 2 * g + cc, c0:c0 + n],
                                start=(cc == 0), stop=(cc == 1)),
                                reads=[pw.r(), pooled.r(2 * g + cc)], writes=[rbank[bk]])
                        P.op("act", lambda e, t_=t_, bk=bk, n=n, k=k: e.activation(
                            t_.f32(n), banks[bk][:, 0:n], AF.Copy, scale=prm[:, j * 8 + k:j * 8 + k + 1]),
                            reads=[rbank[bk], r_c], writes=[t_.r()])
                        xv = xf[:, k, c0:c0 + n]
                        P.op("dve", lambda e, xv=xv, t_=t_, n=n: e.scalar_tensor_tensor(
                            xv, xv, ALPHA, t_.f32(n), ALU.mult, ALU.add),
                            reads=[t_.r(), r_xf[k][ti]], writes=[r_xf[k][ti]])
            A.release(m, bufs)

        def MM(out, lhsT, rhs, r, w, start=True, stop=True):
            P.op("pe", lambda e: e.matmul(out, lhsT, rhs, start=start, stop=stop), reads=r, writes=w)

        def TR(out, in_, idn, r, w):
            P.op("pe", lambda e: e.transpose(out, in_, idn), reads=r, writes=w)

        def TT(eng, out, in0, in1, op, r, w):
            P.op(eng, lambda e: e.tensor_tensor(out, in0, in1, op), reads=r, writes=w)

        def STT(eng, out, in0, scalar, in1, op0, op1, r, w):
            P.op(eng, lambda e: e.scalar_tensor_tensor(out, in0, scalar, in1, op0, op1), reads=r, writes=w)

        def TS(eng, out, in0, s1, s2, op0, op1, r, w):
            if op1 is None:
                P.op(eng, lambda e: e.tensor_scalar(out, in0, s1, None, op0), reads=r, writes=w)
            else:
                P.op(eng, lambda e: e.tensor_scalar(out, in0, s1, s2, op0, op1), reads=r, writes=w)

        def CP(eng, out, in_, r, w):
            if eng == "act":
                P.op("act", lambda e: e.activation(out, in_, AF.Identity), reads=r, writes=w)
            else:
                P.op(eng, lambda e: e.tensor_copy(out, in_), reads=r, writes=w)

        def ACT(out, in_, func, r, w, scale=1.0, bias=None, accum=None):
            def fn(e):
                kw = {}
                if bias is not None:
                    kw["bias"] = bias
                if accum is not None:
                    kw["accum_out"] = accum
                return e.activation(out, in_, func, scale=scale, **kw)
            P.op("act", fn, reads=r, writes=w)

        def RED(eng, out, in_, r, w, op=ALU.add):
            P.op(eng, lambda e: e.tensor_reduce(out, in_, AX.X, op), reads=r, writes=w)

        def MS(eng, out, val, w):
            P.op(eng, lambda e: e.memset(out, val), writes=w)

        def bfps(bank, c0, n):
            return banks[bank][:, c0:c0 + n // 2].bitcast(BF16)

        ptiles = tiles[:-1]
        PT_ = list(range(NTI - 1))

        def out_proj_samples(w_dram, nk, ogsT, ogs_res):
            for dc in range(8):
                for kc in range(nk):
                    wb_, wk_ = wslot()
                    wv_ = wb_.bf(2, 8, 128)
                    if kc % 16 == 0:
                        pass
                    P.dma("pool", wk_, wv_[:, 0, 0, :], w_dram[kc * 128:(kc + 1) * 128, dc * 128:(dc + 1) * 128], writes=[wb_.r()])
                    P.commit(wk_)
                    MM(banks[4][:, 0:NS], wv_[:, 0, 0, :], ogsT[:, kc, :], [wb_.r(), ogs_res], [rbank[4]], start=(kc == 0), stop=(kc == nk - 1))
                xv = xf[:, dc, T:NT]
                STT("dve", xv, xv, ALPHA, banks[4][:, 0:NS], ALU.mult, ALU.add, [r_xf[dc][NTI - 1]], [r_xf[dc][NTI - 1], rbank[4]])

        def gdn_mixer():
            G = 4 if NB >= 4 else 2
            m = A.mark()
            bufs = []

            def al(n, bf=False):
                b = A.bf(n) if bf else A.f32(n)
                bufs.append(b)
                return b
            maskS = al(128); negT = al(128); Um = al(128); gn = al(128); gnb = al(128)
            bmask = al(5 * 128)
            bmv = bmask.f32(5, 128)
            alog = al(8); dtb = al(8); nea = al(8)
            P.dma("sp", "gc", maskS.f32(128), cd["maskS"], writes=[maskS.r()])
            P.dma("sp", "gc", negT.f32(128), cd["negmaskT"], writes=[negT.r()])
            P.dma("sp", "gc", Um.f32(128), cd["U"], writes=[Um.r()])
            P.dma("sp", "gc", bmask.f32(5, 128), cd["bmask"], writes=[bmask.r()])
            P.dma("sp", "gc", gn.f32(128), gdn_norm_g.partition_broadcast(128)[:, 0, :], writes=[gn.r()])
            P.dma("sp", "gc", alog.f32(8), gdn_a_log.partition_broadcast(128)[:, 0, :], writes=[alog.r()])
            P.dma("sp", "gc", dtb.f32(8), gdn_dt_bias.partition_broadcast(128)[:, 0, :], writes=[dtb.r()])
            P.commit("gc")
            TS("dve", gnb.f32(128), gn.f32(128), math.sqrt(128.0), None, ALU.mult, None, [gn.r()], [gnb.r()])
            ACT(nea.f32(8), alog.f32(8), AF.Exp, [alog.r()], [nea.r()])
            TS("dve", nea.f32(8), nea.f32(8), -1.0, None, ALU.mult, None, [nea.r()], [nea.r()])
            NG = NB + 1
            wg = al(8 * 16, bf=True)
            wgv = wg.bf(8, 16)
            P.dma("pool", "gw", wgv, gdn_w_in[:, 4096:4112].rearrange("(k p) n -> p k n", p=128), writes=[wg.r()])
            GA = al(NG * 16)
            GAv = GA.f32(NG, 16)
            MS("dve", GA.f32(NG * 16), 0.0, [GA.r()])
            for b in range(NB):
                for k in range(8):
                    MM(banks[0][:, b * 16:(b + 1) * 16], xb[:, k, b * 128:(b + 1) * 128], wgv[:, k, :],
                       [wg.r()] + rx(r_xb, [k], tiles_of(b * 128, 128)), [rbank[0]], start=(k == 0), stop=(k == 7))
            for k in range(8):
                MM(banks[0][0:NS, NB * 16:NG * 16], xb[:, k, T:NT], wgv[:, k, :],
                   [wg.r(), r_xb[k][NTI - 1]], [rbank[0]], start=(k == 0), stop=(k == 7))
            CP("dve", GA.f32(NB * 16), banks[0][:, 0:NB * 16], [], [GA.r(), rbank[0]])
            CP("dve", GAv[0:NS, NB, :], banks[0][0:NS, NB * 16:NG * 16], [], [GA.r(), rbank[0]])
            beta = al(NG * 8); nbeta = al(NG * 8); gg = al(NG * 8); gcol = al(NG * 8); eg = al(NG * 8)
            beg = al(NG * 8); egl = al(NG * 8); gl = al(NG * 8); t1 = al(NG * 8); t2 = al(NG * 8)
            v8 = lambda b_: b_.f32(NG, 8)
            ACT(v8(beta), GAv[:, :, 0:8], AF.Sigmoid, [GA.r()], [beta.r()])
            TS("dve", v8(nbeta), v8(beta), -1.0, None, ALU.mult, None, [beta.r()], [nbeta.r()])
            TT("dve", v8(t1), GAv[:, :, 8:16], dtb.f32(8).unsqueeze(1).to_broadcast([128, NG, 8]), ALU.add, [GA.r(), dtb.r()], [t1.r()])
            TS("dve", v8(t2), v8(t1), -1.0, None, ALU.mult, None, [t1.r()], [t2.r()])
            TT("dve", v8(t2), v8(t2), v8(t1), ALU.max, [t1.r(), t2.r()], [t2.r()])
            ACT(v8(t2), v8(t2), AF.Exp, [t2.r()], [t2.r()], scale=-1.0)
            ACT(v8(t2), v8(t2), AF.Ln, [t2.r(), r_c], [t2.r()], bias=eps_tile(1.0))
            TS("dve", v8(t1), v8(t1), 0.0, None, ALU.max, None, [t1.r()], [t1.r()])
            TT("dve", v8(t1), v8(t1), v8(t2), ALU.add, [t1.r(), t2.r()], [t1.r()])
            TT("dve", v8(gg), v8(t1), nea.f32(8).unsqueeze(1).to_broadcast([128, NG, 8]), ALU.mult, [t1.r(), nea.r()], [gg.r()])
            MM(banks[0][:, 0:NB * 8], Um.f32(128), gg.f32(NB * 8), [Um.r(), gg.r()], [rbank[0]])
            MM(banks[0][:, 256:256 + NB * 8], onesf[:], gg.f32(NB * 8), [r_c, gg.r()], [rbank[0]])
            CP("dve", gcol.f32(NB * 8), banks[0][:, 0:NB * 8], [], [gcol.r(), rbank[0]])
            CP("dve", gcol.f32(NG, 8)[:, NB, :], gg.f32(NG, 8)[:, NB, :], [gg.r()], [gcol.r()])
            ACT(eg.f32(NG * 8), gcol.f32(NG * 8), AF.Exp, [gcol.r()], [eg.r()])
            TT("dve", beg.f32(NG * 8), beta.f32(NG * 8), eg.f32(NG * 8), ALU.mult, [beta.r(), eg.r()], [beg.r()])
            ACT(gl.f32(NB * 8), banks[0][:, 256:256 + NB * 8], AF.Exp, [], [gl.r(), rbank[0]])
            TT("dve", egl.f32(NB * 8), banks[0][:, 256:256 + NB * 8], gcol.f32(NB * 8), ALU.subtract, [gcol.r()], [egl.r(), rbank[0]])
            ACT(egl.f32(NB * 8), egl.f32(NB * 8), AF.Exp, [egl.r()], [egl.r()])
            projsT = al(4 * 8 * NS)
            pjT = projsT.f32(4, 8, NS)
            lastU = al(24 * 3)
            luv = lastU.f32(24, 3)
            base_bufs = list(bufs)
            mH = A.mark()
            del bufs[:]
            Ub = al(T + 8)
            accq = al(T); vf = al(T)
            knf = vf
            qnb = al(T, bf=True); knb = al(T, bf=True)
            szf = accq
            sqb_ap = Ub.ap[:, 8:8 + T // 2].bitcast(BF16)
            ogb_ap = Ub.ap[:, 8 + T // 2:8 + T].bitcast(BF16)
            rsq = [al(TTS)]
            Sf = al(128); Sb = al(128, bf=True)
            wo = al(1024, bf=True)
            Ubv = Ub.ap[:, 5:8 + T]
            MS("pool", Ubv[:, 0:3], 0.0, [Ub.r()])

            class Ch:
                pass
            chains = []
            for ci in range(G):
                c = Ch()
                c.Ug = al(128); c.e1 = al(128); c.e2 = al(128); c.egrow = al(128)
                c.attnT = al(128, bf=True); c.qgT = al(128, bf=True)
                c.Xf = al(128); c.XTf = al(128); c.X8 = al(128); c.Z8 = al(128)
                c.Y1 = c.Ug; c.Z1 = c.e1; c.Y2 = c.e2; c.E0 = al(128); c.E1 = c.egrow
                c.Xo = al(4 * 128, bf=True); c.Zo = al(128, bf=True)
                c.Xb = al(128, bf=True); c.XTb = al(128, bf=True)
                c.Db = [al(128, bf=True), al(128, bf=True)]; c.Eb = [al(128, bf=True), al(128, bf=True)]
                c.M1 = al(128, bf=True); c.M1p = al(128, bf=True)
                c.PTb = c.Eb[0]
                c.vb = Sub(c.Xo, 0, 64); c.kbg = Sub(c.Xo, 64, 128); c.kg = Sub(c.Xo, 128, 192)
                c.u = c.Xf; c.wkT = Sub(c.Xo, 192, 256); c.vnew = Sub(c.Zo, 0, 64); c.on = c.XTf; c.ssq = al(8)
                c.osb = c.X8
                c.bA = ci
                c.bN = ci
                chains.append(c)
            B = lambda b_: b_.bf(128)
            F = lambda b_: b_.f32(128)

            for h in range(8):
                slots = []
                for typ in range(4):
                    if typ % 2 == 0:
                        wb_, wk_ = wslot()
                        wv_ = wb_.bf(2, 8, 128)
                    col = typ * 1024 + h * 128
                    P.dma("pool", wk_, wv_[:, typ % 2], gdn_w_in[:, col:col + 128].rearrange("(k p) n -> p k n", p=128), writes=[wb_.r()])
                    slots.append((wb_, wv_[:, typ % 2]))
                    if typ % 2 == 1:
                        P.commit(wk_)
                if True:
                    P.dma("pool", "gwo", wo.bf(1024), gdn_w_out[h * 128:(h + 1) * 128, :], writes=[wo.r()])
                cntb = [0]

                def project(typ, sink):
                    wb_, wv_ = slots[typ]
                    for ti, (c0, n) in enumerate(ptiles):
                        bk = 4 + cntb[0] % 2
                        cntb[0] += 1
                        for k in range(8):
                            MM(banks[bk][:, 0:n], wv_[:, k, :], xb[:, k, c0:c0 + n], [wb_.r(), r_xb[k][ti]], [rbank[bk]], start=(k == 0), stop=(k == 7))
                        sink(bk, c0, n)
                    bk = 4 + cntb[0] % 2
                    cntb[0] += 1
                    for k in range(8):
                        MM(banks[bk][:, 0:NS], wv_[:, k, :], xb[:, k, T:NT], [wb_.r(), r_xb[k][NTI - 1]], [rbank[bk]], start=(k == 0), stop=(k == 7))
                    CP("dve", pjT[:, typ, h, :], banks[bk][:, 0:NS], [], [projsT.r(), rbank[bk]])

                for typ in range(3):
                    ch = typ * 8 + h
                    project(typ, lambda bk, c0, n: CP("act", Ubv[:, 3 + c0:3 + c0 + n], banks[bk][:, 0:n], [], [Ub.r(), rbank[bk]]))
                    CP("act", luv[:, ch, :], Ubv[:, T:T + 3], [Ub.r()], [lastU.r()])
                    acc = [accq, knf, vf][typ]
                    ce = "dve"
                    cw = lambda j_: prm[:, 16 + j_ * 24 + ch:16 + j_ * 24 + ch + 1]
                    TS(ce, acc.f32(T), Ubv[:, 3:3 + T], cw(3), None, ALU.mult, None, [Ub.r(), r_c], [acc.r()])
                    for j_ in range(3):
                        STT(ce, acc.f32(T), Ubv[:, j_:j_ + T], cw(j_), acc.f32(T), ALU.mult, ALU.add, [Ub.r(), r_c, acc.r()], [acc.r()])
                    ACT(acc.f32(T), acc.f32(T), AF.Silu, [acc.r()], [acc.r()])
                    if typ < 2:
                        ACT(sqb_ap, acc.f32(T), AF.Square, [acc.r()], [Ub.r()])
                        for ti, (c0, n) in enumerate(ptiles):
                            bk = 4 + cntb[0] % 2
                            cntb[0] += 1
                            rs_ = rsq[0]
                            MM(banks[bk][:, 0:n], onesb[:], sqb_ap[:, c0:c0 + n], [Ub.r(), r_c], [rbank[bk]])
                            CP("dve", rs_.f32(n), banks[bk][:, 0:n], [], [rs_.r(), rbank[bk]])
                            rsqrt_eps(rs_.f32(n), rs_.f32(n), RMS_EPS, rs_.r())
                            if typ == 0:
                                STT("dve", qnb.bf(T)[:, c0:c0 + n], acc.f32(T)[:, c0:c0 + n], 128.0 ** -0.5, rs_.f32(n), ALU.mult, ALU.mult,
                                    [acc.r(), rs_.r()], [qnb.r()])
                            else:
                                TT("dve", acc.f32(T)[:, c0:c0 + n], acc.f32(T)[:, c0:c0 + n], rs_.f32(n), ALU.mult, [acc.r(), rs_.r()], [acc.r()])
                        if typ == 1:
                            CP("pool", knb.bf(T), knf.f32(T), [knf.r()], [knb.r()])
                project(3, lambda bk, c0, n: ACT(szf.f32(T)[:, c0:c0 + n], banks[bk][:, 0:n], AF.Silu, [], [szf.r(), rbank[bk]]))
                MS("dve", F(Sf), 0.0, [Sf.r()])
                MS("dve", B(Sb), 0.0, [Sb.r()])

                def st_a(c, b):
                    bs = slice(b * 128, (b + 1) * 128)
                    ACT(F(c.Ug), F(Um), AF.Identity, [Um.r(), gg.r()], [c.Ug.r()], scale=gg.f32(NG, 8)[:, b, h:h + 1])
                    yield
                    bk = banks[c.bA]
                    MM(bk[:, 0:128], knb.bf(T)[:, bs], knb.bf(T)[:, bs], [knb.r()], [rbank[c.bA]])
                    MM(bk[:, 128:256], knb.bf(T)[:, bs], qnb.bf(T)[:, bs], [knb.r(), qnb.r()], [rbank[c.bA]])
                    MM(bk[:, 256:384], onesf[:], F(c.Ug), [r_c, c.Ug.r()], [rbank[c.bA]])
                    yield
                    gc_ = gcol.f32(NG, 8)[:, b, h:h + 1]
                    STT("dve", F(c.e1), bk[:, 256:384], gc_, F(maskS), ALU.subtract, ALU.max, [gcol.r(), maskS.r()], [c.e1.r(), rbank[c.bA]])
                    STT("dve", F(c.e2), bk[:, 256:384], gc_, F(negT), ALU.subtract, ALU.min, [gcol.r(), negT.r()], [c.e2.r(), rbank[c.bA]])
                    ACT(F(c.egrow), bk[:, 256:384], AF.Exp, [], [c.egrow.r(), rbank[c.bA]])
                    yield
                    ACT(F(c.e1), F(c.e1), AF.Exp, [c.e1.r()], [c.e1.r()], scale=-1.0)
                    ACT(F(c.e2), F(c.e2), AF.Exp, [c.e2.r()], [c.e2.r()])
                    yield
                    STT("dve", F(c.Xf), bk[:, 0:128], nbeta.f32(NG, 8)[:, b, h:h + 1], F(c.e1), ALU.mult, ALU.mult, [nbeta.r(), c.e1.r()], [c.Xf.r(), rbank[c.bA]])
                    TT("dve", B(c.attnT), bk[:, 128:256], F(c.e2), ALU.mult, [c.e2.r()], [c.attnT.r(), rbank[c.bA]])
                    TT("dve", B(c.qgT), qnb.bf(T)[:, bs], F(c.egrow), ALU.mult, [qnb.r(), c.egrow.r()], [c.qgT.r()])

                def st_b(c, b):
                    bk = banks[c.bN]
                    TR(bk[:, 0:128], F(c.Xf), ident[:], [c.Xf.r(), r_c], [rbank[c.bN]])
                    yield
                    CP("act", F(c.XTf), bk[:, 0:128], [], [c.XTf.r(), rbank[c.bN]])
                    CP("act", B(c.XTb), bk[:, 0:128], [], [c.XTb.r(), rbank[c.bN]])
                    CP("act", B(c.Xb), F(c.Xf), [c.Xf.r()], [c.Xb.r()])
                    yield
                    TT("dve", F(c.X8), F(c.Xf), bmv[:, 0, :], ALU.mult, [c.Xf.r(), bmask.r()], [c.X8.r()])
                    TT("dve", F(c.Z8), F(c.XTf), bmv[:, 0, :], ALU.mult, [c.XTf.r(), bmask.r()], [c.Z8.r()])
                    yield
                    TT("dve", F(c.E0), F(c.Z8), ident[:], ALU.add, [c.Z8.r(), r_c], [c.E0.r()])

                def st_base1(c, b):
                    bk = banks[c.bN]
                    MM(bk[:, 0:128], F(c.Z8), F(c.X8), [c.Z8.r(), c.X8.r()], [rbank[c.bN]])
                    MM(bk[:, 128:256], F(c.X8), F(c.Z8), [c.Z8.r(), c.X8.r()], [rbank[c.bN]])
                    yield
                    CP("act", F(c.Y1), bk[:, 0:128], [], [c.Y1.r(), rbank[c.bN]])
                    CP("act", F(c.Z1), bk[:, 128:256], [], [c.Z1.r(), rbank[c.bN]])
                    yield
                    MM(bk[:, 256:384], F(c.Y1), F(c.E0), [c.Y1.r(), c.E0.r()], [rbank[c.bN]])
                    yield
                    TT("dve", F(c.E1), F(c.E0), bk[:, 256:384], ALU.add, [c.E0.r()], [c.E1.r(), rbank[c.bN]])

                def st_base2(c, b):
                    bk = banks[c.bN]
                    MM(bk[:, 0:128], F(c.Z1), F(c.Y1), [c.Z1.r(), c.Y1.r()], [rbank[c.bN]])
                    yield
                    CP("act", F(c.Y2), bk[:, 0:128], [], [c.Y2.r(), rbank[c.bN]])
                    yield
                    MM(bk[:, 128:256], F(c.Y2), F(c.E1), [c.Y2.r(), c.E1.r()], [rbank[c.bN]])
                    yield
                    TT("dve", F(c.E0), F(c.E1), bk[:, 128:256], ALU.add, [c.E1.r()], [c.E0.r(), rbank[c.bN]])
                    yield
                    TR(bk[:, 256:384], F(c.E0), ident[:], [c.E0.r(), r_c], [rbank[c.bN]])
                    yield
                    CP("act", B(c.Db[0]), bk[:, 256:384], [], [c.Db[0].r(), rbank[c.bN]])
                    CP("act", B(c.Eb[0]), F(c.E0), [c.E0.r()], [c.Eb[0].r()])

                def st_merge(l):
                    def f(c, b):
                        bk = banks[c.bN]
                        Dp, Ep = c.Db[l % 2], c.Eb[l % 2]
                        Dn, En = c.Db[(l + 1) % 2], c.Eb[(l + 1) % 2]
                        mk = bmv[:, 1 + l, :]
                        if l < 3:
                            MM(bk[:, 0:128], B(c.XTb), B(Dp), [c.XTb.r(), Dp.r()], [rbank[c.bN]])
                        MM(bk[:, 128:256], B(c.Xb), B(Ep), [c.Xb.r(), Ep.r()], [rbank[c.bN]])
                        yield
                        if l < 3:
                            TT("dve", B(c.M1), bk[:, 0:128], mk, ALU.mult, [bmask.r()], [c.M1.r(), rbank[c.bN]])
                        TT("dve", B(c.M1p), bk[:, 128:256], mk, ALU.mult, [bmask.r()], [c.M1p.r(), rbank[c.bN]])
                        yield
                        if l < 3:
                            MM(bk[:, 256:384], identb[:], B(Dp), [r_c, Dp.r()], [rbank[c.bN]], start=True, stop=False)
                            MM(bk[:, 256:384], B(Ep), B(c.M1), [Ep.r(), c.M1.r()], [rbank[c.bN]], start=False, stop=True)
                        MM(bk[:, 384:512], identb[:], B(Ep), [r_c, Ep.r()], [rbank[c.bN]], start=True, stop=False)
                        MM(bk[:, 384:512], B(Dp), B(c.M1p), [Dp.r(), c.M1p.r()], [rbank[c.bN]], start=False, stop=True)
                        yield
                        if l < 3:
                            CP("act", B(Dn), bk[:, 256:384], [], [Dn.r(), rbank[c.bN]])
                        CP("act", B(En), bk[:, 384:512], [], [En.r(), rbank[c.bN]])
                    return f

                def st_c(c, b):
                    bs = slice(b * 128, (b + 1) * 128)
                    bk = banks[c.bA]
                    TR(bfps(c.bA, 0, 128), knb.bf(T)[:, bs], identb[:], [knb.r(), r_c], [rbank[c.bA]])
                    TR(bk[:, 128:256], vf.f32(T)[:, bs], ident[:], [vf.r(), r_c], [rbank[c.bA]])
                    yield
                    ACT(B(c.vb), bk[:, 128:256], AF.Identity, [beta.r()], [c.vb.r(), rbank[c.bA]], scale=beta.f32(NG, 8)[:, b, h:h + 1])
                    ACT(B(c.kbg), bfps(c.bA, 0, 128), AF.Identity, [beg.r()], [c.kbg.r(), rbank[c.bA]], scale=beg.f32(NG, 8)[:, b, h:h + 1])
                    TS("dve", B(c.kg), bfps(c.bA, 0, 128), egl.f32(NG, 8)[:, b, h:h + 1], None, ALU.mult, None, [egl.r()], [c.kg.r(), rbank[c.bA]])
                    yield
                    MM(bk[:, 256:384], B(c.PTb), B(c.vb), [c.PTb.r(), c.vb.r()], [rbank[c.bA]])
                    MM(bk[:, 384:512], B(c.kbg), B(c.PTb), [c.PTb.r(), c.kbg.r()], [rbank[c.bA]])
                    yield
                    CP("dve", F(c.u), bk[:, 256:384], [], [c.u.r(), rbank[c.bA]])
                    CP("dve", B(c.wkT), bk[:, 384:512], [], [c.wkT.r(), rbank[c.bA]])

                def recur(c, b):
                    b6, b7 = banks[6], banks[7]
                    MM(b6[:, 0:128], B(c.wkT), B(Sb), [c.wkT.r(), Sb.r()], [rbank[6]])
                    TT("dve", B(c.vnew), F(c.u), b6[:, 0:128], ALU.subtract, [c.u.r()], [c.vnew.r(), rbank[6]])
                    MM(b7[:, 0:128], B(c.qgT), B(Sb), [c.qgT.r(), Sb.r()], [rbank[7]], start=True, stop=False)
                    MM(b7[:, 0:128], B(c.attnT), B(c.vnew), [c.attnT.r(), c.vnew.r()], [rbank[7]], start=False, stop=True)
                    MM(b6[:, 128:256], B(c.kg), B(c.vnew), [c.kg.r(), c.vnew.r()], [rbank[6]])
                    STT("dve", F(Sf), F(Sf), gl.f32(NB, 8)[:, b, h:h + 1], b6[:, 128:256], ALU.mult, ALU.add, [gl.r()], [Sf.r(), rbank[6]])
                    CP("act", B(Sb), F(Sf), [Sf.r()], [Sb.r()])
                    CP("act", F(c.osb), b7[:, 0:128], [], [c.osb.r(), rbank[7]])

                def recur_post(c, b):
                    bs = slice(b * 128, (b + 1) * 128)
                    ACT(F(c.on), F(c.osb), AF.Square, [c.osb.r()], [c.on.r(), c.ssq.r()], accum=c.ssq.f32(1))
                    rsqrt_eps(c.ssq.f32(1), c.ssq.f32(1), RMS_EPS * 128.0, c.ssq.r())
                    STT("dve", F(c.on), F(c.osb), c.ssq.f32(1), F(gnb), ALU.mult, ALU.mult, [c.osb.r(), c.ssq.r(), gnb.r()], [c.on.r()])
                    TR(banks[c.bN][:, 384:512], F(c.on), ident[:], [c.on.r(), r_c], [rbank[c.bN]])
                    TT("dve", ogb_ap[:, bs], banks[c.bN][:, 384:512], szf.f32(T)[:, bs], ALU.mult, [szf.r()], [Ub.r(), rbank[c.bN]])

                stages = [st_a, st_b, st_base1, st_base2] + [st_merge(l) for l in range(4)] + [st_c]
                pend_ = []
                for g0 in range(0, NB if "gdn_noblk" not in DBG else 0, G):
                    grp = [(chains[i], g0 + i) for i in range(min(G, NB - g0))]
                    for stg_ in stages:
                        gens_ = [stg_(c, b) for c, b in grp]
                        while gens_:
                            nx_ = []
                            for gn_ in gens_:
                                try:
                                    next(gn_)
                                    nx_.append(gn_)
                                except StopIteration:
                                    pass
                            gens_ = nx_
                    for c, b in grp:
                        recur(c, b)
                        if pend_:
                            recur_post(*pend_.pop())
                        pend_.append((c, b))
                    if pend_:
                        recur_post(*pend_.pop())
                if pend_:
                    recur_post(*pend_.pop())
                for dc in range(8):
                    for ti, (c0, n) in enumerate(ptiles):
                        bk = 4 + cntb[0] % 2
                        cntb[0] += 1
                        MM(banks[bk][:, 0:n], wo.bf(1024)[:, dc * 128:(dc + 1) * 128], ogb_ap[:, c0:c0 + n], [wo.r(), Ub.r()], [rbank[bk]])
                        xv = xf[:, dc, c0:c0 + n]
                        if h == 0:
                            STT("dve", xv, xv, ALPHA, banks[bk][:, 0:n], ALU.mult, ALU.add, [r_xf[dc][ti]], [r_xf[dc][ti], rbank[bk]])
                        else:
                            TT("dve", xv, xv, banks[bk][:, 0:n], ALU.add, [r_xf[dc][ti]], [r_xf[dc][ti], rbank[bk]])
                P.dma("sp", "gsp", o_gdn_p[h], F(Sf), reads=[Sf.r()])
            A.release(mH, list(bufs))
            del bufs[:]
            cst_ = A.f32(3072)
            for q4 in range(6):
                for i4 in range(4):
                    ch = q4 * 4 + i4
                    TR(banks[4][0:3, i4 * 128:(i4 + 1) * 128], luv[:, ch, :], ident[:], [lastU.r(), r_c], [rbank[4]])
                CP("dve", cst_.f32(3072)[0:3, q4 * 512:(q4 + 1) * 512], banks[4][0:3, :], [], [cst_.r(), rbank[4]])
            P.dma("sp", "gcp", o_conv_p, cst_.f32(3072)[0:3, :], reads=[cst_.r()])
            A.release(mH, [cst_])
            mS = A.mark()
            if "gdn_nosamp" in DBG:
                A.release(m, base_bufs)
                return
            sb_ = []

            def als(n, bf=False):
                b = A.bf(n) if bf else A.f32(n)
                sb_.append(b)
                return b
            projs = als(4096)
            for typ in range(4):
                for h2 in range(2):
                    bk = (typ * 2 + h2) % 2
                    for i4 in range(4):
                        hh = h2 * 4 + i4
                        TR(banks[bk][0:NS, i4 * 128:(i4 + 1) * 128], pjT[:, typ, hh, :], ident[:], [projsT.r(), r_c], [rbank[bk]])
                    CP("dve", projs.f32(4096)[0:NS, typ * 1024 + h2 * 512:typ * 1024 + (h2 + 1) * 512], banks[bk][0:NS, :], [], [projs.r(), rbank[bk]])
            P.dma("sp", "gcv", o_conv_s[:, 0:2, :], sconv[:, 1:3, :])
            P.dma("sp", "gcv", o_conv_s[:, 2, :], projs.f32(4096)[0:NS, 0:3072], reads=[projs.r()])
            P.commit("gcv")
            qkv = als(3072)
            zs = als(1024)
            mC = A.mark()
            PW = 256
            ext = A.f32(4 * PW); cwb = A.f32(4 * PW); prod = A.f32(4 * PW)
            for pc in range(3072 // PW):
                cs_ = slice(pc * PW, (pc + 1) * PW)
                P.dma("sp", "gsl", ext.f32(4, PW)[0:NS, 0:3, :], sconv[:, :, cs_], writes=[ext.r()])
                P.dma("sp", "gsl", cwb.f32(4, PW)[0:NS], gdn_conv_w[:, cs_].partition_broadcast(NS), writes=[cwb.r()])
                P.commit("gsl")
                CP("dve", ext.f32(4, PW)[0:NS, 3, :], projs.f32(4096)[0:NS, cs_], [projs.r()], [ext.r()])
                TT("dve", prod.f32(4, PW)[0:NS], ext.f32(4, PW)[0:NS], cwb.f32(4, PW)[0:NS], ALU.mult, [ext.r(), cwb.r()], [prod.r()])
                RED("dve", qkv.f32(3072)[0:NS, cs_], prod.f32(4, PW)[0:NS].rearrange("p j c -> p c j"), [prod.r()], [qkv.r()])
            A.release(mC, [ext, cwb, prod])
            ACT(qkv.f32(3072)[0:NS], qkv.f32(3072)[0:NS], AF.Silu, [qkv.r()], [qkv.r()])
            ACT(zs.f32(1024)[0:NS], projs.f32(4096)[0:NS, 3072:4096], AF.Silu, [projs.r()], [zs.r()])
            q3 = qkv.f32(3, 8, 128)[0:NS, 0]
            k3 = qkv.f32(3, 8, 128)[0:NS, 1]
            v3 = qkv.f32(3, 8, 128)[0:NS, 2]
            tmp = als(1024); ss = als(16)
            t3 = tmp.f32(8, 128)[0:NS]
            for (x3, scl) in ((q3, 128.0 ** -0.5), (k3, 1.0)):
                TT("dve", t3, x3, x3, ALU.mult, [qkv.r()], [tmp.r()])
                RED("dve", ss.f32(8)[0:NS], t3, [tmp.r()], [ss.r()])
                rsqrt_eps(ss.f32(8)[0:NS], ss.f32(8)[0:NS], RMS_EPS, ss.r())
                STT("dve", x3, x3, scl, ss.f32(8)[0:NS].unsqueeze(2).to_broadcast([NS, 8, 128]), ALU.mult, ALU.mult, [ss.r()], [qkv.r()])
            qk = als(8)
            TT("dve", t3, q3, k3, ALU.mult, [qkv.r()], [tmp.r()])
            RED("dve", qk.f32(8)[0:NS], t3, [tmp.r()], [qk.r()])
            kTs = als(8 * NS); qTs = als(8 * NS)
            for (x3, dst, bk) in ((k3, kTs, 4), (q3, qTs, 5)):
                for hh in range(8):
                    TR(banks[bk][:, hh * NS:(hh + 1) * NS], x3[:, hh, :], ident[0:NS, 0:NS], [qkv.r(), r_c], [rbank[bk]])
                CP("dve", dst.f32(8 * NS), banks[bk][:, 0:8 * NS], [], [dst.r(), rbank[bk]])
            dlt = als(NS * NS)
            P.dma("sp", "gsm", dlt.f32(NS, NS), cd["delta"], writes=[dlt.r()])
            dcol = als(NS)
            P.dma("sp", "gsm", dcol.f32(NS), cd["dcol"], writes=[dcol.r()])
            P.commit("gsm")
            KmL = [als(NS * NS), als(NS * NS)]; QmL = [als(NS * NS), als(NS * NS)]
            Sp = [als(128) for _ in range(4)]
            ci_ = 0
            for hh in range(8):
                Km = KmL[hh % 2]; Qm = QmL[hh % 2]
                for (src, dst) in ((kTs, Km), (qTs, Qm)):
                    TT("dve", dst.f32(NS, NS), src.f32(8, NS)[:, hh, :].unsqueeze(1).to_broadcast([128, NS, NS]),
                       dlt.f32(NS, NS), ALU.mult, [src.r(), dlt.r()], [dst.r()])
                for s in range(NS):
                    sp_ = Sp[ci_ % 4]
                    P.dma("sp", "gsq%d" % (ci_ % 4), sp_.f32(128), sgdn[s, hh], writes=[sp_.r()])
                    ci_ += 1
                    MM(banks[hh // 4][0:NS, (hh % 4) * 128:(hh % 4 + 1) * 128], Km.f32(NS, NS)[:, s, :], sp_.f32(128),
                       [Km.r(), sp_.r()], [rbank[hh // 4]], start=(s == 0), stop=(s == NS - 1))
                    MM(banks[2 + hh // 4][0:NS, (hh % 4) * 128:(hh % 4 + 1) * 128], Qm.f32(NS, NS)[:, s, :], sp_.f32(128),
                       [Qm.r(), sp_.r()], [rbank[2 + hh // 4]], start=(s == 0), stop=(s == NS - 1))
            KS = als(1024); QS = als(1024)
            for half in range(2):
                CP("dve", KS.f32(1024)[0:NS, half * 512:(half + 1) * 512], banks[half][0:NS, :], [], [KS.r(), rbank[half]])
                CP("dve", QS.f32(1024)[0:NS, half * 512:(half + 1) * 512], banks[2 + half][0:NS, :], [], [QS.r(), rbank[2 + half]])
            bc8 = lambda b_, col: b_.f32(NG, 8)[0:NS, col, :].unsqueeze(2).to_broadcast([NS, 8, 128])
            KS3 = KS.f32(8, 128)[0:NS]; QS3 = QS.f32(8, 128)[0:NS]
            vn = als(1024)
            vn3 = vn.f32(8, 128)[0:NS]
            TT("dve", KS3, KS3, bc8(eg, NB), ALU.mult, [eg.r()], [KS.r()])
            TT("dve", vn3, v3, KS3, ALU.subtract, [qkv.r(), KS.r()], [vn.r()])
            TT("dve", vn3, vn3, bc8(beta, NB), ALU.mult, [beta.r()], [vn.r()])
            TT("dve", QS3, QS3, bc8(eg, NB), ALU.mult, [eg.r()], [QS.r()])
            TT("dve", t3, vn3, qk.f32(8)[0:NS].unsqueeze(2).to_broadcast([NS, 8, 128]), ALU.mult, [vn.r(), qk.r()], [tmp.r()])
            TT("dve", QS3, QS3, t3, ALU.add, [tmp.r()], [QS.r()])
            TT("dve", t3, QS3, QS3, ALU.mult, [QS.r()], [tmp.r()])
            RED("dve", ss.f32(8)[0:NS], t3, [tmp.r()], [ss.r()])
            rsqrt_eps(ss.f32(8)[0:NS], ss.f32(8)[0:NS], RMS_EPS, ss.r(), scale=1.0 / 128.0)
            TT("dve", QS3, QS3, ss.f32(8)[0:NS].unsqueeze(2).to_broadcast([NS, 8, 128]), ALU.mult, [ss.r()], [QS.r()])
            TT("dve", QS3, QS3, gn.f32(128)[0:NS].unsqueeze(1).to_broadcast([NS, 8, 128]), ALU.mult, [gn.r()], [QS.r()])
            TT("dve", QS3, QS3, zs.f32(8, 128)[0:NS], ALU.mult, [zs.r()], [QS.r()])
            ogsT = als(8 * NS, bf=True)
            for hh in range(8):
                TR(banks[4][:, hh * NS:(hh + 1) * NS], QS3[:, hh, :], ident[0:NS, 0:NS], [QS.r(), r_c], [rbank[4]])
            CP("dve", ogsT.bf(8 * NS), banks[4][:, 0:8 * NS], [], [ogsT.r(), rbank[4]])
            out_proj_samples(gdn_w_out, 8, ogsT.bf(8, NS), ogsT.r())
            Rm = als(NS * 8)
            TT("dve", Rm.f32(NS, 8)[0:NS], dcol.f32(NS)[0:NS].unsqueeze(2).to_broadcast([NS, NS, 8]),
               eg.f32(NG, 8)[0:NS, NB, :].unsqueeze(1).to_broadcast([NS, NS, 8]), ALU.mult, [dcol.r(), eg.r()], [Rm.r()])
            EGb = als(NS * 8)
            MM(banks[4][:, 0:NS * 8], onesf[0:NS, :], Rm.f32(NS * 8)[0:NS], [Rm.r(), r_c], [rbank[4]])
            CP("dve", EGb.f32(NS * 8), banks[4][:, 0:NS * 8], [], [EGb.r(), rbank[4]])
            Vm = [als(1024), als(1024)]
            Sn = [als(128) for _ in range(4)]
            ci_ = 0
            for s in range(NS):
                vm_ = Vm[s % 2]
                TS("dve", vm_.f32(1024)[0:NS], vn.f32(1024)[0:NS], dcol.f32(NS)[0:NS, s:s + 1], None, ALU.mult, None, [vn.r(), dcol.r()], [vm_.r()])
                for hh in range(8):
                    sp_ = Sp[ci_ % 4]; sn_ = Sn[ci_ % 4]
                    bk = 4 + ci_ % 4
                    P.dma("sp", "gsq%d" % (ci_ % 4), sp_.f32(128), sgdn[s, hh], writes=[sp_.r()])
                    MM(banks[bk][:, 0:128], k3[:, hh, :], vm_.f32(8, 128)[0:NS, hh, :], [qkv.r(), vm_.r()], [rbank[bk]])
                    STT("dve", sn_.f32(128), sp_.f32(128), EGb.f32(NS, 8)[:, s, hh:hh + 1], banks[bk][:, 0:128], ALU.mult, ALU.add,
                        [sp_.r(), EGb.r()], [sn_.r(), rbank[bk]])
                    P.dma("pool", "gst%d" % (ci_ % 4), o_gdn_s[s, hh], sn_.f32(128), reads=[sn_.r()])
                    ci_ += 1
            A.release(mS, sb_)
            A.release(m, base_bufs)

        def ret_mixer():
            G = 2
            m = A.mark()
            bufs = []

            def al(n, bf=False):
                b = A.bf(n) if bf else A.f32(n)
                bufs.append(b)
                return b
            cos2 = al(NT); sin2 = al(NT); decT = al(128); xi = al(128); zeta = al(8); dcol = al(NS); dlt = al(NS * NS)
            P.dma("sp", "rc", cos2.f32(NT), cd["cos2"], writes=[cos2.r()])
            P.dma("sp", "rc", sin2.f32(NT), cd["sin2"], writes=[sin2.r()])
            P.dma("sp", "rc", zeta.f32(8), cd["zeta"], writes=[zeta.r()])
            P.dma("sp", "rc", dcol.f32(NS), cd["dcol"], writes=[dcol.r()])
            P.dma("sp", "rc", dlt.f32(NS, NS), cd["delta"], writes=[dlt.r()])
            P.commit("rc")
            SC = 128.0 ** -0.5
            qrb = al(NT, bf=True); krb = al(NT, bf=True); krf = al(NT); qsf = al(NS)
            t1 = [al(TTS)]; t2 = [al(TTS)]
            vtok = al(NB * 256, bf=True)
            vtv = vtok.bf(NB, 256)
            ogT = al(2 * NT, bf=True)
            ogv = ogT.bf(2, NT)
            Sf = al(256); Sb = al(256, bf=True)
            v_s = al(256); sg_s = al(256); kts = al(128); Qm = al(NS * NS); qk = al(8); tmp_s = al(256); o_s = al(256)
            Sp = [al(256) for _ in range(2)]
            Vm = [al(256) for _ in range(2)]; Sn = [al(256) for _ in range(2)]
            stat = [al(8) for _ in range(3)]

            class Ch:
                pass
            chains = []
            for ci in range(2 * G):
                c = Ch()
                c.attnT = al(128, bf=True); c.qxT = al(128, bf=True); c.kz = al(128, bf=True)
                c.sg = al(256); c.on = al(256); c.osb = al(256)
                c.bA = (0, 1, 4, 5)[ci]
                chains.append(c)
            B = lambda b_: b_.bf(128)

            def gnorm_gate(o_ap, np_, on_ap, on_res, sg_ap, sg_res, o_reads, o_writes):
                sm, sq_, mm = stat[0], stat[1], stat[2]
                ACT(on_ap, o_ap, AF.Identity, o_reads, [on_res, sm.r()] + o_writes, accum=sm.f32(1)[0:np_])
                ACT(on_ap, o_ap, AF.Square, o_reads, [on_res, sq_.r()] + o_writes, accum=sq_.f32(1)[0:np_])
                TS("dve", sm.f32(1)[0:np_], sm.f32(1)[0:np_], 1.0 / 256.0, None, ALU.mult, None, [sm.r()], [sm.r()])
                TT("dve", mm.f32(1)[0:np_], sm.f32(1)[0:np_], sm.f32(1)[0:np_], ALU.mult, [sm.r()], [mm.r()])
                STT("dve", sq_.f32(1)[0:np_], sq_.f32(1)[0:np_], 1.0 / 256.0, mm.f32(1)[0:np_], ALU.mult, ALU.subtract, [sq_.r(), mm.r()], [sq_.r()])
                rsqrt_eps(sq_.f32(1)[0:np_], sq_.f32(1)[0:np_], LN_EPS, sq_.r())
                TS("dve", on_ap, o_ap, sm.f32(1)[0:np_], sq_.f32(1)[0:np_], ALU.subtract, ALU.mult, o_reads + [sm.r(), sq_.r()], [on_res] + o_writes)
                TT("pool", on_ap, on_ap, sg_ap, ALU.mult, [on_res, sg_res], [on_res])

            for h in range(8):
                slots = []
                for xi_, base in enumerate((0, 1024)):
                    wb_, wk_ = wslot()
                    wv_ = wb_.bf(2, 8, 128)
                    c0_ = base + h * 128
                    r3 = lambda a, b_: ret_w_in[:, a:b_].rearrange("(k p) n -> p k n", p=128)
                    P.dma("pool", wk_, wv_[:, 0], r3(c0_, c0_ + 128), writes=[wb_.r()])
                    P.dma("pool", wk_, wv_[:, 1, :, 0:64], r3(c0_ + 64, c0_ + 128), writes=[wb_.r()])
                    P.dma("pool", wk_, wv_[:, 1, :, 64:128], r3(c0_, c0_ + 64), writes=[wb_.r()])
                    P.commit(wk_)
                    slots.append((wb_, wv_))
                P.dma("sp", "rdx", decT.f32(128), cd["decT"][:, h, :], writes=[decT.r()])
                P.dma("sp", "rdx", xi.f32(128), cd["xi"][:, h, :], writes=[xi.r()])
                P.commit("rdx")
                wv_t, wkv_ = wslot()
                P.dma("pool", wkv_, wv_t.bf(8, 256), ret_w_in[:, 2048 + h * 256:2048 + (h + 1) * 256].rearrange("(k p) n -> p k n", p=128), writes=[wv_t.r()])
                P.commit(wkv_)
                wg_t, wkg_ = wslot()
                P.dma("pool", wkg_, wg_t.bf(8, 256), ret_w_in[:, 4096 + h * 256:4096 + (h + 1) * 256].rearrange("(k p) n -> p k n", p=128), writes=[wg_t.r()])
                P.commit(wkg_)
                cn = 0
                for xi_ in range(2):
                    wb_, wv_ = slots[xi_]
                    for ti, (c0, n) in enumerate(tiles):
                        a1, a2 = t1[0], t2[0]
                        ba_, bb_ = 4 + 2 * (cn % 2), 5 + 2 * (cn % 2)
                        cn += 1
                        for k in range(8):
                            MM(banks[ba_][:, 0:n], wv_[:, 0, k, :], xb[:, k, c0:c0 + n], [wb_.r(), r_xb[k][ti]], [rbank[ba_]], start=(k == 0), stop=(k == 7))
                        for k in range(8):
                            MM(banks[bb_][:, 0:n], wv_[:, 1, k, :], xb[:, k, c0:c0 + n], [wb_.r(), r_xb[k][ti]], [rbank[bb_]], start=(k == 0), stop=(k == 7))
                        TT("dve", a1.f32(n), banks[ba_][:, 0:n], cos2.f32(NT)[:, c0:c0 + n], ALU.mult, [cos2.r()], [a1.r(), rbank[ba_]])
                        TT("dve", a2.f32(n), banks[bb_][:, 0:n], sin2.f32(NT)[:, c0:c0 + n], ALU.mult, [sin2.r()], [a2.r(), rbank[bb_]])
                        if xi_ == 0:
                            TT("pool", qrb.bf(NT)[:, c0:c0 + n], a1.f32(n), a2.f32(n), ALU.add, [a1.r(), a2.r()], [qrb.r()])
                            if ti == NTI - 1:
                                TT("pool", qsf.f32(NS), a1.f32(n), a2.f32(n), ALU.add, [a1.r(), a2.r()], [qsf.r()])
                        else:
                            TT("pool", krf.f32(NT)[:, c0:c0 + n], a1.f32(n), a2.f32(n), ALU.add, [a1.r(), a2.r()], [krf.r()])
                CP("pool", krb.bf(NT), krf.f32(NT), [krf.r()], [krb.r()])
                for b in range(NB):
                    bs = slice(b * 128, (b + 1) * 128)
                    bv_ = 6 + b % 2
                    for k in range(8):
                        MM(banks[bv_][:, 0:256], xb[:, k, bs], wv_t.bf(8, 256)[:, k, :], [wv_t.r()] + rx(r_xb, [k], tiles_of(b * 128, 128)), [rbank[bv_]],
                           start=(k == 0), stop=(k == 7))
                    CP("act", vtv[:, b, :], banks[bv_][:, 0:256], [], [vtok.r(), rbank[bv_]])
                for k in range(8):
                    MM(banks[6][0:NS, 256:512], xb[:, k, T:NT], wv_t.bf(8, 256)[:, k, :], [wv_t.r(), r_xb[k][NTI - 1]], [rbank[6]], start=(k == 0), stop=(k == 7))
                CP("act", v_s.f32(256)[0:NS], banks[6][0:NS, 256:512], [], [v_s.r(), rbank[6]])
                MS("dve", Sf.f32(256), 0.0, [Sf.r()])
                MS("dve", Sb.bf(256), 0.0, [Sb.r()])

                def st_a(c, b):
                    bs = slice(b * 128, (b + 1) * 128)
                    bk = banks[c.bA]
                    MM(bk[:, 0:128], krb.bf(NT)[:, bs], qrb.bf(NT)[:, bs], [krb.r(), qrb.r()], [rbank[c.bA]])
                    TR(bk[:, 128:256], krf.f32(NT)[:, bs], ident[:], [krf.r(), r_c], [rbank[c.bA]])
                    for k in range(8):
                        MM(bk[:, 256:512], xb[:, k, bs], wg_t.bf(8, 256)[:, k, :], [wg_t.r()] + rx(r_xb, [k], tiles_of(b * 128, 128)), [rbank[c.bA]],
                           start=(k == 0), stop=(k == 7))
                    yield
                    TT("dve", B(c.attnT), bk[:, 0:128], decT.f32(128), ALU.mult, [decT.r()], [c.attnT.r(), rbank[c.bA]])
                    ACT(B(c.kz), bk[:, 128:256], AF.Identity, [zeta.r()], [c.kz.r(), rbank[c.bA]], scale=zeta.f32(8)[:, h:h + 1])
                    ACT(c.sg.f32(256), bk[:, 256:512], AF.Silu, [], [c.sg.r(), rbank[c.bA]])
                    TT("pool", B(c.qxT), qrb.bf(NT)[:, bs], xi.f32(128), ALU.mult, [qrb.r(), xi.r()], [c.qxT.r()])

                def recur(c, b):
                    MM(banks[2][:, 0:256], B(c.qxT), Sb.bf(256), [c.qxT.r(), Sb.r()], [rbank[2]], start=True, stop=False)
                    MM(banks[2][:, 0:256], B(c.attnT), vtv[:, b, :], [c.attnT.r(), vtok.r()], [rbank[2]], start=False, stop=True)
                    MM(banks[3][:, 0:256], B(c.kz), vtv[:, b, :], [c.kz.r(), vtok.r()], [rbank[3]])
                    STT("dve", Sf.f32(256), Sf.f32(256), cst["gC"][h], banks[3][:, 0:256], ALU.mult, ALU.add, [], [Sf.r(), rbank[3]])
                    CP("act", Sb.bf(256), Sf.f32(256), [Sf.r()], [Sb.r()])
                    CP("act", c.osb.f32(256), banks[2][:, 0:256], [], [c.osb.r(), rbank[2]])

                def recur_post(c, b):
                    bs = slice(b * 128, (b + 1) * 128)
                    gnorm_gate(c.osb.f32(256), 128, c.on.f32(256), c.on.r(), c.sg.f32(256), c.sg.r(), [c.osb.r()], [])
                    for cc in range(2):
                        TR(banks[7][:, 256 + cc * 128:256 + (cc + 1) * 128], c.on.f32(256)[:, cc * 128:(cc + 1) * 128], ident[:], [c.on.r(), r_c], [rbank[7]])
                    CP("act", ogv[:, :, bs], banks[7][:, 256:512].rearrange("p (c t) -> p c t", c=2, t=128), [], [ogT.r(), rbank[7]])

                pend_ = []
                for g0 in range(0, NB if "ret_noblk" not in DBG else 0, G):
                    cs_ = ((g0 // G) % 2) * G
                    grp = [(chains[cs_ + i], g0 + i) for i in range(min(G, NB - g0))]
                    gens_ = [st_a(c, b) for c, b in grp]
                    first_ = True
                    while gens_:
                        nx_ = []
                        for gn_ in gens_:
                            try:
                                next(gn_)
                                nx_.append(gn_)
                            except StopIteration:
                                pass
                        gens_ = nx_
                        if first_:
                            for cb_ in pend_:
                                recur_post(*cb_)
                            first_ = False
                    for c, b in grp:
                        recur(c, b)
                    pend_ = list(grp)
                for cb_ in pend_:
                    recur_post(*cb_)
                P.dma("sp", "rsp", o_ret_p[h], Sf.f32(256), reads=[Sf.r()])
                for k in range(8 if "ret_nosamp" not in DBG else 0):
                    MM(banks[6][0:NS, 0:256], xb[:, k, T:NT], wg_t.bf(8, 256)[:, k, :], [wg_t.r(), r_xb[k][NTI - 1]], [rbank[6]], start=(k == 0), stop=(k == 7))
                if "ret_nosamp" not in DBG:
                    ACT(sg_s.f32(256)[0:NS], banks[6][0:NS, 0:256], AF.Silu, [], [sg_s.r(), rbank[6]])
                    ks_ = krf.f32(NT)[:, T:NT]
                    TR(banks[7][0:NS, 0:128], ks_, ident[:], [krf.r(), r_c], [rbank[7]])
                    CP("dve", kts.f32(128)[0:NS], banks[7][0:NS, 0:128], [], [kts.r(), rbank[7]])
                    TT("dve", tmp_s.f32(NS), qsf.f32(NS), ks_, ALU.mult, [qsf.r(), krf.r()], [tmp_s.r()])
                    MM(banks[7][0:NS, 128:129], tmp_s.f32(NS), onesf[:, 0:1], [tmp_s.r(), r_c], [rbank[7]])
                    TS("dve", qk.f32(1)[0:NS], banks[7][0:NS, 128:129], SC, None, ALU.mult, None, [], [qk.r(), rbank[7]])
                    TT("dve", Qm.f32(NS, NS), qsf.f32(NS).unsqueeze(1).to_broadcast([128, NS, NS]), dlt.f32(NS, NS), ALU.mult, [qsf.r(), dlt.r()], [Qm.r()])
                    for s in range(NS):
                        sp_ = Sp[s % 2]; vm_ = Vm[s % 2]; sn_ = Sn[s % 2]
                        bk = 4 + s % 2
                        P.dma("sp", "rsq%d" % (s % 2), sp_.f32(256), sret[s, h], writes=[sp_.r()])
                        MM(banks[6][0:NS, 256:512], Qm.f32(NS, NS)[:, s, :], sp_.f32(256), [Qm.r(), sp_.r()], [rbank[6]], start=(s == 0), stop=(s == NS - 1))
                        TS("dve", vm_.f32(256)[0:NS], v_s.f32(256)[0:NS], dcol.f32(NS)[0:NS, s:s + 1], SC, ALU.mult, ALU.mult, [v_s.r(), dcol.r()], [vm_.r()])
                        MM(banks[bk][:, 0:256], kts.f32(128)[0:NS], vm_.f32(256)[0:NS], [kts.r(), vm_.r()], [rbank[bk]])
                        STT("dve", sn_.f32(256), sp_.f32(256), cst["gam"][h], banks[bk][:, 0:256], ALU.mult, ALU.add, [sp_.r()], [sn_.r(), rbank[bk]])
                        P.dma("pool", "rst%d" % (s % 2), o_ret_s[s, h], sn_.f32(256), reads=[sn_.r()])
                    TS("dve", tmp_s.f32(256)[0:NS], v_s.f32(256)[0:NS], qk.f32(1)[0:NS], None, ALU.mult, None, [v_s.r(), qk.r()], [tmp_s.r()])
                    STT("dve", o_s.f32(256)[0:NS], banks[6][0:NS, 256:512], cst["gam"][h], tmp_s.f32(256)[0:NS], ALU.mult, ALU.add, [tmp_s.r()], [o_s.r(), rbank[6]])
                    gnorm_gate(o_s.f32(256)[0:NS], NS, tmp_s.f32(256)[0:NS], tmp_s.r(), sg_s.f32(256)[0:NS], sg_s.r(), [o_s.r()], [])
                    for cc in range(2):
                        TR(banks[7][:, 256 + cc * NS:256 + (cc + 1) * NS], tmp_s.f32(256)[0:NS, cc * 128:(cc + 1) * 128], ident[0:NS, 0:NS], [tmp_s.r(), r_c], [rbank[7]])
                    CP("dve", ogv[:, :, T:NT], banks[7][:, 256:256 + 2 * NS].rearrange("p (c t) -> p c t", c=2, t=NS), [], [ogT.r(), rbank[7]])
                wo_t, wko_ = wslot()
                P.dma("pool", wko_, wo_t.bf(2, 1024), ret_w_out[h * 256:(h + 1) * 256, :].rearrange("(c p) n -> p c n", p=128), writes=[wo_t.r()])
                P.commit(wko_)
                for dc in range(8):
                    for ti, (c0, n) in enumerate(tiles):
                        bk = 4 + (dc * NTI + ti) % 2
                        for cc in range(2):
                            MM(banks[bk][:, 0:n], wo_t.bf(2, 1024)[:, cc, dc * 128:(dc + 1) * 128], ogv[:, cc, c0:c0 + n], [wo_t.r(), ogT.r()], [rbank[bk]],
                               start=(cc == 0), stop=(cc == 1))
                        xv = xf[:, dc, c0:c0 + n]
                        if h == 0:
                            STT("dve", xv, xv, ALPHA, banks[bk][:, 0:n], ALU.mult, ALU.add, [r_xf[dc][ti]], [r_xf[dc][ti], rbank[bk]])
                        else:
                            TT("dve", xv, xv, banks[bk][:, 0:n], ALU.add, [r_xf[dc][ti]], [r_xf[dc][ti], rbank[bk]])
            A.release(m, bufs)

        if "noload" not in DBG:
            load_x()
        for li in range(nlayers):
            kind = li % 3
            if kind == 0:
                pool_mixer(li // 3)
            elif kind == 1:
                gdn_mixer()
            else:
                ret_mixer()
            layer_norm(2 * li)
            ffn(li)
            layer_norm(2 * li + 1)
        if "nostore" not in DBG:
            store_y()
        if "arena" in DBG:
            print("arena words", AW, "high water", A.hw)
        P.emit(st)
    return nc, cst


_CACHE = {}


def _in_maps(inputs, T, NS, cst):
    f = lambda a: np.ascontiguousarray(np.asarray(a, dtype=np.float32))
    shared = {
        "pool_w": f(inputs["pool_w"]), "pool_scale": f(inputs["pool_scale"]),
        "gdn_w_in": f(inputs["gdn_w_in"][0]), "gdn_conv_w": f(inputs["gdn_conv_w"][0]),
        "gdn_a_log": f(inputs["gdn_a_log"]), "gdn_dt_bias": f(inputs["gdn_dt_bias"]),
        "gdn_norm_g": f(inputs["gdn_norm_g"]), "gdn_w_out": f(inputs["gdn_w_out"][0]),
        "ret_w_in": f(inputs["ret_w_in"][0]), "ret_w_out": f(inputs["ret_w_out"][0]),
        "ffn_w13": f(inputs["ffn_w13"]), "ffn_w2": f(inputs["ffn_w2"]),
        "ln_g": f(inputs["ln_g"]).reshape(8, D), "ln_b": f(inputs["ln_b"]).reshape(8, D),
    }
    for k in list(CONST_SHAPES) + ["cos2", "sin2"]:
        shared["c_" + k] = f(cst[k])
    maps = []
    for c in range(NCORES):
        sl = slice(c * NS, (c + 1) * NS)
        mp = dict(shared)
        mp["xp"] = f(inputs["x_prompt"][c])
        mp["xs"] = f(inputs["x_sample"][sl, 0])
        mp["spool"] = f(inputs["state_pool"][:, sl])
        mp["sconv"] = f(inputs["state_gdn_conv"][0, sl])
        mp["sgdn"] = f(inputs["state_gdn"][0, sl])
        mp["sret"] = f(inputs["state_ret"][0, sl])
        maps.append(mp)
    return maps


def kernel(**inputs):
    T, NS = 2048, 16
    if "nc" not in _CACHE:
        _CACHE["nc"] = build(T, NS)
    nc, cst = _CACHE["nc"]
    maps = _in_maps(inputs, T, NS, cst)
    res = run_bass_kernel_spmd(nc, maps, core_ids=list(range(NCORES))).results
    cat = lambda k, ax: np.concatenate([np.asarray(r[k], dtype=np.float32)[None] if ax is None else np.asarray(r[k], dtype=np.float32)
                                        for r in res], axis=0 if ax is None else ax)
    y_prompt = cat("yp", None)
    y_sample = cat("ys", 0)[:, None, :]
    pool_p = np.stack([np.asarray(r["pool_p"], np.float32) for r in res], axis=1)
    pool_s = cat("pool_s", 1)
    conv_p = cat("conv_p", None)[None]
    conv_s = cat("conv_s", 0)[None]
    gdn_p = cat("gdn_p", None)[None]
    gdn_s = cat("gdn_s", 0)[None]
    ret_p = cat("ret_p", None)[None]
    ret_s = cat("ret_s", 0)[None]
    return (y_prompt, y_sample, pool_p, pool_s, conv_p, conv_s, gdn_p, gdn_s, ret_p, ret_s)
```
